# Optimizing a Trainium2 kernel written in Bass

```python
import jax, jax.numpy as jnp
from jax import lax
import numpy as np

D_MODEL = 1024
BATCH = 8
SEQ = 8192
DEPTH = 1
DEC_BATCH = 8
DEC_SEQ = 4096
PAST_LEN = 128

D_MIX = 2 * D_MODEL
D_POOL = D_MIX // 2
D_RET = D_MIX - D_POOL
POOL_WINDOWS = (2, 4, 8, 16)
N_POOL_GROUPS = len(POOL_WINDOWS)
POOL_GROUP = D_POOL // N_POOL_GROUPS
N_RET_HEADS = 8
RET_HEAD_DIM = D_RET // N_RET_HEADS
CHUNK = 128
ROPE_BASE = 10000.0
NORM_EPS = 1e-6
D_IN = D_POOL + 3 * D_RET + D_MIX

kernel_name = "hybrid_pool_retention_encoder"


def rms_norm(x, g):
    xf = x.astype(jnp.float32)
    y = xf * lax.rsqrt(jnp.mean(xf * xf, axis=-1, keepdims=True) + NORM_EPS)
    return (y * g.astype(jnp.float32)).astype(x.dtype)


def multiscale_pool(u, w_grp, scale):
    B, S, _ = u.shape
    uf = u.astype(jnp.float32).reshape(B, S, N_POOL_GROUPS, POOL_GROUP)
    cs = jnp.concatenate([jnp.zeros((B, 1, N_POOL_GROUPS, POOL_GROUP), jnp.float32),
                          jnp.cumsum(uf, axis=1)], axis=1)
    t = np.arange(S)
    outs = []
    for g, w in enumerate(POOL_WINDOWS):
        lo = np.clip(t - w // 2, 0, S)
        hi = np.clip(t + w // 2, 0, S)
        csg = cs[:, :, g]
        win_sum = jnp.take(csg, hi, axis=1) - jnp.take(csg, lo, axis=1)
        mean = win_sum / jnp.asarray(hi - lo, jnp.float32)[None, :, None]
        outs.append(mean - uf[:, :, g])
    p = jnp.stack(outs, axis=2)
    p = jnp.einsum('bsgc,gcd->bsgd', p, w_grp.astype(jnp.float32)).reshape(B, S, D_POOL)
    return p * scale.astype(jnp.float32)


def rope(x, pos):
    half = x.shape[-1] // 2
    freqs = ROPE_BASE ** (-jnp.arange(half, dtype=jnp.float32) / half)
    ang = pos[:, None] * freqs[None, :]
    cos, sin = jnp.cos(ang), jnp.sin(ang)
    x1, x2 = x[..., :half], x[..., half:]
    return jnp.concatenate([x1 * cos - x2 * sin, x2 * cos + x1 * sin], axis=-1)


def log_decay(a):
    return jnp.log1p(-jnp.exp2(-a.astype(jnp.float32)))


def chunk_retention(q, k, v, log_gamma, strict):
    B, H, S, d = q.shape
    n = S // CHUNK
    to_chunks = lambda t: jnp.moveaxis(t.reshape(B, H, n, CHUNK, d), 2, 0)
    idx = jnp.arange(CHUNK, dtype=jnp.float32)
    diff = idx[:, None] - idx[None, :]
    mask = diff > 0 if strict else diff >= 0
    lg = log_gamma[:, None, None]
    D = jnp.where(mask[None], jnp.exp(jnp.where(mask, diff, 0.0)[None] * lg), 0.0)
    q_decay = jnp.exp((idx + 1.0)[None, :] * log_gamma[:, None])[..., None]
    k_decay = jnp.exp((CHUNK - 1.0 - idx)[None, :] * log_gamma[:, None])[..., None]
    chunk_decay = jnp.exp(CHUNK * log_gamma)[:, None, None]

    def step(state, inp):
        qc, kc, vc = inp
        inner = jnp.einsum('bhid,bhjd->bhij', qc, kc) * D
        o = (jnp.einsum('bhij,bhjv->bhiv', inner, vc)
             + jnp.einsum('bhid,bhdv->bhiv', qc * q_decay, state))
        state = state * chunk_decay + jnp.einsum('bhjd,bhjv->bhdv', kc * k_decay, vc)
        return state, o

    state0 = jnp.zeros((B, H, d, d), jnp.float32)
    _, o = lax.scan(step, state0, (to_chunks(q), to_chunks(k), to_chunks(v)))
    return jnp.moveaxis(o, 0, 2).reshape(B, H, S, d)


def bidir_retention(q, k, v, dec_f, dec_b):
    B, S, _ = q.shape
    heads = lambda t: t.reshape(B, S, N_RET_HEADS, RET_HEAD_DIM).transpose(0, 2, 1, 3).astype(jnp.float32)
    pos = jnp.arange(S, dtype=jnp.float32)
    qh = rope(heads(q), pos)
    kh = rope(heads(k), pos) * (RET_HEAD_DIM ** -0.5)
    vh = heads(v)
    rev = lambda t: jnp.flip(t, axis=2)
    o_f = chunk_retention(qh, kh, vh, log_decay(dec_f), strict=False)
    o_b = rev(chunk_retention(rev(qh), rev(kh), rev(vh), log_decay(dec_b), strict=True))
    o = o_f + o_b
    mu = jnp.mean(o, axis=-1, keepdims=True)
    var = jnp.mean(jnp.square(o - mu), axis=-1, keepdims=True)
    o = (o - mu) * lax.rsqrt(var + NORM_EPS)
    return o.transpose(0, 2, 1, 3).reshape(B, S, D_RET)


def encoder_layer(x, c, ada_w, ada_b, g_pre, g_post, w_in, pool_w, pool_scale, dec_f, dec_b, w_out):
    mod = jnp.einsum('bd,de->be', jax.nn.silu(c), ada_w) + ada_b
    shift, scale, gate = jnp.split(mod, 3, axis=-1)
    h = rms_norm(x, g_pre) * (1.0 + scale[:, None, :]) + shift[:, None, :]
    proj = jnp.einsum('bsd,de->bse', h, w_in)
    u_pool, q, k, v, z = jnp.split(
        proj, [D_POOL, D_POOL + D_RET, D_POOL + 2 * D_RET, D_POOL + 3 * D_RET], axis=-1)
    y_pool = multiscale_pool(u_pool, pool_w, pool_scale)
    y_ret = bidir_retention(q, k, v, dec_f, dec_b)
    y = jnp.concatenate([y_pool, y_ret], axis=-1) * jax.nn.silu(z.astype(jnp.float32))
    out = jnp.einsum('bse,ed->bsd', y.astype(x.dtype), w_out)
    return x + gate[:, None, :] * rms_norm(out, g_post)


def setup_inputs(seed: int = 0) -> dict:
    key = jax.random.key(seed)
    ks = jax.random.split(key, 16)
    f32 = jnp.float32
    nrm = lambda k, shape, s: jax.random.normal(k, shape, f32) * s
    base_decay = 5.0 + jnp.arange(N_RET_HEADS, dtype=f32)
    return {
        "x_prompt": nrm(ks[0], (BATCH, SEQ, D_MODEL), 1.0),
        "x_sample": nrm(ks[1], (DEC_BATCH, DEC_SEQ, D_MODEL), 1.0),
        "c_prompt": nrm(ks[2], (BATCH, D_MODEL), 1.0),
        "c_sample": nrm(ks[3], (DEC_BATCH, D_MODEL), 1.0),
        "ada_w": nrm(ks[4], (DEPTH, D_MODEL, 3 * D_MODEL), D_MODEL ** -0.5),
        "ada_b": nrm(ks[5], (DEPTH, 3 * D_MODEL), 0.02),
        "norm_pre": 1.0 + nrm(ks[6], (DEPTH, D_MODEL), 0.02),
        "norm_post": 1.0 + nrm(ks[7], (DEPTH, D_MODEL), 0.02),
        "w_in": nrm(ks[8], (DEPTH, D_MODEL, D_IN), D_MODEL ** -0.5),
        "pool_w": nrm(ks[9], (DEPTH, N_POOL_GROUPS, POOL_GROUP, POOL_GROUP), POOL_GROUP ** -0.5),
        "pool_scale": 1.0 + nrm(ks[10], (DEPTH, D_POOL), 0.02),
        "ret_decay_fwd": base_decay[None, :] + nrm(ks[11], (DEPTH, N_RET_HEADS), 0.1),
        "ret_decay_bwd": base_decay[None, :] + nrm(ks[12], (DEPTH, N_RET_HEADS), 0.1),
        "w_out": nrm(ks[13], (DEPTH, D_MIX, D_MODEL), D_MIX ** -0.5),
    }


def reference(x_prompt, x_sample, c_prompt, c_sample, ada_w, ada_b, norm_pre, norm_post,
              w_in, pool_w, pool_scale, ret_decay_fwd, ret_decay_bwd, w_out):
    y_prompt = x_prompt
    y_sample = x_sample
    for l in range(DEPTH):
        params = (ada_w[l], ada_b[l], norm_pre[l], norm_post[l], w_in[l], pool_w[l],
                  pool_scale[l], ret_decay_fwd[l], ret_decay_bwd[l], w_out[l])
        y_prompt = encoder_layer(y_prompt, c_prompt, *params)
        y_sample = encoder_layer(y_sample, c_sample, *params)
    return (y_prompt, y_sample)
```

```python
import math
import numpy as np
import concourse.bass as bass
import concourse.mybir as mybir
from concourse.bass_utils import run_bass_kernel_spmd

F32 = mybir.dt.float32
BF16 = mybir.dt.bfloat16
AF = mybir.ActivationFunctionType
ALU = mybir.AluOpType
AX = mybir.AxisListType

D = 1024
H = 8
HD = 128
EPS = 1e-6
POOL_W = (2, 4, 8, 16)
N_CORES = 8
SAME_ENGINE_SYNC = True
SAME_ENGINE_RAW_ONLY = True
CW1 = 6.28125
CW2 = 2.0 * math.pi - 6.28125
PI_SAFE = 3.1415925
HALF_PI_SAFE = 1.5707962


class Op:
    __slots__ = ("idx", "eng", "seng", "fn", "deps", "kind", "inc", "sem", "val", "cost", "lat", "phase",
                 "start", "finish", "key", "meta", "vbs", "prio", "raw")


class Buf:
    __slots__ = ("name", "w", "r", "excl", "phys", "users", "virt", "disjoint")

    def __init__(self, name, excl=False, virt=False, disjoint=True):
        self.name = name
        self.disjoint = disjoint
        self.w = None
        self.r = []
        self.excl = excl
        self.virt = virt
        self.phys = None
        self.users = []


class _Dummy:
    def then_inc(self, *a, **k):
        return self


class _Probe:
    def __init__(self):
        self.calls = []

    def __getattr__(self, name):
        def f(*a, **k):
            self.calls.append((name, a, k))
            return _Dummy()
        return f


def _free(ap):
    try:
        return int(np.prod(ap.shape[1:]))
    except Exception:
        return 1


def _est_cost(eng, calls):
    t = 0.0
    for name, a, k in calls:
        if name == "matmul":
            rhs = a[2] if len(a) > 2 else k["rhs"]
            t += 0.023 + 0.00044 * _free(rhs)
        elif name == "transpose":
            t += 0.08
        else:
            aps = [k.get(n) for n in ("out", "in_", "in0")] + list(a[:1])
            F = max([_free(x) for x in aps if x is not None and hasattr(x, "shape")] + [1])
            if eng == "act":
                t += 0.36 + 0.00062 * F
            elif eng == "dve":
                t += 0.2 + 0.00105 * F
            else:
                if k.get("op") == ALU.pow:
                    t += 0.45 + 0.15 * F
                else:
                    t += 0.45 + 0.0018 * F
    return t


class Sched:
    ENGS = ("pe", "act", "dve", "pool", "sp")
    WINDOW = 96
    XLAT = 0.3

    def __init__(self, nc):
        self.nc = nc
        self.all = []
        self.esem = {e: nc.alloc_semaphore("s_" + e) for e in self.ENGS}
        self.dsem = {}
        self.phase = 0
        self.disabled = False
        self.cur_prio = 0

    def _deps(self, r, w, excl_eng):
        deps = []
        self._raw = set()
        self._strong = set()
        for b in r:
            if b.w is not None:
                deps.append(b.w)
                self._raw.add(id(b.w))
            if b.excl:
                deps.extend(o for o in b.r if o.eng != excl_eng)
        for b in w:
            if b.w is not None:
                deps.append(b.w)
                if not b.disjoint:
                    self._strong.add(id(b.w))
            deps.extend(b.r)
            self._strong.update(id(o) for o in b.r)
        seen = set()
        out = []
        for d in deps:
            if id(d) not in seen:
                seen.add(id(d))
                out.append(d)
        return out

    def _new(self, eng, seng, fn, deps, kind, inc, cost, lat, key=None):
        o = Op()
        o.idx = len(self.all)
        o.eng, o.seng, o.fn, o.deps, o.kind, o.inc = eng, seng, fn, deps, kind, inc
        o.cost, o.lat, o.phase, o.key = cost, lat, self.phase, key
        o.sem = o.val = o.start = o.finish = None
        o.meta = ([], [])
        o.vbs = []
        o.raw = set()
        o.prio = 0 if seng == "pe" else self.cur_prio
        self.all.append(o)
        return o

    def op(self, eng, fn, r=(), w=(), sig=True):
        if self.disabled:
            return
        deps = self._deps(r, w, eng)
        pr = _Probe()
        fn(pr)
        cost = _est_cost(eng, pr.calls)
        o = self._new(eng, eng, fn, deps, "op", 1, cost, cost)
        o.raw = self._raw | self._strong
        o.meta = ([b.name for b in r], [b.name for b in w])
        for b in list(r) + list(w):
            if b.virt and o not in b.users:
                b.users.append(o)
                o.vbs.append(b)
        for b in r:
            b.r.append(o)
        for b in w:
            b.w = o
            b.r = []

    def dma(self, q, out_ap, in_ap, r=(), w=(), key=None, slow=False):
        if self.disabled:
            return None
        if key is None:
            key = w[0]
        if key not in self.dsem:
            self.dsem[key] = [self.nc.alloc_semaphore("d_" + key.name), None]
        ent = self.dsem[key]
        deps = self._deps(r, w, "dma")
        if ent[1] is not None and ent[1] not in deps:
            deps.append(ent[1])
        nc = self.nc

        def fn(e, out_ap=out_ap, in_ap=in_ap, slow=slow):
            if slow:
                with nc.allow_non_contiguous_dma(reason="one-time small strided load"):
                    return e.dma_start(out=out_ap, in_=in_ap)
            return e.dma_start(out=out_ap, in_=in_ap)

        nbytes = int(np.prod(out_ap.shape)) * 4
        issue = 0.4 if q == "sp" else 1.5
        o = self._new("dma", q, fn, deps, "dma", 16, issue, issue + 2.0 + nbytes / 150e3, key=key)
        ent[1] = o
        for b in r:
            b.r.append(o)
        for b in w:
            b.w = o
            b.r = []
        return o

    def barrier(self, bufs):
        if self.disabled:
            return
        evs = []
        for b in bufs:
            if b.w is not None:
                evs.append(b.w)
            evs.extend(b.r)
        self.phase += 1
        for e in self.ENGS:
            self._new(e, e, None, list(evs), "bar", 0, 0.0, 0.0)
        self.phase += 1

    def finish(self):
        self.phase += 1
        last = [ent[1] for ent in self.dsem.values() if ent[1] is not None]
        self._new("sp", "sp", None, last, "bar", 0, 0.0, 0.0)
        self._schedule()

    def _schedule(self):
        free = {e: 0.0 for e in self.ENGS}
        order = {e: [] for e in self.ENGS}
        tenant = [None] * 8
        npend = {}
        nph = self.phase + 1
        byph = [dict((e, []) for e in self.ENGS) for _ in range(nph)]
        for o in self.all:
            byph[o.phase][o.seng].append(o)
        if BL_PRIO:
            succ = {}
            for o in self.all:
                for d in o.deps:
                    succ.setdefault(id(d), []).append(o)
            bl = {}
            for o in reversed(self.all):
                m = 0.0
                for q in succ.get(id(o), ()):
                    v = bl[id(q)] + (0.0 if q.seng == o.seng else self.XLAT)
                    if v > m:
                        m = v
                bl[id(o)] = m + o.lat
            for o in self.all:
                if o.phase in BL_PHASES:
                    o.prio = -bl[id(o)] * BL_SCALE
        for ph in range(nph):
            uns = byph[ph]
            for e in self.ENGS:
                if BL_PRIO and ph in BL_PHASES:
                    uns[e].sort(key=lambda o: (o.prio, o.idx))
                else:
                    uns[e].sort(key=lambda o: (o.idx + o.prio, o.idx))
            remaining = sum(len(v) for v in uns.values())
            wide = False
            while remaining:
                best = None
                for e in self.ENGS:
                    lst = uns[e]
                    cb = None
                    for o in (lst if wide else lst[:self.WINDOW]):
                        ready = 0.0
                        ok = True
                        for d in o.deps:
                            if d.finish is None:
                                ok = False
                                break
                            rr = d.finish + (0.0 if d.seng == e and d.kind != "dma" else self.XLAT)
                            if rr > ready:
                                ready = rr
                        if not ok:
                            continue
                        need_bank = [v for v in o.vbs if v.phys is None]
                        if need_bank:
                            cands = []
                            for p in range(8):
                                tv = tenant[p]
                                if tv is None:
                                    cands.append((0.0, p))
                                elif npend.get(id(tv), len(tv.users)) == 0:
                                    cands.append((max(u.finish for u in tv.users) + self.XLAT, p))
                            if len(cands) < len(need_bank):
                                continue
                            cands.sort()
                            bank_rdy = cands[len(need_bank) - 1][0]
                            if bank_rdy > ready:
                                ready = bank_rdy
                            o_banks = [p for _, p in cands[:len(need_bank)]]
                        else:
                            o_banks = None
                        st = ready if ready > free[e] else free[e]
                        if cb is None or st < cb[0]:
                            cb = (st, o, o_banks)
                        if st <= free[e]:
                            break
                    if cb is not None and (best is None or (cb[0], (cb[1].prio if (BL_PRIO and ph in BL_PHASES) else cb[1].idx + cb[1].prio)) < (best[0], (best[1].prio if (BL_PRIO and ph in BL_PHASES) else best[1].idx + best[1].prio))):
                        best = cb
                if best is None:
                    if not wide:
                        wide = True
                        continue
                    raise RuntimeError("scheduler stuck (PSUM bank deadlock)")
                wide = False
                st, o, o_banks = best
                if o_banks is not None:
                    need_bank = [v for v in o.vbs if v.phys is None]
                    for v, p in zip(need_bank, o_banks):
                        tv = tenant[p]
                        if tv is not None:
                            for u in tv.users:
                                if u not in o.deps:
                                    o.deps.append(u)
                        tenant[p] = v
                        v.phys = p
                for v in o.vbs:
                    npend[id(v)] = npend.get(id(v), len(v.users)) - 1
                o.start = st
                o.finish = st + o.lat
                free[o.seng] = st + o.cost
                uns[o.seng].remove(o)
                order[o.seng].append(o)
                remaining -= 1
        self.order = order
        self.model_us = max(free.values())
        cnt = {e: 0 for e in self.ENGS}
        dcnt = {}
        for e in self.ENGS:
            for o in order[e]:
                if o.kind == "op":
                    cnt[e] += 1
                    o.sem, o.val = self.esem[e], cnt[e]
        for e in self.ENGS:
            for o in order[e]:
                if o.kind == "dma":
                    k = id(o.key)
                    dcnt[k] = dcnt.get(k, 0) + 16
                    o.sem, o.val = self.dsem[o.key][0], dcnt[k]

    def emit(self, eng_name, e):
        waited = {}
        for o in self.order[eng_name]:
            need = {}
            for d in o.deps:
                if d.kind == "bar":
                    continue
                if d.kind == "op" and d.seng == eng_name and o.kind != "dma":
                    if eng_name == "pe" or not SAME_ENGINE_SYNC:
                        continue
                    if SAME_ENGINE_RAW_ONLY and id(d) not in o.raw:
                        continue
                assert d.val is not None
                k = id(d.sem)
                if k not in need or need[k][1] < d.val:
                    need[k] = (d.sem, d.val)
            for k, (sem, v) in need.items():
                if waited.get(k, 0) >= v:
                    continue
                e.wait_ge(sem, v)
                waited[k] = v
            if o.fn is None:
                continue
            inst = o.fn(e)
            inst.then_inc(o.sem, o.inc)


STAGE = 99
CHECK_SBUF = True
NPAIRS = 4
HT_ACT_A = False
FRONT_PRIO = 0
KB_ENG = "dve"
FMUL_ENG = "dve"
KF_ENG = "dve"
QTB_ENG = "dve"
FINAL_PRIO = 100
BL_PRIO = True
BL_SCALE = 1.0
BL_PHASES = (4,)
ROPE_ENG_A = "dve"
FIN_ENG = "dve"
NB = {"xres": 2}
SUB = 9
SUB2 = 9


def build_program(S_list):
    nseq = len(S_list)
    assert nseq == 2
    nchs = [s // 128 for s in S_list]
    Smax = max(S_list)
    nchmax = Smax // 128
    nc = bass.Bass("TRN2", target_bir_lowering=False)

    xs = [nc.dram_tensor(f"x{i}", [S_list[i], D], F32, kind="ExternalInput") for i in range(nseq)]
    ys = [nc.dram_tensor(f"y{i}", [S_list[i], D], F32, kind="ExternalOutput") for i in range(nseq)]
    cvec = nc.dram_tensor("cvec", [nseq, D], F32, kind="ExternalInput")
    ada_w = nc.dram_tensor("ada_w", [D, 3 * D], F32, kind="ExternalInput")
    ada_b = nc.dram_tensor("ada_b", [1, 3 * D], F32, kind="ExternalInput")
    norm_pre = nc.dram_tensor("norm_pre", [1, D], F32, kind="ExternalInput")
    norm_post = nc.dram_tensor("norm_post", [1, D], F32, kind="ExternalInput")
    w_in = nc.dram_tensor("w_in", [D, 6 * D], F32, kind="ExternalInput")
    pool_w = nc.dram_tensor("pool_w", [4, 256, 256], F32, kind="ExternalInput")
    pool_scale = nc.dram_tensor("pool_scale", [1, D], F32, kind="ExternalInput")
    dec_f = nc.dram_tensor("dec_f", [1, H], F32, kind="ExternalInput")
    dec_b = nc.dram_tensor("dec_b", [1, H], F32, kind="ExternalInput")
    w_out = nc.dram_tensor("w_out", [2 * D, D], F32, kind="ExternalInput")
    krscr = [nc.dram_tensor(f"krscr{i}", [S_list[i], D], BF16, kind="Internal") for i in range(nseq)]
    vscr = [nc.dram_tensor(f"vscr{i}", [S_list[i], D], BF16, kind="Internal") for i in range(nseq)]
    bscr = [nc.dram_tensor(f"bscr{i}", [S_list[i], D], BF16, kind="Internal") for i in range(nseq)]
    ropescr = nc.dram_tensor("ropescr", [Smax, 128], F32, kind="Internal")
    ggscr = nc.dram_tensor("ggscr", [nseq * 128, D], F32, kind="Internal")

    sch = Sched(nc)

    sb_lo = (nc.sbuf_base + 63) // 64 * 64
    sb_hi = nc.sbuf_top
    cur = [sb_lo]
    names = [0]
    sbuf_peak = [0]

    def alloc(shape, dt, at=None, name=None):
        nbytes = int(np.prod(shape[1:])) * (2 if dt == BF16 else 4)
        nbytes = (nbytes + 63) // 64 * 64
        if at is None:
            off = cur[0]
            cur[0] += nbytes
        else:
            off = at[0]
            at[0] += nbytes
        names[0] += 1
        if CHECK_SBUF:
            assert off + nbytes <= sb_hi, f"SBUF overflow at {name}: {off + nbytes} > {sb_hi}"
        sbuf_peak[0] = max(sbuf_peak[0], off + nbytes)
        if not CHECK_SBUF and off + nbytes > sb_hi:
            off = sb_lo
        return nc.alloc_sbuf_tensor_at(f"t{names[0]}_{name or ''}", list(shape), dt, offset=off)

    wq_sb = alloc([128, 8, 4096], BF16, name="wq")
    wout_sb = alloc([128, 16, 1024], BF16, name="wout")
    poolw_sb = alloc([128, 8, 256], BF16, name="poolw")
    ident = alloc([128, 128], BF16, name="ident")
    dtot = alloc([128, 8, 128], F32, name="dtot")
    af_tab = alloc([128, 8, 128], BF16, name="aftab")
    ab_tab = alloc([128, 8, 128], BF16, name="abtab")
    bands = alloc([128, 20, 128], BF16, name="bands")
    kdf = alloc([128, 8], F32, name="kdf")
    kdb = alloc([128, 8], F32, name="kdb")
    gcf = alloc([128, 8], F32, name="gcf")
    gcb = alloc([128, 8], F32, name="gcb")
    gprime = alloc([128, 8, 2], F32, name="gprime")
    shiftT = alloc([128, 8, 2], F32, name="shiftT")
    negpi = alloc([128, 1], F32, name="negpi")
    mhalf = alloc([128, 8], F32, name="mhalf")
    halfpi = alloc([128, 1], F32, name="halfpi")
    B_ = {n: Buf(n) for n in ["wq", "wout", "poolw", "ident", "dtot", "aftab", "abtab", "bands", "kd",
                              "gprime", "negpi", "wkv", "ropescr", "ggscr"]}
    arena0 = cur[0]
    wkv_at = [arena0]
    wkv_sb = alloc([128, 8, 2048], BF16, at=wkv_at, name="wkv")
    arena_after_wkv = wkv_at[0]

    psum = nc.alloc_psum_tensor("psum", [128, 4096], F32)
    vbcount = [0]
    pbank = []

    class PT:
        def __init__(self, vbs):
            self.vbs = vbs

        def _phys(self, i):
            p = self.vbs[i].phys
            return 0 if p is None else p

        def bank(self, i):
            b = self._phys(i)
            return psum[:, b * 512:(b + 1) * 512]

        def __getitem__(self, key):
            rows, cols = key
            a0, a1 = cols.start, cols.stop
            bi = a0 // 512
            assert (a1 - 1) // 512 == bi, (a0, a1)
            b = self._phys(bi)
            return psum[:, b * 512 + (a0 - bi * 512): b * 512 + (a1 - bi * 512)]

    def palloc(nb=2, static=None):
        vbs = []
        for i in range(nb):
            vbcount[0] += 1
            v = Buf(f"pb{vbcount[0]}", excl=True, virt=(static is None))
            if static is not None:
                v.phys = static[i]
            vbs.append(v)
            pbank.append(v)
        return PT(vbs), vbs

    sa = [arena_after_wkv]
    diff = alloc([128, 128], F32, at=sa, name="diff")
    irow = alloc([128, 128], F32, at=sa, name="irow")
    pcol = alloc([128, 1], F32, at=sa, name="pcol")
    p127 = alloc([128, 1], F32, at=sa, name="p127")
    mge = alloc([128, 128], F32, at=sa, name="mge")
    mlt = alloc([128, 128], F32, at=sa, name="mlt")
    rpos = alloc([128, 128], F32, at=sa, name="rpos")
    rneg = alloc([128, 128], F32, at=sa, name="rneg")
    identf = alloc([128, 128], F32, at=sa, name="identf")
    tmpa = alloc([128, 128], F32, at=sa, name="tmpa")
    tmpb = alloc([128, 128], F32, at=sa, name="tmpb")
    tmpc = alloc([128, 128], F32, at=sa, name="tmpc")
    rowp1 = alloc([128, 128], F32, at=sa, name="rowp1")
    row128m = alloc([128, 128], F32, at=sa, name="row128m")
    decf_t = alloc([128, 8], F32, at=sa, name="decf")
    decb_t = alloc([128, 8], F32, at=sa, name="decb")
    lgf = alloc([128, 8], F32, at=sa, name="lgf")
    lgb = alloc([128, 8], F32, at=sa, name="lgb")
    etmp = alloc([128, 8], F32, at=sa, name="etmp")
    Bc = Buf("const")
    grp = {"b": Bc}
    groups = [Bc]

    def G():
        return grp["b"]

    def newgroup(name):
        grp["b"] = Buf(name)
        groups.append(grp["b"])

    g = nc.gpsimd

    w_in_v = w_in.ap().rearrange("(dc p) n -> p dc n", p=128)
    sch.dma("pool", wkv_sb[:, :, :], w_in_v[:, :, 2048:4096], w=[B_["wkv"]])

    sch.op("pool", lambda e: e.iota(diff[:, :], [[1, 128]], base=0, channel_multiplier=-1, allow_small_or_imprecise_dtypes=True), w=[G()])
    sch.op("pool", lambda e: e.iota(irow[:, :], [[1, 128]], base=0, channel_multiplier=0, allow_small_or_imprecise_dtypes=True), w=[G()])
    sch.op("pool", lambda e: e.iota(pcol[:, :], [[1, 1]], base=0, channel_multiplier=1, allow_small_or_imprecise_dtypes=True), w=[G()])
    sch.op("pool", lambda e: e.iota(p127[:, :], [[1, 1]], base=127, channel_multiplier=-1, allow_small_or_imprecise_dtypes=True), w=[G()])
    sch.op("dve", lambda e: e.memset(negpi[:, :], -math.pi), w=[B_["negpi"]])
    sch.op("dve", lambda e: e.memset(mhalf[:, :], -0.5), w=[B_["negpi"]])
    sch.op("dve", lambda e: e.memset(halfpi[:, :], HALF_PI_SAFE), w=[B_["negpi"]])

    def dv(fn, r=None, w=None):
        sch.op("dve", fn, r=[Bc, G()] if r is None else list(r), w=[G()] if w is None else list(w))

    def ac(fn, r=None, w=None):
        sch.op("act", fn, r=[Bc, G()] if r is None else list(r), w=[G()] if w is None else list(w))

    dv(lambda e: e.tensor_single_scalar(out=identf[:, :], in_=diff[:, :], scalar=0.0, op=ALU.is_equal))
    dv(lambda e: e.tensor_copy(out=ident[:, :], in_=identf[:, :]), w=(G(), B_["ident"]))
    dv(lambda e: e.tensor_single_scalar(out=mge[:, :], in_=diff[:, :], scalar=0.0, op=ALU.is_ge))
    dv(lambda e: e.tensor_single_scalar(out=mlt[:, :], in_=diff[:, :], scalar=0.0, op=ALU.is_lt))
    dv(lambda e: e.tensor_scalar_max(out=rpos[:, :], in0=diff[:, :], scalar1=0.0))
    dv(lambda e: e.tensor_scalar(out=rneg[:, :], in0=diff[:, :], scalar1=-1.0, scalar2=0.0,
                                 op0=ALU.mult, op1=ALU.max))
    dv(lambda e: e.tensor_scalar_add(out=rowp1[:, :], in0=irow[:, :], scalar1=1.0))
    dv(lambda e: e.tensor_scalar(out=row128m[:, :], in0=irow[:, :], scalar1=-1.0, scalar2=128.0,
                                 op0=ALU.mult, op1=ALU.add))

    newgroup("grpD")
    sch.dma("sp", decf_t[:, :], dec_f.ap().partition_broadcast(128).rearrange("p o n -> p (o n)"), w=[G()])
    sch.dma("sp", decb_t[:, :], dec_b.ap().partition_broadcast(128).rearrange("p o n -> p (o n)"), w=[G()])
    for dsrc, lg in ((decf_t, lgf), (decb_t, lgb)):
        ac(lambda e, dsrc=dsrc: e.activation(out=etmp[:, :], in_=dsrc[:, :], func=AF.Exp, scale=-math.log(2.0)))
        dv(lambda e: e.tensor_scalar(out=etmp[:, :], in0=etmp[:, :], scalar1=-1.0, scalar2=1.0,
                                     op0=ALU.mult, op1=ALU.add))
        ac(lambda e, lg=lg: e.activation(out=lg[:, :], in_=etmp[:, :], func=AF.Ln))
    for h in range(H):
        ac(lambda e, h=h: e.activation(out=tmpa[:, :], in_=rpos[:, :], func=AF.Exp, scale=lgf[:, h:h + 1]))
        ac(lambda e, h=h: e.activation(out=tmpb[:, :], in_=rneg[:, :], func=AF.Exp, scale=lgb[:, h:h + 1]))
        dv(lambda e: e.tensor_tensor(out=tmpa[:, :], in0=tmpa[:, :], in1=mge[:, :], op=ALU.mult))
        dv(lambda e: e.tensor_tensor(out=tmpb[:, :], in0=tmpb[:, :], in1=mlt[:, :], op=ALU.mult))
        dv(lambda e, h=h: e.tensor_tensor(out=dtot[:, h, :], in0=tmpa[:, :], in1=tmpb[:, :], op=ALU.add),
           w=(G(), B_["dtot"]))
        ac(lambda e, h=h: e.activation(out=af_tab[:, h, :], in_=rowp1[:, :], func=AF.Exp, scale=lgf[:, h:h + 1]),
           w=(G(), B_["aftab"]))
        ac(lambda e, h=h: e.activation(out=ab_tab[:, h, :], in_=row128m[:, :], func=AF.Exp, scale=lgb[:, h:h + 1]),
           w=(G(), B_["abtab"]))
    ac(lambda e: e.activation(out=kdf[:, :], in_=lgf[:, :], func=AF.Exp, scale=p127[:, 0:1]), w=(G(), B_["kd"]))
    ac(lambda e: e.activation(out=kdb[:, :], in_=lgb[:, :], func=AF.Exp, scale=pcol[:, 0:1]), w=(G(), B_["kd"]))
    ac(lambda e: e.activation(out=gcf[:, :], in_=lgf[:, :], func=AF.Exp, scale=128.0), w=(G(), B_["kd"]))
    ac(lambda e: e.activation(out=gcb[:, :], in_=lgb[:, :], func=AF.Exp, scale=128.0), w=(G(), B_["kd"]))

    newgroup("grpP")
    tmpa2 = alloc([128, 128], F32, at=sa, name="tmpa2")
    tmpb2 = alloc([128, 128], F32, at=sa, name="tmpb2")
    tmpc2 = alloc([128, 128], F32, at=sa, name="tmpc2")
    for gi, w_ in enumerate(POOL_W):
        hw = w_ // 2
        dv(lambda e, hw=hw: e.tensor_single_scalar(out=tmpa2[:, :], in_=diff[:, :], scalar=float(-(hw - 1)), op=ALU.is_ge))
        dv(lambda e, hw=hw: e.tensor_single_scalar(out=tmpb2[:, :], in_=diff[:, :], scalar=float(hw), op=ALU.is_le))
        dv(lambda e: e.tensor_tensor(out=tmpa2[:, :], in0=tmpa2[:, :], in1=tmpb2[:, :], op=ALU.mult))
        dv(lambda e, gi=gi, w_=w_: e.scalar_tensor_tensor(out=bands[:, gi * 5 + 1, :], in0=tmpa2[:, :], scalar=1.0 / w_,
                                                          in1=identf[:, :], op0=ALU.mult, op1=ALU.subtract),
           w=(G(), B_["bands"]))
        dv(lambda e, gi=gi, w_=w_, hw=hw: e.tensor_scalar(out=bands[:, gi * 5 + 0, :], in0=diff[:, :],
                                                          scalar1=float(hw - 128), scalar2=1.0 / w_,
                                                          op0=ALU.is_le, op1=ALU.mult), w=(G(), B_["bands"]))
        dv(lambda e, gi=gi, w_=w_, hw=hw: e.tensor_scalar(out=bands[:, gi * 5 + 2, :], in0=diff[:, :],
                                                          scalar1=float(129 - hw), scalar2=1.0 / w_,
                                                          op0=ALU.is_ge, op1=ALU.mult), w=(G(), B_["bands"]))
        dv(lambda e, w_=w_, hw=hw: e.tensor_scalar(out=tmpb2[:, :], in0=irow[:, :], scalar1=float(hw), scalar2=float(w_),
                                                   op0=ALU.add, op1=ALU.min))
        dv(lambda e: e.reciprocal(out=tmpb2[:, :], in_=tmpb2[:, :]))
        dv(lambda e: e.tensor_tensor(out=tmpc2[:, :], in0=tmpa2[:, :], in1=tmpb2[:, :], op=ALU.mult))
        dv(lambda e, gi=gi: e.tensor_tensor(out=bands[:, gi * 5 + 3, :], in0=tmpc2[:, :], in1=identf[:, :], op=ALU.subtract),
           w=(G(), B_["bands"]))
        dv(lambda e, w_=w_, hw=hw: e.tensor_scalar(out=tmpb2[:, :], in0=irow[:, :], scalar1=-1.0, scalar2=float(128 + hw),
                                                   op0=ALU.mult, op1=ALU.add))
        dv(lambda e, w_=w_: e.tensor_scalar_min(out=tmpb2[:, :], in0=tmpb2[:, :], scalar1=float(w_)))
        dv(lambda e: e.reciprocal(out=tmpb2[:, :], in_=tmpb2[:, :]))
        dv(lambda e: e.tensor_tensor(out=tmpc2[:, :], in0=tmpa2[:, :], in1=tmpb2[:, :], op=ALU.mult))
        dv(lambda e, gi=gi: e.tensor_tensor(out=bands[:, gi * 5 + 4, :], in0=tmpc2[:, :], in1=identf[:, :], op=ALU.subtract),
           w=(G(), B_["bands"]))

    newgroup("grpR")
    invf = alloc([128, 64], F32, at=sa, name="invf")
    ac(lambda e: e.activation(out=invf[:, :], in_=irow[:, 0:64], func=AF.Exp, scale=-math.log(10000.0) / 64.0))
    RC = 8
    posall = alloc([128, RC], F32, at=sa, name="posall")
    ang = alloc([128, RC, 64], F32, at=sa, name="ang")
    marg = alloc([128, RC, 64], F32, at=sa, name="marg")
    rtab = alloc([128, RC, 128], F32, at=sa, name="rtab")
    rred = alloc([128, RC, 64], F32, at=sa, name="rred")
    qint = alloc([128, RC, 64], mybir.dt.int32, at=sa, name="qint")
    Brt = Buf("rtab")
    rope_v = ropescr.ap().rearrange("(c p) n -> p c n", p=128)
    for c0 in range(0, nchmax, RC):
        ncb = min(RC, nchmax - c0)
        sch.op("pool", lambda e, c0=c0: e.iota(posall[:, :], [[128, RC]], base=128 * c0, channel_multiplier=1, allow_small_or_imprecise_dtypes=True),
               r=[G()], w=[G()])
        dv(lambda e: e.tensor_tensor(out=ang[:, :, :], in0=posall[:, :].unsqueeze(2).to_broadcast([128, RC, 64]),
                                     in1=invf[:, :].unsqueeze(1).to_broadcast([128, RC, 64]), op=ALU.mult))
        dv(lambda e: e.tensor_scalar_mul(out=marg[:, :, :], in0=ang[:, :, :], scalar1=1.0 / (2.0 * math.pi)))
        dv(lambda e: e.tensor_copy(out=qint[:, :, :], in_=marg[:, :, :]))
        dv(lambda e: e.tensor_copy(out=marg[:, :, :], in_=qint[:, :, :]))
        dv(lambda e: e.scalar_tensor_tensor(out=rred[:, :, :], in0=marg[:, :, :], scalar=-CW1, in1=ang[:, :, :],
                                            op0=ALU.mult, op1=ALU.add))
        dv(lambda e: e.scalar_tensor_tensor(out=rred[:, :, :], in0=marg[:, :, :], scalar=-CW2, in1=rred[:, :, :],
                                            op0=ALU.mult, op1=ALU.add))
        dv(lambda e: e.tensor_single_scalar(out=marg[:, :, :], in_=rred[:, :, :], scalar=math.pi, op=ALU.is_gt))
        dv(lambda e: e.scalar_tensor_tensor(out=rred[:, :, :], in0=marg[:, :, :], scalar=-2.0 * math.pi, in1=rred[:, :, :],
                                            op0=ALU.mult, op1=ALU.add))
        dv(lambda e: e.tensor_single_scalar(out=marg[:, :, :], in_=rred[:, :, :], scalar=-math.pi, op=ALU.is_lt))
        dv(lambda e: e.scalar_tensor_tensor(out=rred[:, :, :], in0=marg[:, :, :], scalar=2.0 * math.pi, in1=rred[:, :, :],
                                            op0=ALU.mult, op1=ALU.add))
        dv(lambda e: e.tensor_scalar(out=rred[:, :, :], in0=rred[:, :, :], scalar1=-PI_SAFE, scalar2=PI_SAFE,
                                     op0=ALU.max, op1=ALU.min))
        ac(lambda e: e.activation(out=rtab[:, :, 64:128], in_=rred[:, :, :], func=AF.Sin), w=(G(), Brt))
        dv(lambda e: e.scalar_tensor_tensor(out=marg[:, :, :], in0=rred[:, :, :], scalar=-1.0, in1=rred[:, :, :],
                                            op0=ALU.mult, op1=ALU.max))
        ac(lambda e: e.activation(out=rtab[:, :, 0:64], in_=marg[:, :, :], func=AF.Sin, scale=-1.0, bias=halfpi[:, 0:1]),
           r=(Bc, G(), B_["negpi"]), w=(G(), Brt))
        sch.dma("sp", rope_v[:, c0:c0 + ncb, :], rtab[:, 0:ncb, :], r=[Brt], w=[B_["ropescr"]], key=Brt)

    newgroup("grpA")
    adaw_sb = alloc([128, 8, 1024], BF16, at=sa, name="adaw")
    Badaw = Buf("adaw")
    ada_w_v = ada_w.ap().rearrange("(dc p) n -> p dc n", p=128)

    cT = alloc([128, 8, 2], F32, at=sa, name="cT")
    scT = alloc([128, 8, 2], BF16, at=sa, name="scT")
    scb = alloc([128, 2, 8, 128], BF16, at=sa, name="scb")
    adabT = alloc([128, 24], F32, at=sa, name="adabT")
    gpreT = alloc([128, 8], F32, at=sa, name="gpreT")
    modT = alloc([128, 16, 2], F32, at=sa, name="modT")
    rowb = alloc([128, 1024], F32, at=sa, name="rowb")
    gpostb = alloc([128, 1024], F32, at=sa, name="gpostb")
    ggt = alloc([128, 1024], F32, at=sa, name="ggt")
    pstage = alloc([128, 8, 256], F32, at=sa, name="pstage")
    setup_end = sa[0]

    for s in range(nseq):
        sch.dma("sp", cT[:, :, s], cvec.ap()[s].rearrange("(dc p) -> p dc", p=128), w=[G()], slow=True)
    sch.dma("sp", adabT[:, :], ada_b.ap().rearrange("o (fc p) -> p (o fc)", p=128), w=[G()], slow=True)
    sch.dma("sp", gpreT[:, :], norm_pre.ap().rearrange("o (fc p) -> p (o fc)", p=128), w=[G()], slow=True)
    sch.dma("sp", gpostb[:, :], norm_post.ap().partition_broadcast(128).rearrange("p o n -> p (o n)"), w=[G()])
    sch.dma("sp", rowb[:, :], pool_scale.ap().partition_broadcast(128).rearrange("p o n -> p (o n)"), w=[G()])
    sch.dma("sp", pstage[:, :, :], pool_w.ap().rearrange("g (cc p) d -> p (g cc) d", p=128), w=[G()])
    dv(lambda e: e.tensor_tensor(out=poolw_sb[:, :, :].rearrange("p (g c) d -> p g c d", g=4),
                                 in0=pstage[:, :, :].rearrange("p (g c) d -> p g c d", g=4),
                                 in1=rowb[:, :].rearrange("p (g d) -> p g d", g=4).unsqueeze(2).to_broadcast([128, 4, 2, 256]),
                                 op=ALU.mult), w=(G(), B_["poolw"]))
    sch.dma("sp", rowb[:, :], ada_b.ap()[:, 2048:3072].partition_broadcast(128).rearrange("p o n -> p (o n)"), r=[G()], w=[G()])
    ac(lambda e: e.activation(out=scT[:, :, :], in_=cT[:, :, :], func=AF.Silu))
    for s in range(nseq):
        for dc in range(8):
            dv(lambda e, s=s, dc=dc: e.tensor_copy(out=scb[:, s, dc, :], in_=scT[:, dc, s:s + 1].to_broadcast([128, 128])))
    pm, pmb = palloc(2, static=[0, 1])
    pmv = pm[:, 0:32].rearrange("p (fc s) -> p fc s", s=2)
    for piece in range(2):
        sch.dma("pool", adaw_sb[:, :, :], ada_w_v[:, :, piece * 1024:(piece + 1) * 1024], w=[Badaw])

        def mod_mm(e, piece=piece):
            last = None
            for fc in range(8):
                for dc in range(8):
                    last = e.matmul(pmv[:, piece * 8 + fc, :], adaw_sb[:, dc, fc * 128:(fc + 1) * 128], scT[:, dc, :],
                                    start=(dc == 0), stop=(dc == 7))
            return last
        sch.op("pe", mod_mm, r=[G(), Badaw], w=pmb)
    dv(lambda e: e.tensor_tensor(out=modT[:, :, :], in0=pmv, in1=adabT[:, 0:16].unsqueeze(2).to_broadcast([128, 16, 2]),
                                 op=ALU.add), r=[G()] + pmb, w=[G()])
    dv(lambda e: e.tensor_copy(out=shiftT[:, :, :], in_=modT[:, 0:8, :]), w=(G(), B_["gprime"]))
    dv(lambda e: e.scalar_tensor_tensor(out=gprime[:, :, :], in0=modT[:, 8:16, :], scalar=1.0,
                                        in1=gpreT[:, :].unsqueeze(2).to_broadcast([128, 8, 2]),
                                        op0=ALU.add, op1=ALU.mult), w=(G(), B_["gprime"]))
    dv(lambda e: e.tensor_scalar_mul(out=gprime[:, :, :], in0=gprime[:, :, :], scalar1=float(D) ** 0.5), w=(G(), B_["gprime"]))
    sch.dma("pool", adaw_sb[:, :, :], ada_w_v[:, :, 2048:3072], w=[Badaw])
    gg_v = ggscr.ap()
    for s in range(nseq):
        pg, pgb = palloc(2, static=[2 + 2 * s, 3 + 2 * s])

        def gate_mm(e, s=s, pg=pg):
            last = None
            for half in range(2):
                for dc in range(8):
                    last = e.matmul(pg[:, half * 512:(half + 1) * 512], scb[:, s, dc, :],
                                    adaw_sb[:, dc, half * 512:(half + 1) * 512],
                                    start=(dc == 0), stop=(dc == 7))
            return last
        sch.op("pe", gate_mm, r=[G(), Badaw], w=pgb)
        for half in range(2):
            dv(lambda e, pg=pg, half=half: e.tensor_tensor(out=ggt[:, half * 512:(half + 1) * 512], in0=pg[:, half * 512:(half + 1) * 512],
                                                         in1=rowb[:, half * 512:(half + 1) * 512], op=ALU.add), r=[G()] + pgb, w=[G()])
        dv(lambda e: e.tensor_tensor(out=ggt[:, :], in0=ggt[:, :], in1=gpostb[:, :], op=ALU.mult))
        sch.dma("sp", gg_v[s * 128:(s + 1) * 128, :], ggt[:, :], r=[G()], w=[B_["ggscr"]], key=G())
    sch.dma("pool", wq_sb[:, :, 0:2048], w_in_v[:, :, 0:2048], w=[B_["wq"]])
    sch.dma("pool", wq_sb[:, :, 2048:4096], w_in_v[:, :, 4096:6144], w=[B_["wq"]])
    sch.dma("pool", wout_sb[:, :, :], w_out.ap().rearrange("(ec p) n -> p ec n", p=128), w=[B_["wout"]])

    sch.disabled = STAGE < 2
    sch.barrier(groups + [Brt, Badaw] + pbank)

    act_at = [arena_after_wkv]

    def mk(shape, dt, name, n=1):
        ts = [alloc(shape, dt, at=act_at, name=f"{name}{i}") for i in range(n)]
        bs = [Buf(f"{name}{i}") for i in range(n)]
        return ts, bs

    xin, xin_b = mk([128, 1024], F32, "xin", 2)
    junk = alloc([128, 1024], BF16, at=act_at, name="junk")
    junk_b = Buf("junk", disjoint=False)
    xn, xn_b = mk([128, 1024], BF16, "xn", 1)
    hT, hT_b = mk([128, 8, 128], BF16, "hT", 3)
    ssq, ssq_b = mk([128, 4], F32, "ssq", 2)
    rt, rt_b = mk([128, 128], F32, "rt", 2)
    t1, t1_b = mk([128, 512], F32, "t1", 1)
    t2, t2_b = mk([128, 512], F32, "t2", 1)
    common_end = act_at[0]

    _aux = {}

    def aux(b, tag):
        k = (id(b), tag)
        if k not in _aux:
            _aux[k] = Buf(b.name + tag)
        return _aux[k]

    def hTr(hs):
        return [aux(hT_b[hs], "a"), aux(hT_b[hs], "b")]

    def front(seq, c, hslot, xslot, ht_act=True):
        sch.cur_prio = FRONT_PRIO
        xt, xb = xin[xslot], xin_b[xslot]
        sq, sqb = ssq[xslot], ssq_b[xslot]
        sch.dma("sp", xt[:, :], xs[seq].ap()[c * 128:(c + 1) * 128, :], w=[xb])
        sch.op("act", lambda e: e.activation(out=junk[:, :], in_=xt[:, :], func=AF.Square, accum_out=sq[:, 0:1]),
               r=[xb], w=[sqb, junk_b])
        sch.op("pool", lambda e: e.tensor_scalar_add(out=sq[:, 1:2], in0=sq[:, 0:1], scalar1=float(D) * EPS), r=[sqb], w=[sqb])
        sch.op("pool", lambda e: e.tensor_tensor(out=sq[:, 2:3], in0=sq[:, 1:2], in1=mhalf[:, 0:1], op=ALU.pow),
               r=[sqb], w=[sqb])
        sch.op("act", lambda e: e.activation(out=xn[0][:, :], in_=xt[:, :], func=AF.Copy, scale=sq[:, 2:3]),
               r=[xb, sqb], w=[xn_b[0]])
        pt, ptb = palloc(2)
        ptv = [(lambda i=i: pt.bank(i).bitcast(BF16)[:, 0:512].rearrange("p (dc t) -> p dc t", dc=4)) for i in range(2)]
        for hb in range(2):
            def tr(e, hb=hb):
                last = None
                for d4 in range(4):
                    dc = hb * 4 + d4
                    last = e.transpose(ptv[hb]()[:, d4, :], xn[0][:, dc * 128:(dc + 1) * 128], ident[:, :])
                return last
            sch.op("pe", tr, r=[xn_b[0], B_["ident"]], w=[ptb[hb]])
        for dc in range(8):
            hb, d4 = dc // 4, dc % 4
            if hb == 0 or ht_act:
                sch.op("act", lambda e, dc=dc, d4=d4, hb=hb: e.activation(out=hT[hslot][:, dc, :], in_=ptv[hb]()[:, d4, :], func=AF.Identity,
                                                                          scale=gprime[:, dc, seq:seq + 1], bias=shiftT[:, dc, seq:seq + 1]),
                       r=[ptb[hb], B_["gprime"]], w=[aux(hT_b[hslot], "a" if hb == 0 else "b")])
            else:
                sch.op("dve", lambda e, dc=dc, d4=d4: e.tensor_scalar(out=hT[hslot][:, dc, :], in0=ptv[1]()[:, d4, :],
                                                                      scalar1=gprime[:, dc, seq:seq + 1],
                                                                      scalar2=shiftT[:, dc, seq:seq + 1],
                                                                      op0=ALU.mult, op1=ALU.add),
                       r=[ptb[1], B_["gprime"]], w=[aux(hT_b[hslot], "b")])

    def front_done():
        sch.cur_prio = 0

    def rope(psrc, psrc_b, dst, dst_b, rts, rtb, kscale, ceng="dve"):
        cosb = rts[:, 0:64].unsqueeze(1).to_broadcast([128, 8, 64])
        sinb = rts[:, 64:128].unsqueeze(1).to_broadcast([128, 4, 64])
        for half in range(2):
            src = lambda half=half: psrc[:, half * 512:(half + 1) * 512].rearrange("p (h two d) -> p h two d", h=4, two=2)
            t1v = t1[0][:, :].rearrange("p (h two d) -> p h two d", h=4, two=2)
            t2v = t2[0][:, :].rearrange("p (h two d) -> p h two d", h=4, two=2)
            dv_ = dst[:, half * 512:(half + 1) * 512].rearrange("p (h two d) -> p h two d", h=4, two=2)
            pb = [psrc_b[half]]
            src3 = lambda half=half: psrc[:, half * 512:(half + 1) * 512].rearrange("p (g d) -> p g d", g=8)
            t1v3 = t1[0][:, :].rearrange("p (g d) -> p g d", g=8)
            if kscale is None:
                sch.op("dve", lambda e, src3=src3, t1v3=t1v3: e.tensor_tensor(out=t1v3, in0=src3(), in1=cosb, op=ALU.mult),
                       r=pb + [rtb], w=[t1_b[0]])
                sch.op("dve", lambda e, src=src, t2v=t2v: e.tensor_tensor(out=t2v[:, :, 0, :], in0=src()[:, :, 1, :], in1=sinb, op=ALU.mult),
                       r=pb + [rtb], w=[t2_b[0]])
                sch.op("dve", lambda e, src=src, t2v=t2v: e.tensor_tensor(out=t2v[:, :, 1, :], in0=src()[:, :, 0, :], in1=sinb, op=ALU.mult),
                       r=pb + [rtb], w=[t2_b[0]])
            else:
                sch.op("dve", lambda e, src3=src3, t1v3=t1v3: e.scalar_tensor_tensor(out=t1v3, in0=src3(), scalar=kscale, in1=cosb,
                                                                                 op0=ALU.mult, op1=ALU.mult),
                       r=pb + [rtb], w=[t1_b[0]])
                sch.op("dve", lambda e, src=src, t2v=t2v: e.scalar_tensor_tensor(out=t2v[:, :, 0, :], in0=src()[:, :, 1, :], scalar=kscale,
                                                                                 in1=sinb, op0=ALU.mult, op1=ALU.mult),
                       r=pb + [rtb], w=[t2_b[0]])
                sch.op("dve", lambda e, src=src, t2v=t2v: e.scalar_tensor_tensor(out=t2v[:, :, 1, :], in0=src()[:, :, 0, :], scalar=kscale,
                                                                                 in1=sinb, op0=ALU.mult, op1=ALU.mult),
                       r=pb + [rtb], w=[t2_b[0]])
            sch.op(ceng, lambda e, t1v=t1v, t2v=t2v, dv_=dv_: e.tensor_tensor(out=dv_[:, :, 0, :], in0=t1v[:, :, 0, :], in1=t2v[:, :, 0, :],
                                                                                op=ALU.subtract),
                   r=[t1_b[0], t2_b[0]], w=[dst_b])
            sch.op(ceng, lambda e, t1v=t1v, t2v=t2v, dv_=dv_: e.tensor_tensor(out=dv_[:, :, 1, :], in0=t1v[:, :, 1, :], in1=t2v[:, :, 1, :],
                                                                                op=ALU.add),
                   r=[t1_b[0], t2_b[0]], w=[dst_b])

    def bc_h(tab):
        return tab[:, :].unsqueeze(2).to_broadcast([128, 8, 128])

    def v3(t):
        return t.rearrange("p (h d) -> p h d", h=8)

    pa_at = [common_end]
    krA, krA_b = [], []
    vA, vA_b = [], []
    for i in range(2):
        krA.append(alloc([128, 1024], BF16, at=pa_at, name=f"krA{i}")); krA_b.append(Buf(f"krA{i}"))
        vA.append(alloc([128, 1024], BF16, at=pa_at, name=f"vA{i}")); vA_b.append(Buf(f"vA{i}"))
    kbA = alloc([128, 1024], BF16, at=pa_at, name="kbA"); kbA_b = Buf("kbA")
    Bf32 = alloc([128, 1024], F32, at=pa_at, name="Bf32"); Bf32_b = Buf("Bf32")
    BbfA, BbfA_b = [], []
    for i in range(2):
        BbfA.append(alloc([128, 1024], BF16, at=pa_at, name=f"BbfA{i}")); BbfA_b.append(Buf(f"BbfA{i}"))
    scrB = {("kr", i): Buf(f"krscr{i}") for i in range(nseq)}
    scrB.update({("v", i): Buf(f"vscr{i}") for i in range(nseq)})
    scrB.update({("b", i): Buf(f"bscr{i}") for i in range(nseq)})

    itA = 0
    for seq in range(nseq):
        n = nchs[seq]
        for idx, c in enumerate(range(n - 1, -1, -1)):
            sl = itA % 2
            itA += 1
            hs = itA % 3
            front(seq, c, hs, sl, ht_act=HT_ACT_A)
            front_done()
            sch.dma("sp", rt[sl][:, :], ropescr.ap()[c * 128:(c + 1) * 128, :], r=[B_["ropescr"]], w=[rt_b[sl]])
            pk, pkb = palloc(2)
            pv, pvb = palloc(2)
            for bi, (pp, ppb) in enumerate(((pk, pkb), (pv, pvb))):
                for half in range(2):
                    col0 = bi * 1024 + half * 512

                    def mm(e, pp=pp, half=half, col0=col0, hs=hs):
                        last = None
                        for dc in range(8):
                            last = e.matmul(pp[:, half * 512:(half + 1) * 512], hT[hs][:, dc, :], wkv_sb[:, dc, col0:col0 + 512],
                                            start=(dc == 0), stop=(dc == 7))
                        return last
                    sch.op("pe", mm, r=hTr(hs) + [B_["wkv"]], w=[ppb[half]])
            rope(pk, pkb, krA[sl], krA_b[sl], rt[sl], rt_b[sl], HD ** -0.5, ROPE_ENG_A)
            for half in range(2):
                sch.op("act", lambda e, half=half, pv=pv, sl=sl: e.copy(out=vA[sl][:, half * 512:(half + 1) * 512],
                                                                        in_=pv[:, half * 512:(half + 1) * 512]),
                       r=[pvb[half]], w=[vA_b[sl]])
            sch.dma("sp", krscr[seq].ap()[c * 128:(c + 1) * 128, :], krA[sl][:, :], r=[krA_b[sl]], w=[scrB[("kr", seq)]], key=krA_b[sl])
            sch.dma("sp", vscr[seq].ap()[c * 128:(c + 1) * 128, :], vA[sl][:, :], r=[vA_b[sl]], w=[scrB[("v", seq)]], key=vA_b[sl])
            if idx == 0:
                sch.op("pool", lambda e, sl=sl: e.memset(BbfA[sl][:, :], 0.0), w=[BbfA_b[sl]])
            sch.dma("sp", bscr[seq].ap()[c * 128:(c + 1) * 128, :], BbfA[sl][:, :], r=[BbfA_b[sl]], w=[scrB[("b", seq)]], key=BbfA_b[sl])
            if c == 0:
                continue
            sch.op(KB_ENG, lambda e, sl=sl: e.tensor_tensor(out=v3(kbA[:, :]), in0=v3(krA[sl][:, :]), in1=bc_h(kdb), op=ALU.mult),
                   r=[krA_b[sl], B_["kd"]], w=[kbA_b])
            pkv, pkvb = palloc(2)
            for half in range(2):
                def kvmm(e, half=half, pkv=pkv, sl=sl):
                    last = None
                    for hh in range(4):
                        h = half * 4 + hh
                        last = e.matmul(pkv[:, h * 128:(h + 1) * 128], kbA[:, h * 128:(h + 1) * 128], vA[sl][:, h * 128:(h + 1) * 128],
                                        start=True, stop=True)
                    return last
                sch.op("pe", kvmm, r=[kbA_b, vA_b[sl]], w=[pkvb[half]])
            ns = 1 - sl
            if idx == 0:
                for half in range(2):
                    sch.op("dve", lambda e, half=half, pkv=pkv: e.tensor_copy(out=Bf32[:, half * 512:(half + 1) * 512],
                                                                              in_=pkv[:, half * 512:(half + 1) * 512]),
                           r=[pkvb[half]], w=[Bf32_b])
            else:
                sch.op(FMUL_ENG, lambda e: e.tensor_tensor(out=v3(Bf32[:, :]), in0=v3(Bf32[:, :]), in1=bc_h(gcb), op=ALU.mult),
                       r=[Bf32_b, B_["kd"]], w=[Bf32_b])
                for half in range(2):
                    sch.op("dve", lambda e, half=half, pkv=pkv: e.tensor_tensor(out=Bf32[:, half * 512:(half + 1) * 512],
                                                                                in0=pkv[:, half * 512:(half + 1) * 512],
                                                                                in1=Bf32[:, half * 512:(half + 1) * 512], op=ALU.add),
                           r=[pkvb[half], Bf32_b], w=[Bf32_b])
            sch.op("act", lambda e, ns=ns: e.copy(out=BbfA[ns][:, :], in_=Bf32[:, :]), r=[Bf32_b], w=[BbfA_b[ns]])

    sch.disabled = STAGE < 3
    passA_bufs = [junk_b] + xin_b + xn_b + hT_b + [x for i in range(3) for x in hTr(i)] + ssq_b + rt_b + t1_b + t2_b + krA_b + vA_b + [kbA_b, Bf32_b] + BbfA_b + pbank + [B_["wkv"]]
    sch.barrier(passA_bufs)
    pb_at = [common_end]
    wkv_region = [arena0]

    def mkb(shape, dt, name, at):
        return alloc(shape, dt, at=at, name=name), Buf(name)

    class Ring:
        def __init__(self, name, shape, dt, n, at):
            self.items = [mkb(shape, dt, f"{name}{i}", at) for i in range(n)]

        def __getitem__(self, c):
            return self.items[c % len(self.items)]

    def ring(name, shape, dt, at2=None):
        n = NB.get(name, 1)
        r = Ring.__new__(Ring)
        r.items = []
        for i in range(n):
            r.items.append(mkb(shape, dt, f"{name}{i}", (at2 if at2 is not None else wkv_region) if i == 0 else pb_at))
        return r

    uring = [mkb([128, 1024], BF16, f"u{i}", wkv_region) for i in range(3)]
    qr_r = ring("qr", [128, 1024], BF16)
    sz_r = ring("sz", [128, 2048], BF16)
    krB_r = ring("krB", [128, 1024], BF16)
    vB_r = ring("vB", [128, 1024], BF16)
    BbfB_r = ring("BbfB", [128, 1024], BF16)
    kf_r = ring("kf", [128, 1024], BF16)
    QT_r = ring("QT", [128, 8, 128], BF16)
    QTf_r = ring("QTf", [128, 8, 128], BF16)
    QTb_r = ring("QTb", [128, 8, 128], BF16)
    KT_r = ring("KT", [128, 8, 128], BF16)
    scm_r = ring("scm", [128, 8, 128], BF16)
    if CHECK_SBUF:
        assert wkv_region[0] <= arena_after_wkv, (wkv_region[0], arena_after_wkv)
    Ff32, Ff32_b = mkb([128, 1024], F32, "Ff32", pb_at)
    Fbf_r = ring("Fbf", [128, 1024], BF16, pb_at)
    scr4_r = ring("scr4", [128, 1024], F32, pb_at)
    on_r = ring("on", [128, 1024], F32, pb_at)
    yb_r = ring("y", [128, 2048], BF16, pb_at)
    yT_r = ring("yT", [128, 16, 128], BF16, pb_at)
    plT_r = ring("plT", [128, 8, 128], BF16, pb_at)
    xres_r = ring("xres", [128, 1024], F32, pb_at)
    st_r = ring("st", [128, 64], F32, pb_at)
    ggtab, ggtab_b = mkb([128, 1024], F32, "ggtab", pb_at)
    ysc = {i: Buf(f"yout{i}") for i in range(nseq)}

    def frontB(seq, c, hs, xslot):
        sch.disabled = STAGE < 3
        front(seq, c, hs, xslot)
        front_done()
        n = nchs[seq]
        ut, ub = uring[c % 3]
        pu, pub = palloc(2)
        for half in range(2):
            def mm(e, half=half, pu=pu, hs=hs):
                last = None
                for dc in range(8):
                    last = e.matmul(pu[:, half * 512:(half + 1) * 512], hT[hs][:, dc, :], wq_sb[:, dc, half * 512:(half + 1) * 512],
                                    start=(dc == 0), stop=(dc == 7))
                return last
            sch.op("pe", mm, r=hTr(hs) + [B_["wq"]], w=[pub[half]])
            sch.op("act", lambda e, half=half, pu=pu, ut=ut: e.copy(out=ut[:, half * 512:(half + 1) * 512], in_=pu[:, half * 512:(half + 1) * 512]),
                   r=[pub[half]], w=[ub])

    def loadsB(seq, c):
        sch.disabled = STAGE < 3
        sl = c % 2
        sch.dma("sp", rt[sl][:, :], ropescr.ap()[c * 128:(c + 1) * 128, :], r=[B_["ropescr"]], w=[rt_b[sl]])
        gc = gbase[0] + c
        krB, krB_b = krB_r[gc]
        vB, vB_b = vB_r[gc]
        BbfB, BbfB_b = BbfB_r[gc]
        sch.dma("sp", krB[:, :], krscr[seq].ap()[c * 128:(c + 1) * 128, :], r=[scrB[("kr", seq)]], w=[krB_b])
        sch.dma("sp", vB[:, :], vscr[seq].ap()[c * 128:(c + 1) * 128, :], r=[scrB[("v", seq)]], w=[vB_b])
        sch.dma("sp", BbfB[:, :], bscr[seq].ap()[c * 128:(c + 1) * 128, :], r=[scrB[("b", seq)]], w=[BbfB_b])
        xr, xrb = xres_r[gc]
        sch.dma("sp", xr[:, :], xs[seq].ap()[c * 128:(c + 1) * 128, :], w=[xrb])

    def backB(seq, c, hs):
        n = nchs[seq]
        sl = c % 2
        first, last_c = (c == 0), (c == n - 1)
        gc = gbase[0] + c
        qr, qr_b = qr_r[gc]; sz, sz_b = sz_r[gc]; krB, krB_b = krB_r[gc]; vB, vB_b = vB_r[gc]
        BbfB, BbfB_b = BbfB_r[gc]; kf, kf_b = kf_r[gc]; QT, QT_b = QT_r[gc]; QTf, QTf_b = QTf_r[gc]
        QTb, QTb_b = QTb_r[gc]; KT, KT_b = KT_r[gc]; scm, scm_b = scm_r[gc]
        Fbf, Fbf_b = Fbf_r[gc]; Fbf_n, Fbf_nb = Fbf_r[gc + 1]
        scr4, scr4_b = scr4_r[gc]; on, on_b = on_r[gc]; yb, yb_b = yb_r[gc]; yT, yT_b = yT_r[gc]
        plT, plT_b = plT_r[gc]; st, st_b = st_r[gc]
        szp_b, szr_b = aux(sz_b, "p"), aux(sz_b, "r")
        ybp_b, ybr_b = aux(yb_b, "p"), aux(yb_b, "r")
        yTp_b, yTr_b = aux(yT_b, "p"), aux(yT_b, "r")
        if STAGE < 4:
            sch.disabled = True
        pq, pqb = palloc(2)
        pz0, pz0b = palloc(2)
        pz1, pz1b = palloc(2)
        for (pp, ppb, colbase) in ((pq, pqb, 1024), (pz0, pz0b, 2048), (pz1, pz1b, 3072)):
            for half in range(2):
                col0 = colbase + half * 512

                def mm(e, pp=pp, half=half, col0=col0):
                    last = None
                    for dc in range(8):
                        last = e.matmul(pp[:, half * 512:(half + 1) * 512], hT[hs][:, dc, :], wq_sb[:, dc, col0:col0 + 512],
                                        start=(dc == 0), stop=(dc == 7))
                    return last
                sch.op("pe", mm, r=hTr(hs) + [B_["wq"]], w=[ppb[half]])
        rope(pq, pqb, qr, qr_b, rt[sl], rt_b[sl], None)
        for zi, (pz, pzb) in enumerate(((pz0, pz0b), (pz1, pz1b))):
            for half in range(2):
                sch.op("act", lambda e, zi=zi, half=half, pz=pz: e.activation(out=sz[:, zi * 1024 + half * 512: zi * 1024 + (half + 1) * 512],
                                                                              in_=pz[:, half * 512:(half + 1) * 512], func=AF.Silu),
                       r=[pzb[half]], w=[szp_b if zi == 0 else szr_b])
        if STAGE < 5:
            sch.disabled = True
        if not last_c:
            sch.op(KF_ENG, lambda e: e.tensor_tensor(out=v3(kf[:, :]), in0=v3(krB[:, :]), in1=bc_h(kdf), op=ALU.mult),
                   r=[krB_b, B_["kd"]], w=[kf_b])
        ptq, ptqb = palloc(1)
        ptk, ptkb = palloc(1)
        ptqv = lambda: ptq.bank(0).bitcast(BF16).rearrange("p (h t) -> p h t", h=8)
        ptkv = lambda: ptk.bank(0).bitcast(BF16).rearrange("p (h t) -> p h t", h=8)

        def trq(e):
            last = None
            for h in range(8):
                last = e.transpose(ptqv()[:, h, :], qr[:, h * 128:(h + 1) * 128], ident[:, :])
            return last

        def trk(e):
            last = None
            for h in range(8):
                last = e.transpose(ptkv()[:, h, :], krB[:, h * 128:(h + 1) * 128], ident[:, :])
            return last
        if SUB < 1:
            sch.disabled = True
        sch.op("pe", trq, r=[qr_b, B_["ident"]], w=ptqb)
        sch.op("pe", trk, r=[krB_b, B_["ident"]], w=ptkb)
        if SUB < 2:
            sch.disabled = True
        sch.op("act", lambda e: e.copy(out=QT[:, :, :], in_=ptqv()), r=ptqb, w=[QT_b])
        sch.op("act", lambda e: e.copy(out=KT[:, :, :], in_=ptkv()), r=ptkb, w=[KT_b])
        if SUB < 3:
            sch.disabled = True
        if not first:
            sch.op("dve", lambda e: e.tensor_tensor(out=QTf[:, :, :], in0=QT[:, :, :], in1=af_tab[:, :, :], op=ALU.mult),
                   r=[QT_b, B_["aftab"]], w=[QTf_b])
        if not last_c:
            sch.op(QTB_ENG, lambda e: e.tensor_tensor(out=QTb[:, :, :], in0=QT[:, :, :], in1=ab_tab[:, :, :], op=ALU.mult),
                   r=[QT_b, B_["abtab"]], w=[QTb_b])
        if STAGE < 6:
            sch.disabled = True
        psc, pscb_ = palloc(2)
        for half in range(2):
            def scmm(e, half=half, psc=psc):
                last = None
                for hh in range(4):
                    h = half * 4 + hh
                    last = e.matmul(psc[:, h * 128:(h + 1) * 128], KT[:, h, :], QT[:, h, :], start=True, stop=True)
                return last
            sch.op("pe", scmm, r=[KT_b, QT_b], w=[pscb_[half]])
            sch.op("dve", lambda e, half=half, psc=psc: e.tensor_tensor(out=scm[:, half * 4:(half + 1) * 4, :],
                                                                        in0=psc[:, half * 512:(half + 1) * 512].rearrange("p (h t) -> p h t", h=4),
                                                                        in1=dtot[:, half * 4:(half + 1) * 4, :], op=ALU.mult),
                   r=[pscb_[half], B_["dtot"]], w=[scm_b])
        po, pob = palloc(2)
        for half in range(2):
            def omm(e, half=half, po=po):
                last = None
                for hh in range(4):
                    h = half * 4 + hh
                    terms = [(scm[:, h, :], vB[:, h * 128:(h + 1) * 128])]
                    if not first:
                        terms.append((QTf[:, h, :], Fbf[:, h * 128:(h + 1) * 128]))
                    if not last_c:
                        terms.append((QTb[:, h, :], BbfB[:, h * 128:(h + 1) * 128]))
                    for ti, (l_, r_) in enumerate(terms):
                        last = e.matmul(po[:, h * 128:(h + 1) * 128], l_, r_, start=(ti == 0), stop=(ti == len(terms) - 1))
                return last
            rr = [scm_b, vB_b]
            if not first:
                rr += [QTf_b, Fbf_b]
            if not last_c:
                rr += [QTb_b, BbfB_b]
            sch.op("pe", omm, r=rr, w=[pob[half]])
        if STAGE < 7:
            sch.disabled = True
        if not last_c:
            pkv, pkvb = palloc(2)
            for half in range(2):
                def kvmm(e, half=half, pkv=pkv):
                    last = None
                    for hh in range(4):
                        h = half * 4 + hh
                        last = e.matmul(pkv[:, h * 128:(h + 1) * 128], kf[:, h * 128:(h + 1) * 128], vB[:, h * 128:(h + 1) * 128],
                                        start=True, stop=True)
                    return last
                sch.op("pe", kvmm, r=[kf_b, vB_b], w=[pkvb[half]])
            if first:
                for half in range(2):
                    sch.op("dve", lambda e, half=half, pkv=pkv: e.tensor_copy(out=Ff32[:, half * 512:(half + 1) * 512],
                                                                              in_=pkv[:, half * 512:(half + 1) * 512]),
                           r=[pkvb[half]], w=[Ff32_b])
            else:
                sch.op(FMUL_ENG, lambda e: e.tensor_tensor(out=v3(Ff32[:, :]), in0=v3(Ff32[:, :]), in1=bc_h(gcf), op=ALU.mult),
                       r=[Ff32_b, B_["kd"]], w=[Ff32_b])
                for half in range(2):
                    sch.op("dve", lambda e, half=half, pkv=pkv: e.tensor_tensor(out=Ff32[:, half * 512:(half + 1) * 512],
                                                                                in0=pkv[:, half * 512:(half + 1) * 512],
                                                                                in1=Ff32[:, half * 512:(half + 1) * 512], op=ALU.add),
                           r=[pkvb[half], Ff32_b], w=[Ff32_b])
            sch.op("act", lambda e: e.copy(out=Fbf_n[:, :], in_=Ff32[:, :]), r=[Ff32_b], w=[Fbf_nb])
        if STAGE < 8:
            sch.disabled = True
        onh = [aux(on_b, "0"), aux(on_b, "1")]
        for half in range(2):
            sch.op("act", lambda e, half=half, po=po: e.copy(out=on[:, half * 512:(half + 1) * 512], in_=po[:, half * 512:(half + 1) * 512]),
                   r=[pob[half]], w=[onh[half]])
        sch.op("dve", lambda e: e.tensor_reduce(out=st[:, 0:8], in_=v3(on[:, :]), axis=AX.X, op=ALU.add), r=onh, w=[st_b])
        sch.op("act", lambda e: e.activation(out=scr4[:, :], in_=on[:, :], func=AF.Square), r=onh, w=[scr4_b])
        sch.op("dve", lambda e: e.tensor_reduce(out=st[:, 8:16], in_=v3(scr4[:, :]), axis=AX.X, op=ALU.add), r=[scr4_b], w=[st_b])
        sch.op("dve", lambda e: e.tensor_scalar_mul(out=st[:, 16:24], in0=st[:, 0:8], scalar1=1.0 / HD), r=[st_b], w=[st_b])
        sch.op("dve", lambda e: e.tensor_tensor(out=st[:, 24:32], in0=st[:, 16:24], in1=st[:, 16:24], op=ALU.mult), r=[st_b], w=[st_b])
        sch.op("dve", lambda e: e.scalar_tensor_tensor(out=st[:, 32:40], in0=st[:, 8:16], scalar=1.0 / HD, in1=st[:, 24:32],
                                                       op0=ALU.mult, op1=ALU.subtract), r=[st_b], w=[st_b])
        sch.op("dve", lambda e: e.tensor_scalar_add(out=st[:, 32:40], in0=st[:, 32:40], scalar1=EPS), r=[st_b], w=[st_b])
        sch.op("pool", lambda e: e.tensor_tensor(out=st[:, 40:48], in0=st[:, 32:40], in1=mhalf[:, 0:8], op=ALU.pow), r=[st_b], w=[st_b])
        sch.op("dve", lambda e: e.scalar_tensor_tensor(out=st[:, 48:56], in0=st[:, 16:24], scalar=-1.0, in1=st[:, 40:48],
                                                       op0=ALU.mult, op1=ALU.mult), r=[st_b], w=[st_b])
        for h in range(8):
            half = h // 4
            if half == 0:
                sch.op("act", lambda e, h=h: e.activation(out=on[:, h * 128:(h + 1) * 128], in_=on[:, h * 128:(h + 1) * 128],
                                                          func=AF.Identity, scale=st[:, 40 + h:41 + h], bias=st[:, 48 + h:49 + h]),
                       r=[onh[0], st_b], w=[onh[0]])
            else:
                sch.op("dve", lambda e, h=h: e.tensor_scalar(out=on[:, h * 128:(h + 1) * 128], in0=on[:, h * 128:(h + 1) * 128],
                                                             scalar1=st[:, 40 + h:41 + h], scalar2=st[:, 48 + h:49 + h],
                                                             op0=ALU.mult, op1=ALU.add),
                       r=[onh[1], st_b], w=[onh[1]])
        if SUB2 < 3:
            sch.disabled = True
        sch.op("dve", lambda e: e.tensor_tensor(out=yb[:, 1024:2048], in0=on[:, :], in1=sz[:, 1024:2048], op=ALU.mult),
               r=[aux(on_b, "0"), aux(on_b, "1"), szr_b], w=[ybr_b])
        if STAGE < 9:
            sch.disabled = True
        ppl, pplb = palloc(2)
        for half in range(2):
            def plmm(e, half=half, ppl=ppl):
                last = None
                for cc4 in range(4):
                    cc = half * 4 + cc4
                    gi = cc // 2
                    terms = []
                    if not first:
                        terms.append((uring[(c - 1) % 3][0], gi * 5 + 0))
                    terms.append((uring[c % 3][0], gi * 5 + (3 if first else (4 if last_c else 1))))
                    if not last_c:
                        terms.append((uring[(c + 1) % 3][0], gi * 5 + 2))
                    for ti, (ut, bi) in enumerate(terms):
                        last = e.matmul(ppl[:, cc * 128:(cc + 1) * 128], ut[:, cc * 128:(cc + 1) * 128], bands[:, bi, :],
                                        start=(ti == 0), stop=(ti == len(terms) - 1))
                return last
            rr = [uring[c % 3][1], B_["bands"]]
            if not first:
                rr.append(uring[(c - 1) % 3][1])
            if not last_c:
                rr.append(uring[(c + 1) % 3][1])
            sch.op("pe", plmm, r=rr, w=[pplb[half]])
            sch.op("act", lambda e, half=half, ppl=ppl: e.copy(out=plT[:, half * 4:(half + 1) * 4, :],
                                                               in_=ppl[:, half * 512:(half + 1) * 512].rearrange("p (c t) -> p c t", c=4)),
                   r=[pplb[half]], w=[plT_b])
        pyp, pypb = palloc(2)
        for half in range(2):
            def ypmm(e, half=half, pyp=pyp):
                last = None
                for g2 in range(2):
                    gi = half * 2 + g2
                    for cc2 in range(2):
                        cc = gi * 2 + cc2
                        last = e.matmul(pyp[:, gi * 256:(gi + 1) * 256], plT[:, cc, :], poolw_sb[:, cc, :],
                                        start=(cc2 == 0), stop=(cc2 == 1))
                return last
            sch.op("pe", ypmm, r=[plT_b, B_["poolw"]], w=[pypb[half]])
            sch.op("dve", lambda e, half=half, pyp=pyp: e.tensor_tensor(out=yb[:, half * 512:(half + 1) * 512],
                                                                        in0=pyp[:, half * 512:(half + 1) * 512],
                                                                        in1=sz[:, half * 512:(half + 1) * 512], op=ALU.mult),
                   r=[pypb[half], szp_b], w=[ybp_b])
        if STAGE < 10:
            sch.disabled = True
        pty, ptyb = palloc(2)
        ptyh = [(lambda i=i: pty.bank(i).bitcast(BF16).rearrange("p (e t) -> p e t", e=8)) for i in range(2)]
        for half in range(2):
            def trY(e, half=half):
                last = None
                for e8 in range(8):
                    ec = half * 8 + e8
                    last = e.transpose(ptyh[half]()[:, e8, :], yb[:, ec * 128:(ec + 1) * 128], ident[:, :])
                return last
            sch.op("pe", trY, r=[ybp_b if half == 0 else ybr_b, B_["ident"]], w=[ptyb[half]])
        sch.op("act", lambda e: e.copy(out=yT[:, 0:8, :], in_=ptyh[0]()), r=[ptyb[0]], w=[yTp_b])
        sch.op("dve", lambda e: e.tensor_copy(out=yT[:, 8:16, :], in_=ptyh[1]()), r=[ptyb[1]], w=[yTr_b])
        pout, poutb = palloc(2)
        sch.cur_prio = FINAL_PRIO
        for half in range(2):
            def outmm(e, half=half, pout=pout):
                last = None
                for ec in range(16):
                    last = e.matmul(pout[:, half * 512:(half + 1) * 512], yT[:, ec, :], wout_sb[:, ec, half * 512:(half + 1) * 512],
                                    start=(ec == 0), stop=(ec == 15))
                return last
            sch.op("pe", outmm, r=[yTp_b, yTr_b, B_["wout"]], w=[poutb[half]])
            sch.op("act", lambda e, half=half, pout=pout: e.activation(out=junk[:, half * 512:(half + 1) * 512], in_=pout[:, half * 512:(half + 1) * 512],
                                                                       func=AF.Square, accum_out=st[:, 56 + half:57 + half]),
                   r=[poutb[half]], w=[st_b, junk_b])
        sch.op("dve", lambda e: e.tensor_tensor(out=st[:, 58:59], in0=st[:, 56:57], in1=st[:, 57:58], op=ALU.add), r=[st_b], w=[st_b])
        sch.op("dve", lambda e: e.tensor_scalar(out=st[:, 59:60], in0=st[:, 58:59], scalar1=1.0 / D, scalar2=EPS,
                                                op0=ALU.mult, op1=ALU.add), r=[st_b], w=[st_b])
        sch.op("pool", lambda e: e.tensor_tensor(out=st[:, 60:61], in0=st[:, 59:60], in1=mhalf[:, 0:1], op=ALU.pow), r=[st_b], w=[st_b])
        for half in range(2):
            sch.op("dve", lambda e, half=half, pout=pout: e.scalar_tensor_tensor(out=scr4[:, half * 512:(half + 1) * 512],
                                                                                 in0=pout[:, half * 512:(half + 1) * 512], scalar=st[:, 60:61],
                                                                                 in1=ggtab[:, half * 512:(half + 1) * 512],
                                                                                 op0=ALU.mult, op1=ALU.mult),
                   r=[poutb[half], st_b, ggtab_b], w=[scr4_b])
        xr, xrb = xres_r[gc]
        sch.op(FIN_ENG, lambda e: e.tensor_tensor(out=xr[:, :], in0=xr[:, :], in1=scr4[:, :], op=ALU.add), r=[xrb, scr4_b], w=[xrb])
        sch.dma("sp", ys[seq].ap()[c * 128:(c + 1) * 128, :], xr[:, :], r=[xrb], w=[ysc[seq]], key=xrb)
        sch.cur_prio = 0

    itB = 0
    gbase = [0]
    for seq in range(nseq):
        gbase[0] = itB
        n = nchs[seq]
        sch.dma("sp", ggtab[:, :], ggscr.ap()[seq * 128:(seq + 1) * 128, :], r=[B_["ggscr"]], w=[ggtab_b])
        frontB(seq, 0, itB % 3, itB % 2)
        loadsB(seq, 0)
        for c in range(n):
            hs_c = (itB + c) % 3
            if c + 1 < n:
                frontB(seq, c + 1, (itB + c + 1) % 3, (itB + c + 1) % 2)
            backB(seq, c, hs_c)
            if c + 1 < n:
                loadsB(seq, c + 1)
        itB += n

    sch.disabled = False
    sch.finish()
    sch.sbuf_free = sb_hi - sbuf_peak[0]

    with nc.Block() as block:
        @block.tensor
        def _(e):
            sch.emit("pe", e)

        @block.scalar
        def _(e):
            sch.emit("act", e)

        @block.vector
        def _(e):
            sch.emit("dve", e)

        @block.gpsimd
        def _(e):
            sch.emit("pool", e)

        @block.sync
        def _(e):
            sch.emit("sp", e)
    return nc


_PROG_CACHE = {}


def _get_prog(S_list):
    key = tuple(S_list)
    if key not in _PROG_CACHE:
        _PROG_CACHE[key] = build_program(list(S_list))
    return _PROG_CACHE[key]


def kernel(x_prompt, x_sample, c_prompt, c_sample, ada_w, ada_b, norm_pre, norm_post,
           w_in, pool_w, pool_scale, ret_decay_fwd, ret_decay_bwd, w_out):
    f = lambda a: np.ascontiguousarray(np.asarray(a, dtype=np.float32))
    x_prompt, x_sample = f(x_prompt), f(x_sample)
    nb = x_prompt.shape[0]
    assert nb == N_CORES and x_sample.shape[0] == N_CORES
    S0, S1 = x_prompt.shape[1], x_sample.shape[1]
    nc = _get_prog((S0, S1))
    c_prompt, c_sample = f(c_prompt), f(c_sample)
    shared = {
        "ada_w": f(ada_w)[0], "ada_b": f(ada_b), "norm_pre": f(norm_pre), "norm_post": f(norm_post),
        "w_in": f(w_in)[0], "pool_w": f(pool_w)[0], "pool_scale": f(pool_scale),
        "dec_f": f(ret_decay_fwd), "dec_b": f(ret_decay_bwd), "w_out": f(w_out)[0],
    }
    in_maps = []
    for i in range(N_CORES):
        m = dict(shared)
        m["x0"] = x_prompt[i]
        m["x1"] = x_sample[i]
        m["cvec"] = np.ascontiguousarray(np.stack([c_prompt[i], c_sample[i]], axis=0))
        in_maps.append(m)
    res = run_bass_kernel_spmd(nc, in_maps, core_ids=list(range(N_CORES)))
    y0 = np.stack([np.asarray(r["y0"], dtype=np.float32) for r in res.results], axis=0)
    y1 = np.stack([np.asarray(r["y1"], dtype=np.float32) for r in res.results], axis=0)
    return (y0, y1)
```

```python
import math
import numpy as np
import concourse.bass as bass
import concourse.mybir as mybir
from concourse.bass_utils import run_bass_kernel_spmd

F32 = mybir.dt.float32
BF16 = mybir.dt.bfloat16
AF = mybir.ActivationFunctionType
ALU = mybir.AluOpType
AX = mybir.AxisListType

D = 1024
H = 8
HD = 128
EPS = 1e-6
POOL_W = (2, 4, 8, 16)
N_CORES = 8
SAME_ENGINE_SYNC = True
SAME_ENGINE_RAW_ONLY = True
CW1 = 6.28125
CW2 = 2.0 * math.pi - 6.28125
PI_SAFE = 3.1415925
HALF_PI_SAFE = 1.5707962


class Op:
    __slots__ = ("idx", "eng", "seng", "fn", "deps", "kind", "inc", "sem", "val", "cost", "lat", "phase",
                 "start", "finish", "key", "meta", "vbs", "prio", "raw")


class Buf:
    __slots__ = ("name", "w", "r", "excl", "phys", "users", "virt", "disjoint")

    def __init__(self, name, excl=False, virt=False, disjoint=True):
        self.name = name
        self.disjoint = disjoint
        self.w = None
        self.r = []
        self.excl = excl
        self.virt = virt
        self.phys = None
        self.users = []


class _Dummy:
    def then_inc(self, *a, **k):
        return self


class _Probe:
    def __init__(self):
        self.calls = []

    def __getattr__(self, name):
        def f(*a, **k):
            self.calls.append((name, a, k))
            return _Dummy()
        return f


def _free(ap):
    try:
        return int(np.prod(ap.shape[1:]))
    except Exception:
        return 1


def _est_cost(eng, calls):
    t = 0.0
    for name, a, k in calls:
        if name == "matmul":
            rhs = a[2] if len(a) > 2 else k["rhs"]
            t += 0.023 + 0.00044 * _free(rhs)
        elif name == "transpose":
            t += 0.08
        else:
            aps = [k.get(n) for n in ("out", "in_", "in0")] + list(a[:1])
            F = max([_free(x) for x in aps if x is not None and hasattr(x, "shape")] + [1])
            if eng == "act":
                t += 0.36 + 0.00062 * F
            elif eng == "dve":
                t += 0.2 + 0.00105 * F
            else:
                if k.get("op") == ALU.pow:
                    t += 0.45 + 0.15 * F
                else:
                    t += 0.45 + 0.0018 * F
    return t


class Sched:
    ENGS = ("pe", "act", "dve", "pool", "sp")
    WINDOW = 96
    XLAT = 0.3

    def __init__(self, nc):
        self.nc = nc
        self.all = []
        self.esem = {e: nc.alloc_semaphore("s_" + e) for e in self.ENGS}
        self.dsem = {}
        self.phase = 0
        self.disabled = False
        self.cur_prio = 0
        self.warm_fn = None

    def _deps(self, r, w, excl_eng):
        deps = []
        self._raw = set()
        self._strong = set()
        for b in r:
            if b.w is not None:
                deps.append(b.w)
                self._raw.add(id(b.w))
            if b.excl:
                deps.extend(o for o in b.r if o.eng != excl_eng)
        for b in w:
            if b.w is not None:
                deps.append(b.w)
                if not b.disjoint:
                    self._strong.add(id(b.w))
            deps.extend(b.r)
            self._strong.update(id(o) for o in b.r)
        seen = set()
        out = []
        for d in deps:
            if id(d) not in seen:
                seen.add(id(d))
                out.append(d)
        return out

    def _new(self, eng, seng, fn, deps, kind, inc, cost, lat, key=None):
        o = Op()
        o.idx = len(self.all)
        o.eng, o.seng, o.fn, o.deps, o.kind, o.inc = eng, seng, fn, deps, kind, inc
        o.cost, o.lat, o.phase, o.key = cost, lat, self.phase, key
        o.sem = o.val = o.start = o.finish = None
        o.meta = ([], [])
        o.vbs = []
        o.raw = set()
        o.prio = 0 if seng == "pe" else self.cur_prio
        self.all.append(o)
        return o

    def op(self, eng, fn, r=(), w=(), sig=True):
        if self.disabled:
            return
        deps = self._deps(r, w, eng)
        pr = _Probe()
        fn(pr)
        cost = _est_cost(eng, pr.calls)
        o = self._new(eng, eng, fn, deps, "op", 1, cost, cost)
        o.raw = self._raw | self._strong
        o.meta = ([b.name for b in r], [b.name for b in w])
        for b in list(r) + list(w):
            if b.virt and o not in b.users:
                b.users.append(o)
                o.vbs.append(b)
        for b in r:
            b.r.append(o)
        for b in w:
            b.w = o
            b.r = []

    def dma(self, q, out_ap, in_ap, r=(), w=(), key=None, slow=False):
        if self.disabled:
            return None
        if key is None:
            key = w[0]
        if key not in self.dsem:
            self.dsem[key] = [self.nc.alloc_semaphore("d_" + key.name), None]
        ent = self.dsem[key]
        deps = self._deps(r, w, "dma")
        if ent[1] is not None and ent[1] not in deps:
            deps.append(ent[1])
        nc = self.nc

        def fn(e, out_ap=out_ap, in_ap=in_ap, slow=slow):
            if slow:
                with nc.allow_non_contiguous_dma(reason="one-time small strided load"):
                    return e.dma_start(out=out_ap, in_=in_ap)
            return e.dma_start(out=out_ap, in_=in_ap)

        nbytes = int(np.prod(out_ap.shape)) * 4
        issue = 0.4 if q == "sp" else 1.5
        o = self._new("dma", q, fn, deps, "dma", 16, issue, issue + 2.0 + nbytes / 150e3, key=key)
        ent[1] = o
        for b in r:
            b.r.append(o)
        for b in w:
            b.w = o
            b.r = []
        return o

    def barrier(self, bufs):
        if self.disabled:
            return
        evs = []
        for b in bufs:
            if b.w is not None:
                evs.append(b.w)
            evs.extend(b.r)
        self.phase += 1
        for e in self.ENGS:
            self._new(e, e, None, list(evs), "bar", 0, 0.0, 0.0)
        self.phase += 1

    def finish(self):
        self.phase += 1
        last = [ent[1] for ent in self.dsem.values() if ent[1] is not None]
        self._new("sp", "sp", None, last, "bar", 0, 0.0, 0.0)
        self._schedule()

    def _schedule(self):
        free = {e: 0.0 for e in self.ENGS}
        order = {e: [] for e in self.ENGS}
        tenant = [None] * NDYN_BANKS
        npend = {}
        nph = self.phase + 1
        byph = [dict((e, []) for e in self.ENGS) for _ in range(nph)]
        for o in self.all:
            byph[o.phase][o.seng].append(o)
        if BL_PRIO:
            succ = {}
            for o in self.all:
                for d in o.deps:
                    succ.setdefault(id(d), []).append(o)
            bl = {}
            for o in reversed(self.all):
                m = 0.0
                for q in succ.get(id(o), ()):
                    v = bl[id(q)] + (0.0 if q.seng == o.seng else self.XLAT)
                    if v > m:
                        m = v
                bl[id(o)] = m + o.lat
            for o in self.all:
                if o.phase in BL_PHASES:
                    o.prio = -bl[id(o)] * BL_SCALE
        for ph in range(nph):
            uns = byph[ph]
            for e in self.ENGS:
                if BL_PRIO and ph in BL_PHASES:
                    uns[e].sort(key=lambda o: (o.prio, o.idx))
                else:
                    uns[e].sort(key=lambda o: (o.idx + o.prio, o.idx))
            remaining = sum(len(v) for v in uns.values())
            wide = False
            while remaining:
                best = None
                for e in self.ENGS:
                    lst = uns[e]
                    cb = None
                    for o in (lst if wide else lst[:self.WINDOW]):
                        ready = 0.0
                        ok = True
                        for d in o.deps:
                            if d.finish is None:
                                ok = False
                                break
                            rr = d.finish + (0.0 if d.seng == e and d.kind != "dma" else self.XLAT)
                            if rr > ready:
                                ready = rr
                        if not ok:
                            continue
                        need_bank = [v for v in o.vbs if v.phys is None]
                        if need_bank:
                            cands = []
                            for p in range(NDYN_BANKS):
                                tv = tenant[p]
                                if tv is None:
                                    cands.append((0.0, p))
                                elif npend.get(id(tv), len(tv.users)) == 0:
                                    cands.append((max(u.finish for u in tv.users) + self.XLAT, p))
                            if len(cands) < len(need_bank):
                                continue
                            cands.sort()
                            bank_rdy = cands[len(need_bank) - 1][0]
                            if bank_rdy > ready:
                                ready = bank_rdy
                            o_banks = [p for _, p in cands[:len(need_bank)]]
                        else:
                            o_banks = None
                        st = ready if ready > free[e] else free[e]
                        if cb is None or st < cb[0]:
                            cb = (st, o, o_banks)
                        if st <= free[e]:
                            break
                    if cb is not None and (best is None or (cb[0], (cb[1].prio if (BL_PRIO and ph in BL_PHASES) else cb[1].idx + cb[1].prio)) < (best[0], (best[1].prio if (BL_PRIO and ph in BL_PHASES) else best[1].idx + best[1].prio))):
                        best = cb
                if best is None:
                    if not wide:
                        wide = True
                        continue
                    raise RuntimeError("scheduler stuck (PSUM bank deadlock)")
                wide = False
                st, o, o_banks = best
                if o_banks is not None:
                    need_bank = [v for v in o.vbs if v.phys is None]
                    for v, p in zip(need_bank, o_banks):
                        tv = tenant[p]
                        if tv is not None:
                            for u in tv.users:
                                if u not in o.deps:
                                    o.deps.append(u)
                        tenant[p] = v
                        v.phys = p
                for v in o.vbs:
                    npend[id(v)] = npend.get(id(v), len(v.users)) - 1
                o.start = st
                o.finish = st + o.lat
                free[o.seng] = st + o.cost
                uns[o.seng].remove(o)
                order[o.seng].append(o)
                remaining -= 1
        self.order = order
        self.model_us = max(free.values())
        cnt = {e: 0 for e in self.ENGS}
        dcnt = {}
        for e in self.ENGS:
            for o in order[e]:
                if o.kind == "op":
                    cnt[e] += 1
                    o.sem, o.val = self.esem[e], cnt[e]
        for e in self.ENGS:
            for o in order[e]:
                if o.kind == "dma":
                    k = id(o.key)
                    dcnt[k] = dcnt.get(k, 0) + 16
                    o.sem, o.val = self.dsem[o.key][0], dcnt[k]

    def emit(self, eng_name, e):
        waited = {}
        order = self.order[eng_name]
        for oi, o in enumerate(order):
            need = {}
            for d in o.deps:
                if d.kind == "bar":
                    continue
                if d.kind == "op" and d.seng == eng_name and o.kind != "dma":
                    if eng_name == "pe" or not SAME_ENGINE_SYNC:
                        continue
                    if SAME_ENGINE_RAW_ONLY and id(d) not in o.raw:
                        continue
                assert d.val is not None
                k = id(d.sem)
                if k not in need or need[k][1] < d.val:
                    need[k] = (d.sem, d.val)
            for k, (sem, v) in need.items():
                if waited.get(k, 0) >= v:
                    continue
                e.wait_ge(sem, v)
                waited[k] = v
            if o.fn is None:
                continue
            inst = o.fn(e)
            inst.then_inc(o.sem, o.inc)
            if eng_name == "pe" and self.warm_fn is not None and o.phase >= 2 and oi + 1 < len(order):
                gap = order[oi + 1].start - (o.start + o.cost)
                if gap > WARM_GAP:
                    for _ in range(min(WARM_MAX, int(WARM_FRAC * gap / 0.22))):
                        self.warm_fn(e)


STAGE = 99
CHECK_SBUF = True
NPAIRS = 4
HT_ACT_A = False
FRONT_PRIO = 0
KB_ENG = "dve"
FMUL_ENG = "dve"
KF_ENG = "dve"
QTB_ENG = "dve"
FINAL_PRIO = 100
BL_PRIO = False
BL_SCALE = 1.0
BL_PHASES = (4,)
NDYN_BANKS = 7
WARM = True
WARM_GAP = 0.5
WARM_FRAC = 1.0
WARM_MAX = 16
ROPE_ENG_A = "dve"
FIN_ENG = "dve"
NB = {"xres": 2}
SUB = 9
SUB2 = 9


def build_program(S_list):
    nseq = len(S_list)
    assert nseq == 2
    nchs = [s // 128 for s in S_list]
    Smax = max(S_list)
    nchmax = Smax // 128
    nc = bass.Bass("TRN2", target_bir_lowering=False)

    xs = [nc.dram_tensor(f"x{i}", [S_list[i], D], F32, kind="ExternalInput") for i in range(nseq)]
    ys = [nc.dram_tensor(f"y{i}", [S_list[i], D], F32, kind="ExternalOutput") for i in range(nseq)]
    cvec = nc.dram_tensor("cvec", [nseq, D], F32, kind="ExternalInput")
    ada_w = nc.dram_tensor("ada_w", [D, 3 * D], F32, kind="ExternalInput")
    ada_b = nc.dram_tensor("ada_b", [1, 3 * D], F32, kind="ExternalInput")
    norm_pre = nc.dram_tensor("norm_pre", [1, D], F32, kind="ExternalInput")
    norm_post = nc.dram_tensor("norm_post", [1, D], F32, kind="ExternalInput")
    w_in = nc.dram_tensor("w_in", [D, 6 * D], F32, kind="ExternalInput")
    pool_w = nc.dram_tensor("pool_w", [4, 256, 256], F32, kind="ExternalInput")
    pool_scale = nc.dram_tensor("pool_scale", [1, D], F32, kind="ExternalInput")
    dec_f = nc.dram_tensor("dec_f", [1, H], F32, kind="ExternalInput")
    dec_b = nc.dram_tensor("dec_b", [1, H], F32, kind="ExternalInput")
    w_out = nc.dram_tensor("w_out", [2 * D, D], F32, kind="ExternalInput")
    krscr = [nc.dram_tensor(f"krscr{i}", [S_list[i], D], BF16, kind="Internal") for i in range(nseq)]
    vscr = [nc.dram_tensor(f"vscr{i}", [S_list[i], D], BF16, kind="Internal") for i in range(nseq)]
    bscr = [nc.dram_tensor(f"bscr{i}", [S_list[i], D], BF16, kind="Internal") for i in range(nseq)]
    ropescr = nc.dram_tensor("ropescr", [Smax, 128], F32, kind="Internal")
    ggscr = nc.dram_tensor("ggscr", [nseq * 128, D], F32, kind="Internal")

    sch = Sched(nc)

    sb_lo = (nc.sbuf_base + 63) // 64 * 64
    sb_hi = nc.sbuf_top
    cur = [sb_lo]
    names = [0]
    sbuf_peak = [0]

    def alloc(shape, dt, at=None, name=None):
        nbytes = int(np.prod(shape[1:])) * (2 if dt == BF16 else 4)
        nbytes = (nbytes + 63) // 64 * 64
        if at is None:
            off = cur[0]
            cur[0] += nbytes
        else:
            off = at[0]
            at[0] += nbytes
        names[0] += 1
        if CHECK_SBUF:
            assert off + nbytes <= sb_hi, f"SBUF overflow at {name}: {off + nbytes} > {sb_hi}"
        sbuf_peak[0] = max(sbuf_peak[0], off + nbytes)
        if not CHECK_SBUF and off + nbytes > sb_hi:
            off = sb_lo
        return nc.alloc_sbuf_tensor_at(f"t{names[0]}_{name or ''}", list(shape), dt, offset=off)

    wq_sb = alloc([128, 8, 4096], BF16, name="wq")
    wout_sb = alloc([128, 16, 1024], BF16, name="wout")
    poolw_sb = alloc([128, 8, 256], BF16, name="poolw")
    ident = alloc([128, 128], BF16, name="ident")
    dtot = alloc([128, 8, 128], F32, name="dtot")
    af_tab = alloc([128, 8, 128], BF16, name="aftab")
    ab_tab = alloc([128, 8, 128], BF16, name="abtab")
    bands = alloc([128, 20, 128], BF16, name="bands")
    kdf = alloc([128, 8], F32, name="kdf")
    kdb = alloc([128, 8], F32, name="kdb")
    gcf = alloc([128, 8], F32, name="gcf")
    gcb = alloc([128, 8], F32, name="gcb")
    gprime = alloc([128, 8, 2], F32, name="gprime")
    shiftT = alloc([128, 8, 2], F32, name="shiftT")
    negpi = alloc([128, 1], F32, name="negpi")
    mhalf = alloc([128, 8], F32, name="mhalf")
    halfpi = alloc([128, 1], F32, name="halfpi")
    B_ = {n: Buf(n) for n in ["wq", "wout", "poolw", "ident", "dtot", "aftab", "abtab", "bands", "kd",
                              "gprime", "negpi", "wkv", "ropescr", "ggscr"]}
    arena0 = cur[0]
    wkv_at = [arena0]
    wkv_sb = alloc([128, 8, 2048], BF16, at=wkv_at, name="wkv")
    arena_after_wkv = wkv_at[0]

    psum = nc.alloc_psum_tensor("psum", [128, 4096], F32)
    vbcount = [0]
    pbank = []

    class PT:
        def __init__(self, vbs):
            self.vbs = vbs

        def _phys(self, i):
            p = self.vbs[i].phys
            return 0 if p is None else p

        def bank(self, i):
            b = self._phys(i)
            return psum[:, b * 512:(b + 1) * 512]

        def __getitem__(self, key):
            rows, cols = key
            a0, a1 = cols.start, cols.stop
            bi = a0 // 512
            assert (a1 - 1) // 512 == bi, (a0, a1)
            b = self._phys(bi)
            return psum[:, b * 512 + (a0 - bi * 512): b * 512 + (a1 - bi * 512)]

    def palloc(nb=2, static=None):
        vbs = []
        for i in range(nb):
            vbcount[0] += 1
            v = Buf(f"pb{vbcount[0]}", excl=True, virt=(static is None))
            if static is not None:
                v.phys = static[i]
            vbs.append(v)
            pbank.append(v)
        return PT(vbs), vbs

    sa = [arena_after_wkv]
    diff = alloc([128, 128], F32, at=sa, name="diff")
    irow = alloc([128, 128], F32, at=sa, name="irow")
    pcol = alloc([128, 1], F32, at=sa, name="pcol")
    p127 = alloc([128, 1], F32, at=sa, name="p127")
    mge = alloc([128, 128], F32, at=sa, name="mge")
    mlt = alloc([128, 128], F32, at=sa, name="mlt")
    rpos = alloc([128, 128], F32, at=sa, name="rpos")
    rneg = alloc([128, 128], F32, at=sa, name="rneg")
    identf = alloc([128, 128], F32, at=sa, name="identf")
    tmpa = alloc([128, 128], F32, at=sa, name="tmpa")
    tmpb = alloc([128, 128], F32, at=sa, name="tmpb")
    tmpc = alloc([128, 128], F32, at=sa, name="tmpc")
    rowp1 = alloc([128, 128], F32, at=sa, name="rowp1")
    row128m = alloc([128, 128], F32, at=sa, name="row128m")
    decf_t = alloc([128, 8], F32, at=sa, name="decf")
    decb_t = alloc([128, 8], F32, at=sa, name="decb")
    lgf = alloc([128, 8], F32, at=sa, name="lgf")
    lgb = alloc([128, 8], F32, at=sa, name="lgb")
    etmp = alloc([128, 8], F32, at=sa, name="etmp")
    Bc = Buf("const")
    grp = {"b": Bc}
    groups = [Bc]

    def G():
        return grp["b"]

    def newgroup(name):
        grp["b"] = Buf(name)
        groups.append(grp["b"])

    g = nc.gpsimd

    w_in_v = w_in.ap().rearrange("(dc p) n -> p dc n", p=128)
    sch.dma("pool", wkv_sb[:, :, :], w_in_v[:, :, 2048:4096], w=[B_["wkv"]])

    sch.op("pool", lambda e: e.iota(diff[:, :], [[1, 128]], base=0, channel_multiplier=-1, allow_small_or_imprecise_dtypes=True), w=[G()])
    sch.op("pool", lambda e: e.iota(irow[:, :], [[1, 128]], base=0, channel_multiplier=0, allow_small_or_imprecise_dtypes=True), w=[G()])
    sch.op("pool", lambda e: e.iota(pcol[:, :], [[1, 1]], base=0, channel_multiplier=1, allow_small_or_imprecise_dtypes=True), w=[G()])
    sch.op("pool", lambda e: e.iota(p127[:, :], [[1, 1]], base=127, channel_multiplier=-1, allow_small_or_imprecise_dtypes=True), w=[G()])
    sch.op("dve", lambda e: e.memset(negpi[:, :], -math.pi), w=[B_["negpi"]])
    sch.op("dve", lambda e: e.memset(mhalf[:, :], -0.5), w=[B_["negpi"]])
    sch.op("dve", lambda e: e.memset(halfpi[:, :], HALF_PI_SAFE), w=[B_["negpi"]])

    def dv(fn, r=None, w=None):
        sch.op("dve", fn, r=[Bc, G()] if r is None else list(r), w=[G()] if w is None else list(w))

    def ac(fn, r=None, w=None):
        sch.op("act", fn, r=[Bc, G()] if r is None else list(r), w=[G()] if w is None else list(w))

    dv(lambda e: e.tensor_single_scalar(out=identf[:, :], in_=diff[:, :], scalar=0.0, op=ALU.is_equal))
    dv(lambda e: e.tensor_copy(out=ident[:, :], in_=identf[:, :]), w=(G(), B_["ident"]))
    dv(lambda e: e.tensor_single_scalar(out=mge[:, :], in_=diff[:, :], scalar=0.0, op=ALU.is_ge))
    dv(lambda e: e.tensor_single_scalar(out=mlt[:, :], in_=diff[:, :], scalar=0.0, op=ALU.is_lt))
    dv(lambda e: e.tensor_scalar_max(out=rpos[:, :], in0=diff[:, :], scalar1=0.0))
    dv(lambda e: e.tensor_scalar(out=rneg[:, :], in0=diff[:, :], scalar1=-1.0, scalar2=0.0,
                                 op0=ALU.mult, op1=ALU.max))
    dv(lambda e: e.tensor_scalar_add(out=rowp1[:, :], in0=irow[:, :], scalar1=1.0))
    dv(lambda e: e.tensor_scalar(out=row128m[:, :], in0=irow[:, :], scalar1=-1.0, scalar2=128.0,
                                 op0=ALU.mult, op1=ALU.add))

    newgroup("grpD")
    sch.dma("sp", decf_t[:, :], dec_f.ap().partition_broadcast(128).rearrange("p o n -> p (o n)"), w=[G()])
    sch.dma("sp", decb_t[:, :], dec_b.ap().partition_broadcast(128).rearrange("p o n -> p (o n)"), w=[G()])
    for dsrc, lg in ((decf_t, lgf), (decb_t, lgb)):
        ac(lambda e, dsrc=dsrc: e.activation(out=etmp[:, :], in_=dsrc[:, :], func=AF.Exp, scale=-math.log(2.0)))
        dv(lambda e: e.tensor_scalar(out=etmp[:, :], in0=etmp[:, :], scalar1=-1.0, scalar2=1.0,
                                     op0=ALU.mult, op1=ALU.add))
        ac(lambda e, lg=lg: e.activation(out=lg[:, :], in_=etmp[:, :], func=AF.Ln))
    for h in range(H):
        ac(lambda e, h=h: e.activation(out=tmpa[:, :], in_=rpos[:, :], func=AF.Exp, scale=lgf[:, h:h + 1]))
        ac(lambda e, h=h: e.activation(out=tmpb[:, :], in_=rneg[:, :], func=AF.Exp, scale=lgb[:, h:h + 1]))
        dv(lambda e: e.tensor_tensor(out=tmpa[:, :], in0=tmpa[:, :], in1=mge[:, :], op=ALU.mult))
        dv(lambda e: e.tensor_tensor(out=tmpb[:, :], in0=tmpb[:, :], in1=mlt[:, :], op=ALU.mult))
        dv(lambda e, h=h: e.tensor_tensor(out=dtot[:, h, :], in0=tmpa[:, :], in1=tmpb[:, :], op=ALU.add),
           w=(G(), B_["dtot"]))
        ac(lambda e, h=h: e.activation(out=af_tab[:, h, :], in_=rowp1[:, :], func=AF.Exp, scale=lgf[:, h:h + 1]),
           w=(G(), B_["aftab"]))
        ac(lambda e, h=h: e.activation(out=ab_tab[:, h, :], in_=row128m[:, :], func=AF.Exp, scale=lgb[:, h:h + 1]),
           w=(G(), B_["abtab"]))
    ac(lambda e: e.activation(out=kdf[:, :], in_=lgf[:, :], func=AF.Exp, scale=p127[:, 0:1]), w=(G(), B_["kd"]))
    ac(lambda e: e.activation(out=kdb[:, :], in_=lgb[:, :], func=AF.Exp, scale=pcol[:, 0:1]), w=(G(), B_["kd"]))
    ac(lambda e: e.activation(out=gcf[:, :], in_=lgf[:, :], func=AF.Exp, scale=128.0), w=(G(), B_["kd"]))
    ac(lambda e: e.activation(out=gcb[:, :], in_=lgb[:, :], func=AF.Exp, scale=128.0), w=(G(), B_["kd"]))

    newgroup("grpP")
    tmpa2 = alloc([128, 128], F32, at=sa, name="tmpa2")
    tmpb2 = alloc([128, 128], F32, at=sa, name="tmpb2")
    tmpc2 = alloc([128, 128], F32, at=sa, name="tmpc2")
    for gi, w_ in enumerate(POOL_W):
        hw = w_ // 2
        dv(lambda e, hw=hw: e.tensor_single_scalar(out=tmpa2[:, :], in_=diff[:, :], scalar=float(-(hw - 1)), op=ALU.is_ge))
        dv(lambda e, hw=hw: e.tensor_single_scalar(out=tmpb2[:, :], in_=diff[:, :], scalar=float(hw), op=ALU.is_le))
        dv(lambda e: e.tensor_tensor(out=tmpa2[:, :], in0=tmpa2[:, :], in1=tmpb2[:, :], op=ALU.mult))
        dv(lambda e, gi=gi, w_=w_: e.scalar_tensor_tensor(out=bands[:, gi * 5 + 1, :], in0=tmpa2[:, :], scalar=1.0 / w_,
                                                          in1=identf[:, :], op0=ALU.mult, op1=ALU.subtract),
           w=(G(), B_["bands"]))
        dv(lambda e, gi=gi, w_=w_, hw=hw: e.tensor_scalar(out=bands[:, gi * 5 + 0, :], in0=diff[:, :],
                                                          scalar1=float(hw - 128), scalar2=1.0 / w_,
                                                          op0=ALU.is_le, op1=ALU.mult), w=(G(), B_["bands"]))
        dv(lambda e, gi=gi, w_=w_, hw=hw: e.tensor_scalar(out=bands[:, gi * 5 + 2, :], in0=diff[:, :],
                                                          scalar1=float(129 - hw), scalar2=1.0 / w_,
                                                          op0=ALU.is_ge, op1=ALU.mult), w=(G(), B_["bands"]))
        dv(lambda e, w_=w_, hw=hw: e.tensor_scalar(out=tmpb2[:, :], in0=irow[:, :], scalar1=float(hw), scalar2=float(w_),
                                                   op0=ALU.add, op1=ALU.min))
        dv(lambda e: e.reciprocal(out=tmpb2[:, :], in_=tmpb2[:, :]))
        dv(lambda e: e.tensor_tensor(out=tmpc2[:, :], in0=tmpa2[:, :], in1=tmpb2[:, :], op=ALU.mult))
        dv(lambda e, gi=gi: e.tensor_tensor(out=bands[:, gi * 5 + 3, :], in0=tmpc2[:, :], in1=identf[:, :], op=ALU.subtract),
           w=(G(), B_["bands"]))
        dv(lambda e, w_=w_, hw=hw: e.tensor_scalar(out=tmpb2[:, :], in0=irow[:, :], scalar1=-1.0, scalar2=float(128 + hw),
                                                   op0=ALU.mult, op1=ALU.add))
        dv(lambda e, w_=w_: e.tensor_scalar_min(out=tmpb2[:, :], in0=tmpb2[:, :], scalar1=float(w_)))
        dv(lambda e: e.reciprocal(out=tmpb2[:, :], in_=tmpb2[:, :]))
        dv(lambda e: e.tensor_tensor(out=tmpc2[:, :], in0=tmpa2[:, :], in1=tmpb2[:, :], op=ALU.mult))
        dv(lambda e, gi=gi: e.tensor_tensor(out=bands[:, gi * 5 + 4, :], in0=tmpc2[:, :], in1=identf[:, :], op=ALU.subtract),
           w=(G(), B_["bands"]))

    newgroup("grpR")
    invf = alloc([128, 64], F32, at=sa, name="invf")
    ac(lambda e: e.activation(out=invf[:, :], in_=irow[:, 0:64], func=AF.Exp, scale=-math.log(10000.0) / 64.0))
    RC = 8
    posall = alloc([128, RC], F32, at=sa, name="posall")
    ang = alloc([128, RC, 64], F32, at=sa, name="ang")
    marg = alloc([128, RC, 64], F32, at=sa, name="marg")
    rtab = alloc([128, RC, 128], F32, at=sa, name="rtab")
    rred = alloc([128, RC, 64], F32, at=sa, name="rred")
    qint = alloc([128, RC, 64], mybir.dt.int32, at=sa, name="qint")
    Brt = Buf("rtab")
    rope_v = ropescr.ap().rearrange("(c p) n -> p c n", p=128)
    for c0 in range(0, nchmax, RC):
        ncb = min(RC, nchmax - c0)
        sch.op("pool", lambda e, c0=c0: e.iota(posall[:, :], [[128, RC]], base=128 * c0, channel_multiplier=1, allow_small_or_imprecise_dtypes=True),
               r=[G()], w=[G()])
        dv(lambda e: e.tensor_tensor(out=ang[:, :, :], in0=posall[:, :].unsqueeze(2).to_broadcast([128, RC, 64]),
                                     in1=invf[:, :].unsqueeze(1).to_broadcast([128, RC, 64]), op=ALU.mult))
        dv(lambda e: e.tensor_scalar_mul(out=marg[:, :, :], in0=ang[:, :, :], scalar1=1.0 / (2.0 * math.pi)))
        dv(lambda e: e.tensor_copy(out=qint[:, :, :], in_=marg[:, :, :]))
        dv(lambda e: e.tensor_copy(out=marg[:, :, :], in_=qint[:, :, :]))
        dv(lambda e: e.scalar_tensor_tensor(out=rred[:, :, :], in0=marg[:, :, :], scalar=-CW1, in1=ang[:, :, :],
                                            op0=ALU.mult, op1=ALU.add))
        dv(lambda e: e.scalar_tensor_tensor(out=rred[:, :, :], in0=marg[:, :, :], scalar=-CW2, in1=rred[:, :, :],
                                            op0=ALU.mult, op1=ALU.add))
        dv(lambda e: e.tensor_single_scalar(out=marg[:, :, :], in_=rred[:, :, :], scalar=math.pi, op=ALU.is_gt))
        dv(lambda e: e.scalar_tensor_tensor(out=rred[:, :, :], in0=marg[:, :, :], scalar=-2.0 * math.pi, in1=rred[:, :, :],
                                            op0=ALU.mult, op1=ALU.add))
        dv(lambda e: e.tensor_single_scalar(out=marg[:, :, :], in_=rred[:, :, :], scalar=-math.pi, op=ALU.is_lt))
        dv(lambda e: e.scalar_tensor_tensor(out=rred[:, :, :], in0=marg[:, :, :], scalar=2.0 * math.pi, in1=rred[:, :, :],
                                            op0=ALU.mult, op1=ALU.add))
        dv(lambda e: e.tensor_scalar(out=rred[:, :, :], in0=rred[:, :, :], scalar1=-PI_SAFE, scalar2=PI_SAFE,
                                     op0=ALU.max, op1=ALU.min))
        ac(lambda e: e.activation(out=rtab[:, :, 64:128], in_=rred[:, :, :], func=AF.Sin), w=(G(), Brt))
        dv(lambda e: e.scalar_tensor_tensor(out=marg[:, :, :], in0=rred[:, :, :], scalar=-1.0, in1=rred[:, :, :],
                                            op0=ALU.mult, op1=ALU.max))
        ac(lambda e: e.activation(out=rtab[:, :, 0:64], in_=marg[:, :, :], func=AF.Sin, scale=-1.0, bias=halfpi[:, 0:1]),
           r=(Bc, G(), B_["negpi"]), w=(G(), Brt))
        sch.dma("sp", rope_v[:, c0:c0 + ncb, :], rtab[:, 0:ncb, :], r=[Brt], w=[B_["ropescr"]], key=Brt)

    newgroup("grpA")
    adaw_sb = alloc([128, 8, 1024], BF16, at=sa, name="adaw")
    Badaw = Buf("adaw")
    ada_w_v = ada_w.ap().rearrange("(dc p) n -> p dc n", p=128)

    cT = alloc([128, 8, 2], F32, at=sa, name="cT")
    scT = alloc([128, 8, 2], BF16, at=sa, name="scT")
    scb = alloc([128, 2, 8, 128], BF16, at=sa, name="scb")
    adabT = alloc([128, 24], F32, at=sa, name="adabT")
    gpreT = alloc([128, 8], F32, at=sa, name="gpreT")
    modT = alloc([128, 16, 2], F32, at=sa, name="modT")
    rowb = alloc([128, 1024], F32, at=sa, name="rowb")
    gpostb = alloc([128, 1024], F32, at=sa, name="gpostb")
    ggt = alloc([128, 1024], F32, at=sa, name="ggt")
    pstage = alloc([128, 8, 256], F32, at=sa, name="pstage")
    setup_end = sa[0]

    for s in range(nseq):
        sch.dma("sp", cT[:, :, s], cvec.ap()[s].rearrange("(dc p) -> p dc", p=128), w=[G()], slow=True)
    sch.dma("sp", adabT[:, :], ada_b.ap().rearrange("o (fc p) -> p (o fc)", p=128), w=[G()], slow=True)
    sch.dma("sp", gpreT[:, :], norm_pre.ap().rearrange("o (fc p) -> p (o fc)", p=128), w=[G()], slow=True)
    sch.dma("sp", gpostb[:, :], norm_post.ap().partition_broadcast(128).rearrange("p o n -> p (o n)"), w=[G()])
    sch.dma("sp", rowb[:, :], pool_scale.ap().partition_broadcast(128).rearrange("p o n -> p (o n)"), w=[G()])
    sch.dma("sp", pstage[:, :, :], pool_w.ap().rearrange("g (cc p) d -> p (g cc) d", p=128), w=[G()])
    dv(lambda e: e.tensor_tensor(out=poolw_sb[:, :, :].rearrange("p (g c) d -> p g c d", g=4),
                                 in0=pstage[:, :, :].rearrange("p (g c) d -> p g c d", g=4),
                                 in1=rowb[:, :].rearrange("p (g d) -> p g d", g=4).unsqueeze(2).to_broadcast([128, 4, 2, 256]),
                                 op=ALU.mult), w=(G(), B_["poolw"]))
    sch.dma("sp", rowb[:, :], ada_b.ap()[:, 2048:3072].partition_broadcast(128).rearrange("p o n -> p (o n)"), r=[G()], w=[G()])
    ac(lambda e: e.activation(out=scT[:, :, :], in_=cT[:, :, :], func=AF.Silu))
    for s in range(nseq):
        for dc in range(8):
            dv(lambda e, s=s, dc=dc: e.tensor_copy(out=scb[:, s, dc, :], in_=scT[:, dc, s:s + 1].to_broadcast([128, 128])))
    pm, pmb = palloc(2, static=[0, 1])
    pmv = pm[:, 0:32].rearrange("p (fc s) -> p fc s", s=2)
    for piece in range(2):
        sch.dma("pool", adaw_sb[:, :, :], ada_w_v[:, :, piece * 1024:(piece + 1) * 1024], w=[Badaw])

        def mod_mm(e, piece=piece):
            last = None
            for fc in range(8):
                for dc in range(8):
                    last = e.matmul(pmv[:, piece * 8 + fc, :], adaw_sb[:, dc, fc * 128:(fc + 1) * 128], scT[:, dc, :],
                                    start=(dc == 0), stop=(dc == 7))
            return last
        sch.op("pe", mod_mm, r=[G(), Badaw], w=pmb)
    dv(lambda e: e.tensor_tensor(out=modT[:, :, :], in0=pmv, in1=adabT[:, 0:16].unsqueeze(2).to_broadcast([128, 16, 2]),
                                 op=ALU.add), r=[G()] + pmb, w=[G()])
    dv(lambda e: e.tensor_copy(out=shiftT[:, :, :], in_=modT[:, 0:8, :]), w=(G(), B_["gprime"]))
    dv(lambda e: e.scalar_tensor_tensor(out=gprime[:, :, :], in0=modT[:, 8:16, :], scalar=1.0,
                                        in1=gpreT[:, :].unsqueeze(2).to_broadcast([128, 8, 2]),
                                        op0=ALU.add, op1=ALU.mult), w=(G(), B_["gprime"]))
    dv(lambda e: e.tensor_scalar_mul(out=gprime[:, :, :], in0=gprime[:, :, :], scalar1=float(D) ** 0.5), w=(G(), B_["gprime"]))
    sch.dma("pool", adaw_sb[:, :, :], ada_w_v[:, :, 2048:3072], w=[Badaw])
    gg_v = ggscr.ap()
    for s in range(nseq):
        pg, pgb = palloc(2, static=[2 + 2 * s, 3 + 2 * s])

        def gate_mm(e, s=s, pg=pg):
            last = None
            for half in range(2):
                for dc in range(8):
                    last = e.matmul(pg[:, half * 512:(half + 1) * 512], scb[:, s, dc, :],
                                    adaw_sb[:, dc, half * 512:(half + 1) * 512],
                                    start=(dc == 0), stop=(dc == 7))
            return last
        sch.op("pe", gate_mm, r=[G(), Badaw], w=pgb)
        for half in range(2):
            dv(lambda e, pg=pg, half=half: e.tensor_tensor(out=ggt[:, half * 512:(half + 1) * 512], in0=pg[:, half * 512:(half + 1) * 512],
                                                         in1=rowb[:, half * 512:(half + 1) * 512], op=ALU.add), r=[G()] + pgb, w=[G()])
        dv(lambda e: e.tensor_tensor(out=ggt[:, :], in0=ggt[:, :], in1=gpostb[:, :], op=ALU.mult))
        sch.dma("sp", gg_v[s * 128:(s + 1) * 128, :], ggt[:, :], r=[G()], w=[B_["ggscr"]], key=G())
    sch.dma("pool", wq_sb[:, :, 0:2048], w_in_v[:, :, 0:2048], w=[B_["wq"]])
    sch.dma("pool", wq_sb[:, :, 2048:4096], w_in_v[:, :, 4096:6144], w=[B_["wq"]])
    sch.dma("pool", wout_sb[:, :, :], w_out.ap().rearrange("(ec p) n -> p ec n", p=128), w=[B_["wout"]])

    sch.disabled = STAGE < 2
    sch.barrier(groups + [Brt, Badaw] + pbank)

    act_at = [arena_after_wkv]

    def mk(shape, dt, name, n=1):
        ts = [alloc(shape, dt, at=act_at, name=f"{name}{i}") for i in range(n)]
        bs = [Buf(f"{name}{i}") for i in range(n)]
        return ts, bs

    xin, xin_b = mk([128, 1024], F32, "xin", 2)
    junk = alloc([128, 1024], BF16, at=act_at, name="junk")
    junk_b = Buf("junk", disjoint=False)
    xn, xn_b = mk([128, 1024], BF16, "xn", 1)
    hT, hT_b = mk([128, 8, 128], BF16, "hT", 3)
    ssq, ssq_b = mk([128, 4], F32, "ssq", 2)
    rt, rt_b = mk([128, 128], F32, "rt", 2)
    t1, t1_b = mk([128, 512], F32, "t1", 1)
    t2, t2_b = mk([128, 512], F32, "t2", 1)
    common_end = act_at[0]

    _aux = {}

    def aux(b, tag):
        k = (id(b), tag)
        if k not in _aux:
            _aux[k] = Buf(b.name + tag)
        return _aux[k]

    def hTr(hs):
        return [aux(hT_b[hs], "a"), aux(hT_b[hs], "b")]

    def front(seq, c, hslot, xslot, ht_act=True):
        sch.cur_prio = FRONT_PRIO
        xt, xb = xin[xslot], xin_b[xslot]
        sq, sqb = ssq[xslot], ssq_b[xslot]
        sch.dma("sp", xt[:, :], xs[seq].ap()[c * 128:(c + 1) * 128, :], w=[xb])
        sch.op("act", lambda e: e.activation(out=junk[:, :], in_=xt[:, :], func=AF.Square, accum_out=sq[:, 0:1]),
               r=[xb], w=[sqb, junk_b])
        sch.op("pool", lambda e: e.tensor_scalar_add(out=sq[:, 1:2], in0=sq[:, 0:1], scalar1=float(D) * EPS), r=[sqb], w=[sqb])
        sch.op("pool", lambda e: e.tensor_tensor(out=sq[:, 2:3], in0=sq[:, 1:2], in1=mhalf[:, 0:1], op=ALU.pow),
               r=[sqb], w=[sqb])
        sch.op("act", lambda e: e.activation(out=xn[0][:, :], in_=xt[:, :], func=AF.Copy, scale=sq[:, 2:3]),
               r=[xb, sqb], w=[xn_b[0]])
        pt, ptb = palloc(2)
        ptv = [(lambda i=i: pt.bank(i).bitcast(BF16)[:, 0:512].rearrange("p (dc t) -> p dc t", dc=4)) for i in range(2)]
        for hb in range(2):
            def tr(e, hb=hb):
                last = None
                for d4 in range(4):
                    dc = hb * 4 + d4
                    last = e.transpose(ptv[hb]()[:, d4, :], xn[0][:, dc * 128:(dc + 1) * 128], ident[:, :])
                return last
            sch.op("pe", tr, r=[xn_b[0], B_["ident"]], w=[ptb[hb]])
        for dc in range(8):
            hb, d4 = dc // 4, dc % 4
            if hb == 0 or ht_act:
                sch.op("act", lambda e, dc=dc, d4=d4, hb=hb: e.activation(out=hT[hslot][:, dc, :], in_=ptv[hb]()[:, d4, :], func=AF.Identity,
                                                                          scale=gprime[:, dc, seq:seq + 1], bias=shiftT[:, dc, seq:seq + 1]),
                       r=[ptb[hb], B_["gprime"]], w=[aux(hT_b[hslot], "a" if hb == 0 else "b")])
            else:
                sch.op("dve", lambda e, dc=dc, d4=d4: e.tensor_scalar(out=hT[hslot][:, dc, :], in0=ptv[1]()[:, d4, :],
                                                                      scalar1=gprime[:, dc, seq:seq + 1],
                                                                      scalar2=shiftT[:, dc, seq:seq + 1],
                                                                      op0=ALU.mult, op1=ALU.add),
                       r=[ptb[1], B_["gprime"]], w=[aux(hT_b[hslot], "b")])

    def front_done():
        sch.cur_prio = 0

    def rope(psrc, psrc_b, dst, dst_b, rts, rtb, kscale, ceng="dve"):
        cosb = rts[:, 0:64].unsqueeze(1).to_broadcast([128, 8, 64])
        sinb = rts[:, 64:128].unsqueeze(1).to_broadcast([128, 4, 64])
        for half in range(2):
            src = lambda half=half: psrc[:, half * 512:(half + 1) * 512].rearrange("p (h two d) -> p h two d", h=4, two=2)
            t1v = t1[0][:, :].rearrange("p (h two d) -> p h two d", h=4, two=2)
            t2v = t2[0][:, :].rearrange("p (h two d) -> p h two d", h=4, two=2)
            dv_ = dst[:, half * 512:(half + 1) * 512].rearrange("p (h two d) -> p h two d", h=4, two=2)
            pb = [psrc_b[half]]
            src3 = lambda half=half: psrc[:, half * 512:(half + 1) * 512].rearrange("p (g d) -> p g d", g=8)
            t1v3 = t1[0][:, :].rearrange("p (g d) -> p g d", g=8)
            if kscale is None:
                sch.op("dve", lambda e, src3=src3, t1v3=t1v3: e.tensor_tensor(out=t1v3, in0=src3(), in1=cosb, op=ALU.mult),
                       r=pb + [rtb], w=[t1_b[0]])
                sch.op("dve", lambda e, src=src, t2v=t2v: e.tensor_tensor(out=t2v[:, :, 0, :], in0=src()[:, :, 1, :], in1=sinb, op=ALU.mult),
                       r=pb + [rtb], w=[t2_b[0]])
                sch.op("dve", lambda e, src=src, t2v=t2v: e.tensor_tensor(out=t2v[:, :, 1, :], in0=src()[:, :, 0, :], in1=sinb, op=ALU.mult),
                       r=pb + [rtb], w=[t2_b[0]])
            else:
                sch.op("dve", lambda e, src3=src3, t1v3=t1v3: e.scalar_tensor_tensor(out=t1v3, in0=src3(), scalar=kscale, in1=cosb,
                                                                                 op0=ALU.mult, op1=ALU.mult),
                       r=pb + [rtb], w=[t1_b[0]])
                sch.op("dve", lambda e, src=src, t2v=t2v: e.scalar_tensor_tensor(out=t2v[:, :, 0, :], in0=src()[:, :, 1, :], scalar=kscale,
                                                                                 in1=sinb, op0=ALU.mult, op1=ALU.mult),
                       r=pb + [rtb], w=[t2_b[0]])
                sch.op("dve", lambda e, src=src, t2v=t2v: e.scalar_tensor_tensor(out=t2v[:, :, 1, :], in0=src()[:, :, 0, :], scalar=kscale,
                                                                                 in1=sinb, op0=ALU.mult, op1=ALU.mult),
                       r=pb + [rtb], w=[t2_b[0]])
            sch.op(ceng, lambda e, t1v=t1v, t2v=t2v, dv_=dv_: e.tensor_tensor(out=dv_[:, :, 0, :], in0=t1v[:, :, 0, :], in1=t2v[:, :, 0, :],
                                                                                op=ALU.subtract),
                   r=[t1_b[0], t2_b[0]], w=[dst_b])
            sch.op(ceng, lambda e, t1v=t1v, t2v=t2v, dv_=dv_: e.tensor_tensor(out=dv_[:, :, 1, :], in0=t1v[:, :, 1, :], in1=t2v[:, :, 1, :],
                                                                                op=ALU.add),
                   r=[t1_b[0], t2_b[0]], w=[dst_b])

    def bc_h(tab):
        return tab[:, :].unsqueeze(2).to_broadcast([128, 8, 128])

    def v3(t):
        return t.rearrange("p (h d) -> p h d", h=8)

    pa_at = [common_end]
    krA, krA_b = [], []
    vA, vA_b = [], []
    for i in range(2):
        krA.append(alloc([128, 1024], BF16, at=pa_at, name=f"krA{i}")); krA_b.append(Buf(f"krA{i}"))
        vA.append(alloc([128, 1024], BF16, at=pa_at, name=f"vA{i}")); vA_b.append(Buf(f"vA{i}"))
    kbA = alloc([128, 1024], BF16, at=pa_at, name="kbA"); kbA_b = Buf("kbA")
    Bf32 = alloc([128, 1024], F32, at=pa_at, name="Bf32"); Bf32_b = Buf("Bf32")
    BbfA, BbfA_b = [], []
    for i in range(2):
        BbfA.append(alloc([128, 1024], BF16, at=pa_at, name=f"BbfA{i}")); BbfA_b.append(Buf(f"BbfA{i}"))
    scrB = {("kr", i): Buf(f"krscr{i}") for i in range(nseq)}
    scrB.update({("v", i): Buf(f"vscr{i}") for i in range(nseq)})
    scrB.update({("b", i): Buf(f"bscr{i}") for i in range(nseq)})

    itA = 0
    for seq in range(nseq):
        n = nchs[seq]
        for idx, c in enumerate(range(n - 1, -1, -1)):
            sl = itA % 2
            itA += 1
            hs = itA % 3
            front(seq, c, hs, sl, ht_act=HT_ACT_A)
            front_done()
            sch.dma("sp", rt[sl][:, :], ropescr.ap()[c * 128:(c + 1) * 128, :], r=[B_["ropescr"]], w=[rt_b[sl]])
            pk, pkb = palloc(2)
            pv, pvb = palloc(2)
            for bi, (pp, ppb) in enumerate(((pk, pkb), (pv, pvb))):
                for half in range(2):
                    col0 = bi * 1024 + half * 512

                    def mm(e, pp=pp, half=half, col0=col0, hs=hs):
                        last = None
                        for dc in range(8):
                            last = e.matmul(pp[:, half * 512:(half + 1) * 512], hT[hs][:, dc, :], wkv_sb[:, dc, col0:col0 + 512],
                                            start=(dc == 0), stop=(dc == 7))
                        return last
                    sch.op("pe", mm, r=hTr(hs) + [B_["wkv"]], w=[ppb[half]])
            rope(pk, pkb, krA[sl], krA_b[sl], rt[sl], rt_b[sl], HD ** -0.5, ROPE_ENG_A)
            for half in range(2):
                sch.op("act", lambda e, half=half, pv=pv, sl=sl: e.copy(out=vA[sl][:, half * 512:(half + 1) * 512],
                                                                        in_=pv[:, half * 512:(half + 1) * 512]),
                       r=[pvb[half]], w=[vA_b[sl]])
            sch.dma("sp", krscr[seq].ap()[c * 128:(c + 1) * 128, :], krA[sl][:, :], r=[krA_b[sl]], w=[scrB[("kr", seq)]], key=krA_b[sl])
            sch.dma("sp", vscr[seq].ap()[c * 128:(c + 1) * 128, :], vA[sl][:, :], r=[vA_b[sl]], w=[scrB[("v", seq)]], key=vA_b[sl])
            if idx == 0:
                sch.op("pool", lambda e, sl=sl: e.memset(BbfA[sl][:, :], 0.0), w=[BbfA_b[sl]])
            sch.dma("sp", bscr[seq].ap()[c * 128:(c + 1) * 128, :], BbfA[sl][:, :], r=[BbfA_b[sl]], w=[scrB[("b", seq)]], key=BbfA_b[sl])
            if c == 0:
                continue
            sch.op(KB_ENG, lambda e, sl=sl: e.tensor_tensor(out=v3(kbA[:, :]), in0=v3(krA[sl][:, :]), in1=bc_h(kdb), op=ALU.mult),
                   r=[krA_b[sl], B_["kd"]], w=[kbA_b])
            pkv, pkvb = palloc(2)
            for half in range(2):
                def kvmm(e, half=half, pkv=pkv, sl=sl):
                    last = None
                    for hh in range(4):
                        h = half * 4 + hh
                        last = e.matmul(pkv[:, h * 128:(h + 1) * 128], kbA[:, h * 128:(h + 1) * 128], vA[sl][:, h * 128:(h + 1) * 128],
                                        start=True, stop=True)
                    return last
                sch.op("pe", kvmm, r=[kbA_b, vA_b[sl]], w=[pkvb[half]])
            ns = 1 - sl
            if idx == 0:
                for half in range(2):
                    sch.op("dve", lambda e, half=half, pkv=pkv: e.tensor_copy(out=Bf32[:, half * 512:(half + 1) * 512],
                                                                              in_=pkv[:, half * 512:(half + 1) * 512]),
                           r=[pkvb[half]], w=[Bf32_b])
            else:
                sch.op(FMUL_ENG, lambda e: e.tensor_tensor(out=v3(Bf32[:, :]), in0=v3(Bf32[:, :]), in1=bc_h(gcb), op=ALU.mult),
                       r=[Bf32_b, B_["kd"]], w=[Bf32_b])
                for half in range(2):
                    sch.op("dve", lambda e, half=half, pkv=pkv: e.tensor_tensor(out=Bf32[:, half * 512:(half + 1) * 512],
                                                                                in0=pkv[:, half * 512:(half + 1) * 512],
                                                                                in1=Bf32[:, half * 512:(half + 1) * 512], op=ALU.add),
                           r=[pkvb[half], Bf32_b], w=[Bf32_b])
            sch.op("act", lambda e, ns=ns: e.copy(out=BbfA[ns][:, :], in_=Bf32[:, :]), r=[Bf32_b], w=[BbfA_b[ns]])

    sch.disabled = STAGE < 3
    passA_bufs = [junk_b] + xin_b + xn_b + hT_b + [x for i in range(3) for x in hTr(i)] + ssq_b + rt_b + t1_b + t2_b + krA_b + vA_b + [kbA_b, Bf32_b] + BbfA_b + pbank + [B_["wkv"]]
    sch.barrier(passA_bufs)
    pb_at = [common_end]
    wkv_region = [arena0]

    def mkb(shape, dt, name, at):
        return alloc(shape, dt, at=at, name=name), Buf(name)

    class Ring:
        def __init__(self, name, shape, dt, n, at):
            self.items = [mkb(shape, dt, f"{name}{i}", at) for i in range(n)]

        def __getitem__(self, c):
            return self.items[c % len(self.items)]

    def ring(name, shape, dt, at2=None):
        n = NB.get(name, 1)
        r = Ring.__new__(Ring)
        r.items = []
        for i in range(n):
            r.items.append(mkb(shape, dt, f"{name}{i}", (at2 if at2 is not None else wkv_region) if i == 0 else pb_at))
        return r

    uring = [mkb([128, 1024], BF16, f"u{i}", wkv_region) for i in range(3)]
    qr_r = ring("qr", [128, 1024], BF16)
    sz_r = ring("sz", [128, 2048], BF16)
    krB_r = ring("krB", [128, 1024], BF16)
    vB_r = ring("vB", [128, 1024], BF16)
    BbfB_r = ring("BbfB", [128, 1024], BF16)
    kf_r = ring("kf", [128, 1024], BF16)
    QT_r = ring("QT", [128, 8, 128], BF16)
    QTf_r = ring("QTf", [128, 8, 128], BF16)
    QTb_r = ring("QTb", [128, 8, 128], BF16)
    KT_r = ring("KT", [128, 8, 128], BF16)
    scm_r = ring("scm", [128, 8, 128], BF16)
    if CHECK_SBUF:
        assert wkv_region[0] <= arena_after_wkv, (wkv_region[0], arena_after_wkv)
    Ff32, Ff32_b = mkb([128, 1024], F32, "Ff32", pb_at)
    Fbf_r = ring("Fbf", [128, 1024], BF16, pb_at)
    scr4_r = ring("scr4", [128, 1024], F32, pb_at)
    on_r = ring("on", [128, 1024], F32, pb_at)
    yb_r = ring("y", [128, 2048], BF16, pb_at)
    yT_r = ring("yT", [128, 16, 128], BF16, pb_at)
    plT_r = ring("plT", [128, 8, 128], BF16, pb_at)
    xres_r = ring("xres", [128, 1024], F32, pb_at)
    st_r = ring("st", [128, 64], F32, pb_at)
    ggtab, ggtab_b = mkb([128, 1024], F32, "ggtab", pb_at)
    ysc = {i: Buf(f"yout{i}") for i in range(nseq)}

    def frontB(seq, c, hs, xslot):
        sch.disabled = STAGE < 3
        front(seq, c, hs, xslot)
        front_done()
        n = nchs[seq]
        ut, ub = uring[c % 3]
        pu, pub = palloc(2)
        for half in range(2):
            def mm(e, half=half, pu=pu, hs=hs):
                last = None
                for dc in range(8):
                    last = e.matmul(pu[:, half * 512:(half + 1) * 512], hT[hs][:, dc, :], wq_sb[:, dc, half * 512:(half + 1) * 512],
                                    start=(dc == 0), stop=(dc == 7))
                return last
            sch.op("pe", mm, r=hTr(hs) + [B_["wq"]], w=[pub[half]])
            sch.op("act", lambda e, half=half, pu=pu, ut=ut: e.copy(out=ut[:, half * 512:(half + 1) * 512], in_=pu[:, half * 512:(half + 1) * 512]),
                   r=[pub[half]], w=[ub])

    def loadsB(seq, c):
        sch.disabled = STAGE < 3
        sl = c % 2
        sch.dma("sp", rt[sl][:, :], ropescr.ap()[c * 128:(c + 1) * 128, :], r=[B_["ropescr"]], w=[rt_b[sl]])
        gc = gbase[0] + c
        krB, krB_b = krB_r[gc]
        vB, vB_b = vB_r[gc]
        BbfB, BbfB_b = BbfB_r[gc]
        sch.dma("sp", krB[:, :], krscr[seq].ap()[c * 128:(c + 1) * 128, :], r=[scrB[("kr", seq)]], w=[krB_b])
        sch.dma("sp", vB[:, :], vscr[seq].ap()[c * 128:(c + 1) * 128, :], r=[scrB[("v", seq)]], w=[vB_b])
        sch.dma("sp", BbfB[:, :], bscr[seq].ap()[c * 128:(c + 1) * 128, :], r=[scrB[("b", seq)]], w=[BbfB_b])
        xr, xrb = xres_r[gc]
        sch.dma("sp", xr[:, :], xs[seq].ap()[c * 128:(c + 1) * 128, :], w=[xrb])

    def backB(seq, c, hs):
        n = nchs[seq]
        sl = c % 2
        first, last_c = (c == 0), (c == n - 1)
        gc = gbase[0] + c
        qr, qr_b = qr_r[gc]; sz, sz_b = sz_r[gc]; krB, krB_b = krB_r[gc]; vB, vB_b = vB_r[gc]
        BbfB, BbfB_b = BbfB_r[gc]; kf, kf_b = kf_r[gc]; QT, QT_b = QT_r[gc]; QTf, QTf_b = QTf_r[gc]
        QTb, QTb_b = QTb_r[gc]; KT, KT_b = KT_r[gc]; scm, scm_b = scm_r[gc]
        Fbf, Fbf_b = Fbf_r[gc]; Fbf_n, Fbf_nb = Fbf_r[gc + 1]
        scr4, scr4_b = scr4_r[gc]; on, on_b = on_r[gc]; yb, yb_b = yb_r[gc]; yT, yT_b = yT_r[gc]
        plT, plT_b = plT_r[gc]; st, st_b = st_r[gc]
        szp_b, szr_b = aux(sz_b, "p"), aux(sz_b, "r")
        ybp_b, ybr_b = aux(yb_b, "p"), aux(yb_b, "r")
        yTp_b, yTr_b = aux(yT_b, "p"), aux(yT_b, "r")
        if STAGE < 4:
            sch.disabled = True
        pq, pqb = palloc(2)
        pz0, pz0b = palloc(2)
        pz1, pz1b = palloc(2)
        for (pp, ppb, colbase) in ((pq, pqb, 1024), (pz0, pz0b, 2048), (pz1, pz1b, 3072)):
            for half in range(2):
                col0 = colbase + half * 512

                def mm(e, pp=pp, half=half, col0=col0):
                    last = None
                    for dc in range(8):
                        last = e.matmul(pp[:, half * 512:(half + 1) * 512], hT[hs][:, dc, :], wq_sb[:, dc, col0:col0 + 512],
                                        start=(dc == 0), stop=(dc == 7))
                    return last
                sch.op("pe", mm, r=hTr(hs) + [B_["wq"]], w=[ppb[half]])
        rope(pq, pqb, qr, qr_b, rt[sl], rt_b[sl], None)
        for zi, (pz, pzb) in enumerate(((pz0, pz0b), (pz1, pz1b))):
            for half in range(2):
                sch.op("act", lambda e, zi=zi, half=half, pz=pz: e.activation(out=sz[:, zi * 1024 + half * 512: zi * 1024 + (half + 1) * 512],
                                                                              in_=pz[:, half * 512:(half + 1) * 512], func=AF.Silu),
                       r=[pzb[half]], w=[szp_b if zi == 0 else szr_b])
        if STAGE < 5:
            sch.disabled = True
        if not last_c:
            sch.op(KF_ENG, lambda e: e.tensor_tensor(out=v3(kf[:, :]), in0=v3(krB[:, :]), in1=bc_h(kdf), op=ALU.mult),
                   r=[krB_b, B_["kd"]], w=[kf_b])
        ptq, ptqb = palloc(1)
        ptk, ptkb = palloc(1)
        ptqv = lambda: ptq.bank(0).bitcast(BF16).rearrange("p (h t) -> p h t", h=8)
        ptkv = lambda: ptk.bank(0).bitcast(BF16).rearrange("p (h t) -> p h t", h=8)

        def trq(e):
            last = None
            for h in range(8):
                last = e.transpose(ptqv()[:, h, :], qr[:, h * 128:(h + 1) * 128], ident[:, :])
            return last

        def trk(e):
            last = None
            for h in range(8):
                last = e.transpose(ptkv()[:, h, :], krB[:, h * 128:(h + 1) * 128], ident[:, :])
            return last
        if SUB < 1:
            sch.disabled = True
        sch.op("pe", trq, r=[qr_b, B_["ident"]], w=ptqb)
        sch.op("pe", trk, r=[krB_b, B_["ident"]], w=ptkb)
        if SUB < 2:
            sch.disabled = True
        sch.op("act", lambda e: e.copy(out=QT[:, :, :], in_=ptqv()), r=ptqb, w=[QT_b])
        sch.op("act", lambda e: e.copy(out=KT[:, :, :], in_=ptkv()), r=ptkb, w=[KT_b])
        if SUB < 3:
            sch.disabled = True
        if not first:
            sch.op("dve", lambda e: e.tensor_tensor(out=QTf[:, :, :], in0=QT[:, :, :], in1=af_tab[:, :, :], op=ALU.mult),
                   r=[QT_b, B_["aftab"]], w=[QTf_b])
        if not last_c:
            sch.op(QTB_ENG, lambda e: e.tensor_tensor(out=QTb[:, :, :], in0=QT[:, :, :], in1=ab_tab[:, :, :], op=ALU.mult),
                   r=[QT_b, B_["abtab"]], w=[QTb_b])
        if STAGE < 6:
            sch.disabled = True
        psc, pscb_ = palloc(2)
        for half in range(2):
            def scmm(e, half=half, psc=psc):
                last = None
                for hh in range(4):
                    h = half * 4 + hh
                    last = e.matmul(psc[:, h * 128:(h + 1) * 128], KT[:, h, :], QT[:, h, :], start=True, stop=True)
                return last
            sch.op("pe", scmm, r=[KT_b, QT_b], w=[pscb_[half]])
            sch.op("dve", lambda e, half=half, psc=psc: e.tensor_tensor(out=scm[:, half * 4:(half + 1) * 4, :],
                                                                        in0=psc[:, half * 512:(half + 1) * 512].rearrange("p (h t) -> p h t", h=4),
                                                                        in1=dtot[:, half * 4:(half + 1) * 4, :], op=ALU.mult),
                   r=[pscb_[half], B_["dtot"]], w=[scm_b])
        po, pob = palloc(2)
        for half in range(2):
            def omm(e, half=half, po=po):
                last = None
                for hh in range(4):
                    h = half * 4 + hh
                    terms = [(scm[:, h, :], vB[:, h * 128:(h + 1) * 128])]
                    if not first:
                        terms.append((QTf[:, h, :], Fbf[:, h * 128:(h + 1) * 128]))
                    if not last_c:
                        terms.append((QTb[:, h, :], BbfB[:, h * 128:(h + 1) * 128]))
                    for ti, (l_, r_) in enumerate(terms):
                        last = e.matmul(po[:, h * 128:(h + 1) * 128], l_, r_, start=(ti == 0), stop=(ti == len(terms) - 1))
                return last
            rr = [scm_b, vB_b]
            if not first:
                rr += [QTf_b, Fbf_b]
            if not last_c:
                rr += [QTb_b, BbfB_b]
            sch.op("pe", omm, r=rr, w=[pob[half]])
        if STAGE < 7:
            sch.disabled = True
        if not last_c:
            pkv, pkvb = palloc(2)
            for half in range(2):
                def kvmm(e, half=half, pkv=pkv):
                    last = None
                    for hh in range(4):
                        h = half * 4 + hh
                        last = e.matmul(pkv[:, h * 128:(h + 1) * 128], kf[:, h * 128:(h + 1) * 128], vB[:, h * 128:(h + 1) * 128],
                                        start=True, stop=True)
                    return last
                sch.op("pe", kvmm, r=[kf_b, vB_b], w=[pkvb[half]])
            if first:
                for half in range(2):
                    sch.op("dve", lambda e, half=half, pkv=pkv: e.tensor_copy(out=Ff32[:, half * 512:(half + 1) * 512],
                                                                              in_=pkv[:, half * 512:(half + 1) * 512]),
                           r=[pkvb[half]], w=[Ff32_b])
            else:
                sch.op(FMUL_ENG, lambda e: e.tensor_tensor(out=v3(Ff32[:, :]), in0=v3(Ff32[:, :]), in1=bc_h(gcf), op=ALU.mult),
                       r=[Ff32_b, B_["kd"]], w=[Ff32_b])
                for half in range(2):
                    sch.op("dve", lambda e, half=half, pkv=pkv: e.tensor_tensor(out=Ff32[:, half * 512:(half + 1) * 512],
                                                                                in0=pkv[:, half * 512:(half + 1) * 512],
                                                                                in1=Ff32[:, half * 512:(half + 1) * 512], op=ALU.add),
                           r=[pkvb[half], Ff32_b], w=[Ff32_b])
            sch.op("act", lambda e: e.copy(out=Fbf_n[:, :], in_=Ff32[:, :]), r=[Ff32_b], w=[Fbf_nb])
        if STAGE < 8:
            sch.disabled = True
        onh = [aux(on_b, "0"), aux(on_b, "1")]
        for half in range(2):
            sch.op("act", lambda e, half=half, po=po: e.copy(out=on[:, half * 512:(half + 1) * 512], in_=po[:, half * 512:(half + 1) * 512]),
                   r=[pob[half]], w=[onh[half]])
        sch.op("dve", lambda e: e.tensor_reduce(out=st[:, 0:8], in_=v3(on[:, :]), axis=AX.X, op=ALU.add), r=onh, w=[st_b])
        sch.op("act", lambda e: e.activation(out=scr4[:, :], in_=on[:, :], func=AF.Square), r=onh, w=[scr4_b])
        sch.op("dve", lambda e: e.tensor_reduce(out=st[:, 8:16], in_=v3(scr4[:, :]), axis=AX.X, op=ALU.add), r=[scr4_b], w=[st_b])
        sch.op("dve", lambda e: e.tensor_scalar_mul(out=st[:, 16:24], in0=st[:, 0:8], scalar1=1.0 / HD), r=[st_b], w=[st_b])
        sch.op("dve", lambda e: e.tensor_tensor(out=st[:, 24:32], in0=st[:, 16:24], in1=st[:, 16:24], op=ALU.mult), r=[st_b], w=[st_b])
        sch.op("dve", lambda e: e.scalar_tensor_tensor(out=st[:, 32:40], in0=st[:, 8:16], scalar=1.0 / HD, in1=st[:, 24:32],
                                                       op0=ALU.mult, op1=ALU.subtract), r=[st_b], w=[st_b])
        sch.op("dve", lambda e: e.tensor_scalar_add(out=st[:, 32:40], in0=st[:, 32:40], scalar1=EPS), r=[st_b], w=[st_b])
        sch.op("pool", lambda e: e.tensor_tensor(out=st[:, 40:48], in0=st[:, 32:40], in1=mhalf[:, 0:8], op=ALU.pow), r=[st_b], w=[st_b])
        sch.op("dve", lambda e: e.scalar_tensor_tensor(out=st[:, 48:56], in0=st[:, 16:24], scalar=-1.0, in1=st[:, 40:48],
                                                       op0=ALU.mult, op1=ALU.mult), r=[st_b], w=[st_b])
        for h in range(8):
            half = h // 4
            if half == 0:
                sch.op("act", lambda e, h=h: e.activation(out=on[:, h * 128:(h + 1) * 128], in_=on[:, h * 128:(h + 1) * 128],
                                                          func=AF.Identity, scale=st[:, 40 + h:41 + h], bias=st[:, 48 + h:49 + h]),
                       r=[onh[0], st_b], w=[onh[0]])
            else:
                sch.op("dve", lambda e, h=h: e.tensor_scalar(out=on[:, h * 128:(h + 1) * 128], in0=on[:, h * 128:(h + 1) * 128],
                                                             scalar1=st[:, 40 + h:41 + h], scalar2=st[:, 48 + h:49 + h],
                                                             op0=ALU.mult, op1=ALU.add),
                       r=[onh[1], st_b], w=[onh[1]])
        if SUB2 < 3:
            sch.disabled = True
        sch.op("dve", lambda e: e.tensor_tensor(out=yb[:, 1024:2048], in0=on[:, :], in1=sz[:, 1024:2048], op=ALU.mult),
               r=[aux(on_b, "0"), aux(on_b, "1"), szr_b], w=[ybr_b])
        if STAGE < 9:
            sch.disabled = True
        ppl, pplb = palloc(2)
        for half in range(2):
            def plmm(e, half=half, ppl=ppl):
                last = None
                for cc4 in range(4):
                    cc = half * 4 + cc4
                    gi = cc // 2
                    terms = []
                    if not first:
                        terms.append((uring[(c - 1) % 3][0], gi * 5 + 0))
                    terms.append((uring[c % 3][0], gi * 5 + (3 if first else (4 if last_c else 1))))
                    if not last_c:
                        terms.append((uring[(c + 1) % 3][0], gi * 5 + 2))
                    for ti, (ut, bi) in enumerate(terms):
                        last = e.matmul(ppl[:, cc * 128:(cc + 1) * 128], ut[:, cc * 128:(cc + 1) * 128], bands[:, bi, :],
                                        start=(ti == 0), stop=(ti == len(terms) - 1))
                return last
            rr = [uring[c % 3][1], B_["bands"]]
            if not first:
                rr.append(uring[(c - 1) % 3][1])
            if not last_c:
                rr.append(uring[(c + 1) % 3][1])
            sch.op("pe", plmm, r=rr, w=[pplb[half]])
            sch.op("act", lambda e, half=half, ppl=ppl: e.copy(out=plT[:, half * 4:(half + 1) * 4, :],
                                                               in_=ppl[:, half * 512:(half + 1) * 512].rearrange("p (c t) -> p c t", c=4)),
                   r=[pplb[half]], w=[plT_b])
        pyp, pypb = palloc(2)
        for half in range(2):
            def ypmm(e, half=half, pyp=pyp):
                last = None
                for g2 in range(2):
                    gi = half * 2 + g2
                    for cc2 in range(2):
                        cc = gi * 2 + cc2
                        last = e.matmul(pyp[:, gi * 256:(gi + 1) * 256], plT[:, cc, :], poolw_sb[:, cc, :],
                                        start=(cc2 == 0), stop=(cc2 == 1))
                return last
            sch.op("pe", ypmm, r=[plT_b, B_["poolw"]], w=[pypb[half]])
            sch.op("dve", lambda e, half=half, pyp=pyp: e.tensor_tensor(out=yb[:, half * 512:(half + 1) * 512],
                                                                        in0=pyp[:, half * 512:(half + 1) * 512],
                                                                        in1=sz[:, half * 512:(half + 1) * 512], op=ALU.mult),
                   r=[pypb[half], szp_b], w=[ybp_b])
        if STAGE < 10:
            sch.disabled = True
        pty, ptyb = palloc(2)
        ptyh = [(lambda i=i: pty.bank(i).bitcast(BF16).rearrange("p (e t) -> p e t", e=8)) for i in range(2)]
        for half in range(2):
            def trY(e, half=half):
                last = None
                for e8 in range(8):
                    ec = half * 8 + e8
                    last = e.transpose(ptyh[half]()[:, e8, :], yb[:, ec * 128:(ec + 1) * 128], ident[:, :])
                return last
            sch.op("pe", trY, r=[ybp_b if half == 0 else ybr_b, B_["ident"]], w=[ptyb[half]])
        sch.op("act", lambda e: e.copy(out=yT[:, 0:8, :], in_=ptyh[0]()), r=[ptyb[0]], w=[yTp_b])
        sch.op("dve", lambda e: e.tensor_copy(out=yT[:, 8:16, :], in_=ptyh[1]()), r=[ptyb[1]], w=[yTr_b])
        pout, poutb = palloc(2)
        sch.cur_prio = FINAL_PRIO
        for half in range(2):
            def outmm(e, half=half, pout=pout):
                last = None
                for ec in range(16):
                    last = e.matmul(pout[:, half * 512:(half + 1) * 512], yT[:, ec, :], wout_sb[:, ec, half * 512:(half + 1) * 512],
                                    start=(ec == 0), stop=(ec == 15))
                return last
            sch.op("pe", outmm, r=[yTp_b, yTr_b, B_["wout"]], w=[poutb[half]])
            sch.op("act", lambda e, half=half, pout=pout: e.activation(out=junk[:, half * 512:(half + 1) * 512], in_=pout[:, half * 512:(half + 1) * 512],
                                                                       func=AF.Square, accum_out=st[:, 56 + half:57 + half]),
                   r=[poutb[half]], w=[st_b, junk_b])
        sch.op("dve", lambda e: e.tensor_tensor(out=st[:, 58:59], in0=st[:, 56:57], in1=st[:, 57:58], op=ALU.add), r=[st_b], w=[st_b])
        sch.op("dve", lambda e: e.tensor_scalar(out=st[:, 59:60], in0=st[:, 58:59], scalar1=1.0 / D, scalar2=EPS,
                                                op0=ALU.mult, op1=ALU.add), r=[st_b], w=[st_b])
        sch.op("pool", lambda e: e.tensor_tensor(out=st[:, 60:61], in0=st[:, 59:60], in1=mhalf[:, 0:1], op=ALU.pow), r=[st_b], w=[st_b])
        for half in range(2):
            sch.op("dve", lambda e, half=half, pout=pout: e.scalar_tensor_tensor(out=scr4[:, half * 512:(half + 1) * 512],
                                                                                 in0=pout[:, half * 512:(half + 1) * 512], scalar=st[:, 60:61],
                                                                                 in1=ggtab[:, half * 512:(half + 1) * 512],
                                                                                 op0=ALU.mult, op1=ALU.mult),
                   r=[poutb[half], st_b, ggtab_b], w=[scr4_b])
        xr, xrb = xres_r[gc]
        sch.op(FIN_ENG, lambda e: e.tensor_tensor(out=xr[:, :], in0=xr[:, :], in1=scr4[:, :], op=ALU.add), r=[xrb, scr4_b], w=[xrb])
        sch.dma("sp", ys[seq].ap()[c * 128:(c + 1) * 128, :], xr[:, :], r=[xrb], w=[ysc[seq]], key=xrb)
        sch.cur_prio = 0

    itB = 0
    gbase = [0]
    for seq in range(nseq):
        gbase[0] = itB
        n = nchs[seq]
        sch.dma("sp", ggtab[:, :], ggscr.ap()[seq * 128:(seq + 1) * 128, :], r=[B_["ggscr"]], w=[ggtab_b])
        frontB(seq, 0, itB % 3, itB % 2)
        loadsB(seq, 0)
        for c in range(n):
            hs_c = (itB + c) % 3
            if c + 1 < n:
                frontB(seq, c + 1, (itB + c + 1) % 3, (itB + c + 1) % 2)
            backB(seq, c, hs_c)
            if c + 1 < n:
                loadsB(seq, c + 1)
        itB += n

    sch.disabled = False
    if WARM:
        sch.warm_fn = lambda e: e.matmul(psum[:, 7 * 512:8 * 512], ident[:, :], bands[:, 0:4, :], start=True, stop=True)
    sch.finish()
    sch.sbuf_free = sb_hi - sbuf_peak[0]

    with nc.Block() as block:
        @block.tensor
        def _(e):
            sch.emit("pe", e)

        @block.scalar
        def _(e):
            sch.emit("act", e)

        @block.vector
        def _(e):
            sch.emit("dve", e)

        @block.gpsimd
        def _(e):
            sch.emit("pool", e)

        @block.sync
        def _(e):
            sch.emit("sp", e)
    return nc


_PROG_CACHE = {}


def _get_prog(S_list):
    key = tuple(S_list)
    if key not in _PROG_CACHE:
        _PROG_CACHE[key] = build_program(list(S_list))
    return _PROG_CACHE[key]


def kernel(x_prompt, x_sample, c_prompt, c_sample, ada_w, ada_b, norm_pre, norm_post,
           w_in, pool_w, pool_scale, ret_decay_fwd, ret_decay_bwd, w_out):
    f = lambda a: np.ascontiguousarray(np.asarray(a, dtype=np.float32))
    x_prompt, x_sample = f(x_prompt), f(x_sample)
    nb = x_prompt.shape[0]
    assert nb == N_CORES and x_sample.shape[0] == N_CORES
    S0, S1 = x_prompt.shape[1], x_sample.shape[1]
    nc = _get_prog((S0, S1))
    c_prompt, c_sample = f(c_prompt), f(c_sample)
    shared = {
        "ada_w": f(ada_w)[0], "ada_b": f(ada_b), "norm_pre": f(norm_pre), "norm_post": f(norm_post),
        "w_in": f(w_in)[0], "pool_w": f(pool_w)[0], "pool_scale": f(pool_scale),
        "dec_f": f(ret_decay_fwd), "dec_b": f(ret_decay_bwd), "w_out": f(w_out)[0],
    }
    in_maps = []
    for i in range(N_CORES):
        m = dict(shared)
        m["x0"] = x_prompt[i]
        m["x1"] = x_sample[i]
        m["cvec"] = np.ascontiguousarray(np.stack([c_prompt[i], c_sample[i]], axis=0))
        in_maps.append(m)
    res = run_bass_kernel_spmd(nc, in_maps, core_ids=list(range(N_CORES)))
    y0 = np.stack([np.asarray(r["y0"], dtype=np.float32) for r in res.results], axis=0)
    y1 = np.stack([np.asarray(r["y1"], dtype=np.float32) for r in res.results], axis=0)
    return (y0, y1)
```

```python
import math
import numpy as np
import concourse.bass as bass
import concourse.mybir as mybir
from concourse.bass_utils import run_bass_kernel_spmd

F32 = mybir.dt.float32
BF16 = mybir.dt.bfloat16
AF = mybir.ActivationFunctionType
ALU = mybir.AluOpType
AX = mybir.AxisListType

D = 1024
H = 8
HD = 128
EPS = 1e-6
POOL_W = (2, 4, 8, 16)
N_CORES = 8
SAME_ENGINE_SYNC = True
SAME_ENGINE_RAW_ONLY = True
CW1 = 6.28125
CW2 = 2.0 * math.pi - 6.28125
PI_SAFE = 3.1415925
HALF_PI_SAFE = 1.5707962


class Op:
    __slots__ = ("idx", "eng", "seng", "fn", "deps", "kind", "inc", "sem", "val", "cost", "lat", "phase",
                 "start", "finish", "key", "meta", "vbs", "prio", "raw")


class Buf:
    __slots__ = ("name", "w", "r", "excl", "phys", "users", "virt", "disjoint")

    def __init__(self, name, excl=False, virt=False, disjoint=True):
        self.name = name
        self.disjoint = disjoint
        self.w = None
        self.r = []
        self.excl = excl
        self.virt = virt
        self.phys = None
        self.users = []


class _Dummy:
    def then_inc(self, *a, **k):
        return self


class _Probe:
    def __init__(self):
        self.calls = []

    def __getattr__(self, name):
        def f(*a, **k):
            self.calls.append((name, a, k))
            return _Dummy()
        return f


def _free(ap):
    try:
        return int(np.prod(ap.shape[1:]))
    except Exception:
        return 1


def _est_cost(eng, calls):
    t = 0.0
    for name, a, k in calls:
        if name == "matmul":
            rhs = a[2] if len(a) > 2 else k["rhs"]
            t += 0.023 + 0.00044 * _free(rhs)
        elif name == "transpose":
            t += 0.08
        else:
            aps = [k.get(n) for n in ("out", "in_", "in0")] + list(a[:1])
            F = max([_free(x) for x in aps if x is not None and hasattr(x, "shape")] + [1])
            if eng == "act":
                t += 0.36 + 0.00062 * F
            elif eng == "dve":
                t += 0.2 + 0.00105 * F
            else:
                if k.get("op") == ALU.pow:
                    t += 0.45 + 0.15 * F
                else:
                    t += 0.45 + 0.0018 * F
    return t


class Sched:
    ENGS = ("pe", "act", "dve", "pool", "sp")
    WINDOW = 96
    XLAT = 0.3

    def __init__(self, nc):
        self.nc = nc
        self.all = []
        self.esem = {e: nc.alloc_semaphore("s_" + e) for e in self.ENGS}
        self.dsem = {}
        self.phase = 0
        self.disabled = False
        self.cur_prio = 0
        self.warm_fn = None

    def _deps(self, r, w, excl_eng):
        deps = []
        self._raw = set()
        self._strong = set()
        for b in r:
            if b.w is not None:
                deps.append(b.w)
                self._raw.add(id(b.w))
            if b.excl:
                deps.extend(o for o in b.r if o.eng != excl_eng)
        for b in w:
            if b.w is not None:
                deps.append(b.w)
                if not b.disjoint:
                    self._strong.add(id(b.w))
            deps.extend(b.r)
            self._strong.update(id(o) for o in b.r)
        seen = set()
        out = []
        for d in deps:
            if id(d) not in seen:
                seen.add(id(d))
                out.append(d)
        return out

    def _new(self, eng, seng, fn, deps, kind, inc, cost, lat, key=None):
        o = Op()
        o.idx = len(self.all)
        o.eng, o.seng, o.fn, o.deps, o.kind, o.inc = eng, seng, fn, deps, kind, inc
        o.cost, o.lat, o.phase, o.key = cost, lat, self.phase, key
        o.sem = o.val = o.start = o.finish = None
        o.meta = ([], [])
        o.vbs = []
        o.raw = set()
        o.prio = 0 if seng == "pe" else self.cur_prio
        self.all.append(o)
        return o

    def op(self, eng, fn, r=(), w=(), sig=True):
        if self.disabled:
            return
        deps = self._deps(r, w, eng)
        pr = _Probe()
        fn(pr)
        cost = _est_cost(eng, pr.calls)
        o = self._new(eng, eng, fn, deps, "op", 1, cost, cost)
        o.raw = self._raw | self._strong
        o.meta = ([b.name for b in r], [b.name for b in w])
        for b in list(r) + list(w):
            if b.virt and o not in b.users:
                b.users.append(o)
                o.vbs.append(b)
        for b in r:
            b.r.append(o)
        for b in w:
            b.w = o
            b.r = []

    def dma(self, q, out_ap, in_ap, r=(), w=(), key=None, slow=False):
        if self.disabled:
            return None
        if key is None:
            key = w[0]
        if key not in self.dsem:
            self.dsem[key] = [self.nc.alloc_semaphore("d_" + key.name), None]
        ent = self.dsem[key]
        deps = self._deps(r, w, "dma")
        if ent[1] is not None and ent[1] not in deps:
            deps.append(ent[1])
        nc = self.nc

        def fn(e, out_ap=out_ap, in_ap=in_ap, slow=slow):
            if slow:
                with nc.allow_non_contiguous_dma(reason="one-time small strided load"):
                    return e.dma_start(out=out_ap, in_=in_ap)
            return e.dma_start(out=out_ap, in_=in_ap)

        nbytes = int(np.prod(out_ap.shape)) * 4
        issue = 0.4 if q == "sp" else 1.5
        o = self._new("dma", q, fn, deps, "dma", 16, issue, issue + 2.0 + nbytes / 150e3, key=key)
        ent[1] = o
        for b in r:
            b.r.append(o)
        for b in w:
            b.w = o
            b.r = []
        return o

    def barrier(self, bufs):
        if self.disabled:
            return
        evs = []
        for b in bufs:
            if b.w is not None:
                evs.append(b.w)
            evs.extend(b.r)
        self.phase += 1
        for e in self.ENGS:
            self._new(e, e, None, list(evs), "bar", 0, 0.0, 0.0)
        self.phase += 1

    def finish(self):
        self.phase += 1
        last = [ent[1] for ent in self.dsem.values() if ent[1] is not None]
        self._new("sp", "sp", None, last, "bar", 0, 0.0, 0.0)
        self._schedule()

    def _schedule(self):
        free = {e: 0.0 for e in self.ENGS}
        order = {e: [] for e in self.ENGS}
        tenant = [None] * NDYN_BANKS
        npend = {}
        nph = self.phase + 1
        byph = [dict((e, []) for e in self.ENGS) for _ in range(nph)]
        for o in self.all:
            byph[o.phase][o.seng].append(o)
        if BL_PRIO:
            succ = {}
            for o in self.all:
                for d in o.deps:
                    succ.setdefault(id(d), []).append(o)
            bl = {}
            for o in reversed(self.all):
                m = 0.0
                for q in succ.get(id(o), ()):
                    v = bl[id(q)] + (0.0 if q.seng == o.seng else self.XLAT)
                    if v > m:
                        m = v
                bl[id(o)] = m + o.lat
            for o in self.all:
                if o.phase in BL_PHASES:
                    o.prio = -bl[id(o)] * BL_SCALE
        for ph in range(nph):
            uns = byph[ph]
            for e in self.ENGS:
                if BL_PRIO and ph in BL_PHASES:
                    uns[e].sort(key=lambda o: (o.prio, o.idx))
                else:
                    uns[e].sort(key=lambda o: (o.idx + o.prio, o.idx))
            remaining = sum(len(v) for v in uns.values())
            wide = False
            while remaining:
                best = None
                for e in self.ENGS:
                    lst = uns[e]
                    cb = None
                    for o in (lst if wide else lst[:self.WINDOW]):
                        ready = 0.0
                        ok = True
                        for d in o.deps:
                            if d.finish is None:
                                ok = False
                                break
                            rr = d.finish + (0.0 if d.seng == e and d.kind != "dma" else self.XLAT)
                            if rr > ready:
                                ready = rr
                        if not ok:
                            continue
                        need_bank = [v for v in o.vbs if v.phys is None]
                        if need_bank:
                            cands = []
                            for p in range(NDYN_BANKS):
                                tv = tenant[p]
                                if tv is None:
                                    cands.append((0.0, p))
                                elif npend.get(id(tv), len(tv.users)) == 0:
                                    cands.append((max(u.finish for u in tv.users) + self.XLAT, p))
                            if len(cands) < len(need_bank):
                                continue
                            cands.sort()
                            bank_rdy = cands[len(need_bank) - 1][0]
                            if bank_rdy > ready:
                                ready = bank_rdy
                            o_banks = [p for _, p in cands[:len(need_bank)]]
                        else:
                            o_banks = None
                        st = ready if ready > free[e] else free[e]
                        if cb is None or st < cb[0]:
                            cb = (st, o, o_banks)
                        if st <= free[e]:
                            break
                    if cb is not None and (best is None or (cb[0], (cb[1].prio if (BL_PRIO and ph in BL_PHASES) else cb[1].idx + cb[1].prio)) < (best[0], (best[1].prio if (BL_PRIO and ph in BL_PHASES) else best[1].idx + best[1].prio))):
                        best = cb
                if best is None:
                    if not wide:
                        wide = True
                        continue
                    raise RuntimeError("scheduler stuck (PSUM bank deadlock)")
                wide = False
                st, o, o_banks = best
                if o_banks is not None:
                    need_bank = [v for v in o.vbs if v.phys is None]
                    for v, p in zip(need_bank, o_banks):
                        tv = tenant[p]
                        if tv is not None:
                            for u in tv.users:
                                if u not in o.deps:
                                    o.deps.append(u)
                        tenant[p] = v
                        v.phys = p
                for v in o.vbs:
                    npend[id(v)] = npend.get(id(v), len(v.users)) - 1
                o.start = st
                o.finish = st + o.lat
                free[o.seng] = st + o.cost
                uns[o.seng].remove(o)
                order[o.seng].append(o)
                remaining -= 1
        self.order = order
        self.model_us = max(free.values())
        cnt = {e: 0 for e in self.ENGS}
        dcnt = {}
        for e in self.ENGS:
            for o in order[e]:
                if o.kind == "op":
                    cnt[e] += 1
                    o.sem, o.val = self.esem[e], cnt[e]
        for e in self.ENGS:
            for o in order[e]:
                if o.kind == "dma":
                    k = id(o.key)
                    dcnt[k] = dcnt.get(k, 0) + 16
                    o.sem, o.val = self.dsem[o.key][0], dcnt[k]

    def emit(self, eng_name, e):
        waited = {}
        order = self.order[eng_name]
        for oi, o in enumerate(order):
            need = {}
            for d in o.deps:
                if d.kind == "bar":
                    continue
                if d.kind == "op" and d.seng == eng_name and o.kind != "dma":
                    if eng_name == "pe" or not SAME_ENGINE_SYNC:
                        continue
                    if SAME_ENGINE_RAW_ONLY and id(d) not in o.raw:
                        continue
                assert d.val is not None
                k = id(d.sem)
                if k not in need or need[k][1] < d.val:
                    need[k] = (d.sem, d.val)
            for k, (sem, v) in need.items():
                if waited.get(k, 0) >= v:
                    continue
                e.wait_ge(sem, v)
                waited[k] = v
            if o.fn is None:
                continue
            inst = o.fn(e)
            inst.then_inc(o.sem, o.inc)
            if eng_name == "pe" and self.warm_fn is not None and o.phase >= 2 and oi + 1 < len(order):
                gap = order[oi + 1].start - (o.start + o.cost)
                if gap > WARM_GAP:
                    for _ in range(min(WARM_MAX, int(WARM_FRAC * gap / 0.22))):
                        self.warm_fn(e)


STAGE = 99
CHECK_SBUF = True
NPAIRS = 4
HT_ACT_A = False
FRONT_PRIO = 0
KB_ENG = "dve"
FMUL_ENG = "dve"
KF_ENG = "dve"
QTB_ENG = "dve"
FINAL_PRIO = 100
RET_PRIO = 0
FUPD_PRIO = 0
NORM_PRIO = 0
POOL_PRIO = 0
BL_PRIO = False
BL_SCALE = 1.0
BL_PHASES = (4,)
NDYN_BANKS = 7
WARM = True
SPILL_HT = True
WARM_GAP = 0.5
WARM_FRAC = 1.0
WARM_MAX = 16
ROPE_ENG_A = "dve"
FIN_ENG = "dve"
NB = {"xres": 2}
SUB = 9
SUB2 = 9


def build_program(S_list):
    nseq = len(S_list)
    assert nseq == 2
    nchs = [s // 128 for s in S_list]
    Smax = max(S_list)
    nchmax = Smax // 128
    nc = bass.Bass("TRN2", target_bir_lowering=False)

    xs = [nc.dram_tensor(f"x{i}", [S_list[i], D], F32, kind="ExternalInput") for i in range(nseq)]
    ys = [nc.dram_tensor(f"y{i}", [S_list[i], D], F32, kind="ExternalOutput") for i in range(nseq)]
    cvec = nc.dram_tensor("cvec", [nseq, D], F32, kind="ExternalInput")
    ada_w = nc.dram_tensor("ada_w", [D, 3 * D], F32, kind="ExternalInput")
    ada_b = nc.dram_tensor("ada_b", [1, 3 * D], F32, kind="ExternalInput")
    norm_pre = nc.dram_tensor("norm_pre", [1, D], F32, kind="ExternalInput")
    norm_post = nc.dram_tensor("norm_post", [1, D], F32, kind="ExternalInput")
    w_in = nc.dram_tensor("w_in", [D, 6 * D], F32, kind="ExternalInput")
    pool_w = nc.dram_tensor("pool_w", [4, 256, 256], F32, kind="ExternalInput")
    pool_scale = nc.dram_tensor("pool_scale", [1, D], F32, kind="ExternalInput")
    dec_f = nc.dram_tensor("dec_f", [1, H], F32, kind="ExternalInput")
    dec_b = nc.dram_tensor("dec_b", [1, H], F32, kind="ExternalInput")
    w_out = nc.dram_tensor("w_out", [2 * D, D], F32, kind="ExternalInput")
    krscr = [nc.dram_tensor(f"krscr{i}", [S_list[i], D], BF16, kind="Internal") for i in range(nseq)]
    vscr = [nc.dram_tensor(f"vscr{i}", [S_list[i], D], BF16, kind="Internal") for i in range(nseq)]
    bscr = [nc.dram_tensor(f"bscr{i}", [S_list[i], D], BF16, kind="Internal") for i in range(nseq)]
    hscr = [nc.dram_tensor(f"hscr{i}", [S_list[i], D], BF16, kind="Internal") for i in range(nseq)]
    ropescr = nc.dram_tensor("ropescr", [Smax, 128], F32, kind="Internal")
    ggscr = nc.dram_tensor("ggscr", [nseq * 128, D], F32, kind="Internal")

    sch = Sched(nc)

    sb_lo = (nc.sbuf_base + 63) // 64 * 64
    sb_hi = nc.sbuf_top
    cur = [sb_lo]
    names = [0]
    sbuf_peak = [0]

    def alloc(shape, dt, at=None, name=None):
        nbytes = int(np.prod(shape[1:])) * (2 if dt == BF16 else 4)
        nbytes = (nbytes + 63) // 64 * 64
        if at is None:
            off = cur[0]
            cur[0] += nbytes
        else:
            off = at[0]
            at[0] += nbytes
        names[0] += 1
        if CHECK_SBUF:
            assert off + nbytes <= sb_hi, f"SBUF overflow at {name}: {off + nbytes} > {sb_hi}"
        sbuf_peak[0] = max(sbuf_peak[0], off + nbytes)
        if not CHECK_SBUF and off + nbytes > sb_hi:
            off = sb_lo
        return nc.alloc_sbuf_tensor_at(f"t{names[0]}_{name or ''}", list(shape), dt, offset=off)

    wq_sb = alloc([128, 8, 4096], BF16, name="wq")
    wout_sb = alloc([128, 16, 1024], BF16, name="wout")
    poolw_sb = alloc([128, 8, 256], BF16, name="poolw")
    ident = alloc([128, 128], BF16, name="ident")
    dtot = alloc([128, 8, 128], F32, name="dtot")
    af_tab = alloc([128, 8, 128], BF16, name="aftab")
    ab_tab = alloc([128, 8, 128], BF16, name="abtab")
    bands = alloc([128, 20, 128], BF16, name="bands")
    kdf = alloc([128, 8], F32, name="kdf")
    kdb = alloc([128, 8], F32, name="kdb")
    gcf = alloc([128, 8], F32, name="gcf")
    gcb = alloc([128, 8], F32, name="gcb")
    gprime = alloc([128, 8, 2], F32, name="gprime")
    shiftT = alloc([128, 8, 2], F32, name="shiftT")
    negpi = alloc([128, 1], F32, name="negpi")
    mhalf = alloc([128, 8], F32, name="mhalf")
    halfpi = alloc([128, 1], F32, name="halfpi")
    B_ = {n: Buf(n) for n in ["wq", "wout", "poolw", "ident", "dtot", "aftab", "abtab", "bands", "kd",
                              "gprime", "negpi", "wkv", "ropescr", "ggscr"]}
    arena0 = cur[0]
    wkv_at = [arena0]
    wkv_sb = alloc([128, 8, 2048], BF16, at=wkv_at, name="wkv")
    arena_after_wkv = wkv_at[0]

    psum = nc.alloc_psum_tensor("psum", [128, 4096], F32)
    vbcount = [0]
    pbank = []

    class PT:
        def __init__(self, vbs):
            self.vbs = vbs

        def _phys(self, i):
            p = self.vbs[i].phys
            return 0 if p is None else p

        def bank(self, i):
            b = self._phys(i)
            return psum[:, b * 512:(b + 1) * 512]

        def __getitem__(self, key):
            rows, cols = key
            a0, a1 = cols.start, cols.stop
            bi = a0 // 512
            assert (a1 - 1) // 512 == bi, (a0, a1)
            b = self._phys(bi)
            return psum[:, b * 512 + (a0 - bi * 512): b * 512 + (a1 - bi * 512)]

    def palloc(nb=2, static=None):
        vbs = []
        for i in range(nb):
            vbcount[0] += 1
            v = Buf(f"pb{vbcount[0]}", excl=True, virt=(static is None))
            if static is not None:
                v.phys = static[i]
            vbs.append(v)
            pbank.append(v)
        return PT(vbs), vbs

    sa = [arena_after_wkv]
    diff = alloc([128, 128], F32, at=sa, name="diff")
    irow = alloc([128, 128], F32, at=sa, name="irow")
    pcol = alloc([128, 1], F32, at=sa, name="pcol")
    p127 = alloc([128, 1], F32, at=sa, name="p127")
    mge = alloc([128, 128], F32, at=sa, name="mge")
    mlt = alloc([128, 128], F32, at=sa, name="mlt")
    rpos = alloc([128, 128], F32, at=sa, name="rpos")
    rneg = alloc([128, 128], F32, at=sa, name="rneg")
    identf = alloc([128, 128], F32, at=sa, name="identf")
    tmpa = alloc([128, 128], F32, at=sa, name="tmpa")
    tmpb = alloc([128, 128], F32, at=sa, name="tmpb")
    tmpc = alloc([128, 128], F32, at=sa, name="tmpc")
    rowp1 = alloc([128, 128], F32, at=sa, name="rowp1")
    row128m = alloc([128, 128], F32, at=sa, name="row128m")
    decf_t = alloc([128, 8], F32, at=sa, name="decf")
    decb_t = alloc([128, 8], F32, at=sa, name="decb")
    lgf = alloc([128, 8], F32, at=sa, name="lgf")
    lgb = alloc([128, 8], F32, at=sa, name="lgb")
    etmp = alloc([128, 8], F32, at=sa, name="etmp")
    Bc = Buf("const")
    grp = {"b": Bc}
    groups = [Bc]

    def G():
        return grp["b"]

    def newgroup(name):
        grp["b"] = Buf(name)
        groups.append(grp["b"])

    g = nc.gpsimd

    w_in_v = w_in.ap().rearrange("(dc p) n -> p dc n", p=128)
    sch.dma("pool", wkv_sb[:, :, :], w_in_v[:, :, 2048:4096], w=[B_["wkv"]])

    sch.op("pool", lambda e: e.iota(diff[:, :], [[1, 128]], base=0, channel_multiplier=-1, allow_small_or_imprecise_dtypes=True), w=[G()])
    sch.op("pool", lambda e: e.iota(irow[:, :], [[1, 128]], base=0, channel_multiplier=0, allow_small_or_imprecise_dtypes=True), w=[G()])
    sch.op("pool", lambda e: e.iota(pcol[:, :], [[1, 1]], base=0, channel_multiplier=1, allow_small_or_imprecise_dtypes=True), w=[G()])
    sch.op("pool", lambda e: e.iota(p127[:, :], [[1, 1]], base=127, channel_multiplier=-1, allow_small_or_imprecise_dtypes=True), w=[G()])
    sch.op("dve", lambda e: e.memset(negpi[:, :], -math.pi), w=[B_["negpi"]])
    sch.op("dve", lambda e: e.memset(mhalf[:, :], -0.5), w=[B_["negpi"]])
    sch.op("dve", lambda e: e.memset(halfpi[:, :], HALF_PI_SAFE), w=[B_["negpi"]])

    def dv(fn, r=None, w=None):
        sch.op("dve", fn, r=[Bc, G()] if r is None else list(r), w=[G()] if w is None else list(w))

    def ac(fn, r=None, w=None):
        sch.op("act", fn, r=[Bc, G()] if r is None else list(r), w=[G()] if w is None else list(w))

    dv(lambda e: e.tensor_single_scalar(out=identf[:, :], in_=diff[:, :], scalar=0.0, op=ALU.is_equal))
    dv(lambda e: e.tensor_copy(out=ident[:, :], in_=identf[:, :]), w=(G(), B_["ident"]))
    dv(lambda e: e.tensor_single_scalar(out=mge[:, :], in_=diff[:, :], scalar=0.0, op=ALU.is_ge))
    dv(lambda e: e.tensor_single_scalar(out=mlt[:, :], in_=diff[:, :], scalar=0.0, op=ALU.is_lt))
    dv(lambda e: e.tensor_scalar_max(out=rpos[:, :], in0=diff[:, :], scalar1=0.0))
    dv(lambda e: e.tensor_scalar(out=rneg[:, :], in0=diff[:, :], scalar1=-1.0, scalar2=0.0,
                                 op0=ALU.mult, op1=ALU.max))
    dv(lambda e: e.tensor_scalar_add(out=rowp1[:, :], in0=irow[:, :], scalar1=1.0))
    dv(lambda e: e.tensor_scalar(out=row128m[:, :], in0=irow[:, :], scalar1=-1.0, scalar2=128.0,
                                 op0=ALU.mult, op1=ALU.add))

    newgroup("grpD")
    sch.dma("sp", decf_t[:, :], dec_f.ap().partition_broadcast(128).rearrange("p o n -> p (o n)"), w=[G()])
    sch.dma("sp", decb_t[:, :], dec_b.ap().partition_broadcast(128).rearrange("p o n -> p (o n)"), w=[G()])
    for dsrc, lg in ((decf_t, lgf), (decb_t, lgb)):
        ac(lambda e, dsrc=dsrc: e.activation(out=etmp[:, :], in_=dsrc[:, :], func=AF.Exp, scale=-math.log(2.0)))
        dv(lambda e: e.tensor_scalar(out=etmp[:, :], in0=etmp[:, :], scalar1=-1.0, scalar2=1.0,
                                     op0=ALU.mult, op1=ALU.add))
        ac(lambda e, lg=lg: e.activation(out=lg[:, :], in_=etmp[:, :], func=AF.Ln))
    for h in range(H):
        ac(lambda e, h=h: e.activation(out=tmpa[:, :], in_=rpos[:, :], func=AF.Exp, scale=lgf[:, h:h + 1]))
        ac(lambda e, h=h: e.activation(out=tmpb[:, :], in_=rneg[:, :], func=AF.Exp, scale=lgb[:, h:h + 1]))
        dv(lambda e: e.tensor_tensor(out=tmpa[:, :], in0=tmpa[:, :], in1=mge[:, :], op=ALU.mult))
        dv(lambda e: e.tensor_tensor(out=tmpb[:, :], in0=tmpb[:, :], in1=mlt[:, :], op=ALU.mult))
        dv(lambda e, h=h: e.tensor_tensor(out=dtot[:, h, :], in0=tmpa[:, :], in1=tmpb[:, :], op=ALU.add),
           w=(G(), B_["dtot"]))
        ac(lambda e, h=h: e.activation(out=af_tab[:, h, :], in_=rowp1[:, :], func=AF.Exp, scale=lgf[:, h:h + 1]),
           w=(G(), B_["aftab"]))
        ac(lambda e, h=h: e.activation(out=ab_tab[:, h, :], in_=row128m[:, :], func=AF.Exp, scale=lgb[:, h:h + 1]),
           w=(G(), B_["abtab"]))
    ac(lambda e: e.activation(out=kdf[:, :], in_=lgf[:, :], func=AF.Exp, scale=p127[:, 0:1]), w=(G(), B_["kd"]))
    ac(lambda e: e.activation(out=kdb[:, :], in_=lgb[:, :], func=AF.Exp, scale=pcol[:, 0:1]), w=(G(), B_["kd"]))
    ac(lambda e: e.activation(out=gcf[:, :], in_=lgf[:, :], func=AF.Exp, scale=128.0), w=(G(), B_["kd"]))
    ac(lambda e: e.activation(out=gcb[:, :], in_=lgb[:, :], func=AF.Exp, scale=128.0), w=(G(), B_["kd"]))

    newgroup("grpP")
    tmpa2 = alloc([128, 128], F32, at=sa, name="tmpa2")
    tmpb2 = alloc([128, 128], F32, at=sa, name="tmpb2")
    tmpc2 = alloc([128, 128], F32, at=sa, name="tmpc2")
    for gi, w_ in enumerate(POOL_W):
        hw = w_ // 2
        dv(lambda e, hw=hw: e.tensor_single_scalar(out=tmpa2[:, :], in_=diff[:, :], scalar=float(-(hw - 1)), op=ALU.is_ge))
        dv(lambda e, hw=hw: e.tensor_single_scalar(out=tmpb2[:, :], in_=diff[:, :], scalar=float(hw), op=ALU.is_le))
        dv(lambda e: e.tensor_tensor(out=tmpa2[:, :], in0=tmpa2[:, :], in1=tmpb2[:, :], op=ALU.mult))
        dv(lambda e, gi=gi, w_=w_: e.scalar_tensor_tensor(out=bands[:, gi * 5 + 1, :], in0=tmpa2[:, :], scalar=1.0 / w_,
                                                          in1=identf[:, :], op0=ALU.mult, op1=ALU.subtract),
           w=(G(), B_["bands"]))
        dv(lambda e, gi=gi, w_=w_, hw=hw: e.tensor_scalar(out=bands[:, gi * 5 + 0, :], in0=diff[:, :],
                                                          scalar1=float(hw - 128), scalar2=1.0 / w_,
                                                          op0=ALU.is_le, op1=ALU.mult), w=(G(), B_["bands"]))
        dv(lambda e, gi=gi, w_=w_, hw=hw: e.tensor_scalar(out=bands[:, gi * 5 + 2, :], in0=diff[:, :],
                                                          scalar1=float(129 - hw), scalar2=1.0 / w_,
                                                          op0=ALU.is_ge, op1=ALU.mult), w=(G(), B_["bands"]))
        dv(lambda e, w_=w_, hw=hw: e.tensor_scalar(out=tmpb2[:, :], in0=irow[:, :], scalar1=float(hw), scalar2=float(w_),
                                                   op0=ALU.add, op1=ALU.min))
        dv(lambda e: e.reciprocal(out=tmpb2[:, :], in_=tmpb2[:, :]))
        dv(lambda e: e.tensor_tensor(out=tmpc2[:, :], in0=tmpa2[:, :], in1=tmpb2[:, :], op=ALU.mult))
        dv(lambda e, gi=gi: e.tensor_tensor(out=bands[:, gi * 5 + 3, :], in0=tmpc2[:, :], in1=identf[:, :], op=ALU.subtract),
           w=(G(), B_["bands"]))
        dv(lambda e, w_=w_, hw=hw: e.tensor_scalar(out=tmpb2[:, :], in0=irow[:, :], scalar1=-1.0, scalar2=float(128 + hw),
                                                   op0=ALU.mult, op1=ALU.add))
        dv(lambda e, w_=w_: e.tensor_scalar_min(out=tmpb2[:, :], in0=tmpb2[:, :], scalar1=float(w_)))
        dv(lambda e: e.reciprocal(out=tmpb2[:, :], in_=tmpb2[:, :]))
        dv(lambda e: e.tensor_tensor(out=tmpc2[:, :], in0=tmpa2[:, :], in1=tmpb2[:, :], op=ALU.mult))
        dv(lambda e, gi=gi: e.tensor_tensor(out=bands[:, gi * 5 + 4, :], in0=tmpc2[:, :], in1=identf[:, :], op=ALU.subtract),
           w=(G(), B_["bands"]))

    newgroup("grpR")
    invf = alloc([128, 64], F32, at=sa, name="invf")
    ac(lambda e: e.activation(out=invf[:, :], in_=irow[:, 0:64], func=AF.Exp, scale=-math.log(10000.0) / 64.0))
    RC = 8
    posall = alloc([128, RC], F32, at=sa, name="posall")
    ang = alloc([128, RC, 64], F32, at=sa, name="ang")
    marg = alloc([128, RC, 64], F32, at=sa, name="marg")
    rtab = alloc([128, RC, 128], F32, at=sa, name="rtab")
    rred = alloc([128, RC, 64], F32, at=sa, name="rred")
    qint = alloc([128, RC, 64], mybir.dt.int32, at=sa, name="qint")
    Brt = Buf("rtab")
    rope_v = ropescr.ap().rearrange("(c p) n -> p c n", p=128)
    for c0 in range(0, nchmax, RC):
        ncb = min(RC, nchmax - c0)
        sch.op("pool", lambda e, c0=c0: e.iota(posall[:, :], [[128, RC]], base=128 * c0, channel_multiplier=1, allow_small_or_imprecise_dtypes=True),
               r=[G()], w=[G()])
        dv(lambda e: e.tensor_tensor(out=ang[:, :, :], in0=posall[:, :].unsqueeze(2).to_broadcast([128, RC, 64]),
                                     in1=invf[:, :].unsqueeze(1).to_broadcast([128, RC, 64]), op=ALU.mult))
        dv(lambda e: e.tensor_scalar_mul(out=marg[:, :, :], in0=ang[:, :, :], scalar1=1.0 / (2.0 * math.pi)))
        dv(lambda e: e.tensor_copy(out=qint[:, :, :], in_=marg[:, :, :]))
        dv(lambda e: e.tensor_copy(out=marg[:, :, :], in_=qint[:, :, :]))
        dv(lambda e: e.scalar_tensor_tensor(out=rred[:, :, :], in0=marg[:, :, :], scalar=-CW1, in1=ang[:, :, :],
                                            op0=ALU.mult, op1=ALU.add))
        dv(lambda e: e.scalar_tensor_tensor(out=rred[:, :, :], in0=marg[:, :, :], scalar=-CW2, in1=rred[:, :, :],
                                            op0=ALU.mult, op1=ALU.add))
        dv(lambda e: e.tensor_single_scalar(out=marg[:, :, :], in_=rred[:, :, :], scalar=math.pi, op=ALU.is_gt))
        dv(lambda e: e.scalar_tensor_tensor(out=rred[:, :, :], in0=marg[:, :, :], scalar=-2.0 * math.pi, in1=rred[:, :, :],
                                            op0=ALU.mult, op1=ALU.add))
        dv(lambda e: e.tensor_single_scalar(out=marg[:, :, :], in_=rred[:, :, :], scalar=-math.pi, op=ALU.is_lt))
        dv(lambda e: e.scalar_tensor_tensor(out=rred[:, :, :], in0=marg[:, :, :], scalar=2.0 * math.pi, in1=rred[:, :, :],
                                            op0=ALU.mult, op1=ALU.add))
        dv(lambda e: e.tensor_scalar(out=rred[:, :, :], in0=rred[:, :, :], scalar1=-PI_SAFE, scalar2=PI_SAFE,
                                     op0=ALU.max, op1=ALU.min))
        ac(lambda e: e.activation(out=rtab[:, :, 64:128], in_=rred[:, :, :], func=AF.Sin), w=(G(), Brt))
        dv(lambda e: e.scalar_tensor_tensor(out=marg[:, :, :], in0=rred[:, :, :], scalar=-1.0, in1=rred[:, :, :],
                                            op0=ALU.mult, op1=ALU.max))
        ac(lambda e: e.activation(out=rtab[:, :, 0:64], in_=marg[:, :, :], func=AF.Sin, scale=-1.0, bias=halfpi[:, 0:1]),
           r=(Bc, G(), B_["negpi"]), w=(G(), Brt))
        sch.dma("sp", rope_v[:, c0:c0 + ncb, :], rtab[:, 0:ncb, :], r=[Brt], w=[B_["ropescr"]], key=Brt)

    newgroup("grpA")
    adaw_sb = alloc([128, 8, 1024], BF16, at=sa, name="adaw")
    Badaw = Buf("adaw")
    ada_w_v = ada_w.ap().rearrange("(dc p) n -> p dc n", p=128)

    cT = alloc([128, 8, 2], F32, at=sa, name="cT")
    scT = alloc([128, 8, 2], BF16, at=sa, name="scT")
    scb = alloc([128, 2, 8, 128], BF16, at=sa, name="scb")
    adabT = alloc([128, 24], F32, at=sa, name="adabT")
    gpreT = alloc([128, 8], F32, at=sa, name="gpreT")
    modT = alloc([128, 16, 2], F32, at=sa, name="modT")
    rowb = alloc([128, 1024], F32, at=sa, name="rowb")
    gpostb = alloc([128, 1024], F32, at=sa, name="gpostb")
    ggt = alloc([128, 1024], F32, at=sa, name="ggt")
    pstage = alloc([128, 8, 256], F32, at=sa, name="pstage")
    setup_end = sa[0]

    for s in range(nseq):
        sch.dma("sp", cT[:, :, s], cvec.ap()[s].rearrange("(dc p) -> p dc", p=128), w=[G()], slow=True)
    sch.dma("sp", adabT[:, :], ada_b.ap().rearrange("o (fc p) -> p (o fc)", p=128), w=[G()], slow=True)
    sch.dma("sp", gpreT[:, :], norm_pre.ap().rearrange("o (fc p) -> p (o fc)", p=128), w=[G()], slow=True)
    sch.dma("sp", gpostb[:, :], norm_post.ap().partition_broadcast(128).rearrange("p o n -> p (o n)"), w=[G()])
    sch.dma("sp", rowb[:, :], pool_scale.ap().partition_broadcast(128).rearrange("p o n -> p (o n)"), w=[G()])
    sch.dma("sp", pstage[:, :, :], pool_w.ap().rearrange("g (cc p) d -> p (g cc) d", p=128), w=[G()])
    dv(lambda e: e.tensor_tensor(out=poolw_sb[:, :, :].rearrange("p (g c) d -> p g c d", g=4),
                                 in0=pstage[:, :, :].rearrange("p (g c) d -> p g c d", g=4),
                                 in1=rowb[:, :].rearrange("p (g d) -> p g d", g=4).unsqueeze(2).to_broadcast([128, 4, 2, 256]),
                                 op=ALU.mult), w=(G(), B_["poolw"]))
    sch.dma("sp", rowb[:, :], ada_b.ap()[:, 2048:3072].partition_broadcast(128).rearrange("p o n -> p (o n)"), r=[G()], w=[G()])
    ac(lambda e: e.activation(out=scT[:, :, :], in_=cT[:, :, :], func=AF.Silu))
    for s in range(nseq):
        for dc in range(8):
            dv(lambda e, s=s, dc=dc: e.tensor_copy(out=scb[:, s, dc, :], in_=scT[:, dc, s:s + 1].to_broadcast([128, 128])))
    pm, pmb = palloc(2, static=[0, 1])
    pmv = pm[:, 0:32].rearrange("p (fc s) -> p fc s", s=2)
    for piece in range(2):
        sch.dma("pool", adaw_sb[:, :, :], ada_w_v[:, :, piece * 1024:(piece + 1) * 1024], w=[Badaw])

        def mod_mm(e, piece=piece):
            last = None
            for fc in range(8):
                for dc in range(8):
                    last = e.matmul(pmv[:, piece * 8 + fc, :], adaw_sb[:, dc, fc * 128:(fc + 1) * 128], scT[:, dc, :],
                                    start=(dc == 0), stop=(dc == 7))
            return last
        sch.op("pe", mod_mm, r=[G(), Badaw], w=pmb)
    dv(lambda e: e.tensor_tensor(out=modT[:, :, :], in0=pmv, in1=adabT[:, 0:16].unsqueeze(2).to_broadcast([128, 16, 2]),
                                 op=ALU.add), r=[G()] + pmb, w=[G()])
    dv(lambda e: e.tensor_copy(out=shiftT[:, :, :], in_=modT[:, 0:8, :]), w=(G(), B_["gprime"]))
    dv(lambda e: e.scalar_tensor_tensor(out=gprime[:, :, :], in0=modT[:, 8:16, :], scalar=1.0,
                                        in1=gpreT[:, :].unsqueeze(2).to_broadcast([128, 8, 2]),
                                        op0=ALU.add, op1=ALU.mult), w=(G(), B_["gprime"]))
    dv(lambda e: e.tensor_scalar_mul(out=gprime[:, :, :], in0=gprime[:, :, :], scalar1=float(D) ** 0.5), w=(G(), B_["gprime"]))
    sch.dma("pool", adaw_sb[:, :, :], ada_w_v[:, :, 2048:3072], w=[Badaw])
    gg_v = ggscr.ap()
    for s in range(nseq):
        pg, pgb = palloc(2, static=[2 + 2 * s, 3 + 2 * s])

        def gate_mm(e, s=s, pg=pg):
            last = None
            for half in range(2):
                for dc in range(8):
                    last = e.matmul(pg[:, half * 512:(half + 1) * 512], scb[:, s, dc, :],
                                    adaw_sb[:, dc, half * 512:(half + 1) * 512],
                                    start=(dc == 0), stop=(dc == 7))
            return last
        sch.op("pe", gate_mm, r=[G(), Badaw], w=pgb)
        for half in range(2):
            dv(lambda e, pg=pg, half=half: e.tensor_tensor(out=ggt[:, half * 512:(half + 1) * 512], in0=pg[:, half * 512:(half + 1) * 512],
                                                         in1=rowb[:, half * 512:(half + 1) * 512], op=ALU.add), r=[G()] + pgb, w=[G()])
        dv(lambda e: e.tensor_tensor(out=ggt[:, :], in0=ggt[:, :], in1=gpostb[:, :], op=ALU.mult))
        sch.dma("sp", gg_v[s * 128:(s + 1) * 128, :], ggt[:, :], r=[G()], w=[B_["ggscr"]], key=G())
    sch.dma("pool", wq_sb[:, :, 0:2048], w_in_v[:, :, 0:2048], w=[B_["wq"]])
    sch.dma("pool", wq_sb[:, :, 2048:4096], w_in_v[:, :, 4096:6144], w=[B_["wq"]])
    sch.dma("pool", wout_sb[:, :, :], w_out.ap().rearrange("(ec p) n -> p ec n", p=128), w=[B_["wout"]])

    sch.disabled = STAGE < 2
    sch.barrier(groups + [Brt, Badaw] + pbank)

    act_at = [arena_after_wkv]

    def mk(shape, dt, name, n=1):
        ts = [alloc(shape, dt, at=act_at, name=f"{name}{i}") for i in range(n)]
        bs = [Buf(f"{name}{i}") for i in range(n)]
        return ts, bs

    xin, xin_b = mk([128, 1024], F32, "xin", 2)
    junk = alloc([128, 1024], BF16, at=act_at, name="junk")
    junk_b = Buf("junk", disjoint=False)
    xn, xn_b = mk([128, 1024], BF16, "xn", 1)
    hT, hT_b = mk([128, 8, 128], BF16, "hT", 3)
    ssq, ssq_b = mk([128, 4], F32, "ssq", 2)
    rt, rt_b = mk([128, 128], F32, "rt", 2)
    t1, t1_b = mk([128, 512], F32, "t1", 1)
    t2, t2_b = mk([128, 512], F32, "t2", 1)
    common_end = act_at[0]

    _aux = {}

    def aux(b, tag):
        k = (id(b), tag)
        if k not in _aux:
            _aux[k] = Buf(b.name + tag)
        return _aux[k]

    def hTr(hs):
        return [aux(hT_b[hs], "a"), aux(hT_b[hs], "b")]

    def front(seq, c, hslot, xslot, ht_act=True):
        sch.cur_prio = FRONT_PRIO
        xt, xb = xin[xslot], xin_b[xslot]
        sq, sqb = ssq[xslot], ssq_b[xslot]
        sch.dma("sp", xt[:, :], xs[seq].ap()[c * 128:(c + 1) * 128, :], w=[xb])
        sch.op("act", lambda e: e.activation(out=junk[:, :], in_=xt[:, :], func=AF.Square, accum_out=sq[:, 0:1]),
               r=[xb], w=[sqb, junk_b])
        sch.op("pool", lambda e: e.tensor_scalar_add(out=sq[:, 1:2], in0=sq[:, 0:1], scalar1=float(D) * EPS), r=[sqb], w=[sqb])
        sch.op("pool", lambda e: e.tensor_tensor(out=sq[:, 2:3], in0=sq[:, 1:2], in1=mhalf[:, 0:1], op=ALU.pow),
               r=[sqb], w=[sqb])
        sch.op("act", lambda e: e.activation(out=xn[0][:, :], in_=xt[:, :], func=AF.Copy, scale=sq[:, 2:3]),
               r=[xb, sqb], w=[xn_b[0]])
        pt, ptb = palloc(2)
        ptv = [(lambda i=i: pt.bank(i).bitcast(BF16)[:, 0:512].rearrange("p (dc t) -> p dc t", dc=4)) for i in range(2)]
        for hb in range(2):
            def tr(e, hb=hb):
                last = None
                for d4 in range(4):
                    dc = hb * 4 + d4
                    last = e.transpose(ptv[hb]()[:, d4, :], xn[0][:, dc * 128:(dc + 1) * 128], ident[:, :])
                return last
            sch.op("pe", tr, r=[xn_b[0], B_["ident"]], w=[ptb[hb]])
        for dc in range(8):
            hb, d4 = dc // 4, dc % 4
            if hb == 0 or ht_act:
                sch.op("act", lambda e, dc=dc, d4=d4, hb=hb: e.activation(out=hT[hslot][:, dc, :], in_=ptv[hb]()[:, d4, :], func=AF.Identity,
                                                                          scale=gprime[:, dc, seq:seq + 1], bias=shiftT[:, dc, seq:seq + 1]),
                       r=[ptb[hb], B_["gprime"]], w=[aux(hT_b[hslot], "a" if hb == 0 else "b")])
            else:
                sch.op("dve", lambda e, dc=dc, d4=d4: e.tensor_scalar(out=hT[hslot][:, dc, :], in0=ptv[1]()[:, d4, :],
                                                                      scalar1=gprime[:, dc, seq:seq + 1],
                                                                      scalar2=shiftT[:, dc, seq:seq + 1],
                                                                      op0=ALU.mult, op1=ALU.add),
                       r=[ptb[1], B_["gprime"]], w=[aux(hT_b[hslot], "b")])

    def front_done():
        sch.cur_prio = 0

    def rope(psrc, psrc_b, dst, dst_b, rts, rtb, kscale, ceng="dve"):
        cosb = rts[:, 0:64].unsqueeze(1).to_broadcast([128, 8, 64])
        sinb = rts[:, 64:128].unsqueeze(1).to_broadcast([128, 4, 64])
        for half in range(2):
            src = lambda half=half: psrc[:, half * 512:(half + 1) * 512].rearrange("p (h two d) -> p h two d", h=4, two=2)
            t1v = t1[0][:, :].rearrange("p (h two d) -> p h two d", h=4, two=2)
            t2v = t2[0][:, :].rearrange("p (h two d) -> p h two d", h=4, two=2)
            dv_ = dst[:, half * 512:(half + 1) * 512].rearrange("p (h two d) -> p h two d", h=4, two=2)
            pb = [psrc_b[half]]
            src3 = lambda half=half: psrc[:, half * 512:(half + 1) * 512].rearrange("p (g d) -> p g d", g=8)
            t1v3 = t1[0][:, :].rearrange("p (g d) -> p g d", g=8)
            if kscale is None:
                sch.op("dve", lambda e, src3=src3, t1v3=t1v3: e.tensor_tensor(out=t1v3, in0=src3(), in1=cosb, op=ALU.mult),
                       r=pb + [rtb], w=[t1_b[0]])
                sch.op("dve", lambda e, src=src, t2v=t2v: e.tensor_tensor(out=t2v[:, :, 0, :], in0=src()[:, :, 1, :], in1=sinb, op=ALU.mult),
                       r=pb + [rtb], w=[t2_b[0]])
                sch.op("dve", lambda e, src=src, t2v=t2v: e.tensor_tensor(out=t2v[:, :, 1, :], in0=src()[:, :, 0, :], in1=sinb, op=ALU.mult),
                       r=pb + [rtb], w=[t2_b[0]])
            else:
                sch.op("dve", lambda e, src3=src3, t1v3=t1v3: e.scalar_tensor_tensor(out=t1v3, in0=src3(), scalar=kscale, in1=cosb,
                                                                                 op0=ALU.mult, op1=ALU.mult),
                       r=pb + [rtb], w=[t1_b[0]])
                sch.op("dve", lambda e, src=src, t2v=t2v: e.scalar_tensor_tensor(out=t2v[:, :, 0, :], in0=src()[:, :, 1, :], scalar=kscale,
                                                                                 in1=sinb, op0=ALU.mult, op1=ALU.mult),
                       r=pb + [rtb], w=[t2_b[0]])
                sch.op("dve", lambda e, src=src, t2v=t2v: e.scalar_tensor_tensor(out=t2v[:, :, 1, :], in0=src()[:, :, 0, :], scalar=kscale,
                                                                                 in1=sinb, op0=ALU.mult, op1=ALU.mult),
                       r=pb + [rtb], w=[t2_b[0]])
            sch.op(ceng, lambda e, t1v=t1v, t2v=t2v, dv_=dv_: e.tensor_tensor(out=dv_[:, :, 0, :], in0=t1v[:, :, 0, :], in1=t2v[:, :, 0, :],
                                                                                op=ALU.subtract),
                   r=[t1_b[0], t2_b[0]], w=[dst_b])
            sch.op(ceng, lambda e, t1v=t1v, t2v=t2v, dv_=dv_: e.tensor_tensor(out=dv_[:, :, 1, :], in0=t1v[:, :, 1, :], in1=t2v[:, :, 1, :],
                                                                                op=ALU.add),
                   r=[t1_b[0], t2_b[0]], w=[dst_b])

    def bc_h(tab):
        return tab[:, :].unsqueeze(2).to_broadcast([128, 8, 128])

    def v3(t):
        return t.rearrange("p (h d) -> p h d", h=8)

    pa_at = [common_end]
    krA, krA_b = [], []
    vA, vA_b = [], []
    for i in range(2):
        krA.append(alloc([128, 1024], BF16, at=pa_at, name=f"krA{i}")); krA_b.append(Buf(f"krA{i}"))
        vA.append(alloc([128, 1024], BF16, at=pa_at, name=f"vA{i}")); vA_b.append(Buf(f"vA{i}"))
    kbA = alloc([128, 1024], BF16, at=pa_at, name="kbA"); kbA_b = Buf("kbA")
    Bf32 = alloc([128, 1024], F32, at=pa_at, name="Bf32"); Bf32_b = Buf("Bf32")
    BbfA, BbfA_b = [], []
    for i in range(2):
        BbfA.append(alloc([128, 1024], BF16, at=pa_at, name=f"BbfA{i}")); BbfA_b.append(Buf(f"BbfA{i}"))
    scrB = {("kr", i): Buf(f"krscr{i}") for i in range(nseq)}
    scrB.update({("v", i): Buf(f"vscr{i}") for i in range(nseq)})
    scrB.update({("b", i): Buf(f"bscr{i}") for i in range(nseq)})
    scrB.update({("h", i): Buf(f"hscr{i}") for i in range(nseq)})

    itA = 0
    for seq in range(nseq):
        n = nchs[seq]
        for idx, c in enumerate(range(n - 1, -1, -1)):
            sl = itA % 2
            itA += 1
            hs = itA % 3
            front(seq, c, hs, sl, ht_act=HT_ACT_A)
            front_done()
            if SPILL_HT:
                sch.dma("sp", hscr[seq].ap()[c * 128:(c + 1) * 128, :], hT[hs][:, :, :].rearrange("p a b -> p (a b)"),
                        r=hTr(hs), w=[scrB[("h", seq)]], key=aux(hT_b[hs], "a"))
            sch.dma("sp", rt[sl][:, :], ropescr.ap()[c * 128:(c + 1) * 128, :], r=[B_["ropescr"]], w=[rt_b[sl]])
            pk, pkb = palloc(2)
            pv, pvb = palloc(2)
            for bi, (pp, ppb) in enumerate(((pk, pkb), (pv, pvb))):
                for half in range(2):
                    col0 = bi * 1024 + half * 512

                    def mm(e, pp=pp, half=half, col0=col0, hs=hs):
                        last = None
                        for dc in range(8):
                            last = e.matmul(pp[:, half * 512:(half + 1) * 512], hT[hs][:, dc, :], wkv_sb[:, dc, col0:col0 + 512],
                                            start=(dc == 0), stop=(dc == 7))
                        return last
                    sch.op("pe", mm, r=hTr(hs) + [B_["wkv"]], w=[ppb[half]])
            rope(pk, pkb, krA[sl], krA_b[sl], rt[sl], rt_b[sl], HD ** -0.5, ROPE_ENG_A)
            for half in range(2):
                sch.op("act", lambda e, half=half, pv=pv, sl=sl: e.copy(out=vA[sl][:, half * 512:(half + 1) * 512],
                                                                        in_=pv[:, half * 512:(half + 1) * 512]),
                       r=[pvb[half]], w=[vA_b[sl]])
            sch.dma("sp", krscr[seq].ap()[c * 128:(c + 1) * 128, :], krA[sl][:, :], r=[krA_b[sl]], w=[scrB[("kr", seq)]], key=krA_b[sl])
            sch.dma("sp", vscr[seq].ap()[c * 128:(c + 1) * 128, :], vA[sl][:, :], r=[vA_b[sl]], w=[scrB[("v", seq)]], key=vA_b[sl])
            if idx == 0:
                sch.op("pool", lambda e, sl=sl: e.memset(BbfA[sl][:, :], 0.0), w=[BbfA_b[sl]])
            sch.dma("sp", bscr[seq].ap()[c * 128:(c + 1) * 128, :], BbfA[sl][:, :], r=[BbfA_b[sl]], w=[scrB[("b", seq)]], key=BbfA_b[sl])
            if c == 0:
                continue
            sch.op(KB_ENG, lambda e, sl=sl: e.tensor_tensor(out=v3(kbA[:, :]), in0=v3(krA[sl][:, :]), in1=bc_h(kdb), op=ALU.mult),
                   r=[krA_b[sl], B_["kd"]], w=[kbA_b])
            pkv, pkvb = palloc(2)
            for half in range(2):
                def kvmm(e, half=half, pkv=pkv, sl=sl):
                    last = None
                    for hh in range(4):
                        h = half * 4 + hh
                        last = e.matmul(pkv[:, h * 128:(h + 1) * 128], kbA[:, h * 128:(h + 1) * 128], vA[sl][:, h * 128:(h + 1) * 128],
                                        start=True, stop=True)
                    return last
                sch.op("pe", kvmm, r=[kbA_b, vA_b[sl]], w=[pkvb[half]])
            ns = 1 - sl
            if idx == 0:
                for half in range(2):
                    sch.op("dve", lambda e, half=half, pkv=pkv: e.tensor_copy(out=Bf32[:, half * 512:(half + 1) * 512],
                                                                              in_=pkv[:, half * 512:(half + 1) * 512]),
                           r=[pkvb[half]], w=[Bf32_b])
            else:
                sch.op(FMUL_ENG, lambda e: e.tensor_tensor(out=v3(Bf32[:, :]), in0=v3(Bf32[:, :]), in1=bc_h(gcb), op=ALU.mult),
                       r=[Bf32_b, B_["kd"]], w=[Bf32_b])
                for half in range(2):
                    sch.op("dve", lambda e, half=half, pkv=pkv: e.tensor_tensor(out=Bf32[:, half * 512:(half + 1) * 512],
                                                                                in0=pkv[:, half * 512:(half + 1) * 512],
                                                                                in1=Bf32[:, half * 512:(half + 1) * 512], op=ALU.add),
                           r=[pkvb[half], Bf32_b], w=[Bf32_b])
            sch.op("act", lambda e, ns=ns: e.copy(out=BbfA[ns][:, :], in_=Bf32[:, :]), r=[Bf32_b], w=[BbfA_b[ns]])

    sch.disabled = STAGE < 3
    passA_bufs = [junk_b] + xin_b + xn_b + hT_b + [x for i in range(3) for x in hTr(i)] + ssq_b + rt_b + t1_b + t2_b + krA_b + vA_b + [kbA_b, Bf32_b] + BbfA_b + pbank + [B_["wkv"]]
    sch.barrier(passA_bufs)
    pb_at = [common_end]
    wkv_region = [arena0]

    def mkb(shape, dt, name, at):
        return alloc(shape, dt, at=at, name=name), Buf(name)

    class Ring:
        def __init__(self, name, shape, dt, n, at):
            self.items = [mkb(shape, dt, f"{name}{i}", at) for i in range(n)]

        def __getitem__(self, c):
            return self.items[c % len(self.items)]

    def ring(name, shape, dt, at2=None):
        n = NB.get(name, 1)
        r = Ring.__new__(Ring)
        r.items = []
        for i in range(n):
            r.items.append(mkb(shape, dt, f"{name}{i}", (at2 if at2 is not None else wkv_region) if i == 0 else pb_at))
        return r

    uring = [mkb([128, 1024], BF16, f"u{i}", wkv_region) for i in range(3)]
    qr_r = ring("qr", [128, 1024], BF16)
    sz_r = ring("sz", [128, 2048], BF16)
    krB_r = ring("krB", [128, 1024], BF16)
    vB_r = ring("vB", [128, 1024], BF16)
    BbfB_r = ring("BbfB", [128, 1024], BF16)
    kf_r = ring("kf", [128, 1024], BF16)
    QT_r = ring("QT", [128, 8, 128], BF16)
    QTf_r = ring("QTf", [128, 8, 128], BF16)
    QTb_r = ring("QTb", [128, 8, 128], BF16)
    KT_r = ring("KT", [128, 8, 128], BF16)
    scm_r = ring("scm", [128, 8, 128], BF16)
    if CHECK_SBUF:
        assert wkv_region[0] <= arena_after_wkv, (wkv_region[0], arena_after_wkv)
    Ff32, Ff32_b = mkb([128, 1024], F32, "Ff32", pb_at)
    Fbf_r = ring("Fbf", [128, 1024], BF16, pb_at)
    scr4_r = ring("scr4", [128, 1024], F32, pb_at)
    on_r = ring("on", [128, 1024], F32, pb_at)
    yb_r = ring("y", [128, 2048], BF16, pb_at)
    yT_r = ring("yT", [128, 16, 128], BF16, pb_at)
    plT_r = ring("plT", [128, 8, 128], BF16, pb_at)
    xres_r = ring("xres", [128, 1024], F32, pb_at)
    st_r = ring("st", [128, 64], F32, pb_at)
    ggtab, ggtab_b = mkb([128, 1024], F32, "ggtab", pb_at)
    ysc = {i: Buf(f"yout{i}") for i in range(nseq)}

    def frontB(seq, c, hs, xslot):
        sch.disabled = STAGE < 3
        if SPILL_HT:
            ha, hb_ = hTr(hs)
            sch.dma("sp", hT[hs][:, :, :].rearrange("p a b -> p (a b)"), hscr[seq].ap()[c * 128:(c + 1) * 128, :],
                    r=[scrB[("h", seq)]], w=[ha, hb_], key=ha)
        else:
            front(seq, c, hs, xslot)
            front_done()
        n = nchs[seq]
        ut, ub = uring[c % 3]
        pu, pub = palloc(2)
        for half in range(2):
            def mm(e, half=half, pu=pu, hs=hs):
                last = None
                for dc in range(8):
                    last = e.matmul(pu[:, half * 512:(half + 1) * 512], hT[hs][:, dc, :], wq_sb[:, dc, half * 512:(half + 1) * 512],
                                    start=(dc == 0), stop=(dc == 7))
                return last
            sch.op("pe", mm, r=hTr(hs) + [B_["wq"]], w=[pub[half]])
            sch.op("act", lambda e, half=half, pu=pu, ut=ut: e.copy(out=ut[:, half * 512:(half + 1) * 512], in_=pu[:, half * 512:(half + 1) * 512]),
                   r=[pub[half]], w=[ub])

    def loadsB(seq, c):
        sch.disabled = STAGE < 3
        sl = c % 2
        sch.dma("sp", rt[sl][:, :], ropescr.ap()[c * 128:(c + 1) * 128, :], r=[B_["ropescr"]], w=[rt_b[sl]])
        gc = gbase[0] + c
        krB, krB_b = krB_r[gc]
        vB, vB_b = vB_r[gc]
        BbfB, BbfB_b = BbfB_r[gc]
        sch.dma("sp", krB[:, :], krscr[seq].ap()[c * 128:(c + 1) * 128, :], r=[scrB[("kr", seq)]], w=[krB_b])
        sch.dma("sp", vB[:, :], vscr[seq].ap()[c * 128:(c + 1) * 128, :], r=[scrB[("v", seq)]], w=[vB_b])
        sch.dma("sp", BbfB[:, :], bscr[seq].ap()[c * 128:(c + 1) * 128, :], r=[scrB[("b", seq)]], w=[BbfB_b])
        xr, xrb = xres_r[gc]
        sch.dma("sp", xr[:, :], xs[seq].ap()[c * 128:(c + 1) * 128, :], w=[xrb])

    def backB(seq, c, hs):
        n = nchs[seq]
        sl = c % 2
        first, last_c = (c == 0), (c == n - 1)
        gc = gbase[0] + c
        qr, qr_b = qr_r[gc]; sz, sz_b = sz_r[gc]; krB, krB_b = krB_r[gc]; vB, vB_b = vB_r[gc]
        BbfB, BbfB_b = BbfB_r[gc]; kf, kf_b = kf_r[gc]; QT, QT_b = QT_r[gc]; QTf, QTf_b = QTf_r[gc]
        QTb, QTb_b = QTb_r[gc]; KT, KT_b = KT_r[gc]; scm, scm_b = scm_r[gc]
        Fbf, Fbf_b = Fbf_r[gc]; Fbf_n, Fbf_nb = Fbf_r[gc + 1]
        scr4, scr4_b = scr4_r[gc]; on, on_b = on_r[gc]; yb, yb_b = yb_r[gc]; yT, yT_b = yT_r[gc]
        plT, plT_b = plT_r[gc]; st, st_b = st_r[gc]
        szp_b, szr_b = aux(sz_b, "p"), aux(sz_b, "r")
        ybp_b, ybr_b = aux(yb_b, "p"), aux(yb_b, "r")
        yTp_b, yTr_b = aux(yT_b, "p"), aux(yT_b, "r")
        if STAGE < 4:
            sch.disabled = True
        pq, pqb = palloc(2)
        pz0, pz0b = palloc(2)
        pz1, pz1b = palloc(2)
        for (pp, ppb, colbase) in ((pq, pqb, 1024), (pz0, pz0b, 2048), (pz1, pz1b, 3072)):
            for half in range(2):
                col0 = colbase + half * 512

                def mm(e, pp=pp, half=half, col0=col0):
                    last = None
                    for dc in range(8):
                        last = e.matmul(pp[:, half * 512:(half + 1) * 512], hT[hs][:, dc, :], wq_sb[:, dc, col0:col0 + 512],
                                        start=(dc == 0), stop=(dc == 7))
                    return last
                sch.op("pe", mm, r=hTr(hs) + [B_["wq"]], w=[ppb[half]])
        rope(pq, pqb, qr, qr_b, rt[sl], rt_b[sl], None)
        for zi, (pz, pzb) in enumerate(((pz0, pz0b), (pz1, pz1b))):
            for half in range(2):
                sch.op("act", lambda e, zi=zi, half=half, pz=pz: e.activation(out=sz[:, zi * 1024 + half * 512: zi * 1024 + (half + 1) * 512],
                                                                              in_=pz[:, half * 512:(half + 1) * 512], func=AF.Silu),
                       r=[pzb[half]], w=[szp_b if zi == 0 else szr_b])
        if STAGE < 5:
            sch.disabled = True
        sch.cur_prio = RET_PRIO
        if not last_c:
            sch.op(KF_ENG, lambda e: e.tensor_tensor(out=v3(kf[:, :]), in0=v3(krB[:, :]), in1=bc_h(kdf), op=ALU.mult),
                   r=[krB_b, B_["kd"]], w=[kf_b])
        ptq, ptqb = palloc(1)
        ptk, ptkb = palloc(1)
        ptqv = lambda: ptq.bank(0).bitcast(BF16).rearrange("p (h t) -> p h t", h=8)
        ptkv = lambda: ptk.bank(0).bitcast(BF16).rearrange("p (h t) -> p h t", h=8)

        def trq(e):
            last = None
            for h in range(8):
                last = e.transpose(ptqv()[:, h, :], qr[:, h * 128:(h + 1) * 128], ident[:, :])
            return last

        def trk(e):
            last = None
            for h in range(8):
                last = e.transpose(ptkv()[:, h, :], krB[:, h * 128:(h + 1) * 128], ident[:, :])
            return last
        if SUB < 1:
            sch.disabled = True
        sch.op("pe", trq, r=[qr_b, B_["ident"]], w=ptqb)
        sch.op("pe", trk, r=[krB_b, B_["ident"]], w=ptkb)
        if SUB < 2:
            sch.disabled = True
        sch.op("act", lambda e: e.copy(out=QT[:, :, :], in_=ptqv()), r=ptqb, w=[QT_b])
        sch.op("act", lambda e: e.copy(out=KT[:, :, :], in_=ptkv()), r=ptkb, w=[KT_b])
        if SUB < 3:
            sch.disabled = True
        if not first:
            sch.op("dve", lambda e: e.tensor_tensor(out=QTf[:, :, :], in0=QT[:, :, :], in1=af_tab[:, :, :], op=ALU.mult),
                   r=[QT_b, B_["aftab"]], w=[QTf_b])
        if not last_c:
            sch.op(QTB_ENG, lambda e: e.tensor_tensor(out=QTb[:, :, :], in0=QT[:, :, :], in1=ab_tab[:, :, :], op=ALU.mult),
                   r=[QT_b, B_["abtab"]], w=[QTb_b])
        if STAGE < 6:
            sch.disabled = True
        psc, pscb_ = palloc(2)
        for half in range(2):
            def scmm(e, half=half, psc=psc):
                last = None
                for hh in range(4):
                    h = half * 4 + hh
                    last = e.matmul(psc[:, h * 128:(h + 1) * 128], KT[:, h, :], QT[:, h, :], start=True, stop=True)
                return last
            sch.op("pe", scmm, r=[KT_b, QT_b], w=[pscb_[half]])
            sch.op("dve", lambda e, half=half, psc=psc: e.tensor_tensor(out=scm[:, half * 4:(half + 1) * 4, :],
                                                                        in0=psc[:, half * 512:(half + 1) * 512].rearrange("p (h t) -> p h t", h=4),
                                                                        in1=dtot[:, half * 4:(half + 1) * 4, :], op=ALU.mult),
                   r=[pscb_[half], B_["dtot"]], w=[scm_b])
        po, pob = palloc(2)
        for half in range(2):
            def omm(e, half=half, po=po):
                last = None
                for hh in range(4):
                    h = half * 4 + hh
                    terms = [(scm[:, h, :], vB[:, h * 128:(h + 1) * 128])]
                    if not first:
                        terms.append((QTf[:, h, :], Fbf[:, h * 128:(h + 1) * 128]))
                    if not last_c:
                        terms.append((QTb[:, h, :], BbfB[:, h * 128:(h + 1) * 128]))
                    for ti, (l_, r_) in enumerate(terms):
                        last = e.matmul(po[:, h * 128:(h + 1) * 128], l_, r_, start=(ti == 0), stop=(ti == len(terms) - 1))
                return last
            rr = [scm_b, vB_b]
            if not first:
                rr += [QTf_b, Fbf_b]
            if not last_c:
                rr += [QTb_b, BbfB_b]
            sch.op("pe", omm, r=rr, w=[pob[half]])
        if STAGE < 7:
            sch.disabled = True
        sch.cur_prio = FUPD_PRIO
        if not last_c:
            pkv, pkvb = palloc(2)
            for half in range(2):
                def kvmm(e, half=half, pkv=pkv):
                    last = None
                    for hh in range(4):
                        h = half * 4 + hh
                        last = e.matmul(pkv[:, h * 128:(h + 1) * 128], kf[:, h * 128:(h + 1) * 128], vB[:, h * 128:(h + 1) * 128],
                                        start=True, stop=True)
                    return last
                sch.op("pe", kvmm, r=[kf_b, vB_b], w=[pkvb[half]])
            if first:
                for half in range(2):
                    sch.op("dve", lambda e, half=half, pkv=pkv: e.tensor_copy(out=Ff32[:, half * 512:(half + 1) * 512],
                                                                              in_=pkv[:, half * 512:(half + 1) * 512]),
                           r=[pkvb[half]], w=[Ff32_b])
            else:
                sch.op(FMUL_ENG, lambda e: e.tensor_tensor(out=v3(Ff32[:, :]), in0=v3(Ff32[:, :]), in1=bc_h(gcf), op=ALU.mult),
                       r=[Ff32_b, B_["kd"]], w=[Ff32_b])
                for half in range(2):
                    sch.op("dve", lambda e, half=half, pkv=pkv: e.tensor_tensor(out=Ff32[:, half * 512:(half + 1) * 512],
                                                                                in0=pkv[:, half * 512:(half + 1) * 512],
                                                                                in1=Ff32[:, half * 512:(half + 1) * 512], op=ALU.add),
                           r=[pkvb[half], Ff32_b], w=[Ff32_b])
            sch.op("act", lambda e: e.copy(out=Fbf_n[:, :], in_=Ff32[:, :]), r=[Ff32_b], w=[Fbf_nb])
        if STAGE < 8:
            sch.disabled = True
        sch.cur_prio = NORM_PRIO
        onh = [aux(on_b, "0"), aux(on_b, "1")]
        for half in range(2):
            sch.op("act", lambda e, half=half, po=po: e.copy(out=on[:, half * 512:(half + 1) * 512], in_=po[:, half * 512:(half + 1) * 512]),
                   r=[pob[half]], w=[onh[half]])
        sch.op("dve", lambda e: e.tensor_reduce(out=st[:, 0:8], in_=v3(on[:, :]), axis=AX.X, op=ALU.add), r=onh, w=[st_b])
        sch.op("act", lambda e: e.activation(out=scr4[:, :], in_=on[:, :], func=AF.Square), r=onh, w=[scr4_b])
        sch.op("dve", lambda e: e.tensor_reduce(out=st[:, 8:16], in_=v3(scr4[:, :]), axis=AX.X, op=ALU.add), r=[scr4_b], w=[st_b])
        sch.op("dve", lambda e: e.tensor_scalar_mul(out=st[:, 16:24], in0=st[:, 0:8], scalar1=1.0 / HD), r=[st_b], w=[st_b])
        sch.op("dve", lambda e: e.tensor_tensor(out=st[:, 24:32], in0=st[:, 16:24], in1=st[:, 16:24], op=ALU.mult), r=[st_b], w=[st_b])
        sch.op("dve", lambda e: e.scalar_tensor_tensor(out=st[:, 32:40], in0=st[:, 8:16], scalar=1.0 / HD, in1=st[:, 24:32],
                                                       op0=ALU.mult, op1=ALU.subtract), r=[st_b], w=[st_b])
        sch.op("dve", lambda e: e.tensor_scalar_add(out=st[:, 32:40], in0=st[:, 32:40], scalar1=EPS), r=[st_b], w=[st_b])
        sch.op("pool", lambda e: e.tensor_tensor(out=st[:, 40:48], in0=st[:, 32:40], in1=mhalf[:, 0:8], op=ALU.pow), r=[st_b], w=[st_b])
        sch.op("dve", lambda e: e.scalar_tensor_tensor(out=st[:, 48:56], in0=st[:, 16:24], scalar=-1.0, in1=st[:, 40:48],
                                                       op0=ALU.mult, op1=ALU.mult), r=[st_b], w=[st_b])
        for h in range(8):
            half = h // 4
            if half == 0:
                sch.op("act", lambda e, h=h: e.activation(out=on[:, h * 128:(h + 1) * 128], in_=on[:, h * 128:(h + 1) * 128],
                                                          func=AF.Identity, scale=st[:, 40 + h:41 + h], bias=st[:, 48 + h:49 + h]),
                       r=[onh[0], st_b], w=[onh[0]])
            else:
                sch.op("dve", lambda e, h=h: e.tensor_scalar(out=on[:, h * 128:(h + 1) * 128], in0=on[:, h * 128:(h + 1) * 128],
                                                             scalar1=st[:, 40 + h:41 + h], scalar2=st[:, 48 + h:49 + h],
                                                             op0=ALU.mult, op1=ALU.add),
                       r=[onh[1], st_b], w=[onh[1]])
        if SUB2 < 3:
            sch.disabled = True
        sch.op("dve", lambda e: e.tensor_tensor(out=yb[:, 1024:2048], in0=on[:, :], in1=sz[:, 1024:2048], op=ALU.mult),
               r=[aux(on_b, "0"), aux(on_b, "1"), szr_b], w=[ybr_b])
        if STAGE < 9:
            sch.disabled = True
        sch.cur_prio = POOL_PRIO
        ppl, pplb = palloc(2)
        for half in range(2):
            def plmm(e, half=half, ppl=ppl):
                last = None
                for cc4 in range(4):
                    cc = half * 4 + cc4
                    gi = cc // 2
                    terms = []
                    if not first:
                        terms.append((uring[(c - 1) % 3][0], gi * 5 + 0))
                    terms.append((uring[c % 3][0], gi * 5 + (3 if first else (4 if last_c else 1))))
                    if not last_c:
                        terms.append((uring[(c + 1) % 3][0], gi * 5 + 2))
                    for ti, (ut, bi) in enumerate(terms):
                        last = e.matmul(ppl[:, cc * 128:(cc + 1) * 128], ut[:, cc * 128:(cc + 1) * 128], bands[:, bi, :],
                                        start=(ti == 0), stop=(ti == len(terms) - 1))
                return last
            rr = [uring[c % 3][1], B_["bands"]]
            if not first:
                rr.append(uring[(c - 1) % 3][1])
            if not last_c:
                rr.append(uring[(c + 1) % 3][1])
            sch.op("pe", plmm, r=rr, w=[pplb[half]])
            sch.op("act", lambda e, half=half, ppl=ppl: e.copy(out=plT[:, half * 4:(half + 1) * 4, :],
                                                               in_=ppl[:, half * 512:(half + 1) * 512].rearrange("p (c t) -> p c t", c=4)),
                   r=[pplb[half]], w=[plT_b])
        pyp, pypb = palloc(2)
        for half in range(2):
            def ypmm(e, half=half, pyp=pyp):
                last = None
                for g2 in range(2):
                    gi = half * 2 + g2
                    for cc2 in range(2):
                        cc = gi * 2 + cc2
                        last = e.matmul(pyp[:, gi * 256:(gi + 1) * 256], plT[:, cc, :], poolw_sb[:, cc, :],
                                        start=(cc2 == 0), stop=(cc2 == 1))
                return last
            sch.op("pe", ypmm, r=[plT_b, B_["poolw"]], w=[pypb[half]])
            sch.op("dve", lambda e, half=half, pyp=pyp: e.tensor_tensor(out=yb[:, half * 512:(half + 1) * 512],
                                                                        in0=pyp[:, half * 512:(half + 1) * 512],
                                                                        in1=sz[:, half * 512:(half + 1) * 512], op=ALU.mult),
                   r=[pypb[half], szp_b], w=[ybp_b])
        if STAGE < 10:
            sch.disabled = True
        sch.cur_prio = 0
        pty, ptyb = palloc(2)
        ptyh = [(lambda i=i: pty.bank(i).bitcast(BF16).rearrange("p (e t) -> p e t", e=8)) for i in range(2)]
        for half in range(2):
            def trY(e, half=half):
                last = None
                for e8 in range(8):
                    ec = half * 8 + e8
                    last = e.transpose(ptyh[half]()[:, e8, :], yb[:, ec * 128:(ec + 1) * 128], ident[:, :])
                return last
            sch.op("pe", trY, r=[ybp_b if half == 0 else ybr_b, B_["ident"]], w=[ptyb[half]])
        sch.op("act", lambda e: e.copy(out=yT[:, 0:8, :], in_=ptyh[0]()), r=[ptyb[0]], w=[yTp_b])
        sch.op("dve", lambda e: e.tensor_copy(out=yT[:, 8:16, :], in_=ptyh[1]()), r=[ptyb[1]], w=[yTr_b])
        pout, poutb = palloc(2)
        sch.cur_prio = FINAL_PRIO
        for half in range(2):
            def outmm(e, half=half, pout=pout):
                last = None
                for ec in range(16):
                    last = e.matmul(pout[:, half * 512:(half + 1) * 512], yT[:, ec, :], wout_sb[:, ec, half * 512:(half + 1) * 512],
                                    start=(ec == 0), stop=(ec == 15))
                return last
            sch.op("pe", outmm, r=[yTp_b, yTr_b, B_["wout"]], w=[poutb[half]])
            sch.op("act", lambda e, half=half, pout=pout: e.activation(out=junk[:, half * 512:(half + 1) * 512], in_=pout[:, half * 512:(half + 1) * 512],
                                                                       func=AF.Square, accum_out=st[:, 56 + half:57 + half]),
                   r=[poutb[half]], w=[st_b, junk_b])
        sch.op("dve", lambda e: e.tensor_tensor(out=st[:, 58:59], in0=st[:, 56:57], in1=st[:, 57:58], op=ALU.add), r=[st_b], w=[st_b])
        sch.op("dve", lambda e: e.tensor_scalar(out=st[:, 59:60], in0=st[:, 58:59], scalar1=1.0 / D, scalar2=EPS,
                                                op0=ALU.mult, op1=ALU.add), r=[st_b], w=[st_b])
        sch.op("pool", lambda e: e.tensor_tensor(out=st[:, 60:61], in0=st[:, 59:60], in1=mhalf[:, 0:1], op=ALU.pow), r=[st_b], w=[st_b])
        for half in range(2):
            sch.op("dve", lambda e, half=half, pout=pout: e.scalar_tensor_tensor(out=scr4[:, half * 512:(half + 1) * 512],
                                                                                 in0=pout[:, half * 512:(half + 1) * 512], scalar=st[:, 60:61],
                                                                                 in1=ggtab[:, half * 512:(half + 1) * 512],
                                                                                 op0=ALU.mult, op1=ALU.mult),
                   r=[poutb[half], st_b, ggtab_b], w=[scr4_b])
        xr, xrb = xres_r[gc]
        sch.op(FIN_ENG, lambda e: e.tensor_tensor(out=xr[:, :], in0=xr[:, :], in1=scr4[:, :], op=ALU.add), r=[xrb, scr4_b], w=[xrb])
        sch.dma("sp", ys[seq].ap()[c * 128:(c + 1) * 128, :], xr[:, :], r=[xrb], w=[ysc[seq]], key=xrb)
        sch.cur_prio = 0

    itB = 0
    gbase = [0]
    for seq in range(nseq):
        gbase[0] = itB
        n = nchs[seq]
        sch.dma("sp", ggtab[:, :], ggscr.ap()[seq * 128:(seq + 1) * 128, :], r=[B_["ggscr"]], w=[ggtab_b])
        frontB(seq, 0, itB % 3, itB % 2)
        loadsB(seq, 0)
        for c in range(n):
            hs_c = (itB + c) % 3
            if c + 1 < n:
                frontB(seq, c + 1, (itB + c + 1) % 3, (itB + c + 1) % 2)
            backB(seq, c, hs_c)
            if c + 1 < n:
                loadsB(seq, c + 1)
        itB += n

    sch.disabled = False
    if WARM:
        sch.warm_fn = lambda e: e.matmul(psum[:, 7 * 512:8 * 512], ident[:, :], bands[:, 0:4, :], start=True, stop=True)
    sch.finish()
    sch.sbuf_free = sb_hi - sbuf_peak[0]

    with nc.Block() as block:
        @block.tensor
        def _(e):
            sch.emit("pe", e)

        @block.scalar
        def _(e):
            sch.emit("act", e)

        @block.vector
        def _(e):
            sch.emit("dve", e)

        @block.gpsimd
        def _(e):
            sch.emit("pool", e)

        @block.sync
        def _(e):
            sch.emit("sp", e)
    return nc


_PROG_CACHE = {}


def _get_prog(S_list):
    key = tuple(S_list)
    if key not in _PROG_CACHE:
        _PROG_CACHE[key] = build_program(list(S_list))
    return _PROG_CACHE[key]


def kernel(x_prompt, x_sample, c_prompt, c_sample, ada_w, ada_b, norm_pre, norm_post,
           w_in, pool_w, pool_scale, ret_decay_fwd, ret_decay_bwd, w_out):
    f = lambda a: np.ascontiguousarray(np.asarray(a, dtype=np.float32))
    x_prompt, x_sample = f(x_prompt), f(x_sample)
    nb = x_prompt.shape[0]
    assert nb == N_CORES and x_sample.shape[0] == N_CORES
    S0, S1 = x_prompt.shape[1], x_sample.shape[1]
    nc = _get_prog((S0, S1))
    c_prompt, c_sample = f(c_prompt), f(c_sample)
    shared = {
        "ada_w": f(ada_w)[0], "ada_b": f(ada_b), "norm_pre": f(norm_pre), "norm_post": f(norm_post),
        "w_in": f(w_in)[0], "pool_w": f(pool_w)[0], "pool_scale": f(pool_scale),
        "dec_f": f(ret_decay_fwd), "dec_b": f(ret_decay_bwd), "w_out": f(w_out)[0],
    }
    in_maps = []
    for i in range(N_CORES):
        m = dict(shared)
        m["x0"] = x_prompt[i]
        m["x1"] = x_sample[i]
        m["cvec"] = np.ascontiguousarray(np.stack([c_prompt[i], c_sample[i]], axis=0))
        in_maps.append(m)
    res = run_bass_kernel_spmd(nc, in_maps, core_ids=list(range(N_CORES)))
    y0 = np.stack([np.asarray(r["y0"], dtype=np.float32) for r in res.results], axis=0)
    y1 = np.stack([np.asarray(r["y1"], dtype=np.float32) for r in res.results], axis=0)
    return (y0, y1)
```

```python
import math
import numpy as np
import concourse.bass as bass
import concourse.mybir as mybir
from concourse.bass_utils import run_bass_kernel_spmd

F32 = mybir.dt.float32
BF16 = mybir.dt.bfloat16
AF = mybir.ActivationFunctionType
ALU = mybir.AluOpType
AX = mybir.AxisListType

D = 1024
H = 8
HD = 128
EPS = 1e-6
POOL_W = (2, 4, 8, 16)
N_CORES = 8
SAME_ENGINE_SYNC = True
SAME_ENGINE_RAW_ONLY = True
CW1 = 6.28125
CW2 = 2.0 * math.pi - 6.28125
PI_SAFE = 3.1415925
HALF_PI_SAFE = 1.5707962


class Op:
    __slots__ = ("idx", "eng", "seng", "fn", "deps", "kind", "inc", "sem", "val", "cost", "lat", "phase",
                 "start", "finish", "key", "meta", "vbs", "prio", "raw")


class Buf:
    __slots__ = ("name", "w", "r", "excl", "phys", "users", "virt", "disjoint")

    def __init__(self, name, excl=False, virt=False, disjoint=True):
        self.name = name
        self.disjoint = disjoint
        self.w = None
        self.r = []
        self.excl = excl
        self.virt = virt
        self.phys = None
        self.users = []


class _Dummy:
    def then_inc(self, *a, **k):
        return self


class _Probe:
    def __init__(self):
        self.calls = []

    def __getattr__(self, name):
        def f(*a, **k):
            self.calls.append((name, a, k))
            return _Dummy()
        return f


def _free(ap):
    try:
        return int(np.prod(ap.shape[1:]))
    except Exception:
        return 1


def _est_cost(eng, calls):
    t = 0.0
    for name, a, k in calls:
        if name == "matmul":
            rhs = a[2] if len(a) > 2 else k["rhs"]
            t += 0.023 + 0.00044 * _free(rhs)
        elif name == "transpose":
            t += 0.08
        else:
            aps = [k.get(n) for n in ("out", "in_", "in0")] + list(a[:1])
            F = max([_free(x) for x in aps if x is not None and hasattr(x, "shape")] + [1])
            if eng == "act":
                t += 0.36 + 0.00062 * F
            elif eng == "dve":
                t += 0.2 + 0.00105 * F
            else:
                if k.get("op") == ALU.pow:
                    t += 0.45 + 0.15 * F
                else:
                    t += 0.45 + 0.0018 * F
    return t


class Sched:
    ENGS = ("pe", "act", "dve", "pool", "sp")
    WINDOW = 96
    XLAT = 0.3

    def __init__(self, nc):
        self.nc = nc
        self.all = []
        self.esem = {e: nc.alloc_semaphore("s_" + e) for e in self.ENGS}
        self.dsem = {}
        self.phase = 0
        self.disabled = False
        self.cur_prio = 0
        self.warm_fn = None

    def _deps(self, r, w, excl_eng):
        deps = []
        self._raw = set()
        self._strong = set()
        for b in r:
            if b.w is not None:
                deps.append(b.w)
                self._raw.add(id(b.w))
            if b.excl:
                deps.extend(o for o in b.r if o.eng != excl_eng)
        for b in w:
            if b.w is not None:
                deps.append(b.w)
                if not b.disjoint:
                    self._strong.add(id(b.w))
            deps.extend(b.r)
            self._strong.update(id(o) for o in b.r)
        seen = set()
        out = []
        for d in deps:
            if id(d) not in seen:
                seen.add(id(d))
                out.append(d)
        return out

    def _new(self, eng, seng, fn, deps, kind, inc, cost, lat, key=None):
        o = Op()
        o.idx = len(self.all)
        o.eng, o.seng, o.fn, o.deps, o.kind, o.inc = eng, seng, fn, deps, kind, inc
        o.cost, o.lat, o.phase, o.key = cost, lat, self.phase, key
        o.sem = o.val = o.start = o.finish = None
        o.meta = ([], [])
        o.vbs = []
        o.raw = set()
        o.prio = 0 if seng == "pe" else self.cur_prio
        self.all.append(o)
        return o

    def op(self, eng, fn, r=(), w=(), sig=True):
        if self.disabled:
            return
        deps = self._deps(r, w, eng)
        pr = _Probe()
        fn(pr)
        cost = _est_cost(eng, pr.calls)
        o = self._new(eng, eng, fn, deps, "op", 1, cost, cost)
        o.raw = self._raw | self._strong
        o.meta = ([b.name for b in r], [b.name for b in w])
        for b in list(r) + list(w):
            if b.virt and o not in b.users:
                b.users.append(o)
                o.vbs.append(b)
        for b in r:
            b.r.append(o)
        for b in w:
            b.w = o
            b.r = []

    def dma(self, q, out_ap, in_ap, r=(), w=(), key=None, slow=False):
        if self.disabled:
            return None
        if key is None:
            key = w[0]
        if key not in self.dsem:
            self.dsem[key] = [self.nc.alloc_semaphore("d_" + key.name), None]
        ent = self.dsem[key]
        deps = self._deps(r, w, "dma")
        if ent[1] is not None and ent[1] not in deps:
            deps.append(ent[1])
        nc = self.nc

        def fn(e, out_ap=out_ap, in_ap=in_ap, slow=slow):
            if slow:
                with nc.allow_non_contiguous_dma(reason="one-time small strided load"):
                    return e.dma_start(out=out_ap, in_=in_ap)
            return e.dma_start(out=out_ap, in_=in_ap)

        nbytes = int(np.prod(out_ap.shape)) * 4
        issue = 0.4 if q == "sp" else 1.5
        o = self._new("dma", q, fn, deps, "dma", 16, issue, issue + 2.0 + nbytes / 150e3, key=key)
        ent[1] = o
        for b in r:
            b.r.append(o)
        for b in w:
            b.w = o
            b.r = []
        return o

    def barrier(self, bufs):
        if self.disabled:
            return
        evs = []
        for b in bufs:
            if b.w is not None:
                evs.append(b.w)
            evs.extend(b.r)
        self.phase += 1
        for e in self.ENGS:
            self._new(e, e, None, list(evs), "bar", 0, 0.0, 0.0)
        self.phase += 1

    def finish(self):
        self.phase += 1
        last = [ent[1] for ent in self.dsem.values() if ent[1] is not None]
        self._new("sp", "sp", None, last, "bar", 0, 0.0, 0.0)
        self._schedule()

    def _schedule(self):
        free = {e: 0.0 for e in self.ENGS}
        order = {e: [] for e in self.ENGS}
        tenant = [None] * NDYN_BANKS
        npend = {}
        nph = self.phase + 1
        byph = [dict((e, []) for e in self.ENGS) for _ in range(nph)]
        for o in self.all:
            byph[o.phase][o.seng].append(o)
        if BL_PRIO:
            succ = {}
            for o in self.all:
                for d in o.deps:
                    succ.setdefault(id(d), []).append(o)
            bl = {}
            for o in reversed(self.all):
                m = 0.0
                for q in succ.get(id(o), ()):
                    v = bl[id(q)] + (0.0 if q.seng == o.seng else self.XLAT)
                    if v > m:
                        m = v
                bl[id(o)] = m + o.lat
            for o in self.all:
                if o.phase in BL_PHASES:
                    o.prio = -bl[id(o)] * BL_SCALE
        for ph in range(nph):
            uns = byph[ph]
            for e in self.ENGS:
                if BL_PRIO and ph in BL_PHASES:
                    uns[e].sort(key=lambda o: (o.prio, o.idx))
                else:
                    uns[e].sort(key=lambda o: (o.idx + o.prio, o.idx))
            remaining = sum(len(v) for v in uns.values())
            wide = False
            while remaining:
                best = None
                for e in self.ENGS:
                    lst = uns[e]
                    cb = None
                    for o in (lst if wide else lst[:self.WINDOW]):
                        ready = 0.0
                        ok = True
                        for d in o.deps:
                            if d.finish is None:
                                ok = False
                                break
                            rr = d.finish + (0.0 if d.seng == e and d.kind != "dma" else self.XLAT)
                            if rr > ready:
                                ready = rr
                        if not ok:
                            continue
                        need_bank = [v for v in o.vbs if v.phys is None]
                        if need_bank:
                            cands = []
                            for p in range(NDYN_BANKS):
                                tv = tenant[p]
                                if tv is None:
                                    cands.append((0.0, p))
                                elif npend.get(id(tv), len(tv.users)) == 0:
                                    cands.append((max(u.finish for u in tv.users) + self.XLAT, p))
                            if len(cands) < len(need_bank):
                                continue
                            cands.sort()
                            bank_rdy = cands[len(need_bank) - 1][0]
                            if bank_rdy > ready:
                                ready = bank_rdy
                            o_banks = [p for _, p in cands[:len(need_bank)]]
                        else:
                            o_banks = None
                        st = ready if ready > free[e] else free[e]
                        if cb is None or st < cb[0]:
                            cb = (st, o, o_banks)
                        if st <= free[e]:
                            break
                    if cb is not None and (best is None or (cb[0], (cb[1].prio if (BL_PRIO and ph in BL_PHASES) else cb[1].idx + cb[1].prio)) < (best[0], (best[1].prio if (BL_PRIO and ph in BL_PHASES) else best[1].idx + best[1].prio))):
                        best = cb
                if best is None:
                    if not wide:
                        wide = True
                        continue
                    raise RuntimeError("scheduler stuck (PSUM bank deadlock)")
                wide = False
                st, o, o_banks = best
                if o_banks is not None:
                    need_bank = [v for v in o.vbs if v.phys is None]
                    for v, p in zip(need_bank, o_banks):
                        tv = tenant[p]
                        if tv is not None:
                            for u in tv.users:
                                if u not in o.deps:
                                    o.deps.append(u)
                        tenant[p] = v
                        v.phys = p
                for v in o.vbs:
                    npend[id(v)] = npend.get(id(v), len(v.users)) - 1
                o.start = st
                o.finish = st + o.lat
                free[o.seng] = st + o.cost
                uns[o.seng].remove(o)
                order[o.seng].append(o)
                remaining -= 1
        self.order = order
        self.model_us = max(free.values())
        cnt = {e: 0 for e in self.ENGS}
        dcnt = {}
        for e in self.ENGS:
            for o in order[e]:
                if o.kind == "op":
                    cnt[e] += 1
                    o.sem, o.val = self.esem[e], cnt[e]
        for e in self.ENGS:
            for o in order[e]:
                if o.kind == "dma":
                    k = id(o.key)
                    dcnt[k] = dcnt.get(k, 0) + 16
                    o.sem, o.val = self.dsem[o.key][0], dcnt[k]

    def emit(self, eng_name, e):
        waited = {}
        order = self.order[eng_name]
        for oi, o in enumerate(order):
            need = {}
            for d in o.deps:
                if d.kind == "bar":
                    continue
                if d.kind == "op" and d.seng == eng_name and o.kind != "dma":
                    if eng_name == "pe" or not SAME_ENGINE_SYNC:
                        continue
                    if SAME_ENGINE_RAW_ONLY and id(d) not in o.raw:
                        continue
                assert d.val is not None
                k = id(d.sem)
                if k not in need or need[k][1] < d.val:
                    need[k] = (d.sem, d.val)
            for k, (sem, v) in need.items():
                if waited.get(k, 0) >= v:
                    continue
                e.wait_ge(sem, v)
                waited[k] = v
            if o.fn is None:
                continue
            inst = o.fn(e)
            inst.then_inc(o.sem, o.inc)
            if eng_name == "pe" and self.warm_fn is not None and o.phase >= 2 and oi + 1 < len(order):
                gap = order[oi + 1].start - (o.start + o.cost)
                if gap > WARM_GAP:
                    for _ in range(min(WARM_MAX, int(WARM_FRAC * gap / 0.22))):
                        self.warm_fn(e)


STAGE = 99
CHECK_SBUF = True
NPAIRS = 4
HT_ACT_A = False
FRONT_PRIO = 0
KB_ENG = "dve"
FMUL_ENG = "dve"
KF_ENG = "dve"
QTB_ENG = "dve"
FINAL_PRIO = 100
RET_PRIO = 0
FUPD_PRIO = 0
NORM_PRIO = 0
POOL_PRIO = 0
BL_PRIO = False
BL_SCALE = 1.0
BL_PHASES = (4,)
NDYN_BANKS = 7
WARM = True
SPILL_HT = True
SPILL_KT = False
HALO_MERGE = True
WARM_GAP = 0.5
WARM_FRAC = 1.0
WARM_MAX = 16
ROPE_ENG_A = "dve"
FIN_ENG = "dve"
NB = {"xres": 1}
SUB = 9
SUB2 = 9


def build_program(S_list):
    nseq = len(S_list)
    assert nseq == 2
    nchs = [s // 128 for s in S_list]
    Smax = max(S_list)
    nchmax = Smax // 128
    nc = bass.Bass("TRN2", target_bir_lowering=False)

    xs = [nc.dram_tensor(f"x{i}", [S_list[i], D], F32, kind="ExternalInput") for i in range(nseq)]
    ys = [nc.dram_tensor(f"y{i}", [S_list[i], D], F32, kind="ExternalOutput") for i in range(nseq)]
    cvec = nc.dram_tensor("cvec", [nseq, D], F32, kind="ExternalInput")
    ada_w = nc.dram_tensor("ada_w", [D, 3 * D], F32, kind="ExternalInput")
    ada_b = nc.dram_tensor("ada_b", [1, 3 * D], F32, kind="ExternalInput")
    norm_pre = nc.dram_tensor("norm_pre", [1, D], F32, kind="ExternalInput")
    norm_post = nc.dram_tensor("norm_post", [1, D], F32, kind="ExternalInput")
    w_in = nc.dram_tensor("w_in", [D, 6 * D], F32, kind="ExternalInput")
    pool_w = nc.dram_tensor("pool_w", [4, 256, 256], F32, kind="ExternalInput")
    pool_scale = nc.dram_tensor("pool_scale", [1, D], F32, kind="ExternalInput")
    dec_f = nc.dram_tensor("dec_f", [1, H], F32, kind="ExternalInput")
    dec_b = nc.dram_tensor("dec_b", [1, H], F32, kind="ExternalInput")
    w_out = nc.dram_tensor("w_out", [2 * D, D], F32, kind="ExternalInput")
    krscr = [nc.dram_tensor(f"krscr{i}", [S_list[i], D], BF16, kind="Internal") for i in range(nseq)]
    vscr = [nc.dram_tensor(f"vscr{i}", [S_list[i], D], BF16, kind="Internal") for i in range(nseq)]
    bscr = [nc.dram_tensor(f"bscr{i}", [S_list[i], D], BF16, kind="Internal") for i in range(nseq)]
    ktscr = [nc.dram_tensor(f"ktscr{i}", [S_list[i], D], BF16, kind="Internal") for i in range(nseq)]
    hscr = [nc.dram_tensor(f"hscr{i}", [S_list[i], D], BF16, kind="Internal") for i in range(nseq)]
    ropescr = nc.dram_tensor("ropescr", [Smax, 128], F32, kind="Internal")
    ggscr = nc.dram_tensor("ggscr", [nseq * 128, D], F32, kind="Internal")

    sch = Sched(nc)

    sb_lo = (nc.sbuf_base + 63) // 64 * 64
    sb_hi = nc.sbuf_top
    cur = [sb_lo]
    names = [0]
    sbuf_peak = [0]

    def alloc(shape, dt, at=None, name=None):
        nbytes = int(np.prod(shape[1:])) * (2 if dt == BF16 else 4)
        nbytes = (nbytes + 63) // 64 * 64
        if at is None:
            off = cur[0]
            cur[0] += nbytes
        else:
            off = at[0]
            at[0] += nbytes
        names[0] += 1
        if CHECK_SBUF:
            assert off + nbytes <= sb_hi, f"SBUF overflow at {name}: {off + nbytes} > {sb_hi}"
        sbuf_peak[0] = max(sbuf_peak[0], off + nbytes)
        if not CHECK_SBUF and off + nbytes > sb_hi:
            off = sb_lo
        return nc.alloc_sbuf_tensor_at(f"t{names[0]}_{name or ''}", list(shape), dt, offset=off)

    wq_sb = alloc([128, 8, 4096], BF16, name="wq")
    wout_sb = alloc([128, 16, 1024], BF16, name="wout")
    poolw_sb = alloc([128, 8, 256], BF16, name="poolw")
    ident = alloc([128, 128], BF16, name="ident")
    dtot = alloc([128, 8, 128], F32, name="dtot")
    af_tab = alloc([128, 8, 128], BF16, name="aftab")
    ab_tab = alloc([128, 8, 128], BF16, name="abtab")
    bands = alloc([128, 24, 128], BF16, name="bands")
    kdf = alloc([128, 8], F32, name="kdf")
    kdb = alloc([128, 8], F32, name="kdb")
    gcf = alloc([128, 8], F32, name="gcf")
    gcb = alloc([128, 8], F32, name="gcb")
    gprime = alloc([128, 8, 2], F32, name="gprime")
    shiftT = alloc([128, 8, 2], F32, name="shiftT")
    negpi = alloc([128, 1], F32, name="negpi")
    mhalf = alloc([128, 8], F32, name="mhalf")
    halfpi = alloc([128, 1], F32, name="halfpi")
    B_ = {n: Buf(n) for n in ["wq", "wout", "poolw", "ident", "dtot", "aftab", "abtab", "bands", "kd",
                              "gprime", "negpi", "wkv", "ropescr", "ggscr"]}
    arena0 = cur[0]
    wkv_at = [arena0]
    wkv_sb = alloc([128, 8, 2048], BF16, at=wkv_at, name="wkv")
    arena_after_wkv = wkv_at[0]

    psum = nc.alloc_psum_tensor("psum", [128, 4096], F32)
    vbcount = [0]
    pbank = []

    class PT:
        def __init__(self, vbs):
            self.vbs = vbs

        def _phys(self, i):
            p = self.vbs[i].phys
            return 0 if p is None else p

        def bank(self, i):
            b = self._phys(i)
            return psum[:, b * 512:(b + 1) * 512]

        def __getitem__(self, key):
            rows, cols = key
            a0, a1 = cols.start, cols.stop
            bi = a0 // 512
            assert (a1 - 1) // 512 == bi, (a0, a1)
            b = self._phys(bi)
            return psum[:, b * 512 + (a0 - bi * 512): b * 512 + (a1 - bi * 512)]

    def palloc(nb=2, static=None):
        vbs = []
        for i in range(nb):
            vbcount[0] += 1
            v = Buf(f"pb{vbcount[0]}", excl=True, virt=(static is None))
            if static is not None:
                v.phys = static[i]
            vbs.append(v)
            pbank.append(v)
        return PT(vbs), vbs

    sa = [arena_after_wkv]
    diff = alloc([128, 128], F32, at=sa, name="diff")
    irow = alloc([128, 128], F32, at=sa, name="irow")
    pcol = alloc([128, 1], F32, at=sa, name="pcol")
    p127 = alloc([128, 1], F32, at=sa, name="p127")
    mge = alloc([128, 128], F32, at=sa, name="mge")
    mlt = alloc([128, 128], F32, at=sa, name="mlt")
    rpos = alloc([128, 128], F32, at=sa, name="rpos")
    rneg = alloc([128, 128], F32, at=sa, name="rneg")
    identf = alloc([128, 128], F32, at=sa, name="identf")
    tmpa = alloc([128, 128], F32, at=sa, name="tmpa")
    tmpb = alloc([128, 128], F32, at=sa, name="tmpb")
    tmpc = alloc([128, 128], F32, at=sa, name="tmpc")
    rowp1 = alloc([128, 128], F32, at=sa, name="rowp1")
    row128m = alloc([128, 128], F32, at=sa, name="row128m")
    decf_t = alloc([128, 8], F32, at=sa, name="decf")
    decb_t = alloc([128, 8], F32, at=sa, name="decb")
    lgf = alloc([128, 8], F32, at=sa, name="lgf")
    lgb = alloc([128, 8], F32, at=sa, name="lgb")
    etmp = alloc([128, 8], F32, at=sa, name="etmp")
    Bc = Buf("const")
    grp = {"b": Bc}
    groups = [Bc]

    def G():
        return grp["b"]

    def newgroup(name):
        grp["b"] = Buf(name)
        groups.append(grp["b"])

    g = nc.gpsimd

    w_in_v = w_in.ap().rearrange("(dc p) n -> p dc n", p=128)
    sch.dma("pool", wkv_sb[:, :, :], w_in_v[:, :, 2048:4096], w=[B_["wkv"]])

    sch.op("pool", lambda e: e.iota(diff[:, :], [[1, 128]], base=0, channel_multiplier=-1, allow_small_or_imprecise_dtypes=True), w=[G()])
    sch.op("pool", lambda e: e.iota(irow[:, :], [[1, 128]], base=0, channel_multiplier=0, allow_small_or_imprecise_dtypes=True), w=[G()])
    sch.op("pool", lambda e: e.iota(pcol[:, :], [[1, 1]], base=0, channel_multiplier=1, allow_small_or_imprecise_dtypes=True), w=[G()])
    sch.op("pool", lambda e: e.iota(p127[:, :], [[1, 1]], base=127, channel_multiplier=-1, allow_small_or_imprecise_dtypes=True), w=[G()])
    sch.op("dve", lambda e: e.memset(negpi[:, :], -math.pi), w=[B_["negpi"]])
    sch.op("dve", lambda e: e.memset(mhalf[:, :], -0.5), w=[B_["negpi"]])
    sch.op("dve", lambda e: e.memset(halfpi[:, :], HALF_PI_SAFE), w=[B_["negpi"]])

    def dv(fn, r=None, w=None):
        sch.op("dve", fn, r=[Bc, G()] if r is None else list(r), w=[G()] if w is None else list(w))

    def ac(fn, r=None, w=None):
        sch.op("act", fn, r=[Bc, G()] if r is None else list(r), w=[G()] if w is None else list(w))

    dv(lambda e: e.tensor_single_scalar(out=identf[:, :], in_=diff[:, :], scalar=0.0, op=ALU.is_equal))
    dv(lambda e: e.tensor_copy(out=ident[:, :], in_=identf[:, :]), w=(G(), B_["ident"]))
    dv(lambda e: e.tensor_single_scalar(out=mge[:, :], in_=diff[:, :], scalar=0.0, op=ALU.is_ge))
    dv(lambda e: e.tensor_single_scalar(out=mlt[:, :], in_=diff[:, :], scalar=0.0, op=ALU.is_lt))
    dv(lambda e: e.tensor_scalar_max(out=rpos[:, :], in0=diff[:, :], scalar1=0.0))
    dv(lambda e: e.tensor_scalar(out=rneg[:, :], in0=diff[:, :], scalar1=-1.0, scalar2=0.0,
                                 op0=ALU.mult, op1=ALU.max))
    dv(lambda e: e.tensor_scalar_add(out=rowp1[:, :], in0=irow[:, :], scalar1=1.0))
    dv(lambda e: e.tensor_scalar(out=row128m[:, :], in0=irow[:, :], scalar1=-1.0, scalar2=128.0,
                                 op0=ALU.mult, op1=ALU.add))

    newgroup("grpD")
    sch.dma("sp", decf_t[:, :], dec_f.ap().partition_broadcast(128).rearrange("p o n -> p (o n)"), w=[G()])
    sch.dma("sp", decb_t[:, :], dec_b.ap().partition_broadcast(128).rearrange("p o n -> p (o n)"), w=[G()])
    for dsrc, lg in ((decf_t, lgf), (decb_t, lgb)):
        ac(lambda e, dsrc=dsrc: e.activation(out=etmp[:, :], in_=dsrc[:, :], func=AF.Exp, scale=-math.log(2.0)))
        dv(lambda e: e.tensor_scalar(out=etmp[:, :], in0=etmp[:, :], scalar1=-1.0, scalar2=1.0,
                                     op0=ALU.mult, op1=ALU.add))
        ac(lambda e, lg=lg: e.activation(out=lg[:, :], in_=etmp[:, :], func=AF.Ln))
    for h in range(H):
        ac(lambda e, h=h: e.activation(out=tmpa[:, :], in_=rpos[:, :], func=AF.Exp, scale=lgf[:, h:h + 1]))
        ac(lambda e, h=h: e.activation(out=tmpb[:, :], in_=rneg[:, :], func=AF.Exp, scale=lgb[:, h:h + 1]))
        dv(lambda e: e.tensor_tensor(out=tmpa[:, :], in0=tmpa[:, :], in1=mge[:, :], op=ALU.mult))
        dv(lambda e: e.tensor_tensor(out=tmpb[:, :], in0=tmpb[:, :], in1=mlt[:, :], op=ALU.mult))
        dv(lambda e, h=h: e.tensor_tensor(out=dtot[:, h, :], in0=tmpa[:, :], in1=tmpb[:, :], op=ALU.add),
           w=(G(), B_["dtot"]))
        ac(lambda e, h=h: e.activation(out=af_tab[:, h, :], in_=rowp1[:, :], func=AF.Exp, scale=lgf[:, h:h + 1]),
           w=(G(), B_["aftab"]))
        ac(lambda e, h=h: e.activation(out=ab_tab[:, h, :], in_=row128m[:, :], func=AF.Exp, scale=lgb[:, h:h + 1]),
           w=(G(), B_["abtab"]))
    ac(lambda e: e.activation(out=kdf[:, :], in_=lgf[:, :], func=AF.Exp, scale=p127[:, 0:1]), w=(G(), B_["kd"]))
    ac(lambda e: e.activation(out=kdb[:, :], in_=lgb[:, :], func=AF.Exp, scale=pcol[:, 0:1]), w=(G(), B_["kd"]))
    ac(lambda e: e.activation(out=gcf[:, :], in_=lgf[:, :], func=AF.Exp, scale=128.0), w=(G(), B_["kd"]))
    ac(lambda e: e.activation(out=gcb[:, :], in_=lgb[:, :], func=AF.Exp, scale=128.0), w=(G(), B_["kd"]))

    newgroup("grpP")
    tmpa2 = alloc([128, 128], F32, at=sa, name="tmpa2")
    tmpb2 = alloc([128, 128], F32, at=sa, name="tmpb2")
    tmpc2 = alloc([128, 128], F32, at=sa, name="tmpc2")
    for gi, w_ in enumerate(POOL_W):
        hw = w_ // 2
        dv(lambda e, hw=hw: e.tensor_single_scalar(out=tmpa2[:, :], in_=diff[:, :], scalar=float(-(hw - 1)), op=ALU.is_ge))
        dv(lambda e, hw=hw: e.tensor_single_scalar(out=tmpb2[:, :], in_=diff[:, :], scalar=float(hw), op=ALU.is_le))
        dv(lambda e: e.tensor_tensor(out=tmpa2[:, :], in0=tmpa2[:, :], in1=tmpb2[:, :], op=ALU.mult))
        dv(lambda e, gi=gi, w_=w_: e.scalar_tensor_tensor(out=bands[:, gi * 6 + 1, :], in0=tmpa2[:, :], scalar=1.0 / w_,
                                                          in1=identf[:, :], op0=ALU.mult, op1=ALU.subtract),
           w=(G(), B_["bands"]))
        dv(lambda e, gi=gi, w_=w_, hw=hw: e.tensor_scalar(out=bands[:, gi * 6 + 0, :], in0=diff[:, :],
                                                          scalar1=float(hw - 128), scalar2=1.0 / w_,
                                                          op0=ALU.is_le, op1=ALU.mult), w=(G(), B_["bands"]))
        dv(lambda e, gi=gi, w_=w_, hw=hw: e.tensor_scalar(out=bands[:, gi * 6 + 2, :], in0=diff[:, :],
                                                          scalar1=float(129 - hw), scalar2=1.0 / w_,
                                                          op0=ALU.is_ge, op1=ALU.mult), w=(G(), B_["bands"]))
        dv(lambda e, w_=w_, hw=hw: e.tensor_scalar(out=tmpb2[:, :], in0=irow[:, :], scalar1=float(hw), scalar2=float(w_),
                                                   op0=ALU.add, op1=ALU.min))
        dv(lambda e: e.reciprocal(out=tmpb2[:, :], in_=tmpb2[:, :]))
        dv(lambda e: e.tensor_tensor(out=tmpc2[:, :], in0=tmpa2[:, :], in1=tmpb2[:, :], op=ALU.mult))
        dv(lambda e, gi=gi: e.tensor_tensor(out=bands[:, gi * 6 + 3, :], in0=tmpc2[:, :], in1=identf[:, :], op=ALU.subtract),
           w=(G(), B_["bands"]))
        dv(lambda e, w_=w_, hw=hw: e.tensor_scalar(out=tmpb2[:, :], in0=irow[:, :], scalar1=-1.0, scalar2=float(128 + hw),
                                                   op0=ALU.mult, op1=ALU.add))
        dv(lambda e, w_=w_: e.tensor_scalar_min(out=tmpb2[:, :], in0=tmpb2[:, :], scalar1=float(w_)))
        dv(lambda e: e.reciprocal(out=tmpb2[:, :], in_=tmpb2[:, :]))
        dv(lambda e: e.tensor_tensor(out=tmpc2[:, :], in0=tmpa2[:, :], in1=tmpb2[:, :], op=ALU.mult))
        dv(lambda e, gi=gi: e.tensor_tensor(out=bands[:, gi * 6 + 4, :], in0=tmpc2[:, :], in1=identf[:, :], op=ALU.subtract),
           w=(G(), B_["bands"]))
        dv(lambda e, gi=gi: e.tensor_tensor(out=bands[:, gi * 6 + 5, :], in0=bands[:, gi * 6 + 0, :], in1=bands[:, gi * 6 + 2, :], op=ALU.add),
           r=(Bc, G(), B_["bands"]), w=(G(), B_["bands"]))

    newgroup("grpR")
    invf = alloc([128, 64], F32, at=sa, name="invf")
    ac(lambda e: e.activation(out=invf[:, :], in_=irow[:, 0:64], func=AF.Exp, scale=-math.log(10000.0) / 64.0))
    RC = 8
    posall = alloc([128, RC], F32, at=sa, name="posall")
    ang = alloc([128, RC, 64], F32, at=sa, name="ang")
    marg = alloc([128, RC, 64], F32, at=sa, name="marg")
    rtab = alloc([128, RC, 128], F32, at=sa, name="rtab")
    rred = alloc([128, RC, 64], F32, at=sa, name="rred")
    qint = alloc([128, RC, 64], mybir.dt.int32, at=sa, name="qint")
    Brt = Buf("rtab")
    rope_v = ropescr.ap().rearrange("(c p) n -> p c n", p=128)
    for c0 in range(0, nchmax, RC):
        ncb = min(RC, nchmax - c0)
        sch.op("pool", lambda e, c0=c0: e.iota(posall[:, :], [[128, RC]], base=128 * c0, channel_multiplier=1, allow_small_or_imprecise_dtypes=True),
               r=[G()], w=[G()])
        dv(lambda e: e.tensor_tensor(out=ang[:, :, :], in0=posall[:, :].unsqueeze(2).to_broadcast([128, RC, 64]),
                                     in1=invf[:, :].unsqueeze(1).to_broadcast([128, RC, 64]), op=ALU.mult))
        dv(lambda e: e.tensor_scalar_mul(out=marg[:, :, :], in0=ang[:, :, :], scalar1=1.0 / (2.0 * math.pi)))
        dv(lambda e: e.tensor_copy(out=qint[:, :, :], in_=marg[:, :, :]))
        dv(lambda e: e.tensor_copy(out=marg[:, :, :], in_=qint[:, :, :]))
        dv(lambda e: e.scalar_tensor_tensor(out=rred[:, :, :], in0=marg[:, :, :], scalar=-CW1, in1=ang[:, :, :],
                                            op0=ALU.mult, op1=ALU.add))
        dv(lambda e: e.scalar_tensor_tensor(out=rred[:, :, :], in0=marg[:, :, :], scalar=-CW2, in1=rred[:, :, :],
                                            op0=ALU.mult, op1=ALU.add))
        dv(lambda e: e.tensor_single_scalar(out=marg[:, :, :], in_=rred[:, :, :], scalar=math.pi, op=ALU.is_gt))
        dv(lambda e: e.scalar_tensor_tensor(out=rred[:, :, :], in0=marg[:, :, :], scalar=-2.0 * math.pi, in1=rred[:, :, :],
                                            op0=ALU.mult, op1=ALU.add))
        dv(lambda e: e.tensor_single_scalar(out=marg[:, :, :], in_=rred[:, :, :], scalar=-math.pi, op=ALU.is_lt))
        dv(lambda e: e.scalar_tensor_tensor(out=rred[:, :, :], in0=marg[:, :, :], scalar=2.0 * math.pi, in1=rred[:, :, :],
                                            op0=ALU.mult, op1=ALU.add))
        dv(lambda e: e.tensor_scalar(out=rred[:, :, :], in0=rred[:, :, :], scalar1=-PI_SAFE, scalar2=PI_SAFE,
                                     op0=ALU.max, op1=ALU.min))
        ac(lambda e: e.activation(out=rtab[:, :, 64:128], in_=rred[:, :, :], func=AF.Sin), w=(G(), Brt))
        dv(lambda e: e.scalar_tensor_tensor(out=marg[:, :, :], in0=rred[:, :, :], scalar=-1.0, in1=rred[:, :, :],
                                            op0=ALU.mult, op1=ALU.max))
        ac(lambda e: e.activation(out=rtab[:, :, 0:64], in_=marg[:, :, :], func=AF.Sin, scale=-1.0, bias=halfpi[:, 0:1]),
           r=(Bc, G(), B_["negpi"]), w=(G(), Brt))
        sch.dma("sp", rope_v[:, c0:c0 + ncb, :], rtab[:, 0:ncb, :], r=[Brt], w=[B_["ropescr"]], key=Brt)

    newgroup("grpA")
    adaw_sb = alloc([128, 8, 1024], BF16, at=sa, name="adaw")
    Badaw = Buf("adaw")
    ada_w_v = ada_w.ap().rearrange("(dc p) n -> p dc n", p=128)

    cT = alloc([128, 8, 2], F32, at=sa, name="cT")
    scT = alloc([128, 8, 2], BF16, at=sa, name="scT")
    scb = alloc([128, 2, 8, 128], BF16, at=sa, name="scb")
    adabT = alloc([128, 24], F32, at=sa, name="adabT")
    gpreT = alloc([128, 8], F32, at=sa, name="gpreT")
    modT = alloc([128, 16, 2], F32, at=sa, name="modT")
    rowb = alloc([128, 1024], F32, at=sa, name="rowb")
    gpostb = alloc([128, 1024], F32, at=sa, name="gpostb")
    ggt = alloc([128, 1024], F32, at=sa, name="ggt")
    pstage = alloc([128, 8, 256], F32, at=sa, name="pstage")
    setup_end = sa[0]

    for s in range(nseq):
        sch.dma("sp", cT[:, :, s], cvec.ap()[s].rearrange("(dc p) -> p dc", p=128), w=[G()], slow=True)
    sch.dma("sp", adabT[:, :], ada_b.ap().rearrange("o (fc p) -> p (o fc)", p=128), w=[G()], slow=True)
    sch.dma("sp", gpreT[:, :], norm_pre.ap().rearrange("o (fc p) -> p (o fc)", p=128), w=[G()], slow=True)
    sch.dma("sp", gpostb[:, :], norm_post.ap().partition_broadcast(128).rearrange("p o n -> p (o n)"), w=[G()])
    sch.dma("sp", rowb[:, :], pool_scale.ap().partition_broadcast(128).rearrange("p o n -> p (o n)"), w=[G()])
    sch.dma("sp", pstage[:, :, :], pool_w.ap().rearrange("g (cc p) d -> p (g cc) d", p=128), w=[G()])
    dv(lambda e: e.tensor_tensor(out=poolw_sb[:, :, :].rearrange("p (g c) d -> p g c d", g=4),
                                 in0=pstage[:, :, :].rearrange("p (g c) d -> p g c d", g=4),
                                 in1=rowb[:, :].rearrange("p (g d) -> p g d", g=4).unsqueeze(2).to_broadcast([128, 4, 2, 256]),
                                 op=ALU.mult), w=(G(), B_["poolw"]))
    sch.dma("sp", rowb[:, :], ada_b.ap()[:, 2048:3072].partition_broadcast(128).rearrange("p o n -> p (o n)"), r=[G()], w=[G()])
    ac(lambda e: e.activation(out=scT[:, :, :], in_=cT[:, :, :], func=AF.Silu))
    for s in range(nseq):
        for dc in range(8):
            dv(lambda e, s=s, dc=dc: e.tensor_copy(out=scb[:, s, dc, :], in_=scT[:, dc, s:s + 1].to_broadcast([128, 128])))
    pm, pmb = palloc(2, static=[0, 1])
    pmv = pm[:, 0:32].rearrange("p (fc s) -> p fc s", s=2)
    for piece in range(2):
        sch.dma("pool", adaw_sb[:, :, :], ada_w_v[:, :, piece * 1024:(piece + 1) * 1024], w=[Badaw])

        def mod_mm(e, piece=piece):
            last = None
            for fc in range(8):
                for dc in range(8):
                    last = e.matmul(pmv[:, piece * 8 + fc, :], adaw_sb[:, dc, fc * 128:(fc + 1) * 128], scT[:, dc, :],
                                    start=(dc == 0), stop=(dc == 7))
            return last
        sch.op("pe", mod_mm, r=[G(), Badaw], w=pmb)
    dv(lambda e: e.tensor_tensor(out=modT[:, :, :], in0=pmv, in1=adabT[:, 0:16].unsqueeze(2).to_broadcast([128, 16, 2]),
                                 op=ALU.add), r=[G()] + pmb, w=[G()])
    dv(lambda e: e.tensor_copy(out=shiftT[:, :, :], in_=modT[:, 0:8, :]), w=(G(), B_["gprime"]))
    dv(lambda e: e.scalar_tensor_tensor(out=gprime[:, :, :], in0=modT[:, 8:16, :], scalar=1.0,
                                        in1=gpreT[:, :].unsqueeze(2).to_broadcast([128, 8, 2]),
                                        op0=ALU.add, op1=ALU.mult), w=(G(), B_["gprime"]))
    dv(lambda e: e.tensor_scalar_mul(out=gprime[:, :, :], in0=gprime[:, :, :], scalar1=float(D) ** 0.5), w=(G(), B_["gprime"]))
    sch.dma("pool", adaw_sb[:, :, :], ada_w_v[:, :, 2048:3072], w=[Badaw])
    gg_v = ggscr.ap()
    for s in range(nseq):
        pg, pgb = palloc(2, static=[2 + 2 * s, 3 + 2 * s])

        def gate_mm(e, s=s, pg=pg):
            last = None
            for half in range(2):
                for dc in range(8):
                    last = e.matmul(pg[:, half * 512:(half + 1) * 512], scb[:, s, dc, :],
                                    adaw_sb[:, dc, half * 512:(half + 1) * 512],
                                    start=(dc == 0), stop=(dc == 7))
            return last
        sch.op("pe", gate_mm, r=[G(), Badaw], w=pgb)
        for half in range(2):
            dv(lambda e, pg=pg, half=half: e.tensor_tensor(out=ggt[:, half * 512:(half + 1) * 512], in0=pg[:, half * 512:(half + 1) * 512],
                                                         in1=rowb[:, half * 512:(half + 1) * 512], op=ALU.add), r=[G()] + pgb, w=[G()])
        dv(lambda e: e.tensor_tensor(out=ggt[:, :], in0=ggt[:, :], in1=gpostb[:, :], op=ALU.mult))
        sch.dma("sp", gg_v[s * 128:(s + 1) * 128, :], ggt[:, :], r=[G()], w=[B_["ggscr"]], key=G())
    sch.dma("pool", wq_sb[:, :, 0:2048], w_in_v[:, :, 0:2048], w=[B_["wq"]])
    sch.dma("pool", wq_sb[:, :, 2048:4096], w_in_v[:, :, 4096:6144], w=[B_["wq"]])
    sch.dma("pool", wout_sb[:, :, :], w_out.ap().rearrange("(ec p) n -> p ec n", p=128), w=[B_["wout"]])

    sch.disabled = STAGE < 2
    sch.barrier(groups + [Brt, Badaw] + pbank)

    act_at = [arena_after_wkv]

    def mk(shape, dt, name, n=1):
        ts = [alloc(shape, dt, at=act_at, name=f"{name}{i}") for i in range(n)]
        bs = [Buf(f"{name}{i}") for i in range(n)]
        return ts, bs

    xin, xin_b = mk([128, 1024], F32, "xin", 2)
    junk = alloc([128, 1024], BF16, at=act_at, name="junk")
    junk_b = Buf("junk", disjoint=False)
    xn, xn_b = mk([128, 1024], BF16, "xn", 1)
    hT, hT_b = mk([128, 8, 128], BF16, "hT", 3)
    ssq, ssq_b = mk([128, 4], F32, "ssq", 2)
    rt, rt_b = mk([128, 128], F32, "rt", 2)
    t1, t1_b = mk([128, 512], F32, "t1", 1)
    t2, t2_b = mk([128, 512], F32, "t2", 1)
    common_end = act_at[0]

    _aux = {}

    def aux(b, tag):
        k = (id(b), tag)
        if k not in _aux:
            _aux[k] = Buf(b.name + tag)
        return _aux[k]

    def hTr(hs):
        return [aux(hT_b[hs], "a"), aux(hT_b[hs], "b")]

    def front(seq, c, hslot, xslot, ht_act=True):
        sch.cur_prio = FRONT_PRIO
        xt, xb = xin[xslot], xin_b[xslot]
        sq, sqb = ssq[xslot], ssq_b[xslot]
        sch.dma("sp", xt[:, :], xs[seq].ap()[c * 128:(c + 1) * 128, :], w=[xb])
        sch.op("act", lambda e: e.activation(out=junk[:, :], in_=xt[:, :], func=AF.Square, accum_out=sq[:, 0:1]),
               r=[xb], w=[sqb, junk_b])
        sch.op("pool", lambda e: e.tensor_scalar_add(out=sq[:, 1:2], in0=sq[:, 0:1], scalar1=float(D) * EPS), r=[sqb], w=[sqb])
        sch.op("pool", lambda e: e.tensor_tensor(out=sq[:, 2:3], in0=sq[:, 1:2], in1=mhalf[:, 0:1], op=ALU.pow),
               r=[sqb], w=[sqb])
        sch.op("act", lambda e: e.activation(out=xn[0][:, :], in_=xt[:, :], func=AF.Copy, scale=sq[:, 2:3]),
               r=[xb, sqb], w=[xn_b[0]])
        pt, ptb = palloc(2)
        ptv = [(lambda i=i: pt.bank(i).bitcast(BF16)[:, 0:512].rearrange("p (dc t) -> p dc t", dc=4)) for i in range(2)]
        for hb in range(2):
            def tr(e, hb=hb):
                last = None
                for d4 in range(4):
                    dc = hb * 4 + d4
                    last = e.transpose(ptv[hb]()[:, d4, :], xn[0][:, dc * 128:(dc + 1) * 128], ident[:, :])
                return last
            sch.op("pe", tr, r=[xn_b[0], B_["ident"]], w=[ptb[hb]])
        for dc in range(8):
            hb, d4 = dc // 4, dc % 4
            if hb == 0 or ht_act:
                sch.op("act", lambda e, dc=dc, d4=d4, hb=hb: e.activation(out=hT[hslot][:, dc, :], in_=ptv[hb]()[:, d4, :], func=AF.Identity,
                                                                          scale=gprime[:, dc, seq:seq + 1], bias=shiftT[:, dc, seq:seq + 1]),
                       r=[ptb[hb], B_["gprime"]], w=[aux(hT_b[hslot], "a" if hb == 0 else "b")])
            else:
                sch.op("dve", lambda e, dc=dc, d4=d4: e.tensor_scalar(out=hT[hslot][:, dc, :], in0=ptv[1]()[:, d4, :],
                                                                      scalar1=gprime[:, dc, seq:seq + 1],
                                                                      scalar2=shiftT[:, dc, seq:seq + 1],
                                                                      op0=ALU.mult, op1=ALU.add),
                       r=[ptb[1], B_["gprime"]], w=[aux(hT_b[hslot], "b")])

    def front_done():
        sch.cur_prio = 0

    def rope(psrc, psrc_b, dst, dst_b, rts, rtb, kscale, ceng="dve"):
        cosb = rts[:, 0:64].unsqueeze(1).to_broadcast([128, 8, 64])
        sinb = rts[:, 64:128].unsqueeze(1).to_broadcast([128, 4, 64])
        for half in range(2):
            src = lambda half=half: psrc[:, half * 512:(half + 1) * 512].rearrange("p (h two d) -> p h two d", h=4, two=2)
            t1v = t1[0][:, :].rearrange("p (h two d) -> p h two d", h=4, two=2)
            t2v = t2[0][:, :].rearrange("p (h two d) -> p h two d", h=4, two=2)
            dv_ = dst[:, half * 512:(half + 1) * 512].rearrange("p (h two d) -> p h two d", h=4, two=2)
            pb = [psrc_b[half]]
            src3 = lambda half=half: psrc[:, half * 512:(half + 1) * 512].rearrange("p (g d) -> p g d", g=8)
            t1v3 = t1[0][:, :].rearrange("p (g d) -> p g d", g=8)
            if kscale is None:
                sch.op("dve", lambda e, src3=src3, t1v3=t1v3: e.tensor_tensor(out=t1v3, in0=src3(), in1=cosb, op=ALU.mult),
                       r=pb + [rtb], w=[t1_b[0]])
                sch.op("dve", lambda e, src=src, t2v=t2v: e.tensor_tensor(out=t2v[:, :, 0, :], in0=src()[:, :, 1, :], in1=sinb, op=ALU.mult),
                       r=pb + [rtb], w=[t2_b[0]])
                sch.op("dve", lambda e, src=src, t2v=t2v: e.tensor_tensor(out=t2v[:, :, 1, :], in0=src()[:, :, 0, :], in1=sinb, op=ALU.mult),
                       r=pb + [rtb], w=[t2_b[0]])
            else:
                sch.op("dve", lambda e, src3=src3, t1v3=t1v3: e.scalar_tensor_tensor(out=t1v3, in0=src3(), scalar=kscale, in1=cosb,
                                                                                 op0=ALU.mult, op1=ALU.mult),
                       r=pb + [rtb], w=[t1_b[0]])
                sch.op("dve", lambda e, src=src, t2v=t2v: e.scalar_tensor_tensor(out=t2v[:, :, 0, :], in0=src()[:, :, 1, :], scalar=kscale,
                                                                                 in1=sinb, op0=ALU.mult, op1=ALU.mult),
                       r=pb + [rtb], w=[t2_b[0]])
                sch.op("dve", lambda e, src=src, t2v=t2v: e.scalar_tensor_tensor(out=t2v[:, :, 1, :], in0=src()[:, :, 0, :], scalar=kscale,
                                                                                 in1=sinb, op0=ALU.mult, op1=ALU.mult),
                       r=pb + [rtb], w=[t2_b[0]])
            sch.op(ceng, lambda e, t1v=t1v, t2v=t2v, dv_=dv_: e.tensor_tensor(out=dv_[:, :, 0, :], in0=t1v[:, :, 0, :], in1=t2v[:, :, 0, :],
                                                                                op=ALU.subtract),
                   r=[t1_b[0], t2_b[0]], w=[dst_b])
            sch.op(ceng, lambda e, t1v=t1v, t2v=t2v, dv_=dv_: e.tensor_tensor(out=dv_[:, :, 1, :], in0=t1v[:, :, 1, :], in1=t2v[:, :, 1, :],
                                                                                op=ALU.add),
                   r=[t1_b[0], t2_b[0]], w=[dst_b])

    def bc_h(tab):
        return tab[:, :].unsqueeze(2).to_broadcast([128, 8, 128])

    def v3(t):
        return t.rearrange("p (h d) -> p h d", h=8)

    pa_at = [common_end]
    krA, krA_b = [], []
    vA, vA_b = [], []
    for i in range(2):
        krA.append(alloc([128, 1024], BF16, at=pa_at, name=f"krA{i}")); krA_b.append(Buf(f"krA{i}"))
        vA.append(alloc([128, 1024], BF16, at=pa_at, name=f"vA{i}")); vA_b.append(Buf(f"vA{i}"))
    kbA = alloc([128, 1024], BF16, at=pa_at, name="kbA"); kbA_b = Buf("kbA")
    Bf32 = alloc([128, 1024], F32, at=pa_at, name="Bf32"); Bf32_b = Buf("Bf32")
    BbfA, BbfA_b = [], []
    for i in range(2):
        BbfA.append(alloc([128, 1024], BF16, at=pa_at, name=f"BbfA{i}")); BbfA_b.append(Buf(f"BbfA{i}"))
    scrB = {("kr", i): Buf(f"krscr{i}") for i in range(nseq)}
    scrB.update({("v", i): Buf(f"vscr{i}") for i in range(nseq)})
    scrB.update({("b", i): Buf(f"bscr{i}") for i in range(nseq)})
    scrB.update({("h", i): Buf(f"hscr{i}") for i in range(nseq)})
    scrB.update({("kt", i): Buf(f"ktscr{i}") for i in range(nseq)})
    KTA, KTA_b = [], []
    for i in range(2):
        KTA.append(alloc([128, 8, 128], BF16, at=pa_at, name=f"KTA{i}")); KTA_b.append(Buf(f"KTA{i}"))

    itA = 0
    for seq in range(nseq):
        n = nchs[seq]
        for idx, c in enumerate(range(n - 1, -1, -1)):
            sl = itA % 2
            itA += 1
            hs = itA % 3
            front(seq, c, hs, sl, ht_act=HT_ACT_A)
            front_done()
            if SPILL_HT:
                sch.dma("sp", hscr[seq].ap()[c * 128:(c + 1) * 128, :], hT[hs][:, :, :].rearrange("p a b -> p (a b)"),
                        r=hTr(hs), w=[scrB[("h", seq)]], key=aux(hT_b[hs], "a"))
            sch.dma("sp", rt[sl][:, :], ropescr.ap()[c * 128:(c + 1) * 128, :], r=[B_["ropescr"]], w=[rt_b[sl]])
            pk, pkb = palloc(2)
            pv, pvb = palloc(2)
            for bi, (pp, ppb) in enumerate(((pk, pkb), (pv, pvb))):
                for half in range(2):
                    col0 = bi * 1024 + half * 512

                    def mm(e, pp=pp, half=half, col0=col0, hs=hs):
                        last = None
                        for dc in range(8):
                            last = e.matmul(pp[:, half * 512:(half + 1) * 512], hT[hs][:, dc, :], wkv_sb[:, dc, col0:col0 + 512],
                                            start=(dc == 0), stop=(dc == 7))
                        return last
                    sch.op("pe", mm, r=hTr(hs) + [B_["wkv"]], w=[ppb[half]])
            rope(pk, pkb, krA[sl], krA_b[sl], rt[sl], rt_b[sl], HD ** -0.5, ROPE_ENG_A)
            for half in range(2):
                sch.op("act", lambda e, half=half, pv=pv, sl=sl: e.copy(out=vA[sl][:, half * 512:(half + 1) * 512],
                                                                        in_=pv[:, half * 512:(half + 1) * 512]),
                       r=[pvb[half]], w=[vA_b[sl]])
            sch.dma("sp", krscr[seq].ap()[c * 128:(c + 1) * 128, :], krA[sl][:, :], r=[krA_b[sl]], w=[scrB[("kr", seq)]], key=krA_b[sl])
            sch.dma("sp", vscr[seq].ap()[c * 128:(c + 1) * 128, :], vA[sl][:, :], r=[vA_b[sl]], w=[scrB[("v", seq)]], key=vA_b[sl])
            if SPILL_KT:
                ptk, ptkb = palloc(1)
                ptkv = lambda ptk=ptk: ptk.bank(0).bitcast(BF16).rearrange("p (h t) -> p h t", h=8)

                def trk(e, sl=sl, ptkv=ptkv):
                    last = None
                    for h in range(8):
                        last = e.transpose(ptkv()[:, h, :], krA[sl][:, h * 128:(h + 1) * 128], ident[:, :])
                    return last
                sch.op("pe", trk, r=[krA_b[sl], B_["ident"]], w=ptkb)
                sch.op("act", lambda e, sl=sl, ptkv=ptkv: e.copy(out=KTA[sl][:, :, :], in_=ptkv()), r=ptkb, w=[KTA_b[sl]])
                sch.dma("sp", ktscr[seq].ap()[c * 128:(c + 1) * 128, :], KTA[sl][:, :, :].rearrange("p a b -> p (a b)"),
                        r=[KTA_b[sl]], w=[scrB[("kt", seq)]], key=KTA_b[sl])
            if idx == 0:
                sch.op("pool", lambda e, sl=sl: e.memset(BbfA[sl][:, :], 0.0), w=[BbfA_b[sl]])
            sch.dma("sp", bscr[seq].ap()[c * 128:(c + 1) * 128, :], BbfA[sl][:, :], r=[BbfA_b[sl]], w=[scrB[("b", seq)]], key=BbfA_b[sl])
            if c == 0:
                continue
            sch.op(KB_ENG, lambda e, sl=sl: e.tensor_tensor(out=v3(kbA[:, :]), in0=v3(krA[sl][:, :]), in1=bc_h(kdb), op=ALU.mult),
                   r=[krA_b[sl], B_["kd"]], w=[kbA_b])
            pkv, pkvb = palloc(2)
            for half in range(2):
                def kvmm(e, half=half, pkv=pkv, sl=sl):
                    last = None
                    for hh in range(4):
                        h = half * 4 + hh
                        last = e.matmul(pkv[:, h * 128:(h + 1) * 128], kbA[:, h * 128:(h + 1) * 128], vA[sl][:, h * 128:(h + 1) * 128],
                                        start=True, stop=True)
                    return last
                sch.op("pe", kvmm, r=[kbA_b, vA_b[sl]], w=[pkvb[half]])
            ns = 1 - sl
            if idx == 0:
                for half in range(2):
                    sch.op("dve", lambda e, half=half, pkv=pkv: e.tensor_copy(out=Bf32[:, half * 512:(half + 1) * 512],
                                                                              in_=pkv[:, half * 512:(half + 1) * 512]),
                           r=[pkvb[half]], w=[Bf32_b])
            else:
                sch.op(FMUL_ENG, lambda e: e.tensor_tensor(out=v3(Bf32[:, :]), in0=v3(Bf32[:, :]), in1=bc_h(gcb), op=ALU.mult),
                       r=[Bf32_b, B_["kd"]], w=[Bf32_b])
                for half in range(2):
                    sch.op("dve", lambda e, half=half, pkv=pkv: e.tensor_tensor(out=Bf32[:, half * 512:(half + 1) * 512],
                                                                                in0=pkv[:, half * 512:(half + 1) * 512],
                                                                                in1=Bf32[:, half * 512:(half + 1) * 512], op=ALU.add),
                           r=[pkvb[half], Bf32_b], w=[Bf32_b])
            sch.op("act", lambda e, ns=ns: e.copy(out=BbfA[ns][:, :], in_=Bf32[:, :]), r=[Bf32_b], w=[BbfA_b[ns]])

    sch.disabled = STAGE < 3
    passA_bufs = KTA_b + [junk_b] + xin_b + xn_b + hT_b + [x for i in range(3) for x in hTr(i)] + ssq_b + rt_b + t1_b + t2_b + krA_b + vA_b + [kbA_b, Bf32_b] + BbfA_b + pbank + [B_["wkv"]]
    sch.barrier(passA_bufs)
    pb_at = [common_end]
    wkv_region = [arena0]

    def mkb(shape, dt, name, at):
        return alloc(shape, dt, at=at, name=name), Buf(name)

    class Ring:
        def __init__(self, name, shape, dt, n, at):
            self.items = [mkb(shape, dt, f"{name}{i}", at) for i in range(n)]

        def __getitem__(self, c):
            return self.items[c % len(self.items)]

    def ring(name, shape, dt, at2=None):
        n = NB.get(name, 1)
        r = Ring.__new__(Ring)
        r.items = []
        for i in range(n):
            r.items.append(mkb(shape, dt, f"{name}{i}", (at2 if at2 is not None else wkv_region) if i == 0 else pb_at))
        return r

    uring = [mkb([128, 1024], BF16, f"u{i}", wkv_region) for i in range(3)]
    qr_r = ring("qr", [128, 1024], BF16)
    sz_r = ring("sz", [128, 2048], BF16)
    krB_r = ring("krB", [128, 1024], BF16)
    vB_r = ring("vB", [128, 1024], BF16)
    BbfB_r = ring("BbfB", [128, 1024], BF16)
    kf_r = ring("kf", [128, 1024], BF16)
    QT_r = ring("QT", [128, 8, 128], BF16)
    QTf_r = ring("QTf", [128, 8, 128], BF16)
    QTb_r = ring("QTb", [128, 8, 128], BF16)
    KT_r = ring("KT", [128, 8, 128], BF16)
    scm_r = ring("scm", [128, 8, 128], BF16)
    if CHECK_SBUF:
        assert wkv_region[0] <= arena_after_wkv, (wkv_region[0], arena_after_wkv)
    Ff32, Ff32_b = mkb([128, 1024], F32, "Ff32", pb_at)
    Fbf_r = ring("Fbf", [128, 1024], BF16, pb_at)
    scr4_r = ring("scr4", [128, 1024], F32, pb_at)
    on_r = ring("on", [128, 1024], F32, pb_at)
    yb_r = ring("y", [128, 2048], BF16, pb_at)
    yT_r = ring("yT", [128, 16, 128], BF16, pb_at)
    plT_r = ring("plT", [128, 8, 128], BF16, pb_at)
    xres_r = ring("xres", [128, 1024], F32, pb_at)
    st_r = ring("st", [128, 64], F32, pb_at)
    ggtab, ggtab_b = mkb([128, 1024], F32, "ggtab", pb_at)
    hu, hu_b = mkb([128, 1024], BF16, "hu", pb_at)
    if HALO_MERGE:
        sch.op("pool", lambda e: e.memset(hu[:, :], 0.0), w=[hu_b])
    ysc = {i: Buf(f"yout{i}") for i in range(nseq)}

    def frontB(seq, c, hs, xslot):
        sch.disabled = STAGE < 3
        if SPILL_HT:
            ha, hb_ = hTr(hs)
            sch.dma("sp", hT[hs][:, :, :].rearrange("p a b -> p (a b)"), hscr[seq].ap()[c * 128:(c + 1) * 128, :],
                    r=[scrB[("h", seq)]], w=[ha, hb_], key=ha)
        else:
            front(seq, c, hs, xslot)
            front_done()
        n = nchs[seq]
        ut, ub = uring[c % 3]
        pu, pub = palloc(2)
        for half in range(2):
            def mm(e, half=half, pu=pu, hs=hs):
                last = None
                for dc in range(8):
                    last = e.matmul(pu[:, half * 512:(half + 1) * 512], hT[hs][:, dc, :], wq_sb[:, dc, half * 512:(half + 1) * 512],
                                    start=(dc == 0), stop=(dc == 7))
                return last
            sch.op("pe", mm, r=hTr(hs) + [B_["wq"]], w=[pub[half]])
            sch.op("act", lambda e, half=half, pu=pu, ut=ut: e.copy(out=ut[:, half * 512:(half + 1) * 512], in_=pu[:, half * 512:(half + 1) * 512]),
                   r=[pub[half]], w=[ub])

    def loadsB(seq, c):
        sch.disabled = STAGE < 3
        sl = c % 2
        sch.dma("sp", rt[sl][:, :], ropescr.ap()[c * 128:(c + 1) * 128, :], r=[B_["ropescr"]], w=[rt_b[sl]])
        gc = gbase[0] + c
        krB, krB_b = krB_r[gc]
        vB, vB_b = vB_r[gc]
        BbfB, BbfB_b = BbfB_r[gc]
        sch.dma("sp", krB[:, :], krscr[seq].ap()[c * 128:(c + 1) * 128, :], r=[scrB[("kr", seq)]], w=[krB_b])
        sch.dma("sp", vB[:, :], vscr[seq].ap()[c * 128:(c + 1) * 128, :], r=[scrB[("v", seq)]], w=[vB_b])
        sch.dma("sp", BbfB[:, :], bscr[seq].ap()[c * 128:(c + 1) * 128, :], r=[scrB[("b", seq)]], w=[BbfB_b])
        if SPILL_KT:
            KT_l, KT_lb = KT_r[gc]
            sch.dma("sp", KT_l[:, :, :].rearrange("p a b -> p (a b)"), ktscr[seq].ap()[c * 128:(c + 1) * 128, :],
                    r=[scrB[("kt", seq)]], w=[KT_lb])
        xr, xrb = xres_r[gc]
        sch.dma("sp", xr[:, :], xs[seq].ap()[c * 128:(c + 1) * 128, :], w=[xrb])

    def backB(seq, c, hs):
        n = nchs[seq]
        sl = c % 2
        first, last_c = (c == 0), (c == n - 1)
        gc = gbase[0] + c
        qr, qr_b = qr_r[gc]; sz, sz_b = sz_r[gc]; krB, krB_b = krB_r[gc]; vB, vB_b = vB_r[gc]
        BbfB, BbfB_b = BbfB_r[gc]; kf, kf_b = kf_r[gc]; QT, QT_b = QT_r[gc]; QTf, QTf_b = QTf_r[gc]
        QTb, QTb_b = QTb_r[gc]; KT, KT_b = KT_r[gc]; scm, scm_b = scm_r[gc]
        Fbf, Fbf_b = Fbf_r[gc]; Fbf_n, Fbf_nb = Fbf_r[gc + 1]
        scr4, scr4_b = scr4_r[gc]; on, on_b = on_r[gc]; yb, yb_b = yb_r[gc]; yT, yT_b = yT_r[gc]
        plT, plT_b = plT_r[gc]; st, st_b = st_r[gc]
        szp_b, szr_b = aux(sz_b, "p"), aux(sz_b, "r")
        ybp_b, ybr_b = aux(yb_b, "p"), aux(yb_b, "r")
        yTp_b, yTr_b = aux(yT_b, "p"), aux(yT_b, "r")
        if STAGE < 4:
            sch.disabled = True
        pq, pqb = palloc(2)
        pz0, pz0b = palloc(2)
        pz1, pz1b = palloc(2)
        for (pp, ppb, colbase) in ((pq, pqb, 1024), (pz0, pz0b, 2048), (pz1, pz1b, 3072)):
            for half in range(2):
                col0 = colbase + half * 512

                def mm(e, pp=pp, half=half, col0=col0):
                    last = None
                    for dc in range(8):
                        last = e.matmul(pp[:, half * 512:(half + 1) * 512], hT[hs][:, dc, :], wq_sb[:, dc, col0:col0 + 512],
                                        start=(dc == 0), stop=(dc == 7))
                    return last
                sch.op("pe", mm, r=hTr(hs) + [B_["wq"]], w=[ppb[half]])
        rope(pq, pqb, qr, qr_b, rt[sl], rt_b[sl], None)
        for zi, (pz, pzb) in enumerate(((pz0, pz0b), (pz1, pz1b))):
            for half in range(2):
                sch.op("act", lambda e, zi=zi, half=half, pz=pz: e.activation(out=sz[:, zi * 1024 + half * 512: zi * 1024 + (half + 1) * 512],
                                                                              in_=pz[:, half * 512:(half + 1) * 512], func=AF.Silu),
                       r=[pzb[half]], w=[szp_b if zi == 0 else szr_b])
        if STAGE < 5:
            sch.disabled = True
        sch.cur_prio = RET_PRIO
        if not last_c:
            sch.op(KF_ENG, lambda e: e.tensor_tensor(out=v3(kf[:, :]), in0=v3(krB[:, :]), in1=bc_h(kdf), op=ALU.mult),
                   r=[krB_b, B_["kd"]], w=[kf_b])
        ptq, ptqb = palloc(1)
        ptk, ptkb = palloc(1) if not SPILL_KT else (None, None)
        ptqv = lambda: ptq.bank(0).bitcast(BF16).rearrange("p (h t) -> p h t", h=8)
        ptkv = lambda: ptk.bank(0).bitcast(BF16).rearrange("p (h t) -> p h t", h=8)

        def trq(e):
            last = None
            for h in range(8):
                last = e.transpose(ptqv()[:, h, :], qr[:, h * 128:(h + 1) * 128], ident[:, :])
            return last

        def trk(e):
            last = None
            for h in range(8):
                last = e.transpose(ptkv()[:, h, :], krB[:, h * 128:(h + 1) * 128], ident[:, :])
            return last
        if SUB < 1:
            sch.disabled = True
        sch.op("pe", trq, r=[qr_b, B_["ident"]], w=ptqb)
        if not SPILL_KT:
            sch.op("pe", trk, r=[krB_b, B_["ident"]], w=ptkb)
        if SUB < 2:
            sch.disabled = True
        sch.op("act", lambda e: e.copy(out=QT[:, :, :], in_=ptqv()), r=ptqb, w=[QT_b])
        if not SPILL_KT:
            sch.op("act", lambda e: e.copy(out=KT[:, :, :], in_=ptkv()), r=ptkb, w=[KT_b])
        if SUB < 3:
            sch.disabled = True
        if not first:
            sch.op("dve", lambda e: e.tensor_tensor(out=QTf[:, :, :], in0=QT[:, :, :], in1=af_tab[:, :, :], op=ALU.mult),
                   r=[QT_b, B_["aftab"]], w=[QTf_b])
        if not last_c:
            sch.op(QTB_ENG, lambda e: e.tensor_tensor(out=QTb[:, :, :], in0=QT[:, :, :], in1=ab_tab[:, :, :], op=ALU.mult),
                   r=[QT_b, B_["abtab"]], w=[QTb_b])
        if STAGE < 6:
            sch.disabled = True
        psc, pscb_ = palloc(2)
        for half in range(2):
            def scmm(e, half=half, psc=psc):
                last = None
                for hh in range(4):
                    h = half * 4 + hh
                    last = e.matmul(psc[:, h * 128:(h + 1) * 128], KT[:, h, :], QT[:, h, :], start=True, stop=True)
                return last
            sch.op("pe", scmm, r=[KT_b, QT_b], w=[pscb_[half]])
            sch.op("dve", lambda e, half=half, psc=psc: e.tensor_tensor(out=scm[:, half * 4:(half + 1) * 4, :],
                                                                        in0=psc[:, half * 512:(half + 1) * 512].rearrange("p (h t) -> p h t", h=4),
                                                                        in1=dtot[:, half * 4:(half + 1) * 4, :], op=ALU.mult),
                   r=[pscb_[half], B_["dtot"]], w=[scm_b])
        po, pob = palloc(2)
        for half in range(2):
            def omm(e, half=half, po=po):
                last = None
                for hh in range(4):
                    h = half * 4 + hh
                    terms = [(scm[:, h, :], vB[:, h * 128:(h + 1) * 128])]
                    if not first:
                        terms.append((QTf[:, h, :], Fbf[:, h * 128:(h + 1) * 128]))
                    if not last_c:
                        terms.append((QTb[:, h, :], BbfB[:, h * 128:(h + 1) * 128]))
                    for ti, (l_, r_) in enumerate(terms):
                        last = e.matmul(po[:, h * 128:(h + 1) * 128], l_, r_, start=(ti == 0), stop=(ti == len(terms) - 1))
                return last
            rr = [scm_b, vB_b]
            if not first:
                rr += [QTf_b, Fbf_b]
            if not last_c:
                rr += [QTb_b, BbfB_b]
            sch.op("pe", omm, r=rr, w=[pob[half]])
        if STAGE < 7:
            sch.disabled = True
        sch.cur_prio = FUPD_PRIO
        if not last_c:
            pkv, pkvb = palloc(2)
            for half in range(2):
                def kvmm(e, half=half, pkv=pkv):
                    last = None
                    for hh in range(4):
                        h = half * 4 + hh
                        last = e.matmul(pkv[:, h * 128:(h + 1) * 128], kf[:, h * 128:(h + 1) * 128], vB[:, h * 128:(h + 1) * 128],
                                        start=True, stop=True)
                    return last
                sch.op("pe", kvmm, r=[kf_b, vB_b], w=[pkvb[half]])
            if first:
                for half in range(2):
                    sch.op("dve", lambda e, half=half, pkv=pkv: e.tensor_copy(out=Ff32[:, half * 512:(half + 1) * 512],
                                                                              in_=pkv[:, half * 512:(half + 1) * 512]),
                           r=[pkvb[half]], w=[Ff32_b])
            else:
                sch.op(FMUL_ENG, lambda e: e.tensor_tensor(out=v3(Ff32[:, :]), in0=v3(Ff32[:, :]), in1=bc_h(gcf), op=ALU.mult),
                       r=[Ff32_b, B_["kd"]], w=[Ff32_b])
                for half in range(2):
                    sch.op("dve", lambda e, half=half, pkv=pkv: e.tensor_tensor(out=Ff32[:, half * 512:(half + 1) * 512],
                                                                                in0=pkv[:, half * 512:(half + 1) * 512],
                                                                                in1=Ff32[:, half * 512:(half + 1) * 512], op=ALU.add),
                           r=[pkvb[half], Ff32_b], w=[Ff32_b])
            sch.op("act", lambda e: e.copy(out=Fbf_n[:, :], in_=Ff32[:, :]), r=[Ff32_b], w=[Fbf_nb])
        if STAGE < 8:
            sch.disabled = True
        sch.cur_prio = NORM_PRIO
        onh = [aux(on_b, "0"), aux(on_b, "1")]
        for half in range(2):
            sch.op("act", lambda e, half=half, po=po: e.copy(out=on[:, half * 512:(half + 1) * 512], in_=po[:, half * 512:(half + 1) * 512]),
                   r=[pob[half]], w=[onh[half]])
        sch.op("dve", lambda e: e.tensor_reduce(out=st[:, 0:8], in_=v3(on[:, :]), axis=AX.X, op=ALU.add), r=onh, w=[st_b])
        sch.op("act", lambda e: e.activation(out=scr4[:, :], in_=on[:, :], func=AF.Square), r=onh, w=[scr4_b])
        sch.op("dve", lambda e: e.tensor_reduce(out=st[:, 8:16], in_=v3(scr4[:, :]), axis=AX.X, op=ALU.add), r=[scr4_b], w=[st_b])
        sch.op("dve", lambda e: e.tensor_scalar_mul(out=st[:, 16:24], in0=st[:, 0:8], scalar1=1.0 / HD), r=[st_b], w=[st_b])
        sch.op("dve", lambda e: e.tensor_tensor(out=st[:, 24:32], in0=st[:, 16:24], in1=st[:, 16:24], op=ALU.mult), r=[st_b], w=[st_b])
        sch.op("dve", lambda e: e.scalar_tensor_tensor(out=st[:, 32:40], in0=st[:, 8:16], scalar=1.0 / HD, in1=st[:, 24:32],
                                                       op0=ALU.mult, op1=ALU.subtract), r=[st_b], w=[st_b])
        sch.op("dve", lambda e: e.tensor_scalar_add(out=st[:, 32:40], in0=st[:, 32:40], scalar1=EPS), r=[st_b], w=[st_b])
        sch.op("pool", lambda e: e.tensor_tensor(out=st[:, 40:48], in0=st[:, 32:40], in1=mhalf[:, 0:8], op=ALU.pow), r=[st_b], w=[st_b])
        sch.op("dve", lambda e: e.scalar_tensor_tensor(out=st[:, 48:56], in0=st[:, 16:24], scalar=-1.0, in1=st[:, 40:48],
                                                       op0=ALU.mult, op1=ALU.mult), r=[st_b], w=[st_b])
        for h in range(8):
            half = h // 4
            if half == 0:
                sch.op("act", lambda e, h=h: e.activation(out=on[:, h * 128:(h + 1) * 128], in_=on[:, h * 128:(h + 1) * 128],
                                                          func=AF.Identity, scale=st[:, 40 + h:41 + h], bias=st[:, 48 + h:49 + h]),
                       r=[onh[0], st_b], w=[onh[0]])
            else:
                sch.op("dve", lambda e, h=h: e.tensor_scalar(out=on[:, h * 128:(h + 1) * 128], in0=on[:, h * 128:(h + 1) * 128],
                                                             scalar1=st[:, 40 + h:41 + h], scalar2=st[:, 48 + h:49 + h],
                                                             op0=ALU.mult, op1=ALU.add),
                       r=[onh[1], st_b], w=[onh[1]])
        if SUB2 < 3:
            sch.disabled = True
        sch.op("dve", lambda e: e.tensor_tensor(out=yb[:, 1024:2048], in0=on[:, :], in1=sz[:, 1024:2048], op=ALU.mult),
               r=[aux(on_b, "0"), aux(on_b, "1"), szr_b], w=[ybr_b])
        if STAGE < 9:
            sch.disabled = True
        sch.cur_prio = POOL_PRIO
        use_hu = HALO_MERGE and (not first) and (not last_c)
        if use_hu:
            sch.op("act", lambda e: e.copy(out=hu[0:32, :], in_=uring[(c + 1) % 3][0][0:32, :]), r=[uring[(c + 1) % 3][1]], w=[hu_b])
            sch.op("act", lambda e: e.copy(out=hu[96:128, :], in_=uring[(c - 1) % 3][0][96:128, :]), r=[uring[(c - 1) % 3][1]], w=[hu_b])
        ppl, pplb = palloc(2)
        for half in range(2):
            def plmm(e, half=half, ppl=ppl):
                last = None
                for cc4 in range(4):
                    cc = half * 4 + cc4
                    gi = cc // 2
                    terms = []
                    if use_hu:
                        terms.append((hu, gi * 6 + 5))
                    elif not first:
                        terms.append((uring[(c - 1) % 3][0], gi * 6 + 0))
                    terms.append((uring[c % 3][0], gi * 6 + (3 if first else (4 if last_c else 1))))
                    if not last_c and not use_hu:
                        terms.append((uring[(c + 1) % 3][0], gi * 6 + 2))
                    for ti, (ut, bi) in enumerate(terms):
                        last = e.matmul(ppl[:, cc * 128:(cc + 1) * 128], ut[:, cc * 128:(cc + 1) * 128], bands[:, bi, :],
                                        start=(ti == 0), stop=(ti == len(terms) - 1))
                return last
            rr = [uring[c % 3][1], B_["bands"]]
            if use_hu:
                rr.append(hu_b)
            else:
                if not first:
                    rr.append(uring[(c - 1) % 3][1])
                if not last_c:
                    rr.append(uring[(c + 1) % 3][1])
            sch.op("pe", plmm, r=rr, w=[pplb[half]])
            sch.op("act", lambda e, half=half, ppl=ppl: e.copy(out=plT[:, half * 4:(half + 1) * 4, :],
                                                               in_=ppl[:, half * 512:(half + 1) * 512].rearrange("p (c t) -> p c t", c=4)),
                   r=[pplb[half]], w=[plT_b])
        pyp, pypb = palloc(2)
        for half in range(2):
            def ypmm(e, half=half, pyp=pyp):
                last = None
                for g2 in range(2):
                    gi = half * 2 + g2
                    for cc2 in range(2):
                        cc = gi * 2 + cc2
                        last = e.matmul(pyp[:, gi * 256:(gi + 1) * 256], plT[:, cc, :], poolw_sb[:, cc, :],
                                        start=(cc2 == 0), stop=(cc2 == 1))
                return last
            sch.op("pe", ypmm, r=[plT_b, B_["poolw"]], w=[pypb[half]])
            sch.op("dve", lambda e, half=half, pyp=pyp: e.tensor_tensor(out=yb[:, half * 512:(half + 1) * 512],
                                                                        in0=pyp[:, half * 512:(half + 1) * 512],
                                                                        in1=sz[:, half * 512:(half + 1) * 512], op=ALU.mult),
                   r=[pypb[half], szp_b], w=[ybp_b])
        if STAGE < 10:
            sch.disabled = True
        sch.cur_prio = 0
        pty, ptyb = palloc(2)
        ptyh = [(lambda i=i: pty.bank(i).bitcast(BF16).rearrange("p (e t) -> p e t", e=8)) for i in range(2)]
        for half in range(2):
            def trY(e, half=half):
                last = None
                for e8 in range(8):
                    ec = half * 8 + e8
                    last = e.transpose(ptyh[half]()[:, e8, :], yb[:, ec * 128:(ec + 1) * 128], ident[:, :])
                return last
            sch.op("pe", trY, r=[ybp_b if half == 0 else ybr_b, B_["ident"]], w=[ptyb[half]])
        sch.op("act", lambda e: e.copy(out=yT[:, 0:8, :], in_=ptyh[0]()), r=[ptyb[0]], w=[yTp_b])
        sch.op("dve", lambda e: e.tensor_copy(out=yT[:, 8:16, :], in_=ptyh[1]()), r=[ptyb[1]], w=[yTr_b])
        pout, poutb = palloc(2)
        sch.cur_prio = FINAL_PRIO
        for half in range(2):
            def outmm(e, half=half, pout=pout):
                last = None
                for ec in range(16):
                    last = e.matmul(pout[:, half * 512:(half + 1) * 512], yT[:, ec, :], wout_sb[:, ec, half * 512:(half + 1) * 512],
                                    start=(ec == 0), stop=(ec == 15))
                return last
            sch.op("pe", outmm, r=[yTp_b, yTr_b, B_["wout"]], w=[poutb[half]])
            sch.op("act", lambda e, half=half, pout=pout: e.activation(out=junk[:, half * 512:(half + 1) * 512], in_=pout[:, half * 512:(half + 1) * 512],
                                                                       func=AF.Square, accum_out=st[:, 56 + half:57 + half]),
                   r=[poutb[half]], w=[st_b, junk_b])
        sch.op("dve", lambda e: e.tensor_tensor(out=st[:, 58:59], in0=st[:, 56:57], in1=st[:, 57:58], op=ALU.add), r=[st_b], w=[st_b])
        sch.op("dve", lambda e: e.tensor_scalar(out=st[:, 59:60], in0=st[:, 58:59], scalar1=1.0 / D, scalar2=EPS,
                                                op0=ALU.mult, op1=ALU.add), r=[st_b], w=[st_b])
        sch.op("pool", lambda e: e.tensor_tensor(out=st[:, 60:61], in0=st[:, 59:60], in1=mhalf[:, 0:1], op=ALU.pow), r=[st_b], w=[st_b])
        for half in range(2):
            sch.op("dve", lambda e, half=half, pout=pout: e.scalar_tensor_tensor(out=scr4[:, half * 512:(half + 1) * 512],
                                                                                 in0=pout[:, half * 512:(half + 1) * 512], scalar=st[:, 60:61],
                                                                                 in1=ggtab[:, half * 512:(half + 1) * 512],
                                                                                 op0=ALU.mult, op1=ALU.mult),
                   r=[poutb[half], st_b, ggtab_b], w=[scr4_b])
        xr, xrb = xres_r[gc]
        sch.op(FIN_ENG, lambda e: e.tensor_tensor(out=xr[:, :], in0=xr[:, :], in1=scr4[:, :], op=ALU.add), r=[xrb, scr4_b], w=[xrb])
        sch.dma("sp", ys[seq].ap()[c * 128:(c + 1) * 128, :], xr[:, :], r=[xrb], w=[ysc[seq]], key=xrb)
        sch.cur_prio = 0

    itB = 0
    gbase = [0]
    for seq in range(nseq):
        gbase[0] = itB
        n = nchs[seq]
        sch.dma("sp", ggtab[:, :], ggscr.ap()[seq * 128:(seq + 1) * 128, :], r=[B_["ggscr"]], w=[ggtab_b])
        frontB(seq, 0, itB % 3, itB % 2)
        loadsB(seq, 0)
        for c in range(n):
            hs_c = (itB + c) % 3
            if c + 1 < n:
                frontB(seq, c + 1, (itB + c + 1) % 3, (itB + c + 1) % 2)
            backB(seq, c, hs_c)
            if c + 1 < n:
                loadsB(seq, c + 1)
        itB += n

    sch.disabled = False
    if WARM:
        sch.warm_fn = lambda e: e.matmul(psum[:, 7 * 512:8 * 512], ident[:, :], bands[:, 0:4, :], start=True, stop=True)
    sch.finish()
    sch.sbuf_free = sb_hi - sbuf_peak[0]

    with nc.Block() as block:
        @block.tensor
        def _(e):
            sch.emit("pe", e)

        @block.scalar
        def _(e):
            sch.emit("act", e)

        @block.vector
        def _(e):
            sch.emit("dve", e)

        @block.gpsimd
        def _(e):
            sch.emit("pool", e)

        @block.sync
        def _(e):
            sch.emit("sp", e)
    return nc


_PROG_CACHE = {}


def _get_prog(S_list):
    key = tuple(S_list)
    if key not in _PROG_CACHE:
        _PROG_CACHE[key] = build_program(list(S_list))
    return _PROG_CACHE[key]


def kernel(x_prompt, x_sample, c_prompt, c_sample, ada_w, ada_b, norm_pre, norm_post,
           w_in, pool_w, pool_scale, ret_decay_fwd, ret_decay_bwd, w_out):
    f = lambda a: np.ascontiguousarray(np.asarray(a, dtype=np.float32))
    x_prompt, x_sample = f(x_prompt), f(x_sample)
    nb = x_prompt.shape[0]
    assert nb == N_CORES and x_sample.shape[0] == N_CORES
    S0, S1 = x_prompt.shape[1], x_sample.shape[1]
    nc = _get_prog((S0, S1))
    c_prompt, c_sample = f(c_prompt), f(c_sample)
    shared = {
        "ada_w": f(ada_w)[0], "ada_b": f(ada_b), "norm_pre": f(norm_pre), "norm_post": f(norm_post),
        "w_in": f(w_in)[0], "pool_w": f(pool_w)[0], "pool_scale": f(pool_scale),
        "dec_f": f(ret_decay_fwd), "dec_b": f(ret_decay_bwd), "w_out": f(w_out)[0],
    }
    in_maps = []
    for i in range(N_CORES):
        m = dict(shared)
        m["x0"] = x_prompt[i]
        m["x1"] = x_sample[i]
        m["cvec"] = np.ascontiguousarray(np.stack([c_prompt[i], c_sample[i]], axis=0))
        in_maps.append(m)
    res = run_bass_kernel_spmd(nc, in_maps, core_ids=list(range(N_CORES)))
    y0 = np.stack([np.asarray(r["y0"], dtype=np.float32) for r in res.results], axis=0)
    y1 = np.stack([np.asarray(r["y1"], dtype=np.float32) for r in res.results], axis=0)
    return (y0, y1)
```

```python
import math
import numpy as np
import concourse.bass as bass
import concourse.mybir as mybir
from concourse.bass_utils import run_bass_kernel_spmd

F32 = mybir.dt.float32
BF16 = mybir.dt.bfloat16
AF = mybir.ActivationFunctionType
ALU = mybir.AluOpType
AX = mybir.AxisListType

D = 1024
H = 8
HD = 128
EPS = 1e-6
POOL_W = (2, 4, 8, 16)
N_CORES = 8
SAME_ENGINE_SYNC = True
SAME_ENGINE_RAW_ONLY = True
CW1 = 6.28125
CW2 = 2.0 * math.pi - 6.28125
PI_SAFE = 3.1415925
HALF_PI_SAFE = 1.5707962


class Op:
    __slots__ = ("idx", "eng", "seng", "fn", "deps", "kind", "inc", "sem", "val", "cost", "lat", "phase",
                 "start", "finish", "key", "meta", "vbs", "prio", "raw")


class Buf:
    __slots__ = ("name", "w", "r", "excl", "phys", "users", "virt", "disjoint")

    def __init__(self, name, excl=False, virt=False, disjoint=True):
        self.name = name
        self.disjoint = disjoint
        self.w = None
        self.r = []
        self.excl = excl
        self.virt = virt
        self.phys = None
        self.users = []


class _Dummy:
    def then_inc(self, *a, **k):
        return self


class _Probe:
    def __init__(self):
        self.calls = []

    def __getattr__(self, name):
        def f(*a, **k):
            self.calls.append((name, a, k))
            return _Dummy()
        return f


def _free(ap):
    try:
        return int(np.prod(ap.shape[1:]))
    except Exception:
        return 1


def _est_cost(eng, calls):
    t = 0.0
    for name, a, k in calls:
        if name == "matmul":
            rhs = a[2] if len(a) > 2 else k["rhs"]
            t += 0.023 + 0.00044 * _free(rhs)
        elif name == "transpose":
            t += 0.08
        else:
            aps = [k.get(n) for n in ("out", "in_", "in0")] + list(a[:1])
            F = max([_free(x) for x in aps if x is not None and hasattr(x, "shape")] + [1])
            if eng == "act":
                t += 0.36 + 0.00062 * F
            elif eng == "dve":
                t += 0.2 + 0.00105 * F
            else:
                if k.get("op") == ALU.pow:
                    t += 0.45 + 0.15 * F
                else:
                    t += 0.45 + 0.0018 * F
    return t


class Sched:
    ENGS = ("pe", "act", "dve", "pool", "sp")
    WINDOW = 96
    XLAT = 0.3

    def __init__(self, nc):
        self.nc = nc
        self.all = []
        self.esem = {e: nc.alloc_semaphore("s_" + e) for e in self.ENGS}
        self.dsem = {}
        self.phase = 0
        self.disabled = False
        self.cur_prio = 0
        self.warm_fn = None

    def _deps(self, r, w, excl_eng):
        deps = []
        self._raw = set()
        self._strong = set()
        for b in r:
            if b.w is not None:
                deps.append(b.w)
                self._raw.add(id(b.w))
            if b.excl:
                deps.extend(o for o in b.r if o.eng != excl_eng)
        for b in w:
            if b.w is not None:
                deps.append(b.w)
                if not b.disjoint:
                    self._strong.add(id(b.w))
            deps.extend(b.r)
            self._strong.update(id(o) for o in b.r)
        seen = set()
        out = []
        for d in deps:
            if id(d) not in seen:
                seen.add(id(d))
                out.append(d)
        return out

    def _new(self, eng, seng, fn, deps, kind, inc, cost, lat, key=None):
        o = Op()
        o.idx = len(self.all)
        o.eng, o.seng, o.fn, o.deps, o.kind, o.inc = eng, seng, fn, deps, kind, inc
        o.cost, o.lat, o.phase, o.key = cost, lat, self.phase, key
        o.sem = o.val = o.start = o.finish = None
        o.meta = ([], [])
        o.vbs = []
        o.raw = set()
        o.prio = 0 if seng == "pe" else self.cur_prio
        self.all.append(o)
        return o

    def op(self, eng, fn, r=(), w=(), sig=True):
        if self.disabled:
            return
        deps = self._deps(r, w, eng)
        pr = _Probe()
        fn(pr)
        cost = _est_cost(eng, pr.calls)
        o = self._new(eng, eng, fn, deps, "op", 1, cost, cost)
        o.raw = self._raw | self._strong
        o.meta = ([b.name for b in r], [b.name for b in w])
        for b in list(r) + list(w):
            if b.virt and o not in b.users:
                b.users.append(o)
                o.vbs.append(b)
        for b in r:
            b.r.append(o)
        for b in w:
            b.w = o
            b.r = []

    def dma(self, q, out_ap, in_ap, r=(), w=(), key=None, slow=False):
        if self.disabled:
            return None
        if key is None:
            key = w[0]
        if key not in self.dsem:
            self.dsem[key] = [self.nc.alloc_semaphore("d_" + key.name), None]
        ent = self.dsem[key]
        deps = self._deps(r, w, "dma")
        if ent[1] is not None and ent[1] not in deps:
            deps.append(ent[1])
        nc = self.nc

        def fn(e, out_ap=out_ap, in_ap=in_ap, slow=slow):
            if slow:
                with nc.allow_non_contiguous_dma(reason="one-time small strided load"):
                    return e.dma_start(out=out_ap, in_=in_ap)
            return e.dma_start(out=out_ap, in_=in_ap)

        nbytes = int(np.prod(out_ap.shape)) * 4
        issue = 0.4 if q == "sp" else 1.5
        o = self._new("dma", q, fn, deps, "dma", 16, issue, issue + 2.0 + nbytes / 150e3, key=key)
        ent[1] = o
        for b in r:
            b.r.append(o)
        for b in w:
            b.w = o
            b.r = []
        return o

    def barrier(self, bufs):
        if self.disabled:
            return
        evs = []
        for b in bufs:
            if b.w is not None:
                evs.append(b.w)
            evs.extend(b.r)
        self.phase += 1
        for e in self.ENGS:
            self._new(e, e, None, list(evs), "bar", 0, 0.0, 0.0)
        self.phase += 1

    def finish(self):
        self.phase += 1
        last = [ent[1] for ent in self.dsem.values() if ent[1] is not None]
        self._new("sp", "sp", None, last, "bar", 0, 0.0, 0.0)
        self._schedule()

    def _schedule(self):
        free = {e: 0.0 for e in self.ENGS}
        order = {e: [] for e in self.ENGS}
        tenant = [None] * NDYN_BANKS
        npend = {}
        nph = self.phase + 1
        byph = [dict((e, []) for e in self.ENGS) for _ in range(nph)]
        for o in self.all:
            byph[o.phase][o.seng].append(o)
        if BL_PRIO:
            succ = {}
            for o in self.all:
                for d in o.deps:
                    succ.setdefault(id(d), []).append(o)
            bl = {}
            for o in reversed(self.all):
                m = 0.0
                for q in succ.get(id(o), ()):
                    v = bl[id(q)] + (0.0 if q.seng == o.seng else self.XLAT)
                    if v > m:
                        m = v
                bl[id(o)] = m + o.lat
            for o in self.all:
                if o.phase in BL_PHASES:
                    o.prio = -bl[id(o)] * BL_SCALE
        for ph in range(nph):
            uns = byph[ph]
            for e in self.ENGS:
                if BL_PRIO and ph in BL_PHASES:
                    uns[e].sort(key=lambda o: (o.prio, o.idx))
                else:
                    uns[e].sort(key=lambda o: (o.idx + o.prio, o.idx))
            remaining = sum(len(v) for v in uns.values())
            wide = False
            while remaining:
                best = None
                for e in self.ENGS:
                    lst = uns[e]
                    cb = None
                    for o in (lst if wide else lst[:self.WINDOW]):
                        ready = 0.0
                        ok = True
                        for d in o.deps:
                            if d.finish is None:
                                ok = False
                                break
                            rr = d.finish + (0.0 if d.seng == e and d.kind != "dma" else self.XLAT)
                            if rr > ready:
                                ready = rr
                        if not ok:
                            continue
                        need_bank = [v for v in o.vbs if v.phys is None]
                        if need_bank:
                            cands = []
                            for p in range(NDYN_BANKS):
                                tv = tenant[p]
                                if tv is None:
                                    cands.append((0.0, p))
                                elif npend.get(id(tv), len(tv.users)) == 0:
                                    cands.append((max(u.finish for u in tv.users) + self.XLAT, p))
                            if len(cands) < len(need_bank):
                                continue
                            cands.sort()
                            bank_rdy = cands[len(need_bank) - 1][0]
                            if bank_rdy > ready:
                                ready = bank_rdy
                            o_banks = [p for _, p in cands[:len(need_bank)]]
                        else:
                            o_banks = None
                        st = ready if ready > free[e] else free[e]
                        if cb is None or st < cb[0]:
                            cb = (st, o, o_banks)
                        if st <= free[e]:
                            break
                    if cb is not None and (best is None or (cb[0], (cb[1].prio if (BL_PRIO and ph in BL_PHASES) else cb[1].idx + cb[1].prio)) < (best[0], (best[1].prio if (BL_PRIO and ph in BL_PHASES) else best[1].idx + best[1].prio))):
                        best = cb
                if best is None:
                    if not wide:
                        wide = True
                        continue
                    raise RuntimeError("scheduler stuck (PSUM bank deadlock)")
                wide = False
                st, o, o_banks = best
                if o_banks is not None:
                    need_bank = [v for v in o.vbs if v.phys is None]
                    for v, p in zip(need_bank, o_banks):
                        tv = tenant[p]
                        if tv is not None:
                            for u in tv.users:
                                if u not in o.deps:
                                    o.deps.append(u)
                        tenant[p] = v
                        v.phys = p
                for v in o.vbs:
                    npend[id(v)] = npend.get(id(v), len(v.users)) - 1
                o.start = st
                o.finish = st + o.lat
                free[o.seng] = st + o.cost
                uns[o.seng].remove(o)
                order[o.seng].append(o)
                remaining -= 1
        self.order = order
        self.model_us = max(free.values())
        cnt = {e: 0 for e in self.ENGS}
        dcnt = {}
        for e in self.ENGS:
            for o in order[e]:
                if o.kind == "op":
                    cnt[e] += 1
                    o.sem, o.val = self.esem[e], cnt[e]
        for e in self.ENGS:
            for o in order[e]:
                if o.kind == "dma":
                    k = id(o.key)
                    dcnt[k] = dcnt.get(k, 0) + 16
                    o.sem, o.val = self.dsem[o.key][0], dcnt[k]

    def emit(self, eng_name, e):
        waited = {}
        order = self.order[eng_name]
        for oi, o in enumerate(order):
            need = {}
            for d in o.deps:
                if d.kind == "bar":
                    continue
                if d.kind == "op" and d.seng == eng_name and o.kind != "dma":
                    if eng_name == "pe" or not SAME_ENGINE_SYNC:
                        continue
                    if SAME_ENGINE_RAW_ONLY and id(d) not in o.raw:
                        continue
                assert d.val is not None
                k = id(d.sem)
                if k not in need or need[k][1] < d.val:
                    need[k] = (d.sem, d.val)
            for k, (sem, v) in need.items():
                if waited.get(k, 0) >= v:
                    continue
                e.wait_ge(sem, v)
                waited[k] = v
            if o.fn is None:
                continue
            inst = o.fn(e)
            inst.then_inc(o.sem, o.inc)
            if eng_name == "pe" and self.warm_fn is not None and o.phase >= 2 and oi + 1 < len(order):
                gap = order[oi + 1].start - (o.start + o.cost)
                if gap > WARM_GAP:
                    for _ in range(min(WARM_MAX, int(WARM_FRAC * gap / 0.22))):
                        self.warm_fn(e)


STAGE = 99
CHECK_SBUF = True
NPAIRS = 4
HT_ACT_A = False
FRONT_PRIO = 0
KB_ENG = "dve"
FMUL_ENG = "dve"
KF_ENG = "dve"
QTB_ENG = "dve"
FINAL_PRIO = 100
RET_PRIO = 0
FUPD_PRIO = 0
NORM_PRIO = 0
POOL_PRIO = 0
BL_PRIO = False
BL_SCALE = 1.0
BL_PHASES = (4,)
NDYN_BANKS = 7
WARM = True
SPILL_HT = True
SPILL_KT = False
HALO_MERGE = True
SPILL_U = False
FUSE_POOL = True
VSCALE_A = False
WARM_GAP = 0.5
WARM_FRAC = 1.0
WARM_MAX = 16
ROPE_ENG_A = "dve"
FIN_ENG = "dve"
NB = {"xres": 1}
SUB = 9
SUB2 = 9


def build_program(S_list):
    nseq = len(S_list)
    assert nseq == 2
    nchs = [s // 128 for s in S_list]
    Smax = max(S_list)
    nchmax = Smax // 128
    nc = bass.Bass("TRN2", target_bir_lowering=False)

    xs = [nc.dram_tensor(f"x{i}", [S_list[i], D], F32, kind="ExternalInput") for i in range(nseq)]
    ys = [nc.dram_tensor(f"y{i}", [S_list[i], D], F32, kind="ExternalOutput") for i in range(nseq)]
    cvec = nc.dram_tensor("cvec", [nseq, D], F32, kind="ExternalInput")
    ada_w = nc.dram_tensor("ada_w", [D, 3 * D], F32, kind="ExternalInput")
    ada_b = nc.dram_tensor("ada_b", [1, 3 * D], F32, kind="ExternalInput")
    norm_pre = nc.dram_tensor("norm_pre", [1, D], F32, kind="ExternalInput")
    norm_post = nc.dram_tensor("norm_post", [1, D], F32, kind="ExternalInput")
    w_in = nc.dram_tensor("w_in", [D, 6 * D], F32, kind="ExternalInput")
    pool_w = nc.dram_tensor("pool_w", [4, 256, 256], F32, kind="ExternalInput")
    pool_scale = nc.dram_tensor("pool_scale", [1, D], F32, kind="ExternalInput")
    dec_f = nc.dram_tensor("dec_f", [1, H], F32, kind="ExternalInput")
    dec_b = nc.dram_tensor("dec_b", [1, H], F32, kind="ExternalInput")
    w_out = nc.dram_tensor("w_out", [2 * D, D], F32, kind="ExternalInput")
    krscr = [nc.dram_tensor(f"krscr{i}", [S_list[i], D], BF16, kind="Internal") for i in range(nseq)]
    vscr = [nc.dram_tensor(f"vscr{i}", [S_list[i], D], BF16, kind="Internal") for i in range(nseq)]
    bscr = [nc.dram_tensor(f"bscr{i}", [S_list[i], D], BF16, kind="Internal") for i in range(nseq)]
    ktscr = [nc.dram_tensor(f"ktscr{i}", [S_list[i], D], BF16, kind="Internal") for i in range(nseq)]
    uscr = [nc.dram_tensor(f"uscr{i}", [S_list[i], D], BF16, kind="Internal") for i in range(nseq)]
    hscr = [nc.dram_tensor(f"hscr{i}", [S_list[i], D], BF16, kind="Internal") for i in range(nseq)]
    ropescr = nc.dram_tensor("ropescr", [Smax, 128], F32, kind="Internal")
    ggscr = nc.dram_tensor("ggscr", [nseq * 128, D], F32, kind="Internal")

    sch = Sched(nc)

    sb_lo = (nc.sbuf_base + 63) // 64 * 64
    sb_hi = nc.sbuf_top
    cur = [sb_lo]
    names = [0]
    sbuf_peak = [0]

    def alloc(shape, dt, at=None, name=None):
        nbytes = int(np.prod(shape[1:])) * (2 if dt == BF16 else 4)
        nbytes = (nbytes + 63) // 64 * 64
        if at is None:
            off = cur[0]
            cur[0] += nbytes
        else:
            off = at[0]
            at[0] += nbytes
        names[0] += 1
        if CHECK_SBUF:
            assert off + nbytes <= sb_hi, f"SBUF overflow at {name}: {off + nbytes} > {sb_hi}"
        sbuf_peak[0] = max(sbuf_peak[0], off + nbytes)
        if not CHECK_SBUF and off + nbytes > sb_hi:
            off = sb_lo
        return nc.alloc_sbuf_tensor_at(f"t{names[0]}_{name or ''}", list(shape), dt, offset=off)

    wq_sb = alloc([128, 8, 4096], BF16, name="wq")
    wout_sb = alloc([128, 16, 1024], BF16, name="wout")
    poolw_sb = alloc([128, 8, 256], BF16, name="poolw")
    ident = alloc([128, 128], BF16, name="ident")
    dtot = alloc([128, 8, 128], F32, name="dtot")
    af_tab = alloc([128, 8, 128], BF16, name="aftab")
    ab_tab = alloc([128, 8, 128], BF16, name="abtab")
    bands = alloc([128, 24, 128], BF16, name="bands")
    kdf = alloc([128, 8], F32, name="kdf")
    kdb = alloc([128, 8], F32, name="kdb")
    gcf = alloc([128, 8], F32, name="gcf")
    gcb = alloc([128, 8], F32, name="gcb")
    gprime = alloc([128, 8, 2], F32, name="gprime")
    shiftT = alloc([128, 8, 2], F32, name="shiftT")
    negpi = alloc([128, 1], F32, name="negpi")
    mhalf = alloc([128, 8], F32, name="mhalf")
    halfpi = alloc([128, 1], F32, name="halfpi")
    B_ = {n: Buf(n) for n in ["wq", "wout", "poolw", "ident", "dtot", "aftab", "abtab", "bands", "kd",
                              "gprime", "negpi", "wkv", "ropescr", "ggscr"]}
    arena0 = cur[0]
    wkv_at = [arena0]
    wkv_sb = alloc([128, 8, 2048], BF16, at=wkv_at, name="wkv")
    arena_after_wkv = wkv_at[0]

    psum = nc.alloc_psum_tensor("psum", [128, 4096], F32)
    vbcount = [0]
    pbank = []

    class PT:
        def __init__(self, vbs):
            self.vbs = vbs

        def _phys(self, i):
            p = self.vbs[i].phys
            return 0 if p is None else p

        def bank(self, i):
            b = self._phys(i)
            return psum[:, b * 512:(b + 1) * 512]

        def __getitem__(self, key):
            rows, cols = key
            a0, a1 = cols.start, cols.stop
            bi = a0 // 512
            assert (a1 - 1) // 512 == bi, (a0, a1)
            b = self._phys(bi)
            return psum[:, b * 512 + (a0 - bi * 512): b * 512 + (a1 - bi * 512)]

    def palloc(nb=2, static=None):
        vbs = []
        for i in range(nb):
            vbcount[0] += 1
            v = Buf(f"pb{vbcount[0]}", excl=True, virt=(static is None))
            if static is not None:
                v.phys = static[i]
            vbs.append(v)
            pbank.append(v)
        return PT(vbs), vbs

    sa = [arena_after_wkv]
    diff = alloc([128, 128], F32, at=sa, name="diff")
    irow = alloc([128, 128], F32, at=sa, name="irow")
    pcol = alloc([128, 1], F32, at=sa, name="pcol")
    p127 = alloc([128, 1], F32, at=sa, name="p127")
    mge = alloc([128, 128], F32, at=sa, name="mge")
    mlt = alloc([128, 128], F32, at=sa, name="mlt")
    rpos = alloc([128, 128], F32, at=sa, name="rpos")
    rneg = alloc([128, 128], F32, at=sa, name="rneg")
    identf = alloc([128, 128], F32, at=sa, name="identf")
    tmpa = alloc([128, 128], F32, at=sa, name="tmpa")
    tmpb = alloc([128, 128], F32, at=sa, name="tmpb")
    tmpc = alloc([128, 128], F32, at=sa, name="tmpc")
    rowp1 = alloc([128, 128], F32, at=sa, name="rowp1")
    row128m = alloc([128, 128], F32, at=sa, name="row128m")
    decf_t = alloc([128, 8], F32, at=sa, name="decf")
    decb_t = alloc([128, 8], F32, at=sa, name="decb")
    lgf = alloc([128, 8], F32, at=sa, name="lgf")
    lgb = alloc([128, 8], F32, at=sa, name="lgb")
    etmp = alloc([128, 8], F32, at=sa, name="etmp")
    Bc = Buf("const")
    grp = {"b": Bc}
    groups = [Bc]

    def G():
        return grp["b"]

    def newgroup(name):
        grp["b"] = Buf(name)
        groups.append(grp["b"])

    g = nc.gpsimd

    w_in_v = w_in.ap().rearrange("(dc p) n -> p dc n", p=128)
    sch.dma("pool", wkv_sb[:, :, :], w_in_v[:, :, 2048:4096], w=[B_["wkv"]])

    sch.op("pool", lambda e: e.iota(diff[:, :], [[1, 128]], base=0, channel_multiplier=-1, allow_small_or_imprecise_dtypes=True), w=[G()])
    sch.op("pool", lambda e: e.iota(irow[:, :], [[1, 128]], base=0, channel_multiplier=0, allow_small_or_imprecise_dtypes=True), w=[G()])
    sch.op("pool", lambda e: e.iota(pcol[:, :], [[1, 1]], base=0, channel_multiplier=1, allow_small_or_imprecise_dtypes=True), w=[G()])
    sch.op("pool", lambda e: e.iota(p127[:, :], [[1, 1]], base=127, channel_multiplier=-1, allow_small_or_imprecise_dtypes=True), w=[G()])
    sch.op("dve", lambda e: e.memset(negpi[:, :], -math.pi), w=[B_["negpi"]])
    sch.op("dve", lambda e: e.memset(mhalf[:, :], -0.5), w=[B_["negpi"]])
    sch.op("dve", lambda e: e.memset(halfpi[:, :], HALF_PI_SAFE), w=[B_["negpi"]])

    def dv(fn, r=None, w=None):
        sch.op("dve", fn, r=[Bc, G()] if r is None else list(r), w=[G()] if w is None else list(w))

    def ac(fn, r=None, w=None):
        sch.op("act", fn, r=[Bc, G()] if r is None else list(r), w=[G()] if w is None else list(w))

    dv(lambda e: e.tensor_single_scalar(out=identf[:, :], in_=diff[:, :], scalar=0.0, op=ALU.is_equal))
    dv(lambda e: e.tensor_copy(out=ident[:, :], in_=identf[:, :]), w=(G(), B_["ident"]))
    dv(lambda e: e.tensor_single_scalar(out=mge[:, :], in_=diff[:, :], scalar=0.0, op=ALU.is_ge))
    dv(lambda e: e.tensor_single_scalar(out=mlt[:, :], in_=diff[:, :], scalar=0.0, op=ALU.is_lt))
    dv(lambda e: e.tensor_scalar_max(out=rpos[:, :], in0=diff[:, :], scalar1=0.0))
    dv(lambda e: e.tensor_scalar(out=rneg[:, :], in0=diff[:, :], scalar1=-1.0, scalar2=0.0,
                                 op0=ALU.mult, op1=ALU.max))
    dv(lambda e: e.tensor_scalar_add(out=rowp1[:, :], in0=irow[:, :], scalar1=1.0))
    dv(lambda e: e.tensor_scalar(out=row128m[:, :], in0=irow[:, :], scalar1=-1.0, scalar2=128.0,
                                 op0=ALU.mult, op1=ALU.add))

    newgroup("grpD")
    sch.dma("sp", decf_t[:, :], dec_f.ap().partition_broadcast(128).rearrange("p o n -> p (o n)"), w=[G()])
    sch.dma("sp", decb_t[:, :], dec_b.ap().partition_broadcast(128).rearrange("p o n -> p (o n)"), w=[G()])
    for dsrc, lg in ((decf_t, lgf), (decb_t, lgb)):
        ac(lambda e, dsrc=dsrc: e.activation(out=etmp[:, :], in_=dsrc[:, :], func=AF.Exp, scale=-math.log(2.0)))
        dv(lambda e: e.tensor_scalar(out=etmp[:, :], in0=etmp[:, :], scalar1=-1.0, scalar2=1.0,
                                     op0=ALU.mult, op1=ALU.add))
        ac(lambda e, lg=lg: e.activation(out=lg[:, :], in_=etmp[:, :], func=AF.Ln))
    for h in range(H):
        ac(lambda e, h=h: e.activation(out=tmpa[:, :], in_=rpos[:, :], func=AF.Exp, scale=lgf[:, h:h + 1]))
        ac(lambda e, h=h: e.activation(out=tmpb[:, :], in_=rneg[:, :], func=AF.Exp, scale=lgb[:, h:h + 1]))
        dv(lambda e: e.tensor_tensor(out=tmpa[:, :], in0=tmpa[:, :], in1=mge[:, :], op=ALU.mult))
        dv(lambda e: e.tensor_tensor(out=tmpb[:, :], in0=tmpb[:, :], in1=mlt[:, :], op=ALU.mult))
        dv(lambda e, h=h: e.tensor_tensor(out=dtot[:, h, :], in0=tmpa[:, :], in1=tmpb[:, :], op=ALU.add),
           w=(G(), B_["dtot"]))
        ac(lambda e, h=h: e.activation(out=af_tab[:, h, :], in_=rowp1[:, :], func=AF.Exp, scale=lgf[:, h:h + 1]),
           w=(G(), B_["aftab"]))
        ac(lambda e, h=h: e.activation(out=ab_tab[:, h, :], in_=row128m[:, :], func=AF.Exp, scale=lgb[:, h:h + 1]),
           w=(G(), B_["abtab"]))
    ac(lambda e: e.activation(out=kdf[:, :], in_=lgf[:, :], func=AF.Exp, scale=p127[:, 0:1]), w=(G(), B_["kd"]))
    ac(lambda e: e.activation(out=kdb[:, :], in_=lgb[:, :], func=AF.Exp, scale=pcol[:, 0:1]), w=(G(), B_["kd"]))
    ac(lambda e: e.activation(out=gcf[:, :], in_=lgf[:, :], func=AF.Exp, scale=128.0), w=(G(), B_["kd"]))
    ac(lambda e: e.activation(out=gcb[:, :], in_=lgb[:, :], func=AF.Exp, scale=128.0), w=(G(), B_["kd"]))

    newgroup("grpP")
    tmpa2 = alloc([128, 128], F32, at=sa, name="tmpa2")
    tmpb2 = alloc([128, 128], F32, at=sa, name="tmpb2")
    tmpc2 = alloc([128, 128], F32, at=sa, name="tmpc2")
    for gi, w_ in enumerate(POOL_W):
        hw = w_ // 2
        dv(lambda e, hw=hw: e.tensor_single_scalar(out=tmpa2[:, :], in_=diff[:, :], scalar=float(-(hw - 1)), op=ALU.is_ge))
        dv(lambda e, hw=hw: e.tensor_single_scalar(out=tmpb2[:, :], in_=diff[:, :], scalar=float(hw), op=ALU.is_le))
        dv(lambda e: e.tensor_tensor(out=tmpa2[:, :], in0=tmpa2[:, :], in1=tmpb2[:, :], op=ALU.mult))
        dv(lambda e, gi=gi, w_=w_: e.scalar_tensor_tensor(out=bands[:, gi * 6 + 1, :], in0=tmpa2[:, :], scalar=1.0 / w_,
                                                          in1=identf[:, :], op0=ALU.mult, op1=ALU.subtract),
           w=(G(), B_["bands"]))
        dv(lambda e, gi=gi, w_=w_, hw=hw: e.tensor_scalar(out=bands[:, gi * 6 + 0, :], in0=diff[:, :],
                                                          scalar1=float(hw - 128), scalar2=1.0 / w_,
                                                          op0=ALU.is_le, op1=ALU.mult), w=(G(), B_["bands"]))
        dv(lambda e, gi=gi, w_=w_, hw=hw: e.tensor_scalar(out=bands[:, gi * 6 + 2, :], in0=diff[:, :],
                                                          scalar1=float(129 - hw), scalar2=1.0 / w_,
                                                          op0=ALU.is_ge, op1=ALU.mult), w=(G(), B_["bands"]))
        dv(lambda e, w_=w_, hw=hw: e.tensor_scalar(out=tmpb2[:, :], in0=irow[:, :], scalar1=float(hw), scalar2=float(w_),
                                                   op0=ALU.add, op1=ALU.min))
        dv(lambda e: e.reciprocal(out=tmpb2[:, :], in_=tmpb2[:, :]))
        dv(lambda e: e.tensor_tensor(out=tmpc2[:, :], in0=tmpa2[:, :], in1=tmpb2[:, :], op=ALU.mult))
        dv(lambda e, gi=gi: e.tensor_tensor(out=bands[:, gi * 6 + 3, :], in0=tmpc2[:, :], in1=identf[:, :], op=ALU.subtract),
           w=(G(), B_["bands"]))
        dv(lambda e, w_=w_, hw=hw: e.tensor_scalar(out=tmpb2[:, :], in0=irow[:, :], scalar1=-1.0, scalar2=float(128 + hw),
                                                   op0=ALU.mult, op1=ALU.add))
        dv(lambda e, w_=w_: e.tensor_scalar_min(out=tmpb2[:, :], in0=tmpb2[:, :], scalar1=float(w_)))
        dv(lambda e: e.reciprocal(out=tmpb2[:, :], in_=tmpb2[:, :]))
        dv(lambda e: e.tensor_tensor(out=tmpc2[:, :], in0=tmpa2[:, :], in1=tmpb2[:, :], op=ALU.mult))
        dv(lambda e, gi=gi: e.tensor_tensor(out=bands[:, gi * 6 + 4, :], in0=tmpc2[:, :], in1=identf[:, :], op=ALU.subtract),
           w=(G(), B_["bands"]))
        dv(lambda e, gi=gi: e.tensor_tensor(out=bands[:, gi * 6 + 5, :], in0=bands[:, gi * 6 + 0, :], in1=bands[:, gi * 6 + 2, :], op=ALU.add),
           r=(Bc, G(), B_["bands"]), w=(G(), B_["bands"]))

    newgroup("grpR")
    invf = alloc([128, 64], F32, at=sa, name="invf")
    ac(lambda e: e.activation(out=invf[:, :], in_=irow[:, 0:64], func=AF.Exp, scale=-math.log(10000.0) / 64.0))
    RC = 8
    posall = alloc([128, RC], F32, at=sa, name="posall")
    ang = alloc([128, RC, 64], F32, at=sa, name="ang")
    marg = alloc([128, RC, 64], F32, at=sa, name="marg")
    rtab = alloc([128, RC, 128], F32, at=sa, name="rtab")
    rred = alloc([128, RC, 64], F32, at=sa, name="rred")
    qint = alloc([128, RC, 64], mybir.dt.int32, at=sa, name="qint")
    Brt = Buf("rtab")
    rope_v = ropescr.ap().rearrange("(c p) n -> p c n", p=128)
    for c0 in range(0, nchmax, RC):
        ncb = min(RC, nchmax - c0)
        sch.op("pool", lambda e, c0=c0: e.iota(posall[:, :], [[128, RC]], base=128 * c0, channel_multiplier=1, allow_small_or_imprecise_dtypes=True),
               r=[G()], w=[G()])
        dv(lambda e: e.tensor_tensor(out=ang[:, :, :], in0=posall[:, :].unsqueeze(2).to_broadcast([128, RC, 64]),
                                     in1=invf[:, :].unsqueeze(1).to_broadcast([128, RC, 64]), op=ALU.mult))
        dv(lambda e: e.tensor_scalar_mul(out=marg[:, :, :], in0=ang[:, :, :], scalar1=1.0 / (2.0 * math.pi)))
        dv(lambda e: e.tensor_copy(out=qint[:, :, :], in_=marg[:, :, :]))
        dv(lambda e: e.tensor_copy(out=marg[:, :, :], in_=qint[:, :, :]))
        dv(lambda e: e.scalar_tensor_tensor(out=rred[:, :, :], in0=marg[:, :, :], scalar=-CW1, in1=ang[:, :, :],
                                            op0=ALU.mult, op1=ALU.add))
        dv(lambda e: e.scalar_tensor_tensor(out=rred[:, :, :], in0=marg[:, :, :], scalar=-CW2, in1=rred[:, :, :],
                                            op0=ALU.mult, op1=ALU.add))
        dv(lambda e: e.tensor_single_scalar(out=marg[:, :, :], in_=rred[:, :, :], scalar=math.pi, op=ALU.is_gt))
        dv(lambda e: e.scalar_tensor_tensor(out=rred[:, :, :], in0=marg[:, :, :], scalar=-2.0 * math.pi, in1=rred[:, :, :],
                                            op0=ALU.mult, op1=ALU.add))
        dv(lambda e: e.tensor_single_scalar(out=marg[:, :, :], in_=rred[:, :, :], scalar=-math.pi, op=ALU.is_lt))
        dv(lambda e: e.scalar_tensor_tensor(out=rred[:, :, :], in0=marg[:, :, :], scalar=2.0 * math.pi, in1=rred[:, :, :],
                                            op0=ALU.mult, op1=ALU.add))
        dv(lambda e: e.tensor_scalar(out=rred[:, :, :], in0=rred[:, :, :], scalar1=-PI_SAFE, scalar2=PI_SAFE,
                                     op0=ALU.max, op1=ALU.min))
        ac(lambda e: e.activation(out=rtab[:, :, 64:128], in_=rred[:, :, :], func=AF.Sin), w=(G(), Brt))
        dv(lambda e: e.scalar_tensor_tensor(out=marg[:, :, :], in0=rred[:, :, :], scalar=-1.0, in1=rred[:, :, :],
                                            op0=ALU.mult, op1=ALU.max))
        ac(lambda e: e.activation(out=rtab[:, :, 0:64], in_=marg[:, :, :], func=AF.Sin, scale=-1.0, bias=halfpi[:, 0:1]),
           r=(Bc, G(), B_["negpi"]), w=(G(), Brt))
        sch.dma("sp", rope_v[:, c0:c0 + ncb, :], rtab[:, 0:ncb, :], r=[Brt], w=[B_["ropescr"]], key=Brt)

    newgroup("grpA")
    adaw_sb = alloc([128, 8, 1024], BF16, at=sa, name="adaw")
    Badaw = Buf("adaw")
    ada_w_v = ada_w.ap().rearrange("(dc p) n -> p dc n", p=128)

    cT = alloc([128, 8, 2], F32, at=sa, name="cT")
    scT = alloc([128, 8, 2], BF16, at=sa, name="scT")
    scb = alloc([128, 2, 8, 128], BF16, at=sa, name="scb")
    adabT = alloc([128, 24], F32, at=sa, name="adabT")
    gpreT = alloc([128, 8], F32, at=sa, name="gpreT")
    modT = alloc([128, 16, 2], F32, at=sa, name="modT")
    rowb = alloc([128, 1024], F32, at=sa, name="rowb")
    gpostb = alloc([128, 1024], F32, at=sa, name="gpostb")
    ggt = alloc([128, 1024], F32, at=sa, name="ggt")
    pstage = alloc([128, 8, 256], F32, at=sa, name="pstage")
    setup_end = sa[0]

    for s in range(nseq):
        sch.dma("sp", cT[:, :, s], cvec.ap()[s].rearrange("(dc p) -> p dc", p=128), w=[G()], slow=True)
    sch.dma("sp", adabT[:, :], ada_b.ap().rearrange("o (fc p) -> p (o fc)", p=128), w=[G()], slow=True)
    sch.dma("sp", gpreT[:, :], norm_pre.ap().rearrange("o (fc p) -> p (o fc)", p=128), w=[G()], slow=True)
    sch.dma("sp", gpostb[:, :], norm_post.ap().partition_broadcast(128).rearrange("p o n -> p (o n)"), w=[G()])
    sch.dma("sp", rowb[:, :], pool_scale.ap().partition_broadcast(128).rearrange("p o n -> p (o n)"), w=[G()])
    sch.dma("sp", pstage[:, :, :], pool_w.ap().rearrange("g (cc p) d -> p (g cc) d", p=128), w=[G()])
    dv(lambda e: e.tensor_tensor(out=poolw_sb[:, :, :].rearrange("p (g c) d -> p g c d", g=4),
                                 in0=pstage[:, :, :].rearrange("p (g c) d -> p g c d", g=4),
                                 in1=rowb[:, :].rearrange("p (g d) -> p g d", g=4).unsqueeze(2).to_broadcast([128, 4, 2, 256]),
                                 op=ALU.mult), w=(G(), B_["poolw"]))
    sch.dma("sp", rowb[:, :], ada_b.ap()[:, 2048:3072].partition_broadcast(128).rearrange("p o n -> p (o n)"), r=[G()], w=[G()])
    ac(lambda e: e.activation(out=scT[:, :, :], in_=cT[:, :, :], func=AF.Silu))
    for s in range(nseq):
        for dc in range(8):
            dv(lambda e, s=s, dc=dc: e.tensor_copy(out=scb[:, s, dc, :], in_=scT[:, dc, s:s + 1].to_broadcast([128, 128])))
    pm, pmb = palloc(2, static=[0, 1])
    pmv = pm[:, 0:32].rearrange("p (fc s) -> p fc s", s=2)
    for piece in range(2):
        sch.dma("pool", adaw_sb[:, :, :], ada_w_v[:, :, piece * 1024:(piece + 1) * 1024], w=[Badaw])

        def mod_mm(e, piece=piece):
            last = None
            for fc in range(8):
                for dc in range(8):
                    last = e.matmul(pmv[:, piece * 8 + fc, :], adaw_sb[:, dc, fc * 128:(fc + 1) * 128], scT[:, dc, :],
                                    start=(dc == 0), stop=(dc == 7))
            return last
        sch.op("pe", mod_mm, r=[G(), Badaw], w=pmb)
    dv(lambda e: e.tensor_tensor(out=modT[:, :, :], in0=pmv, in1=adabT[:, 0:16].unsqueeze(2).to_broadcast([128, 16, 2]),
                                 op=ALU.add), r=[G()] + pmb, w=[G()])
    dv(lambda e: e.tensor_copy(out=shiftT[:, :, :], in_=modT[:, 0:8, :]), w=(G(), B_["gprime"]))
    dv(lambda e: e.scalar_tensor_tensor(out=gprime[:, :, :], in0=modT[:, 8:16, :], scalar=1.0,
                                        in1=gpreT[:, :].unsqueeze(2).to_broadcast([128, 8, 2]),
                                        op0=ALU.add, op1=ALU.mult), w=(G(), B_["gprime"]))
    dv(lambda e: e.tensor_scalar_mul(out=gprime[:, :, :], in0=gprime[:, :, :], scalar1=float(D) ** 0.5), w=(G(), B_["gprime"]))
    sch.dma("pool", adaw_sb[:, :, :], ada_w_v[:, :, 2048:3072], w=[Badaw])
    gg_v = ggscr.ap()
    for s in range(nseq):
        pg, pgb = palloc(2, static=[2 + 2 * s, 3 + 2 * s])

        def gate_mm(e, s=s, pg=pg):
            last = None
            for half in range(2):
                for dc in range(8):
                    last = e.matmul(pg[:, half * 512:(half + 1) * 512], scb[:, s, dc, :],
                                    adaw_sb[:, dc, half * 512:(half + 1) * 512],
                                    start=(dc == 0), stop=(dc == 7))
            return last
        sch.op("pe", gate_mm, r=[G(), Badaw], w=pgb)
        for half in range(2):
            dv(lambda e, pg=pg, half=half: e.tensor_tensor(out=ggt[:, half * 512:(half + 1) * 512], in0=pg[:, half * 512:(half + 1) * 512],
                                                         in1=rowb[:, half * 512:(half + 1) * 512], op=ALU.add), r=[G()] + pgb, w=[G()])
        dv(lambda e: e.tensor_tensor(out=ggt[:, :], in0=ggt[:, :], in1=gpostb[:, :], op=ALU.mult))
        sch.dma("sp", gg_v[s * 128:(s + 1) * 128, :], ggt[:, :], r=[G()], w=[B_["ggscr"]], key=G())
    sch.dma("pool", wq_sb[:, :, 0:2048], w_in_v[:, :, 0:2048], w=[B_["wq"]])
    sch.dma("pool", wq_sb[:, :, 2048:4096], w_in_v[:, :, 4096:6144], w=[B_["wq"]])
    sch.dma("pool", wout_sb[:, :, :], w_out.ap().rearrange("(ec p) n -> p ec n", p=128), w=[B_["wout"]])

    sch.disabled = STAGE < 2
    sch.barrier(groups + [Brt, Badaw] + pbank)

    act_at = [arena_after_wkv]

    def mk(shape, dt, name, n=1):
        ts = [alloc(shape, dt, at=act_at, name=f"{name}{i}") for i in range(n)]
        bs = [Buf(f"{name}{i}") for i in range(n)]
        return ts, bs

    xin, xin_b = mk([128, 1024], F32, "xin", 2)
    junk = alloc([128, 1024], BF16, at=act_at, name="junk")
    junk_b = Buf("junk", disjoint=False)
    xn, xn_b = mk([128, 1024], BF16, "xn", 1)
    hT, hT_b = mk([128, 8, 128], BF16, "hT", 3)
    ssq, ssq_b = mk([128, 4], F32, "ssq", 2)
    rt, rt_b = mk([128, 128], F32, "rt", 2)
    t1, t1_b = mk([128, 512], F32, "t1", 1)
    t2, t2_b = mk([128, 512], F32, "t2", 1)
    common_end = act_at[0]

    _aux = {}

    def aux(b, tag):
        k = (id(b), tag)
        if k not in _aux:
            _aux[k] = Buf(b.name + tag)
        return _aux[k]

    def hTr(hs):
        return [aux(hT_b[hs], "a"), aux(hT_b[hs], "b")]

    def front(seq, c, hslot, xslot, ht_act=True):
        sch.cur_prio = FRONT_PRIO
        xt, xb = xin[xslot], xin_b[xslot]
        sq, sqb = ssq[xslot], ssq_b[xslot]
        sch.dma("sp", xt[:, :], xs[seq].ap()[c * 128:(c + 1) * 128, :], w=[xb])
        sch.op("act", lambda e: e.activation(out=junk[:, :], in_=xt[:, :], func=AF.Square, accum_out=sq[:, 0:1]),
               r=[xb], w=[sqb, junk_b])
        sch.op("pool", lambda e: e.tensor_scalar_add(out=sq[:, 1:2], in0=sq[:, 0:1], scalar1=float(D) * EPS), r=[sqb], w=[sqb])
        sch.op("pool", lambda e: e.tensor_tensor(out=sq[:, 2:3], in0=sq[:, 1:2], in1=mhalf[:, 0:1], op=ALU.pow),
               r=[sqb], w=[sqb])
        sch.op("act", lambda e: e.activation(out=xn[0][:, :], in_=xt[:, :], func=AF.Copy, scale=sq[:, 2:3]),
               r=[xb, sqb], w=[xn_b[0]])
        pt, ptb = palloc(2)
        ptv = [(lambda i=i: pt.bank(i).bitcast(BF16)[:, 0:512].rearrange("p (dc t) -> p dc t", dc=4)) for i in range(2)]
        for hb in range(2):
            def tr(e, hb=hb):
                last = None
                for d4 in range(4):
                    dc = hb * 4 + d4
                    last = e.transpose(ptv[hb]()[:, d4, :], xn[0][:, dc * 128:(dc + 1) * 128], ident[:, :])
                return last
            sch.op("pe", tr, r=[xn_b[0], B_["ident"]], w=[ptb[hb]])
        for dc in range(8):
            hb, d4 = dc // 4, dc % 4
            if hb == 0 or ht_act:
                sch.op("act", lambda e, dc=dc, d4=d4, hb=hb: e.activation(out=hT[hslot][:, dc, :], in_=ptv[hb]()[:, d4, :], func=AF.Identity,
                                                                          scale=gprime[:, dc, seq:seq + 1], bias=shiftT[:, dc, seq:seq + 1]),
                       r=[ptb[hb], B_["gprime"]], w=[aux(hT_b[hslot], "a" if hb == 0 else "b")])
            else:
                sch.op("dve", lambda e, dc=dc, d4=d4: e.tensor_scalar(out=hT[hslot][:, dc, :], in0=ptv[1]()[:, d4, :],
                                                                      scalar1=gprime[:, dc, seq:seq + 1],
                                                                      scalar2=shiftT[:, dc, seq:seq + 1],
                                                                      op0=ALU.mult, op1=ALU.add),
                       r=[ptb[1], B_["gprime"]], w=[aux(hT_b[hslot], "b")])

    def front_done():
        sch.cur_prio = 0

    def rope(psrc, psrc_b, dst, dst_b, rts, rtb, kscale, ceng="dve"):
        cosb = rts[:, 0:64].unsqueeze(1).to_broadcast([128, 8, 64])
        sinb = rts[:, 64:128].unsqueeze(1).to_broadcast([128, 4, 64])
        for half in range(2):
            src = lambda half=half: psrc[:, half * 512:(half + 1) * 512].rearrange("p (h two d) -> p h two d", h=4, two=2)
            t1v = t1[0][:, :].rearrange("p (h two d) -> p h two d", h=4, two=2)
            t2v = t2[0][:, :].rearrange("p (h two d) -> p h two d", h=4, two=2)
            dv_ = dst[:, half * 512:(half + 1) * 512].rearrange("p (h two d) -> p h two d", h=4, two=2)
            pb = [psrc_b[half]]
            src3 = lambda half=half: psrc[:, half * 512:(half + 1) * 512].rearrange("p (g d) -> p g d", g=8)
            t1v3 = t1[0][:, :].rearrange("p (g d) -> p g d", g=8)
            if kscale is None:
                sch.op("dve", lambda e, src3=src3, t1v3=t1v3: e.tensor_tensor(out=t1v3, in0=src3(), in1=cosb, op=ALU.mult),
                       r=pb + [rtb], w=[t1_b[0]])
                sch.op("dve", lambda e, src=src, t2v=t2v: e.tensor_tensor(out=t2v[:, :, 0, :], in0=src()[:, :, 1, :], in1=sinb, op=ALU.mult),
                       r=pb + [rtb], w=[t2_b[0]])
                sch.op("dve", lambda e, src=src, t2v=t2v: e.tensor_tensor(out=t2v[:, :, 1, :], in0=src()[:, :, 0, :], in1=sinb, op=ALU.mult),
                       r=pb + [rtb], w=[t2_b[0]])
            else:
                sch.op("dve", lambda e, src3=src3, t1v3=t1v3: e.scalar_tensor_tensor(out=t1v3, in0=src3(), scalar=kscale, in1=cosb,
                                                                                 op0=ALU.mult, op1=ALU.mult),
                       r=pb + [rtb], w=[t1_b[0]])
                sch.op("dve", lambda e, src=src, t2v=t2v: e.scalar_tensor_tensor(out=t2v[:, :, 0, :], in0=src()[:, :, 1, :], scalar=kscale,
                                                                                 in1=sinb, op0=ALU.mult, op1=ALU.mult),
                       r=pb + [rtb], w=[t2_b[0]])
                sch.op("dve", lambda e, src=src, t2v=t2v: e.scalar_tensor_tensor(out=t2v[:, :, 1, :], in0=src()[:, :, 0, :], scalar=kscale,
                                                                                 in1=sinb, op0=ALU.mult, op1=ALU.mult),
                       r=pb + [rtb], w=[t2_b[0]])
            sch.op(ceng, lambda e, t1v=t1v, t2v=t2v, dv_=dv_: e.tensor_tensor(out=dv_[:, :, 0, :], in0=t1v[:, :, 0, :], in1=t2v[:, :, 0, :],
                                                                                op=ALU.subtract),
                   r=[t1_b[0], t2_b[0]], w=[dst_b])
            sch.op(ceng, lambda e, t1v=t1v, t2v=t2v, dv_=dv_: e.tensor_tensor(out=dv_[:, :, 1, :], in0=t1v[:, :, 1, :], in1=t2v[:, :, 1, :],
                                                                                op=ALU.add),
                   r=[t1_b[0], t2_b[0]], w=[dst_b])

    def bc_h(tab):
        return tab[:, :].unsqueeze(2).to_broadcast([128, 8, 128])

    def v3(t):
        return t.rearrange("p (h d) -> p h d", h=8)

    pa_at = [common_end]
    krA, krA_b = [], []
    vA, vA_b = [], []
    for i in range(2):
        krA.append(alloc([128, 1024], BF16, at=pa_at, name=f"krA{i}")); krA_b.append(Buf(f"krA{i}"))
        vA.append(alloc([128, 1024], BF16, at=pa_at, name=f"vA{i}")); vA_b.append(Buf(f"vA{i}"))
    kbA = alloc([128, 1024], BF16, at=pa_at, name="kbA"); kbA_b = Buf("kbA")
    Bf32 = alloc([128, 1024], F32, at=pa_at, name="Bf32"); Bf32_b = Buf("Bf32")
    BbfA, BbfA_b = [], []
    for i in range(2):
        BbfA.append(alloc([128, 1024], BF16, at=pa_at, name=f"BbfA{i}")); BbfA_b.append(Buf(f"BbfA{i}"))
    scrB = {("kr", i): Buf(f"krscr{i}") for i in range(nseq)}
    scrB.update({("v", i): Buf(f"vscr{i}") for i in range(nseq)})
    scrB.update({("b", i): Buf(f"bscr{i}") for i in range(nseq)})
    scrB.update({("h", i): Buf(f"hscr{i}") for i in range(nseq)})
    scrB.update({("kt", i): Buf(f"ktscr{i}") for i in range(nseq)})
    scrB.update({("u", i): Buf(f"uscr{i}") for i in range(nseq)})
    uA, uA_b = [], []
    for i in range(2):
        uA.append(alloc([128, 1024], BF16, at=pa_at, name=f"uA{i}")); uA_b.append(Buf(f"uA{i}"))
    KTA, KTA_b = [], []
    for i in range(2):
        KTA.append(alloc([128, 8, 128], BF16, at=pa_at, name=f"KTA{i}")); KTA_b.append(Buf(f"KTA{i}"))

    itA = 0
    for seq in range(nseq):
        n = nchs[seq]
        for idx, c in enumerate(range(n - 1, -1, -1)):
            sl = itA % 2
            itA += 1
            hs = itA % 3
            front(seq, c, hs, sl, ht_act=HT_ACT_A)
            front_done()
            if SPILL_HT:
                sch.dma("sp", hscr[seq].ap()[c * 128:(c + 1) * 128, :], hT[hs][:, :, :].rearrange("p a b -> p (a b)"),
                        r=hTr(hs), w=[scrB[("h", seq)]], key=aux(hT_b[hs], "a"))
            if SPILL_U:
                pu, pub = palloc(2)
                for half in range(2):
                    def umm(e, half=half, pu=pu, hs=hs):
                        last = None
                        for dc in range(8):
                            last = e.matmul(pu[:, half * 512:(half + 1) * 512], hT[hs][:, dc, :], wq_sb[:, dc, half * 512:(half + 1) * 512],
                                            start=(dc == 0), stop=(dc == 7))
                        return last
                    sch.op("pe", umm, r=hTr(hs) + [B_["wq"]], w=[pub[half]])
                    sch.op("act", lambda e, half=half, pu=pu, sl=sl: e.copy(out=uA[sl][:, half * 512:(half + 1) * 512],
                                                                            in_=pu[:, half * 512:(half + 1) * 512]),
                           r=[pub[half]], w=[uA_b[sl]])
                sch.dma("sp", uscr[seq].ap()[c * 128:(c + 1) * 128, :], uA[sl][:, :], r=[uA_b[sl]], w=[scrB[("u", seq)]], key=uA_b[sl])
            sch.dma("sp", rt[sl][:, :], ropescr.ap()[c * 128:(c + 1) * 128, :], r=[B_["ropescr"]], w=[rt_b[sl]])
            pk, pkb = palloc(2)
            pv, pvb = palloc(2)
            for bi, (pp, ppb) in enumerate(((pk, pkb), (pv, pvb))):
                for half in range(2):
                    col0 = bi * 1024 + half * 512

                    def mm(e, pp=pp, half=half, col0=col0, hs=hs):
                        last = None
                        for dc in range(8):
                            last = e.matmul(pp[:, half * 512:(half + 1) * 512], hT[hs][:, dc, :], wkv_sb[:, dc, col0:col0 + 512],
                                            start=(dc == 0), stop=(dc == 7))
                        return last
                    sch.op("pe", mm, r=hTr(hs) + [B_["wkv"]], w=[ppb[half]])
            rope(pk, pkb, krA[sl], krA_b[sl], rt[sl], rt_b[sl], HD ** -0.5, ROPE_ENG_A)
            for half in range(2):
                sch.op("act", lambda e, half=half, pv=pv, sl=sl: e.copy(out=vA[sl][:, half * 512:(half + 1) * 512],
                                                                        in_=pv[:, half * 512:(half + 1) * 512]),
                       r=[pvb[half]], w=[vA_b[sl]])
            sch.dma("sp", krscr[seq].ap()[c * 128:(c + 1) * 128, :], krA[sl][:, :], r=[krA_b[sl]], w=[scrB[("kr", seq)]], key=krA_b[sl])
            sch.dma("sp", vscr[seq].ap()[c * 128:(c + 1) * 128, :], vA[sl][:, :], r=[vA_b[sl]], w=[scrB[("v", seq)]], key=vA_b[sl])
            if SPILL_KT:
                ptk, ptkb = palloc(1)
                ptkv = lambda ptk=ptk: ptk.bank(0).bitcast(BF16).rearrange("p (h t) -> p h t", h=8)

                def trk(e, sl=sl, ptkv=ptkv):
                    last = None
                    for h in range(8):
                        last = e.transpose(ptkv()[:, h, :], krA[sl][:, h * 128:(h + 1) * 128], ident[:, :])
                    return last
                sch.op("pe", trk, r=[krA_b[sl], B_["ident"]], w=ptkb)
                sch.op("act", lambda e, sl=sl, ptkv=ptkv: e.copy(out=KTA[sl][:, :, :], in_=ptkv()), r=ptkb, w=[KTA_b[sl]])
                sch.dma("sp", ktscr[seq].ap()[c * 128:(c + 1) * 128, :], KTA[sl][:, :, :].rearrange("p a b -> p (a b)"),
                        r=[KTA_b[sl]], w=[scrB[("kt", seq)]], key=KTA_b[sl])
            if idx == 0:
                sch.op("pool", lambda e, sl=sl: e.memset(BbfA[sl][:, :], 0.0), w=[BbfA_b[sl]])
            sch.dma("sp", bscr[seq].ap()[c * 128:(c + 1) * 128, :], BbfA[sl][:, :], r=[BbfA_b[sl]], w=[scrB[("b", seq)]], key=BbfA_b[sl])
            if c == 0:
                continue
            if VSCALE_A:
                for h in range(8):
                    sch.op("act", lambda e, h=h, pv=pv: e.activation(out=kbA[:, h * 128:(h + 1) * 128], in_=pv[:, h * 128:(h + 1) * 128],
                                                                     func=AF.Copy, scale=kdb[:, h:h + 1]),
                           r=[pvb[h // 4], B_["kd"]], w=[kbA_b])
            else:
                sch.op(KB_ENG, lambda e, sl=sl: e.tensor_tensor(out=v3(kbA[:, :]), in0=v3(krA[sl][:, :]), in1=bc_h(kdb), op=ALU.mult),
                       r=[krA_b[sl], B_["kd"]], w=[kbA_b])
            pkv, pkvb = palloc(2)
            for half in range(2):
                def kvmm(e, half=half, pkv=pkv, sl=sl):
                    last = None
                    for hh in range(4):
                        h = half * 4 + hh
                        l_, r_ = (krA[sl], kbA) if VSCALE_A else (kbA, vA[sl])
                        last = e.matmul(pkv[:, h * 128:(h + 1) * 128], l_[:, h * 128:(h + 1) * 128], r_[:, h * 128:(h + 1) * 128],
                                        start=True, stop=True)
                    return last
                sch.op("pe", kvmm, r=[kbA_b, vA_b[sl], krA_b[sl]], w=[pkvb[half]])
            ns = 1 - sl
            if idx == 0:
                for half in range(2):
                    sch.op("dve", lambda e, half=half, pkv=pkv: e.tensor_copy(out=Bf32[:, half * 512:(half + 1) * 512],
                                                                              in_=pkv[:, half * 512:(half + 1) * 512]),
                           r=[pkvb[half]], w=[Bf32_b])
            else:
                sch.op(FMUL_ENG, lambda e: e.tensor_tensor(out=v3(Bf32[:, :]), in0=v3(Bf32[:, :]), in1=bc_h(gcb), op=ALU.mult),
                       r=[Bf32_b, B_["kd"]], w=[Bf32_b])
                for half in range(2):
                    sch.op("dve", lambda e, half=half, pkv=pkv: e.tensor_tensor(out=Bf32[:, half * 512:(half + 1) * 512],
                                                                                in0=pkv[:, half * 512:(half + 1) * 512],
                                                                                in1=Bf32[:, half * 512:(half + 1) * 512], op=ALU.add),
                           r=[pkvb[half], Bf32_b], w=[Bf32_b])
            sch.op("act", lambda e, ns=ns: e.copy(out=BbfA[ns][:, :], in_=Bf32[:, :]), r=[Bf32_b], w=[BbfA_b[ns]])

    sch.disabled = STAGE < 3
    passA_bufs = KTA_b + uA_b + [junk_b] + xin_b + xn_b + hT_b + [x for i in range(3) for x in hTr(i)] + ssq_b + rt_b + t1_b + t2_b + krA_b + vA_b + [kbA_b, Bf32_b] + BbfA_b + pbank + [B_["wkv"]]
    sch.barrier(passA_bufs)
    if FUSE_POOL:
        fx_at = [arena0]
        WuT = alloc([128, 8, 1024], BF16, at=fx_at, name="WuT")
        WuT_b = Buf("WuT")
        for cc in range(8):
            pT, pTb = palloc(1)
            pTv = lambda pT=pT: pT.bank(0).bitcast(BF16).rearrange("p (dc t) -> p dc t", dc=8)

            def trw(e, cc=cc, pTv=pTv):
                last = None
                for dc in range(8):
                    last = e.transpose(pTv()[:, dc, :], wq_sb[:, dc, cc * 128:(cc + 1) * 128], ident[:, :])
                return last
            sch.op("pe", trw, r=[B_["wq"], B_["ident"]], w=pTb)
            sch.op("act" if cc % 2 == 0 else "dve",
                   (lambda e, cc=cc, pTv=pTv: e.copy(out=WuT[:, cc, :].rearrange("p (dc t) -> p dc t", dc=8), in_=pTv())) if cc % 2 == 0 else
                   (lambda e, cc=cc, pTv=pTv: e.tensor_copy(out=WuT[:, cc, :].rearrange("p (dc t) -> p dc t", dc=8), in_=pTv())),
                   r=pTb, w=[WuT_b])
        for dc in range(8):
            pw, pwb = palloc(2)
            for half in range(2):
                def wmm(e, dc=dc, half=half, pw=pw):
                    last = None
                    for g2 in range(2):
                        gi = half * 2 + g2
                        for cc2 in range(2):
                            cc = gi * 2 + cc2
                            last = e.matmul(pw[:, gi * 256:(gi + 1) * 256], WuT[:, cc, dc * 128:(dc + 1) * 128], poolw_sb[:, cc, :],
                                            start=(cc2 == 0), stop=(cc2 == 1))
                    return last
                sch.op("pe", wmm, r=[WuT_b, B_["poolw"]], w=[pwb[half]])
                sch.op("act", lambda e, dc=dc, half=half, pw=pw: e.copy(out=wq_sb[:, dc, half * 512:(half + 1) * 512],
                                                                        in_=pw[:, half * 512:(half + 1) * 512]),
                       r=[pwb[half]], w=[B_["wq"]])
        sch.barrier([WuT_b] + pbank)
    pb_at = [common_end]
    wkv_region = [arena0]

    def mkb(shape, dt, name, at):
        return alloc(shape, dt, at=at, name=name), Buf(name)

    class Ring:
        def __init__(self, name, shape, dt, n, at):
            self.items = [mkb(shape, dt, f"{name}{i}", at) for i in range(n)]

        def __getitem__(self, c):
            return self.items[c % len(self.items)]

    def ring(name, shape, dt, at2=None):
        n = NB.get(name, 1)
        r = Ring.__new__(Ring)
        r.items = []
        for i in range(n):
            r.items.append(mkb(shape, dt, f"{name}{i}", (at2 if at2 is not None else wkv_region) if i == 0 else pb_at))
        return r

    uring = [mkb([128, 1024], BF16, f"u{i}", wkv_region) for i in range(3)]
    qr_r = ring("qr", [128, 1024], BF16)
    sz_r = ring("sz", [128, 2048], BF16)
    krB_r = ring("krB", [128, 1024], BF16)
    vB_r = ring("vB", [128, 1024], BF16)
    BbfB_r = ring("BbfB", [128, 1024], BF16)
    kf_r = ring("kf", [128, 1024], BF16)
    QT_r = ring("QT", [128, 8, 128], BF16)
    QTf_r = ring("QTf", [128, 8, 128], BF16)
    QTb_r = ring("QTb", [128, 8, 128], BF16)
    KT_r = ring("KT", [128, 8, 128], BF16)
    scm_r = ring("scm", [128, 8, 128], BF16)
    if CHECK_SBUF:
        assert wkv_region[0] <= arena_after_wkv, (wkv_region[0], arena_after_wkv)
    Ff32, Ff32_b = mkb([128, 1024], F32, "Ff32", pb_at)
    Fbf_r = ring("Fbf", [128, 1024], BF16, pb_at)
    scr4_r = ring("scr4", [128, 1024], F32, pb_at)
    on_r = ring("on", [128, 1024], F32, pb_at)
    yb_r = ring("y", [128, 2048], BF16, pb_at)
    yT_r = ring("yT", [128, 16, 128], BF16, pb_at)
    plT_r = ring("plT", [128, 8, 128], BF16, pb_at)
    xres_r = ring("xres", [128, 1024], F32, pb_at)
    st_r = ring("st", [128, 64], F32, pb_at)
    ggtab, ggtab_b = mkb([128, 1024], F32, "ggtab", pb_at)
    hu, hu_b = mkb([128, 1024], BF16, "hu", pb_at)
    if HALO_MERGE:
        sch.op("pool", lambda e: e.memset(hu[:, :], 0.0), w=[hu_b])
    ysc = {i: Buf(f"yout{i}") for i in range(nseq)}

    def frontB(seq, c, hs, xslot):
        sch.disabled = STAGE < 3
        if SPILL_HT:
            ha, hb_ = hTr(hs)
            sch.dma("sp", hT[hs][:, :, :].rearrange("p a b -> p (a b)"), hscr[seq].ap()[c * 128:(c + 1) * 128, :],
                    r=[scrB[("h", seq)]], w=[ha, hb_], key=ha)
        else:
            front(seq, c, hs, xslot)
            front_done()
        n = nchs[seq]
        ut, ub = uring[c % 3]
        if SPILL_U:
            sch.dma("sp", ut[:, :], uscr[seq].ap()[c * 128:(c + 1) * 128, :], r=[scrB[("u", seq)]], w=[ub])
            return
        pu, pub = palloc(2)
        for half in range(2):
            def mm(e, half=half, pu=pu, hs=hs):
                last = None
                for dc in range(8):
                    last = e.matmul(pu[:, half * 512:(half + 1) * 512], hT[hs][:, dc, :], wq_sb[:, dc, half * 512:(half + 1) * 512],
                                    start=(dc == 0), stop=(dc == 7))
                return last
            sch.op("pe", mm, r=hTr(hs) + [B_["wq"]], w=[pub[half]])
            sch.op("act", lambda e, half=half, pu=pu, ut=ut: e.copy(out=ut[:, half * 512:(half + 1) * 512], in_=pu[:, half * 512:(half + 1) * 512]),
                   r=[pub[half]], w=[ub])

    def loadsB(seq, c):
        sch.disabled = STAGE < 3
        sl = c % 2
        sch.dma("sp", rt[sl][:, :], ropescr.ap()[c * 128:(c + 1) * 128, :], r=[B_["ropescr"]], w=[rt_b[sl]])
        gc = gbase[0] + c
        krB, krB_b = krB_r[gc]
        vB, vB_b = vB_r[gc]
        BbfB, BbfB_b = BbfB_r[gc]
        sch.dma("sp", krB[:, :], krscr[seq].ap()[c * 128:(c + 1) * 128, :], r=[scrB[("kr", seq)]], w=[krB_b])
        sch.dma("sp", vB[:, :], vscr[seq].ap()[c * 128:(c + 1) * 128, :], r=[scrB[("v", seq)]], w=[vB_b])
        sch.dma("sp", BbfB[:, :], bscr[seq].ap()[c * 128:(c + 1) * 128, :], r=[scrB[("b", seq)]], w=[BbfB_b])
        if SPILL_KT:
            KT_l, KT_lb = KT_r[gc]
            sch.dma("sp", KT_l[:, :, :].rearrange("p a b -> p (a b)"), ktscr[seq].ap()[c * 128:(c + 1) * 128, :],
                    r=[scrB[("kt", seq)]], w=[KT_lb])
        xr, xrb = xres_r[gc]
        sch.dma("sp", xr[:, :], xs[seq].ap()[c * 128:(c + 1) * 128, :], w=[xrb])

    def backB(seq, c, hs):
        n = nchs[seq]
        sl = c % 2
        first, last_c = (c == 0), (c == n - 1)
        gc = gbase[0] + c
        qr, qr_b = qr_r[gc]; sz, sz_b = sz_r[gc]; krB, krB_b = krB_r[gc]; vB, vB_b = vB_r[gc]
        BbfB, BbfB_b = BbfB_r[gc]; kf, kf_b = kf_r[gc]; QT, QT_b = QT_r[gc]; QTf, QTf_b = QTf_r[gc]
        QTb, QTb_b = QTb_r[gc]; KT, KT_b = KT_r[gc]; scm, scm_b = scm_r[gc]
        Fbf, Fbf_b = Fbf_r[gc]; Fbf_n, Fbf_nb = Fbf_r[gc + 1]
        scr4, scr4_b = scr4_r[gc]; on, on_b = on_r[gc]; yb, yb_b = yb_r[gc]; yT, yT_b = yT_r[gc]
        plT, plT_b = plT_r[gc]; st, st_b = st_r[gc]
        szp_b, szr_b = aux(sz_b, "p"), aux(sz_b, "r")
        ybp_b, ybr_b = aux(yb_b, "p"), aux(yb_b, "r")
        yTp_b, yTr_b = aux(yT_b, "p"), aux(yT_b, "r")
        if STAGE < 4:
            sch.disabled = True
        pq, pqb = palloc(2)
        pz0, pz0b = palloc(2)
        pz1, pz1b = palloc(2)
        for (pp, ppb, colbase) in ((pq, pqb, 1024), (pz0, pz0b, 2048), (pz1, pz1b, 3072)):
            for half in range(2):
                col0 = colbase + half * 512

                def mm(e, pp=pp, half=half, col0=col0):
                    last = None
                    for dc in range(8):
                        last = e.matmul(pp[:, half * 512:(half + 1) * 512], hT[hs][:, dc, :], wq_sb[:, dc, col0:col0 + 512],
                                        start=(dc == 0), stop=(dc == 7))
                    return last
                sch.op("pe", mm, r=hTr(hs) + [B_["wq"]], w=[ppb[half]])
        rope(pq, pqb, qr, qr_b, rt[sl], rt_b[sl], None)
        for zi, (pz, pzb) in enumerate(((pz0, pz0b), (pz1, pz1b))):
            for half in range(2):
                sch.op("act", lambda e, zi=zi, half=half, pz=pz: e.activation(out=sz[:, zi * 1024 + half * 512: zi * 1024 + (half + 1) * 512],
                                                                              in_=pz[:, half * 512:(half + 1) * 512], func=AF.Silu),
                       r=[pzb[half]], w=[szp_b if zi == 0 else szr_b])
        if STAGE < 5:
            sch.disabled = True
        sch.cur_prio = RET_PRIO
        if not last_c:
            sch.op(KF_ENG, lambda e: e.tensor_tensor(out=v3(kf[:, :]), in0=v3(krB[:, :]), in1=bc_h(kdf), op=ALU.mult),
                   r=[krB_b, B_["kd"]], w=[kf_b])
        ptq, ptqb = palloc(1)
        ptk, ptkb = palloc(1) if not SPILL_KT else (None, None)
        ptqv = lambda: ptq.bank(0).bitcast(BF16).rearrange("p (h t) -> p h t", h=8)
        ptkv = lambda: ptk.bank(0).bitcast(BF16).rearrange("p (h t) -> p h t", h=8)

        def trq(e):
            last = None
            for h in range(8):
                last = e.transpose(ptqv()[:, h, :], qr[:, h * 128:(h + 1) * 128], ident[:, :])
            return last

        def trk(e):
            last = None
            for h in range(8):
                last = e.transpose(ptkv()[:, h, :], krB[:, h * 128:(h + 1) * 128], ident[:, :])
            return last
        if SUB < 1:
            sch.disabled = True
        sch.op("pe", trq, r=[qr_b, B_["ident"]], w=ptqb)
        if not SPILL_KT:
            sch.op("pe", trk, r=[krB_b, B_["ident"]], w=ptkb)
        if SUB < 2:
            sch.disabled = True
        sch.op("act", lambda e: e.copy(out=QT[:, :, :], in_=ptqv()), r=ptqb, w=[QT_b])
        if not SPILL_KT:
            sch.op("act", lambda e: e.copy(out=KT[:, :, :], in_=ptkv()), r=ptkb, w=[KT_b])
        if SUB < 3:
            sch.disabled = True
        if not first:
            sch.op("dve", lambda e: e.tensor_tensor(out=QTf[:, :, :], in0=QT[:, :, :], in1=af_tab[:, :, :], op=ALU.mult),
                   r=[QT_b, B_["aftab"]], w=[QTf_b])
        if not last_c:
            sch.op(QTB_ENG, lambda e: e.tensor_tensor(out=QTb[:, :, :], in0=QT[:, :, :], in1=ab_tab[:, :, :], op=ALU.mult),
                   r=[QT_b, B_["abtab"]], w=[QTb_b])
        if STAGE < 6:
            sch.disabled = True
        psc, pscb_ = palloc(2)
        for half in range(2):
            def scmm(e, half=half, psc=psc):
                last = None
                for hh in range(4):
                    h = half * 4 + hh
                    last = e.matmul(psc[:, h * 128:(h + 1) * 128], KT[:, h, :], QT[:, h, :], start=True, stop=True)
                return last
            sch.op("pe", scmm, r=[KT_b, QT_b], w=[pscb_[half]])
            sch.op("dve", lambda e, half=half, psc=psc: e.tensor_tensor(out=scm[:, half * 4:(half + 1) * 4, :],
                                                                        in0=psc[:, half * 512:(half + 1) * 512].rearrange("p (h t) -> p h t", h=4),
                                                                        in1=dtot[:, half * 4:(half + 1) * 4, :], op=ALU.mult),
                   r=[pscb_[half], B_["dtot"]], w=[scm_b])
        po, pob = palloc(2)
        for half in range(2):
            def omm(e, half=half, po=po):
                last = None
                for hh in range(4):
                    h = half * 4 + hh
                    terms = [(scm[:, h, :], vB[:, h * 128:(h + 1) * 128])]
                    if not first:
                        terms.append((QTf[:, h, :], Fbf[:, h * 128:(h + 1) * 128]))
                    if not last_c:
                        terms.append((QTb[:, h, :], BbfB[:, h * 128:(h + 1) * 128]))
                    for ti, (l_, r_) in enumerate(terms):
                        last = e.matmul(po[:, h * 128:(h + 1) * 128], l_, r_, start=(ti == 0), stop=(ti == len(terms) - 1))
                return last
            rr = [scm_b, vB_b]
            if not first:
                rr += [QTf_b, Fbf_b]
            if not last_c:
                rr += [QTb_b, BbfB_b]
            sch.op("pe", omm, r=rr, w=[pob[half]])
        if STAGE < 7:
            sch.disabled = True
        sch.cur_prio = FUPD_PRIO
        if not last_c:
            pkv, pkvb = palloc(2)
            for half in range(2):
                def kvmm(e, half=half, pkv=pkv):
                    last = None
                    for hh in range(4):
                        h = half * 4 + hh
                        last = e.matmul(pkv[:, h * 128:(h + 1) * 128], kf[:, h * 128:(h + 1) * 128], vB[:, h * 128:(h + 1) * 128],
                                        start=True, stop=True)
                    return last
                sch.op("pe", kvmm, r=[kf_b, vB_b], w=[pkvb[half]])
            if first:
                for half in range(2):
                    sch.op("dve", lambda e, half=half, pkv=pkv: e.tensor_copy(out=Ff32[:, half * 512:(half + 1) * 512],
                                                                              in_=pkv[:, half * 512:(half + 1) * 512]),
                           r=[pkvb[half]], w=[Ff32_b])
            else:
                sch.op(FMUL_ENG, lambda e: e.tensor_tensor(out=v3(Ff32[:, :]), in0=v3(Ff32[:, :]), in1=bc_h(gcf), op=ALU.mult),
                       r=[Ff32_b, B_["kd"]], w=[Ff32_b])
                for half in range(2):
                    sch.op("dve", lambda e, half=half, pkv=pkv: e.tensor_tensor(out=Ff32[:, half * 512:(half + 1) * 512],
                                                                                in0=pkv[:, half * 512:(half + 1) * 512],
                                                                                in1=Ff32[:, half * 512:(half + 1) * 512], op=ALU.add),
                           r=[pkvb[half], Ff32_b], w=[Ff32_b])
            sch.op("act", lambda e: e.copy(out=Fbf_n[:, :], in_=Ff32[:, :]), r=[Ff32_b], w=[Fbf_nb])
        if STAGE < 8:
            sch.disabled = True
        sch.cur_prio = NORM_PRIO
        onh = [aux(on_b, "0"), aux(on_b, "1")]
        for half in range(2):
            sch.op("act", lambda e, half=half, po=po: e.copy(out=on[:, half * 512:(half + 1) * 512], in_=po[:, half * 512:(half + 1) * 512]),
                   r=[pob[half]], w=[onh[half]])
        sch.op("dve", lambda e: e.tensor_reduce(out=st[:, 0:8], in_=v3(on[:, :]), axis=AX.X, op=ALU.add), r=onh, w=[st_b])
        sch.op("act", lambda e: e.activation(out=scr4[:, :], in_=on[:, :], func=AF.Square), r=onh, w=[scr4_b])
        sch.op("dve", lambda e: e.tensor_reduce(out=st[:, 8:16], in_=v3(scr4[:, :]), axis=AX.X, op=ALU.add), r=[scr4_b], w=[st_b])
        sch.op("dve", lambda e: e.tensor_tensor(out=st[:, 24:32], in0=st[:, 0:8], in1=st[:, 0:8], op=ALU.mult), r=[st_b], w=[st_b])
        sch.op("dve", lambda e: e.tensor_scalar(out=st[:, 16:24], in0=st[:, 8:16], scalar1=1.0 / HD, scalar2=EPS,
                                                op0=ALU.mult, op1=ALU.add), r=[st_b], w=[st_b])
        sch.op("dve", lambda e: e.scalar_tensor_tensor(out=st[:, 32:40], in0=st[:, 24:32], scalar=-1.0 / (HD * HD), in1=st[:, 16:24],
                                                       op0=ALU.mult, op1=ALU.add), r=[st_b], w=[st_b])
        sch.op("pool", lambda e: e.tensor_tensor(out=st[:, 40:48], in0=st[:, 32:40], in1=mhalf[:, 0:8], op=ALU.pow), r=[st_b], w=[st_b])
        sch.op("dve", lambda e: e.scalar_tensor_tensor(out=st[:, 48:56], in0=st[:, 0:8], scalar=-1.0 / HD, in1=st[:, 40:48],
                                                       op0=ALU.mult, op1=ALU.mult), r=[st_b], w=[st_b])
        for h in range(8):
            half = h // 4
            if half == 0:
                sch.op("act", lambda e, h=h: e.activation(out=on[:, h * 128:(h + 1) * 128], in_=on[:, h * 128:(h + 1) * 128],
                                                          func=AF.Identity, scale=st[:, 40 + h:41 + h], bias=st[:, 48 + h:49 + h]),
                       r=[onh[0], st_b], w=[onh[0]])
            else:
                sch.op("dve", lambda e, h=h: e.tensor_scalar(out=on[:, h * 128:(h + 1) * 128], in0=on[:, h * 128:(h + 1) * 128],
                                                             scalar1=st[:, 40 + h:41 + h], scalar2=st[:, 48 + h:49 + h],
                                                             op0=ALU.mult, op1=ALU.add),
                       r=[onh[1], st_b], w=[onh[1]])
        if SUB2 < 3:
            sch.disabled = True
        sch.op("dve", lambda e: e.tensor_tensor(out=yb[:, 1024:2048], in0=on[:, :], in1=sz[:, 1024:2048], op=ALU.mult),
               r=[aux(on_b, "0"), aux(on_b, "1"), szr_b], w=[ybr_b])
        if STAGE < 9:
            sch.disabled = True
        sch.cur_prio = POOL_PRIO
        if FUSE_POOL:
            use_hu = HALO_MERGE and (not first) and (not last_c)
            if use_hu:
                sch.op("act", lambda e: e.copy(out=hu[0:32, :], in_=uring[(c + 1) % 3][0][0:32, :]), r=[uring[(c + 1) % 3][1]], w=[hu_b])
                sch.op("act", lambda e: e.copy(out=hu[96:128, :], in_=uring[(c - 1) % 3][0][96:128, :]), r=[uring[(c - 1) % 3][1]], w=[hu_b])
            pyp, pypb = palloc(2)
            for half in range(2):
                def ypmm(e, half=half, pyp=pyp):
                    last = None
                    for g2 in range(2):
                        gi = half * 2 + g2
                        terms = []
                        if use_hu:
                            terms.append((hu, gi * 6 + 5))
                        elif not first:
                            terms.append((uring[(c - 1) % 3][0], gi * 6 + 0))
                        terms.append((uring[c % 3][0], gi * 6 + (3 if first else (4 if last_c else 1))))
                        if not last_c and not use_hu:
                            terms.append((uring[(c + 1) % 3][0], gi * 6 + 2))
                        for ti, (ut, bi) in enumerate(terms):
                            last = e.matmul(pyp[:, gi * 256:(gi + 1) * 256], bands[:, bi, :], ut[:, gi * 256:(gi + 1) * 256],
                                            start=(ti == 0), stop=(ti == len(terms) - 1))
                    return last
                rr = [uring[c % 3][1], B_["bands"]]
                if use_hu:
                    rr.append(hu_b)
                else:
                    if not first:
                        rr.append(uring[(c - 1) % 3][1])
                    if not last_c:
                        rr.append(uring[(c + 1) % 3][1])
                sch.op("pe", ypmm, r=rr, w=[pypb[half]])
                sch.op("dve", lambda e, half=half, pyp=pyp: e.tensor_tensor(out=yb[:, half * 512:(half + 1) * 512],
                                                                            in0=pyp[:, half * 512:(half + 1) * 512],
                                                                            in1=sz[:, half * 512:(half + 1) * 512], op=ALU.mult),
                       r=[pypb[half], szp_b], w=[ybp_b])
        else:
            use_hu = HALO_MERGE and (not first) and (not last_c)
            if use_hu:
                sch.op("act", lambda e: e.copy(out=hu[0:32, :], in_=uring[(c + 1) % 3][0][0:32, :]), r=[uring[(c + 1) % 3][1]], w=[hu_b])
                sch.op("act", lambda e: e.copy(out=hu[96:128, :], in_=uring[(c - 1) % 3][0][96:128, :]), r=[uring[(c - 1) % 3][1]], w=[hu_b])
            ppl, pplb = palloc(2)
            for half in range(2):
                def plmm(e, half=half, ppl=ppl):
                    last = None
                    for cc4 in range(4):
                        cc = half * 4 + cc4
                        gi = cc // 2
                        terms = []
                        if use_hu:
                            terms.append((hu, gi * 6 + 5))
                        elif not first:
                            terms.append((uring[(c - 1) % 3][0], gi * 6 + 0))
                        terms.append((uring[c % 3][0], gi * 6 + (3 if first else (4 if last_c else 1))))
                        if not last_c and not use_hu:
                            terms.append((uring[(c + 1) % 3][0], gi * 6 + 2))
                        for ti, (ut, bi) in enumerate(terms):
                            last = e.matmul(ppl[:, cc * 128:(cc + 1) * 128], ut[:, cc * 128:(cc + 1) * 128], bands[:, bi, :],
                                            start=(ti == 0), stop=(ti == len(terms) - 1))
                    return last
                rr = [uring[c % 3][1], B_["bands"]]
                if use_hu:
                    rr.append(hu_b)
                else:
                    if not first:
                        rr.append(uring[(c - 1) % 3][1])
                    if not last_c:
                        rr.append(uring[(c + 1) % 3][1])
                sch.op("pe", plmm, r=rr, w=[pplb[half]])
                sch.op("act", lambda e, half=half, ppl=ppl: e.copy(out=plT[:, half * 4:(half + 1) * 4, :],
                                                                   in_=ppl[:, half * 512:(half + 1) * 512].rearrange("p (c t) -> p c t", c=4)),
                       r=[pplb[half]], w=[plT_b])
            pyp, pypb = palloc(2)
            for half in range(2):
                def ypmm(e, half=half, pyp=pyp):
                    last = None
                    for g2 in range(2):
                        gi = half * 2 + g2
                        for cc2 in range(2):
                            cc = gi * 2 + cc2
                            last = e.matmul(pyp[:, gi * 256:(gi + 1) * 256], plT[:, cc, :], poolw_sb[:, cc, :],
                                            start=(cc2 == 0), stop=(cc2 == 1))
                    return last
                sch.op("pe", ypmm, r=[plT_b, B_["poolw"]], w=[pypb[half]])
                sch.op("dve", lambda e, half=half, pyp=pyp: e.tensor_tensor(out=yb[:, half * 512:(half + 1) * 512],
                                                                            in0=pyp[:, half * 512:(half + 1) * 512],
                                                                            in1=sz[:, half * 512:(half + 1) * 512], op=ALU.mult),
                       r=[pypb[half], szp_b], w=[ybp_b])
        if STAGE < 10:
            sch.disabled = True
        sch.cur_prio = 0
        pty, ptyb = palloc(2)
        ptyh = [(lambda i=i: pty.bank(i).bitcast(BF16).rearrange("p (e t) -> p e t", e=8)) for i in range(2)]
        for half in range(2):
            def trY(e, half=half):
                last = None
                for e8 in range(8):
                    ec = half * 8 + e8
                    last = e.transpose(ptyh[half]()[:, e8, :], yb[:, ec * 128:(ec + 1) * 128], ident[:, :])
                return last
            sch.op("pe", trY, r=[ybp_b if half == 0 else ybr_b, B_["ident"]], w=[ptyb[half]])
        sch.op("act", lambda e: e.copy(out=yT[:, 0:8, :], in_=ptyh[0]()), r=[ptyb[0]], w=[yTp_b])
        sch.op("dve", lambda e: e.tensor_copy(out=yT[:, 8:16, :], in_=ptyh[1]()), r=[ptyb[1]], w=[yTr_b])
        pout, poutb = palloc(2)
        sch.cur_prio = FINAL_PRIO
        for half in range(2):
            def outmm(e, half=half, pout=pout):
                last = None
                for ec in range(16):
                    last = e.matmul(pout[:, half * 512:(half + 1) * 512], yT[:, ec, :], wout_sb[:, ec, half * 512:(half + 1) * 512],
                                    start=(ec == 0), stop=(ec == 15))
                return last
            sch.op("pe", outmm, r=[yTp_b, yTr_b, B_["wout"]], w=[poutb[half]])
            sch.op("act", lambda e, half=half, pout=pout: e.activation(out=junk[:, half * 512:(half + 1) * 512], in_=pout[:, half * 512:(half + 1) * 512],
                                                                       func=AF.Square, accum_out=st[:, 56 + half:57 + half]),
                   r=[poutb[half]], w=[st_b, junk_b])
        sch.op("dve", lambda e: e.tensor_tensor(out=st[:, 58:59], in0=st[:, 56:57], in1=st[:, 57:58], op=ALU.add), r=[st_b], w=[st_b])
        sch.op("dve", lambda e: e.tensor_scalar(out=st[:, 59:60], in0=st[:, 58:59], scalar1=1.0 / D, scalar2=EPS,
                                                op0=ALU.mult, op1=ALU.add), r=[st_b], w=[st_b])
        sch.op("pool", lambda e: e.tensor_tensor(out=st[:, 60:61], in0=st[:, 59:60], in1=mhalf[:, 0:1], op=ALU.pow), r=[st_b], w=[st_b])
        for half in range(2):
            sch.op("dve", lambda e, half=half, pout=pout: e.scalar_tensor_tensor(out=scr4[:, half * 512:(half + 1) * 512],
                                                                                 in0=pout[:, half * 512:(half + 1) * 512], scalar=st[:, 60:61],
                                                                                 in1=ggtab[:, half * 512:(half + 1) * 512],
                                                                                 op0=ALU.mult, op1=ALU.mult),
                   r=[poutb[half], st_b, ggtab_b], w=[scr4_b])
        xr, xrb = xres_r[gc]
        sch.op(FIN_ENG, lambda e: e.tensor_tensor(out=xr[:, :], in0=xr[:, :], in1=scr4[:, :], op=ALU.add), r=[xrb, scr4_b], w=[xrb])
        sch.dma("sp", ys[seq].ap()[c * 128:(c + 1) * 128, :], xr[:, :], r=[xrb], w=[ysc[seq]], key=xrb)
        sch.cur_prio = 0

    itB = 0
    gbase = [0]
    for seq in range(nseq):
        gbase[0] = itB
        n = nchs[seq]
        sch.dma("sp", ggtab[:, :], ggscr.ap()[seq * 128:(seq + 1) * 128, :], r=[B_["ggscr"]], w=[ggtab_b])
        frontB(seq, 0, itB % 3, itB % 2)
        loadsB(seq, 0)
        for c in range(n):
            hs_c = (itB + c) % 3
            if c + 1 < n:
                frontB(seq, c + 1, (itB + c + 1) % 3, (itB + c + 1) % 2)
            backB(seq, c, hs_c)
            if c + 1 < n:
                loadsB(seq, c + 1)
        itB += n

    sch.disabled = False
    if WARM:
        sch.warm_fn = lambda e: e.matmul(psum[:, 7 * 512:8 * 512], ident[:, :], bands[:, 0:4, :], start=True, stop=True)
    sch.finish()
    sch.sbuf_free = sb_hi - sbuf_peak[0]

    with nc.Block() as block:
        @block.tensor
        def _(e):
            sch.emit("pe", e)

        @block.scalar
        def _(e):
            sch.emit("act", e)

        @block.vector
        def _(e):
            sch.emit("dve", e)

        @block.gpsimd
        def _(e):
            sch.emit("pool", e)

        @block.sync
        def _(e):
            sch.emit("sp", e)
    return nc


_PROG_CACHE = {}


def _get_prog(S_list):
    key = tuple(S_list)
    if key not in _PROG_CACHE:
        _PROG_CACHE[key] = build_program(list(S_list))
    return _PROG_CACHE[key]


def kernel(x_prompt, x_sample, c_prompt, c_sample, ada_w, ada_b, norm_pre, norm_post,
           w_in, pool_w, pool_scale, ret_decay_fwd, ret_decay_bwd, w_out):
    f = lambda a: np.ascontiguousarray(np.asarray(a, dtype=np.float32))
    x_prompt, x_sample = f(x_prompt), f(x_sample)
    nb = x_prompt.shape[0]
    assert nb == N_CORES and x_sample.shape[0] == N_CORES
    S0, S1 = x_prompt.shape[1], x_sample.shape[1]
    nc = _get_prog((S0, S1))
    c_prompt, c_sample = f(c_prompt), f(c_sample)
    shared = {
        "ada_w": f(ada_w)[0], "ada_b": f(ada_b), "norm_pre": f(norm_pre), "norm_post": f(norm_post),
        "w_in": f(w_in)[0], "pool_w": f(pool_w)[0], "pool_scale": f(pool_scale),
        "dec_f": f(ret_decay_fwd), "dec_b": f(ret_decay_bwd), "w_out": f(w_out)[0],
    }
    in_maps = []
    for i in range(N_CORES):
        m = dict(shared)
        m["x0"] = x_prompt[i]
        m["x1"] = x_sample[i]
        m["cvec"] = np.ascontiguousarray(np.stack([c_prompt[i], c_sample[i]], axis=0))
        in_maps.append(m)
    res = run_bass_kernel_spmd(nc, in_maps, core_ids=list(range(N_CORES)))
    y0 = np.stack([np.asarray(r["y0"], dtype=np.float32) for r in res.results], axis=0)
    y1 = np.stack([np.asarray(r["y1"], dtype=np.float32) for r in res.results], axis=0)
    return (y0, y1)
```

```python
import math
import numpy as np
import concourse.bass as bass
import concourse.mybir as mybir
from concourse.bass_utils import run_bass_kernel_spmd

F32 = mybir.dt.float32
BF16 = mybir.dt.bfloat16
AF = mybir.ActivationFunctionType
ALU = mybir.AluOpType
AX = mybir.AxisListType

D = 1024
H = 8
HD = 128
EPS = 1e-6
POOL_W = (2, 4, 8, 16)
N_CORES = 8
SAME_ENGINE_SYNC = True
SAME_ENGINE_RAW_ONLY = True
CW1 = 6.28125
CW2 = 2.0 * math.pi - 6.28125
PI_SAFE = 3.1415925
HALF_PI_SAFE = 1.5707962
_S = -math.log(10000.0) / 64.0
INVF_S32 = float(np.float32(_S))
INVF_DS = _S - INVF_S32


class Op:
    __slots__ = ("idx", "eng", "seng", "fn", "deps", "kind", "inc", "sem", "val", "cost", "lat", "phase",
                 "start", "finish", "key", "meta", "vbs", "prio", "raw")


class Buf:
    __slots__ = ("name", "w", "r", "excl", "phys", "users", "virt", "disjoint")

    def __init__(self, name, excl=False, virt=False, disjoint=True):
        self.name = name
        self.disjoint = disjoint
        self.w = None
        self.r = []
        self.excl = excl
        self.virt = virt
        self.phys = None
        self.users = []


class _Dummy:
    def then_inc(self, *a, **k):
        return self


class _Probe:
    def __init__(self):
        self.calls = []

    def __getattr__(self, name):
        def f(*a, **k):
            self.calls.append((name, a, k))
            return _Dummy()
        return f


def _free(ap):
    try:
        return int(np.prod(ap.shape[1:]))
    except Exception:
        return 1


def _est_cost(eng, calls):
    t = 0.0
    for name, a, k in calls:
        if name == "matmul":
            rhs = a[2] if len(a) > 2 else k["rhs"]
            t += 0.023 + 0.00044 * _free(rhs)
        elif name == "transpose":
            t += 0.08
        else:
            aps = [k.get(n) for n in ("out", "in_", "in0")] + list(a[:1])
            F = max([_free(x) for x in aps if x is not None and hasattr(x, "shape")] + [1])
            if eng == "act":
                t += 0.36 + 0.00062 * F
            elif eng == "dve":
                t += 0.2 + 0.00105 * F
            else:
                if k.get("op") == ALU.pow:
                    t += 0.45 + 0.15 * F
                else:
                    t += 0.45 + 0.0018 * F
    return t


class Sched:
    ENGS = ("pe", "act", "dve", "pool", "sp")
    WINDOW = 96
    XLAT = 0.3

    def __init__(self, nc):
        self.nc = nc
        self.all = []
        self.esem = {e: nc.alloc_semaphore("s_" + e) for e in self.ENGS}
        self.dsem = {}
        self.phase = 0
        self.disabled = False
        self.cur_prio = 0
        self.warm_fn = None

    def _deps(self, r, w, excl_eng):
        deps = []
        self._raw = set()
        self._strong = set()
        for b in r:
            if b.w is not None:
                deps.append(b.w)
                self._raw.add(id(b.w))
            if b.excl:
                deps.extend(o for o in b.r if o.eng != excl_eng)
        for b in w:
            if b.w is not None:
                deps.append(b.w)
                if not b.disjoint:
                    self._strong.add(id(b.w))
            deps.extend(b.r)
            self._strong.update(id(o) for o in b.r)
        seen = set()
        out = []
        for d in deps:
            if id(d) not in seen:
                seen.add(id(d))
                out.append(d)
        return out

    def _new(self, eng, seng, fn, deps, kind, inc, cost, lat, key=None):
        o = Op()
        o.idx = len(self.all)
        o.eng, o.seng, o.fn, o.deps, o.kind, o.inc = eng, seng, fn, deps, kind, inc
        o.cost, o.lat, o.phase, o.key = cost, lat, self.phase, key
        o.sem = o.val = o.start = o.finish = None
        o.meta = ([], [])
        o.vbs = []
        o.raw = set()
        o.prio = 0 if seng == "pe" else self.cur_prio
        self.all.append(o)
        return o

    def op(self, eng, fn, r=(), w=(), sig=True):
        if self.disabled:
            return
        deps = self._deps(r, w, eng)
        pr = _Probe()
        fn(pr)
        cost = _est_cost(eng, pr.calls)
        o = self._new(eng, eng, fn, deps, "op", 1, cost, cost)
        o.raw = self._raw | self._strong
        o.meta = ([b.name for b in r], [b.name for b in w])
        for b in list(r) + list(w):
            if b.virt and o not in b.users:
                b.users.append(o)
                o.vbs.append(b)
        for b in r:
            b.r.append(o)
        for b in w:
            b.w = o
            b.r = []

    def dma(self, q, out_ap, in_ap, r=(), w=(), key=None, slow=False):
        if self.disabled:
            return None
        if key is None:
            key = w[0]
        if key not in self.dsem:
            self.dsem[key] = [self.nc.alloc_semaphore("d_" + key.name), None]
        ent = self.dsem[key]
        deps = self._deps(r, w, "dma")
        if ent[1] is not None and ent[1] not in deps:
            deps.append(ent[1])
        nc = self.nc

        def fn(e, out_ap=out_ap, in_ap=in_ap, slow=slow):
            if slow:
                with nc.allow_non_contiguous_dma(reason="one-time small strided load"):
                    return e.dma_start(out=out_ap, in_=in_ap)
            return e.dma_start(out=out_ap, in_=in_ap)

        nbytes = int(np.prod(out_ap.shape)) * 4
        issue = 0.4 if q == "sp" else 1.5
        o = self._new("dma", q, fn, deps, "dma", 16, issue, issue + 2.0 + nbytes / 150e3, key=key)
        ent[1] = o
        for b in r:
            b.r.append(o)
        for b in w:
            b.w = o
            b.r = []
        return o

    def barrier(self, bufs):
        if self.disabled:
            return
        evs = []
        for b in bufs:
            if b.w is not None:
                evs.append(b.w)
            evs.extend(b.r)
        self.phase += 1
        for e in self.ENGS:
            self._new(e, e, None, list(evs), "bar", 0, 0.0, 0.0)
        self.phase += 1

    def finish(self):
        self.phase += 1
        last = [ent[1] for ent in self.dsem.values() if ent[1] is not None]
        self._new("sp", "sp", None, last, "bar", 0, 0.0, 0.0)
        self._schedule()

    def _schedule(self):
        free = {e: 0.0 for e in self.ENGS}
        order = {e: [] for e in self.ENGS}
        tenant = [None] * NDYN_BANKS
        npend = {}
        nph = self.phase + 1
        byph = [dict((e, []) for e in self.ENGS) for _ in range(nph)]
        for o in self.all:
            byph[o.phase][o.seng].append(o)
        if BL_PRIO:
            succ = {}
            for o in self.all:
                for d in o.deps:
                    succ.setdefault(id(d), []).append(o)
            bl = {}
            for o in reversed(self.all):
                m = 0.0
                for q in succ.get(id(o), ()):
                    v = bl[id(q)] + (0.0 if q.seng == o.seng else self.XLAT)
                    if v > m:
                        m = v
                bl[id(o)] = m + o.lat
            for o in self.all:
                if o.phase in BL_PHASES:
                    o.prio = -bl[id(o)] * BL_SCALE
        for ph in range(nph):
            uns = byph[ph]
            for e in self.ENGS:
                if BL_PRIO and ph in BL_PHASES:
                    uns[e].sort(key=lambda o: (o.prio, o.idx))
                else:
                    uns[e].sort(key=lambda o: (o.idx + o.prio, o.idx))
            remaining = sum(len(v) for v in uns.values())
            wide = False
            while remaining:
                best = None
                for e in self.ENGS:
                    lst = uns[e]
                    cb = None
                    for o in (lst if wide else lst[:self.WINDOW]):
                        ready = 0.0
                        ok = True
                        for d in o.deps:
                            if d.finish is None:
                                ok = False
                                break
                            rr = d.finish + (0.0 if d.seng == e and d.kind != "dma" else self.XLAT)
                            if rr > ready:
                                ready = rr
                        if not ok:
                            continue
                        need_bank = [v for v in o.vbs if v.phys is None]
                        if need_bank:
                            cands = []
                            for p in range(NDYN_BANKS):
                                tv = tenant[p]
                                if tv is None:
                                    cands.append((0.0, p))
                                elif npend.get(id(tv), len(tv.users)) == 0:
                                    cands.append((max(u.finish for u in tv.users) + self.XLAT, p))
                            if len(cands) < len(need_bank):
                                continue
                            cands.sort()
                            bank_rdy = cands[len(need_bank) - 1][0]
                            if bank_rdy > ready:
                                ready = bank_rdy
                            o_banks = [p for _, p in cands[:len(need_bank)]]
                        else:
                            o_banks = None
                        st = ready if ready > free[e] else free[e]
                        if cb is None or st < cb[0]:
                            cb = (st, o, o_banks)
                        if st <= free[e]:
                            break
                    if cb is not None and (best is None or (cb[0], (cb[1].prio if (BL_PRIO and ph in BL_PHASES) else cb[1].idx + cb[1].prio)) < (best[0], (best[1].prio if (BL_PRIO and ph in BL_PHASES) else best[1].idx + best[1].prio))):
                        best = cb
                if best is None:
                    if not wide:
                        wide = True
                        continue
                    raise RuntimeError("scheduler stuck (PSUM bank deadlock)")
                wide = False
                st, o, o_banks = best
                if o_banks is not None:
                    need_bank = [v for v in o.vbs if v.phys is None]
                    for v, p in zip(need_bank, o_banks):
                        tv = tenant[p]
                        if tv is not None:
                            for u in tv.users:
                                if u not in o.deps:
                                    o.deps.append(u)
                        tenant[p] = v
                        v.phys = p
                for v in o.vbs:
                    npend[id(v)] = npend.get(id(v), len(v.users)) - 1
                o.start = st
                o.finish = st + o.lat
                free[o.seng] = st + o.cost
                uns[o.seng].remove(o)
                order[o.seng].append(o)
                remaining -= 1
        self.order = order
        self.model_us = max(free.values())
        cnt = {e: 0 for e in self.ENGS}
        dcnt = {}
        for e in self.ENGS:
            for o in order[e]:
                if o.kind == "op":
                    cnt[e] += 1
                    o.sem, o.val = self.esem[e], cnt[e]
        for e in self.ENGS:
            for o in order[e]:
                if o.kind == "dma":
                    k = id(o.key)
                    dcnt[k] = dcnt.get(k, 0) + 16
                    o.sem, o.val = self.dsem[o.key][0], dcnt[k]

    def emit(self, eng_name, e):
        waited = {}
        order = self.order[eng_name]
        for oi, o in enumerate(order):
            need = {}
            for d in o.deps:
                if d.kind == "bar":
                    continue
                if d.kind == "op" and d.seng == eng_name and o.kind != "dma":
                    if eng_name == "pe" or not SAME_ENGINE_SYNC:
                        continue
                    if SAME_ENGINE_RAW_ONLY and id(d) not in o.raw:
                        continue
                assert d.val is not None
                k = id(d.sem)
                if k not in need or need[k][1] < d.val:
                    need[k] = (d.sem, d.val)
            for k, (sem, v) in need.items():
                if waited.get(k, 0) >= v:
                    continue
                e.wait_ge(sem, v)
                waited[k] = v
            if o.fn is None:
                continue
            inst = o.fn(e)
            inst.then_inc(o.sem, o.inc)
            if eng_name == "pe" and self.warm_fn is not None and o.phase >= 2 and oi + 1 < len(order):
                gap = order[oi + 1].start - (o.start + o.cost)
                if gap > WARM_GAP:
                    for _ in range(min(WARM_MAX, int(WARM_FRAC * gap / 0.22))):
                        self.warm_fn(e)


STAGE = 99
CHECK_SBUF = True
NPAIRS = 4
HT_ACT_A = False
FRONT_PRIO = 0
KB_ENG = "dve"
FMUL_ENG = "dve"
KF_ENG = "dve"
QTB_ENG = "dve"
FINAL_PRIO = 100
RET_PRIO = 0
FUPD_PRIO = 0
NORM_PRIO = 0
POOL_PRIO = 0
BL_PRIO = False
BL_SCALE = 1.0
BL_PHASES = (4,)
NDYN_BANKS = 7
WARM = True
SPILL_HT = True
SPILL_KT = False
HALO_MERGE = True
SPILL_U = False
FUSE_POOL = True
STATE_STT = True
VSCALE_A = False
WARM_GAP = 0.5
WARM_FRAC = 1.0
WARM_MAX = 16
ROPE_ENG_A = "dve"
FIN_ENG = "dve"
NB = {"xres": 1}
SUB = 9
SUB2 = 9


def build_program(S_list):
    nseq = len(S_list)
    assert nseq == 2
    nchs = [s // 128 for s in S_list]
    Smax = max(S_list)
    nchmax = Smax // 128
    nc = bass.Bass("TRN2", target_bir_lowering=False)

    xs = [nc.dram_tensor(f"x{i}", [S_list[i], D], F32, kind="ExternalInput") for i in range(nseq)]
    ys = [nc.dram_tensor(f"y{i}", [S_list[i], D], F32, kind="ExternalOutput") for i in range(nseq)]
    cvec = nc.dram_tensor("cvec", [nseq, D], F32, kind="ExternalInput")
    ada_w = nc.dram_tensor("ada_w", [D, 3 * D], F32, kind="ExternalInput")
    ada_b = nc.dram_tensor("ada_b", [1, 3 * D], F32, kind="ExternalInput")
    norm_pre = nc.dram_tensor("norm_pre", [1, D], F32, kind="ExternalInput")
    norm_post = nc.dram_tensor("norm_post", [1, D], F32, kind="ExternalInput")
    w_in = nc.dram_tensor("w_in", [D, 6 * D], F32, kind="ExternalInput")
    pool_w = nc.dram_tensor("pool_w", [4, 256, 256], F32, kind="ExternalInput")
    pool_scale = nc.dram_tensor("pool_scale", [1, D], F32, kind="ExternalInput")
    dec_f = nc.dram_tensor("dec_f", [1, H], F32, kind="ExternalInput")
    dec_b = nc.dram_tensor("dec_b", [1, H], F32, kind="ExternalInput")
    w_out = nc.dram_tensor("w_out", [2 * D, D], F32, kind="ExternalInput")
    krscr = [nc.dram_tensor(f"krscr{i}", [S_list[i], D], BF16, kind="Internal") for i in range(nseq)]
    vscr = [nc.dram_tensor(f"vscr{i}", [S_list[i], D], BF16, kind="Internal") for i in range(nseq)]
    bscr = [nc.dram_tensor(f"bscr{i}", [S_list[i], D], BF16, kind="Internal") for i in range(nseq)]
    ktscr = [nc.dram_tensor(f"ktscr{i}", [S_list[i], D], BF16, kind="Internal") for i in range(nseq)]
    uscr = [nc.dram_tensor(f"uscr{i}", [S_list[i], D], BF16, kind="Internal") for i in range(nseq)]
    hscr = [nc.dram_tensor(f"hscr{i}", [S_list[i], D], BF16, kind="Internal") for i in range(nseq)]
    ropescr = nc.dram_tensor("ropescr", [Smax, 128], F32, kind="Internal")
    ggscr = nc.dram_tensor("ggscr", [nseq * 128, D], F32, kind="Internal")

    sch = Sched(nc)

    sb_lo = (nc.sbuf_base + 63) // 64 * 64
    sb_hi = nc.sbuf_top
    cur = [sb_lo]
    names = [0]
    sbuf_peak = [0]

    def alloc(shape, dt, at=None, name=None):
        nbytes = int(np.prod(shape[1:])) * (2 if dt == BF16 else 4)
        nbytes = (nbytes + 63) // 64 * 64
        if at is None:
            off = cur[0]
            cur[0] += nbytes
        else:
            off = at[0]
            at[0] += nbytes
        names[0] += 1
        if CHECK_SBUF:
            assert off + nbytes <= sb_hi, f"SBUF overflow at {name}: {off + nbytes} > {sb_hi}"
        sbuf_peak[0] = max(sbuf_peak[0], off + nbytes)
        if not CHECK_SBUF and off + nbytes > sb_hi:
            off = sb_lo
        return nc.alloc_sbuf_tensor_at(f"t{names[0]}_{name or ''}", list(shape), dt, offset=off)

    wq_sb = alloc([128, 8, 4096], BF16, name="wq")
    wout_sb = alloc([128, 16, 1024], BF16, name="wout")
    poolw_sb = alloc([128, 8, 256], BF16, name="poolw")
    ident = alloc([128, 128], BF16, name="ident")
    dtot = alloc([128, 8, 128], F32, name="dtot")
    af_tab = alloc([128, 8, 128], BF16, name="aftab")
    ab_tab = alloc([128, 8, 128], BF16, name="abtab")
    bands = alloc([128, 24, 128], BF16, name="bands")
    kdf = alloc([128, 8], F32, name="kdf")
    kdb = alloc([128, 8], F32, name="kdb")
    gcf = alloc([128, 8], F32, name="gcf")
    gcb = alloc([128, 8], F32, name="gcb")
    gprime = alloc([128, 8, 2], F32, name="gprime")
    shiftT = alloc([128, 8, 2], F32, name="shiftT")
    negpi = alloc([128, 1], F32, name="negpi")
    mhalf = alloc([128, 8], F32, name="mhalf")
    halfpi = alloc([128, 1], F32, name="halfpi")
    B_ = {n: Buf(n) for n in ["wq", "wout", "poolw", "ident", "dtot", "aftab", "abtab", "bands", "kd",
                              "gprime", "negpi", "wkv", "ropescr", "ggscr"]}
    arena0 = cur[0]
    wkv_at = [arena0]
    wkv_sb = alloc([128, 8, 2048], BF16, at=wkv_at, name="wkv")
    arena_after_wkv = wkv_at[0]

    psum = nc.alloc_psum_tensor("psum", [128, 4096], F32)
    vbcount = [0]
    pbank = []

    class PT:
        def __init__(self, vbs):
            self.vbs = vbs

        def _phys(self, i):
            p = self.vbs[i].phys
            return 0 if p is None else p

        def bank(self, i):
            b = self._phys(i)
            return psum[:, b * 512:(b + 1) * 512]

        def __getitem__(self, key):
            rows, cols = key
            a0, a1 = cols.start, cols.stop
            bi = a0 // 512
            assert (a1 - 1) // 512 == bi, (a0, a1)
            b = self._phys(bi)
            return psum[:, b * 512 + (a0 - bi * 512): b * 512 + (a1 - bi * 512)]

    def palloc(nb=2, static=None):
        vbs = []
        for i in range(nb):
            vbcount[0] += 1
            v = Buf(f"pb{vbcount[0]}", excl=True, virt=(static is None))
            if static is not None:
                v.phys = static[i]
            vbs.append(v)
            pbank.append(v)
        return PT(vbs), vbs

    sa = [arena_after_wkv]
    diff = alloc([128, 128], F32, at=sa, name="diff")
    irow = alloc([128, 128], F32, at=sa, name="irow")
    pcol = alloc([128, 1], F32, at=sa, name="pcol")
    p127 = alloc([128, 1], F32, at=sa, name="p127")
    mge = alloc([128, 128], F32, at=sa, name="mge")
    mlt = alloc([128, 128], F32, at=sa, name="mlt")
    rpos = alloc([128, 128], F32, at=sa, name="rpos")
    rneg = alloc([128, 128], F32, at=sa, name="rneg")
    identf = alloc([128, 128], F32, at=sa, name="identf")
    tmpa = alloc([128, 128], F32, at=sa, name="tmpa")
    tmpb = alloc([128, 128], F32, at=sa, name="tmpb")
    tmpc = alloc([128, 128], F32, at=sa, name="tmpc")
    rowp1 = alloc([128, 128], F32, at=sa, name="rowp1")
    row128m = alloc([128, 128], F32, at=sa, name="row128m")
    decf_t = alloc([128, 8], F32, at=sa, name="decf")
    decb_t = alloc([128, 8], F32, at=sa, name="decb")
    lgf = alloc([128, 8], F32, at=sa, name="lgf")
    lgb = alloc([128, 8], F32, at=sa, name="lgb")
    etmp = alloc([128, 8], F32, at=sa, name="etmp")
    Bc = Buf("const")
    grp = {"b": Bc}
    groups = [Bc]

    def G():
        return grp["b"]

    def newgroup(name):
        grp["b"] = Buf(name)
        groups.append(grp["b"])

    g = nc.gpsimd

    w_in_v = w_in.ap().rearrange("(dc p) n -> p dc n", p=128)
    sch.dma("pool", wkv_sb[:, :, :], w_in_v[:, :, 2048:4096], w=[B_["wkv"]])

    sch.op("pool", lambda e: e.iota(diff[:, :], [[1, 128]], base=0, channel_multiplier=-1, allow_small_or_imprecise_dtypes=True), w=[G()])
    sch.op("pool", lambda e: e.iota(irow[:, :], [[1, 128]], base=0, channel_multiplier=0, allow_small_or_imprecise_dtypes=True), w=[G()])
    sch.op("pool", lambda e: e.iota(pcol[:, :], [[1, 1]], base=0, channel_multiplier=1, allow_small_or_imprecise_dtypes=True), w=[G()])
    sch.op("pool", lambda e: e.iota(p127[:, :], [[1, 1]], base=127, channel_multiplier=-1, allow_small_or_imprecise_dtypes=True), w=[G()])
    sch.op("dve", lambda e: e.memset(negpi[:, :], -math.pi), w=[B_["negpi"]])
    sch.op("dve", lambda e: e.memset(mhalf[:, :], -0.5), w=[B_["negpi"]])
    sch.op("dve", lambda e: e.memset(halfpi[:, :], HALF_PI_SAFE), w=[B_["negpi"]])

    def dv(fn, r=None, w=None):
        sch.op("dve", fn, r=[Bc, G()] if r is None else list(r), w=[G()] if w is None else list(w))

    def ac(fn, r=None, w=None):
        sch.op("act", fn, r=[Bc, G()] if r is None else list(r), w=[G()] if w is None else list(w))

    dv(lambda e: e.tensor_single_scalar(out=identf[:, :], in_=diff[:, :], scalar=0.0, op=ALU.is_equal))
    dv(lambda e: e.tensor_copy(out=ident[:, :], in_=identf[:, :]), w=(G(), B_["ident"]))
    dv(lambda e: e.tensor_single_scalar(out=mge[:, :], in_=diff[:, :], scalar=0.0, op=ALU.is_ge))
    dv(lambda e: e.tensor_single_scalar(out=mlt[:, :], in_=diff[:, :], scalar=0.0, op=ALU.is_lt))
    dv(lambda e: e.tensor_scalar_max(out=rpos[:, :], in0=diff[:, :], scalar1=0.0))
    dv(lambda e: e.tensor_scalar(out=rneg[:, :], in0=diff[:, :], scalar1=-1.0, scalar2=0.0,
                                 op0=ALU.mult, op1=ALU.max))
    dv(lambda e: e.tensor_scalar_add(out=rowp1[:, :], in0=irow[:, :], scalar1=1.0))
    dv(lambda e: e.tensor_scalar(out=row128m[:, :], in0=irow[:, :], scalar1=-1.0, scalar2=128.0,
                                 op0=ALU.mult, op1=ALU.add))

    newgroup("grpD")
    sch.dma("sp", decf_t[:, :], dec_f.ap().partition_broadcast(128).rearrange("p o n -> p (o n)"), w=[G()])
    sch.dma("sp", decb_t[:, :], dec_b.ap().partition_broadcast(128).rearrange("p o n -> p (o n)"), w=[G()])
    for dsrc, lg in ((decf_t, lgf), (decb_t, lgb)):
        ac(lambda e, dsrc=dsrc: e.activation(out=etmp[:, :], in_=dsrc[:, :], func=AF.Exp, scale=-math.log(2.0)))
        dv(lambda e: e.tensor_scalar(out=etmp[:, :], in0=etmp[:, :], scalar1=-1.0, scalar2=1.0,
                                     op0=ALU.mult, op1=ALU.add))
        ac(lambda e, lg=lg: e.activation(out=lg[:, :], in_=etmp[:, :], func=AF.Ln))
    for h in range(H):
        ac(lambda e, h=h: e.activation(out=tmpa[:, :], in_=rpos[:, :], func=AF.Exp, scale=lgf[:, h:h + 1]))
        ac(lambda e, h=h: e.activation(out=tmpb[:, :], in_=rneg[:, :], func=AF.Exp, scale=lgb[:, h:h + 1]))
        dv(lambda e: e.tensor_tensor(out=tmpa[:, :], in0=tmpa[:, :], in1=mge[:, :], op=ALU.mult))
        dv(lambda e: e.tensor_tensor(out=tmpb[:, :], in0=tmpb[:, :], in1=mlt[:, :], op=ALU.mult))
        dv(lambda e, h=h: e.tensor_tensor(out=dtot[:, h, :], in0=tmpa[:, :], in1=tmpb[:, :], op=ALU.add),
           w=(G(), B_["dtot"]))
        ac(lambda e, h=h: e.activation(out=af_tab[:, h, :], in_=rowp1[:, :], func=AF.Exp, scale=lgf[:, h:h + 1]),
           w=(G(), B_["aftab"]))
        ac(lambda e, h=h: e.activation(out=ab_tab[:, h, :], in_=row128m[:, :], func=AF.Exp, scale=lgb[:, h:h + 1]),
           w=(G(), B_["abtab"]))
    ac(lambda e: e.activation(out=kdf[:, :], in_=lgf[:, :], func=AF.Exp, scale=p127[:, 0:1]), w=(G(), B_["kd"]))
    ac(lambda e: e.activation(out=kdb[:, :], in_=lgb[:, :], func=AF.Exp, scale=pcol[:, 0:1]), w=(G(), B_["kd"]))
    ac(lambda e: e.activation(out=gcf[:, :], in_=lgf[:, :], func=AF.Exp, scale=128.0), w=(G(), B_["kd"]))
    ac(lambda e: e.activation(out=gcb[:, :], in_=lgb[:, :], func=AF.Exp, scale=128.0), w=(G(), B_["kd"]))

    newgroup("grpP")
    tmpa2 = alloc([128, 128], F32, at=sa, name="tmpa2")
    tmpb2 = alloc([128, 128], F32, at=sa, name="tmpb2")
    tmpc2 = alloc([128, 128], F32, at=sa, name="tmpc2")
    for gi, w_ in enumerate(POOL_W):
        hw = w_ // 2
        dv(lambda e, hw=hw: e.tensor_single_scalar(out=tmpa2[:, :], in_=diff[:, :], scalar=float(-(hw - 1)), op=ALU.is_ge))
        dv(lambda e, hw=hw: e.tensor_single_scalar(out=tmpb2[:, :], in_=diff[:, :], scalar=float(hw), op=ALU.is_le))
        dv(lambda e: e.tensor_tensor(out=tmpa2[:, :], in0=tmpa2[:, :], in1=tmpb2[:, :], op=ALU.mult))
        dv(lambda e, gi=gi, w_=w_: e.scalar_tensor_tensor(out=bands[:, gi * 6 + 1, :], in0=tmpa2[:, :], scalar=1.0 / w_,
                                                          in1=identf[:, :], op0=ALU.mult, op1=ALU.subtract),
           w=(G(), B_["bands"]))
        dv(lambda e, gi=gi, w_=w_, hw=hw: e.tensor_scalar(out=bands[:, gi * 6 + 0, :], in0=diff[:, :],
                                                          scalar1=float(hw - 128), scalar2=1.0 / w_,
                                                          op0=ALU.is_le, op1=ALU.mult), w=(G(), B_["bands"]))
        dv(lambda e, gi=gi, w_=w_, hw=hw: e.tensor_scalar(out=bands[:, gi * 6 + 2, :], in0=diff[:, :],
                                                          scalar1=float(129 - hw), scalar2=1.0 / w_,
                                                          op0=ALU.is_ge, op1=ALU.mult), w=(G(), B_["bands"]))
        dv(lambda e, w_=w_, hw=hw: e.tensor_scalar(out=tmpb2[:, :], in0=irow[:, :], scalar1=float(hw), scalar2=float(w_),
                                                   op0=ALU.add, op1=ALU.min))
        dv(lambda e: e.reciprocal(out=tmpb2[:, :], in_=tmpb2[:, :]))
        dv(lambda e: e.tensor_tensor(out=tmpc2[:, :], in0=tmpa2[:, :], in1=tmpb2[:, :], op=ALU.mult))
        dv(lambda e, gi=gi: e.tensor_tensor(out=bands[:, gi * 6 + 3, :], in0=tmpc2[:, :], in1=identf[:, :], op=ALU.subtract),
           w=(G(), B_["bands"]))
        dv(lambda e, w_=w_, hw=hw: e.tensor_scalar(out=tmpb2[:, :], in0=irow[:, :], scalar1=-1.0, scalar2=float(128 + hw),
                                                   op0=ALU.mult, op1=ALU.add))
        dv(lambda e, w_=w_: e.tensor_scalar_min(out=tmpb2[:, :], in0=tmpb2[:, :], scalar1=float(w_)))
        dv(lambda e: e.reciprocal(out=tmpb2[:, :], in_=tmpb2[:, :]))
        dv(lambda e: e.tensor_tensor(out=tmpc2[:, :], in0=tmpa2[:, :], in1=tmpb2[:, :], op=ALU.mult))
        dv(lambda e, gi=gi: e.tensor_tensor(out=bands[:, gi * 6 + 4, :], in0=tmpc2[:, :], in1=identf[:, :], op=ALU.subtract),
           w=(G(), B_["bands"]))
        dv(lambda e, gi=gi: e.tensor_tensor(out=bands[:, gi * 6 + 5, :], in0=bands[:, gi * 6 + 0, :], in1=bands[:, gi * 6 + 2, :], op=ALU.add),
           r=(Bc, G(), B_["bands"]), w=(G(), B_["bands"]))

    newgroup("grpR")
    invf = alloc([128, 64], F32, at=sa, name="invf")
    invfc = alloc([128, 64], F32, at=sa, name="invfc")
    ac(lambda e: e.activation(out=invf[:, :], in_=irow[:, 0:64], func=AF.Exp, scale=INVF_S32))
    dv(lambda e: e.tensor_scalar(out=invfc[:, :], in0=irow[:, 0:64], scalar1=INVF_DS, scalar2=1.0, op0=ALU.mult, op1=ALU.add))
    dv(lambda e: e.tensor_tensor(out=invf[:, :], in0=invf[:, :], in1=invfc[:, :], op=ALU.mult))
    RC = 8
    posall = alloc([128, RC], F32, at=sa, name="posall")
    ang = alloc([128, RC, 64], F32, at=sa, name="ang")
    marg = alloc([128, RC, 64], F32, at=sa, name="marg")
    rtab = alloc([128, RC, 128], F32, at=sa, name="rtab")
    rred = alloc([128, RC, 64], F32, at=sa, name="rred")
    qint = alloc([128, RC, 64], mybir.dt.int32, at=sa, name="qint")
    Brt = Buf("rtab")
    rope_v = ropescr.ap().rearrange("(c p) n -> p c n", p=128)
    for c0 in range(0, nchmax, RC):
        ncb = min(RC, nchmax - c0)
        sch.op("pool", lambda e, c0=c0: e.iota(posall[:, :], [[128, RC]], base=128 * c0, channel_multiplier=1, allow_small_or_imprecise_dtypes=True),
               r=[G()], w=[G()])
        dv(lambda e: e.tensor_tensor(out=ang[:, :, :], in0=posall[:, :].unsqueeze(2).to_broadcast([128, RC, 64]),
                                     in1=invf[:, :].unsqueeze(1).to_broadcast([128, RC, 64]), op=ALU.mult))
        dv(lambda e: e.tensor_scalar_mul(out=marg[:, :, :], in0=ang[:, :, :], scalar1=1.0 / (2.0 * math.pi)))
        dv(lambda e: e.tensor_copy(out=qint[:, :, :], in_=marg[:, :, :]))
        dv(lambda e: e.tensor_copy(out=marg[:, :, :], in_=qint[:, :, :]))
        dv(lambda e: e.scalar_tensor_tensor(out=rred[:, :, :], in0=marg[:, :, :], scalar=-CW1, in1=ang[:, :, :],
                                            op0=ALU.mult, op1=ALU.add))
        dv(lambda e: e.scalar_tensor_tensor(out=rred[:, :, :], in0=marg[:, :, :], scalar=-CW2, in1=rred[:, :, :],
                                            op0=ALU.mult, op1=ALU.add))
        dv(lambda e: e.tensor_single_scalar(out=marg[:, :, :], in_=rred[:, :, :], scalar=math.pi, op=ALU.is_gt))
        dv(lambda e: e.scalar_tensor_tensor(out=rred[:, :, :], in0=marg[:, :, :], scalar=-2.0 * math.pi, in1=rred[:, :, :],
                                            op0=ALU.mult, op1=ALU.add))
        dv(lambda e: e.tensor_single_scalar(out=marg[:, :, :], in_=rred[:, :, :], scalar=-math.pi, op=ALU.is_lt))
        dv(lambda e: e.scalar_tensor_tensor(out=rred[:, :, :], in0=marg[:, :, :], scalar=2.0 * math.pi, in1=rred[:, :, :],
                                            op0=ALU.mult, op1=ALU.add))
        dv(lambda e: e.tensor_scalar(out=rred[:, :, :], in0=rred[:, :, :], scalar1=-PI_SAFE, scalar2=PI_SAFE,
                                     op0=ALU.max, op1=ALU.min))
        ac(lambda e: e.activation(out=rtab[:, :, 64:128], in_=rred[:, :, :], func=AF.Sin), w=(G(), Brt))
        dv(lambda e: e.scalar_tensor_tensor(out=marg[:, :, :], in0=rred[:, :, :], scalar=-1.0, in1=rred[:, :, :],
                                            op0=ALU.mult, op1=ALU.max))
        ac(lambda e: e.activation(out=rtab[:, :, 0:64], in_=marg[:, :, :], func=AF.Sin, scale=-1.0, bias=halfpi[:, 0:1]),
           r=(Bc, G(), B_["negpi"]), w=(G(), Brt))
        sch.dma("sp", rope_v[:, c0:c0 + ncb, :], rtab[:, 0:ncb, :], r=[Brt], w=[B_["ropescr"]], key=Brt)

    newgroup("grpA")
    adaw_sb = alloc([128, 8, 1024], BF16, at=sa, name="adaw")
    Badaw = Buf("adaw")
    ada_w_v = ada_w.ap().rearrange("(dc p) n -> p dc n", p=128)

    cT = alloc([128, 8, 2], F32, at=sa, name="cT")
    scT = alloc([128, 8, 2], BF16, at=sa, name="scT")
    scb = alloc([128, 2, 8, 128], BF16, at=sa, name="scb")
    adabT = alloc([128, 24], F32, at=sa, name="adabT")
    gpreT = alloc([128, 8], F32, at=sa, name="gpreT")
    modT = alloc([128, 16, 2], F32, at=sa, name="modT")
    rowb = alloc([128, 1024], F32, at=sa, name="rowb")
    gpostb = alloc([128, 1024], F32, at=sa, name="gpostb")
    ggt = alloc([128, 1024], F32, at=sa, name="ggt")
    pstage = alloc([128, 8, 256], F32, at=sa, name="pstage")
    setup_end = sa[0]

    for s in range(nseq):
        sch.dma("sp", cT[:, :, s], cvec.ap()[s].rearrange("(dc p) -> p dc", p=128), w=[G()], slow=True)
    sch.dma("sp", adabT[:, :], ada_b.ap().rearrange("o (fc p) -> p (o fc)", p=128), w=[G()], slow=True)
    sch.dma("sp", gpreT[:, :], norm_pre.ap().rearrange("o (fc p) -> p (o fc)", p=128), w=[G()], slow=True)
    sch.dma("sp", gpostb[:, :], norm_post.ap().partition_broadcast(128).rearrange("p o n -> p (o n)"), w=[G()])
    sch.dma("sp", rowb[:, :], pool_scale.ap().partition_broadcast(128).rearrange("p o n -> p (o n)"), w=[G()])
    sch.dma("sp", pstage[:, :, :], pool_w.ap().rearrange("g (cc p) d -> p (g cc) d", p=128), w=[G()])
    dv(lambda e: e.tensor_tensor(out=poolw_sb[:, :, :].rearrange("p (g c) d -> p g c d", g=4),
                                 in0=pstage[:, :, :].rearrange("p (g c) d -> p g c d", g=4),
                                 in1=rowb[:, :].rearrange("p (g d) -> p g d", g=4).unsqueeze(2).to_broadcast([128, 4, 2, 256]),
                                 op=ALU.mult), w=(G(), B_["poolw"]))
    sch.dma("sp", rowb[:, :], ada_b.ap()[:, 2048:3072].partition_broadcast(128).rearrange("p o n -> p (o n)"), r=[G()], w=[G()])
    ac(lambda e: e.activation(out=scT[:, :, :], in_=cT[:, :, :], func=AF.Silu))
    for s in range(nseq):
        for dc in range(8):
            dv(lambda e, s=s, dc=dc: e.tensor_copy(out=scb[:, s, dc, :], in_=scT[:, dc, s:s + 1].to_broadcast([128, 128])))
    pm, pmb = palloc(2, static=[0, 1])
    pmv = pm[:, 0:32].rearrange("p (fc s) -> p fc s", s=2)
    for piece in range(2):
        sch.dma("pool", adaw_sb[:, :, :], ada_w_v[:, :, piece * 1024:(piece + 1) * 1024], w=[Badaw])

        def mod_mm(e, piece=piece):
            last = None
            for fc in range(8):
                for dc in range(8):
                    last = e.matmul(pmv[:, piece * 8 + fc, :], adaw_sb[:, dc, fc * 128:(fc + 1) * 128], scT[:, dc, :],
                                    start=(dc == 0), stop=(dc == 7))
            return last
        sch.op("pe", mod_mm, r=[G(), Badaw], w=pmb)
    dv(lambda e: e.tensor_tensor(out=modT[:, :, :], in0=pmv, in1=adabT[:, 0:16].unsqueeze(2).to_broadcast([128, 16, 2]),
                                 op=ALU.add), r=[G()] + pmb, w=[G()])
    dv(lambda e: e.tensor_copy(out=shiftT[:, :, :], in_=modT[:, 0:8, :]), w=(G(), B_["gprime"]))
    dv(lambda e: e.scalar_tensor_tensor(out=gprime[:, :, :], in0=modT[:, 8:16, :], scalar=1.0,
                                        in1=gpreT[:, :].unsqueeze(2).to_broadcast([128, 8, 2]),
                                        op0=ALU.add, op1=ALU.mult), w=(G(), B_["gprime"]))
    dv(lambda e: e.tensor_scalar_mul(out=gprime[:, :, :], in0=gprime[:, :, :], scalar1=float(D) ** 0.5), w=(G(), B_["gprime"]))
    sch.dma("pool", adaw_sb[:, :, :], ada_w_v[:, :, 2048:3072], w=[Badaw])
    gg_v = ggscr.ap()
    for s in range(nseq):
        pg, pgb = palloc(2, static=[2 + 2 * s, 3 + 2 * s])

        def gate_mm(e, s=s, pg=pg):
            last = None
            for half in range(2):
                for dc in range(8):
                    last = e.matmul(pg[:, half * 512:(half + 1) * 512], scb[:, s, dc, :],
                                    adaw_sb[:, dc, half * 512:(half + 1) * 512],
                                    start=(dc == 0), stop=(dc == 7))
            return last
        sch.op("pe", gate_mm, r=[G(), Badaw], w=pgb)
        for half in range(2):
            dv(lambda e, pg=pg, half=half: e.tensor_tensor(out=ggt[:, half * 512:(half + 1) * 512], in0=pg[:, half * 512:(half + 1) * 512],
                                                         in1=rowb[:, half * 512:(half + 1) * 512], op=ALU.add), r=[G()] + pgb, w=[G()])
        dv(lambda e: e.tensor_tensor(out=ggt[:, :], in0=ggt[:, :], in1=gpostb[:, :], op=ALU.mult))
        sch.dma("sp", gg_v[s * 128:(s + 1) * 128, :], ggt[:, :], r=[G()], w=[B_["ggscr"]], key=G())
    sch.dma("pool", wq_sb[:, :, 0:2048], w_in_v[:, :, 0:2048], w=[B_["wq"]])
    sch.dma("pool", wq_sb[:, :, 2048:4096], w_in_v[:, :, 4096:6144], w=[B_["wq"]])
    sch.dma("pool", wout_sb[:, :, :], w_out.ap().rearrange("(ec p) n -> p ec n", p=128), w=[B_["wout"]])

    sch.disabled = STAGE < 2
    sch.barrier(groups + [Brt, Badaw] + pbank)

    act_at = [arena_after_wkv]

    def mk(shape, dt, name, n=1):
        ts = [alloc(shape, dt, at=act_at, name=f"{name}{i}") for i in range(n)]
        bs = [Buf(f"{name}{i}") for i in range(n)]
        return ts, bs

    xin, xin_b = mk([128, 1024], F32, "xin", 2)
    junk = alloc([128, 1024], BF16, at=act_at, name="junk")
    junk_b = Buf("junk", disjoint=False)
    xn, xn_b = mk([128, 1024], BF16, "xn", 1)
    hT, hT_b = mk([128, 8, 128], BF16, "hT", 3)
    ssq, ssq_b = mk([128, 4], F32, "ssq", 2)
    rt, rt_b = mk([128, 128], F32, "rt", 2)
    t1, t1_b = mk([128, 512], F32, "t1", 1)
    t2, t2_b = mk([128, 512], F32, "t2", 1)
    common_end = act_at[0]

    _aux = {}

    def aux(b, tag):
        k = (id(b), tag)
        if k not in _aux:
            _aux[k] = Buf(b.name + tag)
        return _aux[k]

    def hTr(hs):
        return [aux(hT_b[hs], "a"), aux(hT_b[hs], "b")]

    def front(seq, c, hslot, xslot, ht_act=True):
        sch.cur_prio = FRONT_PRIO
        xt, xb = xin[xslot], xin_b[xslot]
        sq, sqb = ssq[xslot], ssq_b[xslot]
        sch.dma("sp", xt[:, :], xs[seq].ap()[c * 128:(c + 1) * 128, :], w=[xb])
        sch.op("act", lambda e: e.activation(out=junk[:, :], in_=xt[:, :], func=AF.Square, accum_out=sq[:, 0:1]),
               r=[xb], w=[sqb, junk_b])
        sch.op("pool", lambda e: e.tensor_scalar_add(out=sq[:, 1:2], in0=sq[:, 0:1], scalar1=float(D) * EPS), r=[sqb], w=[sqb])
        sch.op("pool", lambda e: e.tensor_tensor(out=sq[:, 2:3], in0=sq[:, 1:2], in1=mhalf[:, 0:1], op=ALU.pow),
               r=[sqb], w=[sqb])
        sch.op("act", lambda e: e.activation(out=xn[0][:, :], in_=xt[:, :], func=AF.Copy, scale=sq[:, 2:3]),
               r=[xb, sqb], w=[xn_b[0]])
        pt, ptb = palloc(2)
        ptv = [(lambda i=i: pt.bank(i).bitcast(BF16)[:, 0:512].rearrange("p (dc t) -> p dc t", dc=4)) for i in range(2)]
        for hb in range(2):
            def tr(e, hb=hb):
                last = None
                for d4 in range(4):
                    dc = hb * 4 + d4
                    last = e.transpose(ptv[hb]()[:, d4, :], xn[0][:, dc * 128:(dc + 1) * 128], ident[:, :])
                return last
            sch.op("pe", tr, r=[xn_b[0], B_["ident"]], w=[ptb[hb]])
        for dc in range(8):
            hb, d4 = dc // 4, dc % 4
            if hb == 0 or ht_act:
                sch.op("act", lambda e, dc=dc, d4=d4, hb=hb: e.activation(out=hT[hslot][:, dc, :], in_=ptv[hb]()[:, d4, :], func=AF.Identity,
                                                                          scale=gprime[:, dc, seq:seq + 1], bias=shiftT[:, dc, seq:seq + 1]),
                       r=[ptb[hb], B_["gprime"]], w=[aux(hT_b[hslot], "a" if hb == 0 else "b")])
            else:
                sch.op("dve", lambda e, dc=dc, d4=d4: e.tensor_scalar(out=hT[hslot][:, dc, :], in0=ptv[1]()[:, d4, :],
                                                                      scalar1=gprime[:, dc, seq:seq + 1],
                                                                      scalar2=shiftT[:, dc, seq:seq + 1],
                                                                      op0=ALU.mult, op1=ALU.add),
                       r=[ptb[1], B_["gprime"]], w=[aux(hT_b[hslot], "b")])

    def front_done():
        sch.cur_prio = 0

    def rope(psrc, psrc_b, dst, dst_b, rts, rtb, kscale, ceng="dve"):
        cosb = rts[:, 0:64].unsqueeze(1).to_broadcast([128, 8, 64])
        sinb = rts[:, 64:128].unsqueeze(1).to_broadcast([128, 4, 64])
        for half in range(2):
            src = lambda half=half: psrc[:, half * 512:(half + 1) * 512].rearrange("p (h two d) -> p h two d", h=4, two=2)
            t1v = t1[0][:, :].rearrange("p (h two d) -> p h two d", h=4, two=2)
            t2v = t2[0][:, :].rearrange("p (h two d) -> p h two d", h=4, two=2)
            dv_ = dst[:, half * 512:(half + 1) * 512].rearrange("p (h two d) -> p h two d", h=4, two=2)
            pb = [psrc_b[half]]
            src3 = lambda half=half: psrc[:, half * 512:(half + 1) * 512].rearrange("p (g d) -> p g d", g=8)
            t1v3 = t1[0][:, :].rearrange("p (g d) -> p g d", g=8)
            if kscale is None:
                sch.op("dve", lambda e, src3=src3, t1v3=t1v3: e.tensor_tensor(out=t1v3, in0=src3(), in1=cosb, op=ALU.mult),
                       r=pb + [rtb], w=[t1_b[0]])
                sch.op("dve", lambda e, src=src, t2v=t2v: e.tensor_tensor(out=t2v[:, :, 0, :], in0=src()[:, :, 1, :], in1=sinb, op=ALU.mult),
                       r=pb + [rtb], w=[t2_b[0]])
                sch.op("dve", lambda e, src=src, t2v=t2v: e.tensor_tensor(out=t2v[:, :, 1, :], in0=src()[:, :, 0, :], in1=sinb, op=ALU.mult),
                       r=pb + [rtb], w=[t2_b[0]])
            else:
                sch.op("dve", lambda e, src3=src3, t1v3=t1v3: e.scalar_tensor_tensor(out=t1v3, in0=src3(), scalar=kscale, in1=cosb,
                                                                                 op0=ALU.mult, op1=ALU.mult),
                       r=pb + [rtb], w=[t1_b[0]])
                sch.op("dve", lambda e, src=src, t2v=t2v: e.scalar_tensor_tensor(out=t2v[:, :, 0, :], in0=src()[:, :, 1, :], scalar=kscale,
                                                                                 in1=sinb, op0=ALU.mult, op1=ALU.mult),
                       r=pb + [rtb], w=[t2_b[0]])
                sch.op("dve", lambda e, src=src, t2v=t2v: e.scalar_tensor_tensor(out=t2v[:, :, 1, :], in0=src()[:, :, 0, :], scalar=kscale,
                                                                                 in1=sinb, op0=ALU.mult, op1=ALU.mult),
                       r=pb + [rtb], w=[t2_b[0]])
            sch.op(ceng, lambda e, t1v=t1v, t2v=t2v, dv_=dv_: e.tensor_tensor(out=dv_[:, :, 0, :], in0=t1v[:, :, 0, :], in1=t2v[:, :, 0, :],
                                                                                op=ALU.subtract),
                   r=[t1_b[0], t2_b[0]], w=[dst_b])
            sch.op(ceng, lambda e, t1v=t1v, t2v=t2v, dv_=dv_: e.tensor_tensor(out=dv_[:, :, 1, :], in0=t1v[:, :, 1, :], in1=t2v[:, :, 1, :],
                                                                                op=ALU.add),
                   r=[t1_b[0], t2_b[0]], w=[dst_b])

    def bc_h(tab):
        return tab[:, :].unsqueeze(2).to_broadcast([128, 8, 128])

    def v3(t):
        return t.rearrange("p (h d) -> p h d", h=8)

    pa_at = [common_end]
    krA, krA_b = [], []
    vA, vA_b = [], []
    for i in range(2):
        krA.append(alloc([128, 1024], BF16, at=pa_at, name=f"krA{i}")); krA_b.append(Buf(f"krA{i}"))
        vA.append(alloc([128, 1024], BF16, at=pa_at, name=f"vA{i}")); vA_b.append(Buf(f"vA{i}"))
    kbA = alloc([128, 1024], BF16, at=pa_at, name="kbA"); kbA_b = Buf("kbA")
    Bf32 = alloc([128, 1024], F32, at=pa_at, name="Bf32"); Bf32_b = Buf("Bf32")
    BbfA, BbfA_b = [], []
    for i in range(2):
        BbfA.append(alloc([128, 1024], BF16, at=pa_at, name=f"BbfA{i}")); BbfA_b.append(Buf(f"BbfA{i}"))
    scrB = {("kr", i): Buf(f"krscr{i}") for i in range(nseq)}
    scrB.update({("v", i): Buf(f"vscr{i}") for i in range(nseq)})
    scrB.update({("b", i): Buf(f"bscr{i}") for i in range(nseq)})
    scrB.update({("h", i): Buf(f"hscr{i}") for i in range(nseq)})
    scrB.update({("kt", i): Buf(f"ktscr{i}") for i in range(nseq)})
    scrB.update({("u", i): Buf(f"uscr{i}") for i in range(nseq)})
    uA, uA_b = [], []
    for i in range(2):
        uA.append(alloc([128, 1024], BF16, at=pa_at, name=f"uA{i}")); uA_b.append(Buf(f"uA{i}"))
    KTA, KTA_b = [], []
    for i in range(2):
        KTA.append(alloc([128, 8, 128], BF16, at=pa_at, name=f"KTA{i}")); KTA_b.append(Buf(f"KTA{i}"))

    itA = 0
    for seq in range(nseq):
        n = nchs[seq]
        for idx, c in enumerate(range(n - 1, -1, -1)):
            sl = itA % 2
            itA += 1
            hs = itA % 3
            front(seq, c, hs, sl, ht_act=HT_ACT_A)
            front_done()
            if SPILL_HT:
                sch.dma("sp", hscr[seq].ap()[c * 128:(c + 1) * 128, :], hT[hs][:, :, :].rearrange("p a b -> p (a b)"),
                        r=hTr(hs), w=[scrB[("h", seq)]], key=aux(hT_b[hs], "a"))
            if SPILL_U:
                pu, pub = palloc(2)
                for half in range(2):
                    def umm(e, half=half, pu=pu, hs=hs):
                        last = None
                        for dc in range(8):
                            last = e.matmul(pu[:, half * 512:(half + 1) * 512], hT[hs][:, dc, :], wq_sb[:, dc, half * 512:(half + 1) * 512],
                                            start=(dc == 0), stop=(dc == 7))
                        return last
                    sch.op("pe", umm, r=hTr(hs) + [B_["wq"]], w=[pub[half]])
                    sch.op("act", lambda e, half=half, pu=pu, sl=sl: e.copy(out=uA[sl][:, half * 512:(half + 1) * 512],
                                                                            in_=pu[:, half * 512:(half + 1) * 512]),
                           r=[pub[half]], w=[uA_b[sl]])
                sch.dma("sp", uscr[seq].ap()[c * 128:(c + 1) * 128, :], uA[sl][:, :], r=[uA_b[sl]], w=[scrB[("u", seq)]], key=uA_b[sl])
            sch.dma("sp", rt[sl][:, :], ropescr.ap()[c * 128:(c + 1) * 128, :], r=[B_["ropescr"]], w=[rt_b[sl]])
            pk, pkb = palloc(2)
            pv, pvb = palloc(2)
            for bi, (pp, ppb) in enumerate(((pk, pkb), (pv, pvb))):
                for half in range(2):
                    col0 = bi * 1024 + half * 512

                    def mm(e, pp=pp, half=half, col0=col0, hs=hs):
                        last = None
                        for dc in range(8):
                            last = e.matmul(pp[:, half * 512:(half + 1) * 512], hT[hs][:, dc, :], wkv_sb[:, dc, col0:col0 + 512],
                                            start=(dc == 0), stop=(dc == 7))
                        return last
                    sch.op("pe", mm, r=hTr(hs) + [B_["wkv"]], w=[ppb[half]])
            rope(pk, pkb, krA[sl], krA_b[sl], rt[sl], rt_b[sl], HD ** -0.5, ROPE_ENG_A)
            for half in range(2):
                sch.op("act", lambda e, half=half, pv=pv, sl=sl: e.copy(out=vA[sl][:, half * 512:(half + 1) * 512],
                                                                        in_=pv[:, half * 512:(half + 1) * 512]),
                       r=[pvb[half]], w=[vA_b[sl]])
            sch.dma("sp", krscr[seq].ap()[c * 128:(c + 1) * 128, :], krA[sl][:, :], r=[krA_b[sl]], w=[scrB[("kr", seq)]], key=krA_b[sl])
            sch.dma("sp", vscr[seq].ap()[c * 128:(c + 1) * 128, :], vA[sl][:, :], r=[vA_b[sl]], w=[scrB[("v", seq)]], key=vA_b[sl])
            if SPILL_KT:
                ptk, ptkb = palloc(1)
                ptkv = lambda ptk=ptk: ptk.bank(0).bitcast(BF16).rearrange("p (h t) -> p h t", h=8)

                def trk(e, sl=sl, ptkv=ptkv):
                    last = None
                    for h in range(8):
                        last = e.transpose(ptkv()[:, h, :], krA[sl][:, h * 128:(h + 1) * 128], ident[:, :])
                    return last
                sch.op("pe", trk, r=[krA_b[sl], B_["ident"]], w=ptkb)
                sch.op("act", lambda e, sl=sl, ptkv=ptkv: e.copy(out=KTA[sl][:, :, :], in_=ptkv()), r=ptkb, w=[KTA_b[sl]])
                sch.dma("sp", ktscr[seq].ap()[c * 128:(c + 1) * 128, :], KTA[sl][:, :, :].rearrange("p a b -> p (a b)"),
                        r=[KTA_b[sl]], w=[scrB[("kt", seq)]], key=KTA_b[sl])
            if idx == 0:
                sch.op("pool", lambda e, sl=sl: e.memset(BbfA[sl][:, :], 0.0), w=[BbfA_b[sl]])
            sch.dma("sp", bscr[seq].ap()[c * 128:(c + 1) * 128, :], BbfA[sl][:, :], r=[BbfA_b[sl]], w=[scrB[("b", seq)]], key=BbfA_b[sl])
            if c == 0:
                continue
            if VSCALE_A:
                for h in range(8):
                    sch.op("act", lambda e, h=h, pv=pv: e.activation(out=kbA[:, h * 128:(h + 1) * 128], in_=pv[:, h * 128:(h + 1) * 128],
                                                                     func=AF.Copy, scale=kdb[:, h:h + 1]),
                           r=[pvb[h // 4], B_["kd"]], w=[kbA_b])
            else:
                sch.op(KB_ENG, lambda e, sl=sl: e.tensor_tensor(out=v3(kbA[:, :]), in0=v3(krA[sl][:, :]), in1=bc_h(kdb), op=ALU.mult),
                       r=[krA_b[sl], B_["kd"]], w=[kbA_b])
            pkv, pkvb = palloc(2)
            for half in range(2):
                def kvmm(e, half=half, pkv=pkv, sl=sl):
                    last = None
                    for hh in range(4):
                        h = half * 4 + hh
                        l_, r_ = (krA[sl], kbA) if VSCALE_A else (kbA, vA[sl])
                        last = e.matmul(pkv[:, h * 128:(h + 1) * 128], l_[:, h * 128:(h + 1) * 128], r_[:, h * 128:(h + 1) * 128],
                                        start=True, stop=True)
                    return last
                sch.op("pe", kvmm, r=[kbA_b, vA_b[sl], krA_b[sl]], w=[pkvb[half]])
            ns = 1 - sl
            if idx == 0:
                for half in range(2):
                    sch.op("dve", lambda e, half=half, pkv=pkv: e.tensor_copy(out=Bf32[:, half * 512:(half + 1) * 512],
                                                                              in_=pkv[:, half * 512:(half + 1) * 512]),
                           r=[pkvb[half]], w=[Bf32_b])
            else:
                if STATE_STT:
                    for h in range(8):
                        sch.op("dve", lambda e, h=h, pkv=pkv: e.scalar_tensor_tensor(out=Bf32[:, h * 128:(h + 1) * 128],
                                                                                     in0=Bf32[:, h * 128:(h + 1) * 128], scalar=gcb[:, h:h + 1],
                                                                                     in1=pkv[:, h * 128:(h + 1) * 128],
                                                                                     op0=ALU.mult, op1=ALU.add),
                               r=[pkvb[h // 4], Bf32_b, B_["kd"]], w=[Bf32_b])
                else:
                    sch.op(FMUL_ENG, lambda e: e.tensor_tensor(out=v3(Bf32[:, :]), in0=v3(Bf32[:, :]), in1=bc_h(gcb), op=ALU.mult),
                           r=[Bf32_b, B_["kd"]], w=[Bf32_b])
                    for half in range(2):
                        sch.op("dve", lambda e, half=half, pkv=pkv: e.tensor_tensor(out=Bf32[:, half * 512:(half + 1) * 512],
                                                                                    in0=pkv[:, half * 512:(half + 1) * 512],
                                                                                    in1=Bf32[:, half * 512:(half + 1) * 512], op=ALU.add),
                               r=[pkvb[half], Bf32_b], w=[Bf32_b])
            sch.op("act", lambda e, ns=ns: e.copy(out=BbfA[ns][:, :], in_=Bf32[:, :]), r=[Bf32_b], w=[BbfA_b[ns]])

    sch.disabled = STAGE < 3
    passA_bufs = KTA_b + uA_b + [junk_b] + xin_b + xn_b + hT_b + [x for i in range(3) for x in hTr(i)] + ssq_b + rt_b + t1_b + t2_b + krA_b + vA_b + [kbA_b, Bf32_b] + BbfA_b + pbank + [B_["wkv"]]
    sch.barrier(passA_bufs)
    if FUSE_POOL:
        fx_at = [arena0]
        WuT = alloc([128, 8, 1024], BF16, at=fx_at, name="WuT")
        WuT_b = Buf("WuT")
        for cc in range(8):
            pT, pTb = palloc(1)
            pTv = lambda pT=pT: pT.bank(0).bitcast(BF16).rearrange("p (dc t) -> p dc t", dc=8)

            def trw(e, cc=cc, pTv=pTv):
                last = None
                for dc in range(8):
                    last = e.transpose(pTv()[:, dc, :], wq_sb[:, dc, cc * 128:(cc + 1) * 128], ident[:, :])
                return last
            sch.op("pe", trw, r=[B_["wq"], B_["ident"]], w=pTb)
            sch.op("act" if cc % 2 == 0 else "dve",
                   (lambda e, cc=cc, pTv=pTv: e.copy(out=WuT[:, cc, :].rearrange("p (dc t) -> p dc t", dc=8), in_=pTv())) if cc % 2 == 0 else
                   (lambda e, cc=cc, pTv=pTv: e.tensor_copy(out=WuT[:, cc, :].rearrange("p (dc t) -> p dc t", dc=8), in_=pTv())),
                   r=pTb, w=[WuT_b])
        for dc in range(8):
            pw, pwb = palloc(2)
            for half in range(2):
                def wmm(e, dc=dc, half=half, pw=pw):
                    last = None
                    for g2 in range(2):
                        gi = half * 2 + g2
                        for cc2 in range(2):
                            cc = gi * 2 + cc2
                            last = e.matmul(pw[:, gi * 256:(gi + 1) * 256], WuT[:, cc, dc * 128:(dc + 1) * 128], poolw_sb[:, cc, :],
                                            start=(cc2 == 0), stop=(cc2 == 1))
                    return last
                sch.op("pe", wmm, r=[WuT_b, B_["poolw"]], w=[pwb[half]])
                sch.op("act", lambda e, dc=dc, half=half, pw=pw: e.copy(out=wq_sb[:, dc, half * 512:(half + 1) * 512],
                                                                        in_=pw[:, half * 512:(half + 1) * 512]),
                       r=[pwb[half]], w=[B_["wq"]])
        sch.barrier([WuT_b] + pbank)
    pb_at = [common_end]
    wkv_region = [arena0]

    def mkb(shape, dt, name, at):
        return alloc(shape, dt, at=at, name=name), Buf(name)

    class Ring:
        def __init__(self, name, shape, dt, n, at):
            self.items = [mkb(shape, dt, f"{name}{i}", at) for i in range(n)]

        def __getitem__(self, c):
            return self.items[c % len(self.items)]

    def ring(name, shape, dt, at2=None):
        n = NB.get(name, 1)
        r = Ring.__new__(Ring)
        r.items = []
        for i in range(n):
            r.items.append(mkb(shape, dt, f"{name}{i}", (at2 if at2 is not None else wkv_region) if i == 0 else pb_at))
        return r

    uring = [mkb([128, 1024], BF16, f"u{i}", wkv_region) for i in range(3)]
    qr_r = ring("qr", [128, 1024], BF16)
    sz_r = ring("sz", [128, 2048], BF16)
    krB_r = ring("krB", [128, 1024], BF16)
    vB_r = ring("vB", [128, 1024], BF16)
    BbfB_r = ring("BbfB", [128, 1024], BF16)
    kf_r = ring("kf", [128, 1024], BF16)
    QT_r = ring("QT", [128, 8, 128], BF16)
    QTf_r = ring("QTf", [128, 8, 128], BF16)
    QTb_r = ring("QTb", [128, 8, 128], BF16)
    KT_r = ring("KT", [128, 8, 128], BF16)
    scm_r = ring("scm", [128, 8, 128], BF16)
    if CHECK_SBUF:
        assert wkv_region[0] <= arena_after_wkv, (wkv_region[0], arena_after_wkv)
    Ff32, Ff32_b = mkb([128, 1024], F32, "Ff32", pb_at)
    Fbf_r = ring("Fbf", [128, 1024], BF16, pb_at)
    scr4_r = ring("scr4", [128, 1024], F32, pb_at)
    on_r = ring("on", [128, 1024], F32, pb_at)
    yb_r = ring("y", [128, 2048], BF16, pb_at)
    yT_r = ring("yT", [128, 16, 128], BF16, pb_at)
    plT_r = ring("plT", [128, 8, 128], BF16, pb_at)
    xres_r = ring("xres", [128, 1024], F32, pb_at)
    st_r = ring("st", [128, 64], F32, pb_at)
    ggtab, ggtab_b = mkb([128, 1024], F32, "ggtab", pb_at)
    hu, hu_b = mkb([128, 1024], BF16, "hu", pb_at)
    if HALO_MERGE:
        sch.op("pool", lambda e: e.memset(hu[:, :], 0.0), w=[hu_b])
    ysc = {i: Buf(f"yout{i}") for i in range(nseq)}

    def frontB(seq, c, hs, xslot):
        sch.disabled = STAGE < 3
        if SPILL_HT:
            ha, hb_ = hTr(hs)
            sch.dma("sp", hT[hs][:, :, :].rearrange("p a b -> p (a b)"), hscr[seq].ap()[c * 128:(c + 1) * 128, :],
                    r=[scrB[("h", seq)]], w=[ha, hb_], key=ha)
        else:
            front(seq, c, hs, xslot)
            front_done()
        n = nchs[seq]
        ut, ub = uring[c % 3]
        if SPILL_U:
            sch.dma("sp", ut[:, :], uscr[seq].ap()[c * 128:(c + 1) * 128, :], r=[scrB[("u", seq)]], w=[ub])
            return
        pu, pub = palloc(2)
        for half in range(2):
            def mm(e, half=half, pu=pu, hs=hs):
                last = None
                for dc in range(8):
                    last = e.matmul(pu[:, half * 512:(half + 1) * 512], hT[hs][:, dc, :], wq_sb[:, dc, half * 512:(half + 1) * 512],
                                    start=(dc == 0), stop=(dc == 7))
                return last
            sch.op("pe", mm, r=hTr(hs) + [B_["wq"]], w=[pub[half]])
            sch.op("act", lambda e, half=half, pu=pu, ut=ut: e.copy(out=ut[:, half * 512:(half + 1) * 512], in_=pu[:, half * 512:(half + 1) * 512]),
                   r=[pub[half]], w=[ub])

    def loadsB(seq, c):
        sch.disabled = STAGE < 3
        sl = c % 2
        sch.dma("sp", rt[sl][:, :], ropescr.ap()[c * 128:(c + 1) * 128, :], r=[B_["ropescr"]], w=[rt_b[sl]])
        gc = gbase[0] + c
        krB, krB_b = krB_r[gc]
        vB, vB_b = vB_r[gc]
        BbfB, BbfB_b = BbfB_r[gc]
        sch.dma("sp", krB[:, :], krscr[seq].ap()[c * 128:(c + 1) * 128, :], r=[scrB[("kr", seq)]], w=[krB_b])
        sch.dma("sp", vB[:, :], vscr[seq].ap()[c * 128:(c + 1) * 128, :], r=[scrB[("v", seq)]], w=[vB_b])
        sch.dma("sp", BbfB[:, :], bscr[seq].ap()[c * 128:(c + 1) * 128, :], r=[scrB[("b", seq)]], w=[BbfB_b])
        if SPILL_KT:
            KT_l, KT_lb = KT_r[gc]
            sch.dma("sp", KT_l[:, :, :].rearrange("p a b -> p (a b)"), ktscr[seq].ap()[c * 128:(c + 1) * 128, :],
                    r=[scrB[("kt", seq)]], w=[KT_lb])
        xr, xrb = xres_r[gc]
        sch.dma("sp", xr[:, :], xs[seq].ap()[c * 128:(c + 1) * 128, :], w=[xrb])

    def backB(seq, c, hs):
        n = nchs[seq]
        sl = c % 2
        first, last_c = (c == 0), (c == n - 1)
        gc = gbase[0] + c
        qr, qr_b = qr_r[gc]; sz, sz_b = sz_r[gc]; krB, krB_b = krB_r[gc]; vB, vB_b = vB_r[gc]
        BbfB, BbfB_b = BbfB_r[gc]; kf, kf_b = kf_r[gc]; QT, QT_b = QT_r[gc]; QTf, QTf_b = QTf_r[gc]
        QTb, QTb_b = QTb_r[gc]; KT, KT_b = KT_r[gc]; scm, scm_b = scm_r[gc]
        Fbf, Fbf_b = Fbf_r[gc]; Fbf_n, Fbf_nb = Fbf_r[gc + 1]
        scr4, scr4_b = scr4_r[gc]; on, on_b = on_r[gc]; yb, yb_b = yb_r[gc]; yT, yT_b = yT_r[gc]
        plT, plT_b = plT_r[gc]; st, st_b = st_r[gc]
        szp_b, szr_b = aux(sz_b, "p"), aux(sz_b, "r")
        ybp_b, ybr_b = aux(yb_b, "p"), aux(yb_b, "r")
        yTp_b, yTr_b = aux(yT_b, "p"), aux(yT_b, "r")
        if STAGE < 4:
            sch.disabled = True
        pq, pqb = palloc(2)
        pz0, pz0b = palloc(2)
        pz1, pz1b = palloc(2)
        for (pp, ppb, colbase) in ((pq, pqb, 1024), (pz0, pz0b, 2048), (pz1, pz1b, 3072)):
            for half in range(2):
                col0 = colbase + half * 512

                def mm(e, pp=pp, half=half, col0=col0):
                    last = None
                    for dc in range(8):
                        last = e.matmul(pp[:, half * 512:(half + 1) * 512], hT[hs][:, dc, :], wq_sb[:, dc, col0:col0 + 512],
                                        start=(dc == 0), stop=(dc == 7))
                    return last
                sch.op("pe", mm, r=hTr(hs) + [B_["wq"]], w=[ppb[half]])
        rope(pq, pqb, qr, qr_b, rt[sl], rt_b[sl], None)
        for zi, (pz, pzb) in enumerate(((pz0, pz0b), (pz1, pz1b))):
            for half in range(2):
                sch.op("act", lambda e, zi=zi, half=half, pz=pz: e.activation(out=sz[:, zi * 1024 + half * 512: zi * 1024 + (half + 1) * 512],
                                                                              in_=pz[:, half * 512:(half + 1) * 512], func=AF.Silu),
                       r=[pzb[half]], w=[szp_b if zi == 0 else szr_b])
        if STAGE < 5:
            sch.disabled = True
        sch.cur_prio = RET_PRIO
        if not last_c:
            sch.op(KF_ENG, lambda e: e.tensor_tensor(out=v3(kf[:, :]), in0=v3(krB[:, :]), in1=bc_h(kdf), op=ALU.mult),
                   r=[krB_b, B_["kd"]], w=[kf_b])
        ptq, ptqb = palloc(1)
        ptk, ptkb = palloc(1) if not SPILL_KT else (None, None)
        ptqv = lambda: ptq.bank(0).bitcast(BF16).rearrange("p (h t) -> p h t", h=8)
        ptkv = lambda: ptk.bank(0).bitcast(BF16).rearrange("p (h t) -> p h t", h=8)

        def trq(e):
            last = None
            for h in range(8):
                last = e.transpose(ptqv()[:, h, :], qr[:, h * 128:(h + 1) * 128], ident[:, :])
            return last

        def trk(e):
            last = None
            for h in range(8):
                last = e.transpose(ptkv()[:, h, :], krB[:, h * 128:(h + 1) * 128], ident[:, :])
            return last
        if SUB < 1:
            sch.disabled = True
        sch.op("pe", trq, r=[qr_b, B_["ident"]], w=ptqb)
        if not SPILL_KT:
            sch.op("pe", trk, r=[krB_b, B_["ident"]], w=ptkb)
        if SUB < 2:
            sch.disabled = True
        sch.op("act", lambda e: e.copy(out=QT[:, :, :], in_=ptqv()), r=ptqb, w=[QT_b])
        if not SPILL_KT:
            sch.op("act", lambda e: e.copy(out=KT[:, :, :], in_=ptkv()), r=ptkb, w=[KT_b])
        if SUB < 3:
            sch.disabled = True
        if not first:
            sch.op("dve", lambda e: e.tensor_tensor(out=QTf[:, :, :], in0=QT[:, :, :], in1=af_tab[:, :, :], op=ALU.mult),
                   r=[QT_b, B_["aftab"]], w=[QTf_b])
        if not last_c:
            sch.op(QTB_ENG, lambda e: e.tensor_tensor(out=QTb[:, :, :], in0=QT[:, :, :], in1=ab_tab[:, :, :], op=ALU.mult),
                   r=[QT_b, B_["abtab"]], w=[QTb_b])
        if STAGE < 6:
            sch.disabled = True
        psc, pscb_ = palloc(2)
        for half in range(2):
            def scmm(e, half=half, psc=psc):
                last = None
                for hh in range(4):
                    h = half * 4 + hh
                    last = e.matmul(psc[:, h * 128:(h + 1) * 128], KT[:, h, :], QT[:, h, :], start=True, stop=True)
                return last
            sch.op("pe", scmm, r=[KT_b, QT_b], w=[pscb_[half]])
            sch.op("dve", lambda e, half=half, psc=psc: e.tensor_tensor(out=scm[:, half * 4:(half + 1) * 4, :],
                                                                        in0=psc[:, half * 512:(half + 1) * 512].rearrange("p (h t) -> p h t", h=4),
                                                                        in1=dtot[:, half * 4:(half + 1) * 4, :], op=ALU.mult),
                   r=[pscb_[half], B_["dtot"]], w=[scm_b])
        po, pob = palloc(2)
        for half in range(2):
            def omm(e, half=half, po=po):
                last = None
                for hh in range(4):
                    h = half * 4 + hh
                    terms = [(scm[:, h, :], vB[:, h * 128:(h + 1) * 128])]
                    if not first:
                        terms.append((QTf[:, h, :], Fbf[:, h * 128:(h + 1) * 128]))
                    if not last_c:
                        terms.append((QTb[:, h, :], BbfB[:, h * 128:(h + 1) * 128]))
                    for ti, (l_, r_) in enumerate(terms):
                        last = e.matmul(po[:, h * 128:(h + 1) * 128], l_, r_, start=(ti == 0), stop=(ti == len(terms) - 1))
                return last
            rr = [scm_b, vB_b]
            if not first:
                rr += [QTf_b, Fbf_b]
            if not last_c:
                rr += [QTb_b, BbfB_b]
            sch.op("pe", omm, r=rr, w=[pob[half]])
        if STAGE < 7:
            sch.disabled = True
        sch.cur_prio = FUPD_PRIO
        if not last_c:
            pkv, pkvb = palloc(2)
            for half in range(2):
                def kvmm(e, half=half, pkv=pkv):
                    last = None
                    for hh in range(4):
                        h = half * 4 + hh
                        last = e.matmul(pkv[:, h * 128:(h + 1) * 128], kf[:, h * 128:(h + 1) * 128], vB[:, h * 128:(h + 1) * 128],
                                        start=True, stop=True)
                    return last
                sch.op("pe", kvmm, r=[kf_b, vB_b], w=[pkvb[half]])
            if first:
                for half in range(2):
                    sch.op("dve", lambda e, half=half, pkv=pkv: e.tensor_copy(out=Ff32[:, half * 512:(half + 1) * 512],
                                                                              in_=pkv[:, half * 512:(half + 1) * 512]),
                           r=[pkvb[half]], w=[Ff32_b])
            else:
                if STATE_STT:
                    for h in range(8):
                        sch.op("dve", lambda e, h=h, pkv=pkv: e.scalar_tensor_tensor(out=Ff32[:, h * 128:(h + 1) * 128],
                                                                                     in0=Ff32[:, h * 128:(h + 1) * 128], scalar=gcf[:, h:h + 1],
                                                                                     in1=pkv[:, h * 128:(h + 1) * 128],
                                                                                     op0=ALU.mult, op1=ALU.add),
                               r=[pkvb[h // 4], Ff32_b, B_["kd"]], w=[Ff32_b])
                else:
                    sch.op(FMUL_ENG, lambda e: e.tensor_tensor(out=v3(Ff32[:, :]), in0=v3(Ff32[:, :]), in1=bc_h(gcf), op=ALU.mult),
                           r=[Ff32_b, B_["kd"]], w=[Ff32_b])
                    for half in range(2):
                        sch.op("dve", lambda e, half=half, pkv=pkv: e.tensor_tensor(out=Ff32[:, half * 512:(half + 1) * 512],
                                                                                    in0=pkv[:, half * 512:(half + 1) * 512],
                                                                                    in1=Ff32[:, half * 512:(half + 1) * 512], op=ALU.add),
                               r=[pkvb[half], Ff32_b], w=[Ff32_b])
            sch.op("act", lambda e: e.copy(out=Fbf_n[:, :], in_=Ff32[:, :]), r=[Ff32_b], w=[Fbf_nb])
        if STAGE < 8:
            sch.disabled = True
        sch.cur_prio = NORM_PRIO
        onh = [aux(on_b, "0"), aux(on_b, "1")]
        for half in range(2):
            sch.op("act", lambda e, half=half, po=po: e.copy(out=on[:, half * 512:(half + 1) * 512], in_=po[:, half * 512:(half + 1) * 512]),
                   r=[pob[half]], w=[onh[half]])
        sch.op("dve", lambda e: e.tensor_reduce(out=st[:, 0:8], in_=v3(on[:, :]), axis=AX.X, op=ALU.add), r=onh, w=[st_b])
        sch.op("act", lambda e: e.activation(out=scr4[:, :], in_=on[:, :], func=AF.Square), r=onh, w=[scr4_b])
        sch.op("dve", lambda e: e.tensor_reduce(out=st[:, 8:16], in_=v3(scr4[:, :]), axis=AX.X, op=ALU.add), r=[scr4_b], w=[st_b])
        sch.op("dve", lambda e: e.tensor_tensor(out=st[:, 24:32], in0=st[:, 0:8], in1=st[:, 0:8], op=ALU.mult), r=[st_b], w=[st_b])
        sch.op("dve", lambda e: e.tensor_scalar(out=st[:, 16:24], in0=st[:, 8:16], scalar1=1.0 / HD, scalar2=EPS,
                                                op0=ALU.mult, op1=ALU.add), r=[st_b], w=[st_b])
        sch.op("dve", lambda e: e.scalar_tensor_tensor(out=st[:, 32:40], in0=st[:, 24:32], scalar=-1.0 / (HD * HD), in1=st[:, 16:24],
                                                       op0=ALU.mult, op1=ALU.add), r=[st_b], w=[st_b])
        sch.op("pool", lambda e: e.tensor_tensor(out=st[:, 40:48], in0=st[:, 32:40], in1=mhalf[:, 0:8], op=ALU.pow), r=[st_b], w=[st_b])
        sch.op("dve", lambda e: e.scalar_tensor_tensor(out=st[:, 48:56], in0=st[:, 0:8], scalar=-1.0 / HD, in1=st[:, 40:48],
                                                       op0=ALU.mult, op1=ALU.mult), r=[st_b], w=[st_b])
        for h in range(8):
            half = h // 4
            if half == 0:
                sch.op("act", lambda e, h=h: e.activation(out=on[:, h * 128:(h + 1) * 128], in_=on[:, h * 128:(h + 1) * 128],
                                                          func=AF.Identity, scale=st[:, 40 + h:41 + h], bias=st[:, 48 + h:49 + h]),
                       r=[onh[0], st_b], w=[onh[0]])
            else:
                sch.op("dve", lambda e, h=h: e.tensor_scalar(out=on[:, h * 128:(h + 1) * 128], in0=on[:, h * 128:(h + 1) * 128],
                                                             scalar1=st[:, 40 + h:41 + h], scalar2=st[:, 48 + h:49 + h],
                                                             op0=ALU.mult, op1=ALU.add),
                       r=[onh[1], st_b], w=[onh[1]])
        if SUB2 < 3:
            sch.disabled = True
        sch.op("dve", lambda e: e.tensor_tensor(out=yb[:, 1024:2048], in0=on[:, :], in1=sz[:, 1024:2048], op=ALU.mult),
               r=[aux(on_b, "0"), aux(on_b, "1"), szr_b], w=[ybr_b])
        if STAGE < 9:
            sch.disabled = True
        sch.cur_prio = POOL_PRIO
        if FUSE_POOL:
            use_hu = HALO_MERGE and (not first) and (not last_c)
            if use_hu:
                sch.op("act", lambda e: e.copy(out=hu[0:32, :], in_=uring[(c + 1) % 3][0][0:32, :]), r=[uring[(c + 1) % 3][1]], w=[hu_b])
                sch.op("act", lambda e: e.copy(out=hu[96:128, :], in_=uring[(c - 1) % 3][0][96:128, :]), r=[uring[(c - 1) % 3][1]], w=[hu_b])
            pyp, pypb = palloc(2)
            for half in range(2):
                def ypmm(e, half=half, pyp=pyp):
                    last = None
                    for g2 in range(2):
                        gi = half * 2 + g2
                        terms = []
                        if use_hu:
                            terms.append((hu, gi * 6 + 5))
                        elif not first:
                            terms.append((uring[(c - 1) % 3][0], gi * 6 + 0))
                        terms.append((uring[c % 3][0], gi * 6 + (3 if first else (4 if last_c else 1))))
                        if not last_c and not use_hu:
                            terms.append((uring[(c + 1) % 3][0], gi * 6 + 2))
                        for ti, (ut, bi) in enumerate(terms):
                            last = e.matmul(pyp[:, gi * 256:(gi + 1) * 256], bands[:, bi, :], ut[:, gi * 256:(gi + 1) * 256],
                                            start=(ti == 0), stop=(ti == len(terms) - 1))
                    return last
                rr = [uring[c % 3][1], B_["bands"]]
                if use_hu:
                    rr.append(hu_b)
                else:
                    if not first:
                        rr.append(uring[(c - 1) % 3][1])
                    if not last_c:
                        rr.append(uring[(c + 1) % 3][1])
                sch.op("pe", ypmm, r=rr, w=[pypb[half]])
                sch.op("dve", lambda e, half=half, pyp=pyp: e.tensor_tensor(out=yb[:, half * 512:(half + 1) * 512],
                                                                            in0=pyp[:, half * 512:(half + 1) * 512],
                                                                            in1=sz[:, half * 512:(half + 1) * 512], op=ALU.mult),
                       r=[pypb[half], szp_b], w=[ybp_b])
        else:
            use_hu = HALO_MERGE and (not first) and (not last_c)
            if use_hu:
                sch.op("act", lambda e: e.copy(out=hu[0:32, :], in_=uring[(c + 1) % 3][0][0:32, :]), r=[uring[(c + 1) % 3][1]], w=[hu_b])
                sch.op("act", lambda e: e.copy(out=hu[96:128, :], in_=uring[(c - 1) % 3][0][96:128, :]), r=[uring[(c - 1) % 3][1]], w=[hu_b])
            ppl, pplb = palloc(2)
            for half in range(2):
                def plmm(e, half=half, ppl=ppl):
                    last = None
                    for cc4 in range(4):
                        cc = half * 4 + cc4
                        gi = cc // 2
                        terms = []
                        if use_hu:
                            terms.append((hu, gi * 6 + 5))
                        elif not first:
                            terms.append((uring[(c - 1) % 3][0], gi * 6 + 0))
                        terms.append((uring[c % 3][0], gi * 6 + (3 if first else (4 if last_c else 1))))
                        if not last_c and not use_hu:
                            terms.append((uring[(c + 1) % 3][0], gi * 6 + 2))
                        for ti, (ut, bi) in enumerate(terms):
                            last = e.matmul(ppl[:, cc * 128:(cc + 1) * 128], ut[:, cc * 128:(cc + 1) * 128], bands[:, bi, :],
                                            start=(ti == 0), stop=(ti == len(terms) - 1))
                    return last
                rr = [uring[c % 3][1], B_["bands"]]
                if use_hu:
                    rr.append(hu_b)
                else:
                    if not first:
                        rr.append(uring[(c - 1) % 3][1])
                    if not last_c:
                        rr.append(uring[(c + 1) % 3][1])
                sch.op("pe", plmm, r=rr, w=[pplb[half]])
                sch.op("act", lambda e, half=half, ppl=ppl: e.copy(out=plT[:, half * 4:(half + 1) * 4, :],
                                                                   in_=ppl[:, half * 512:(half + 1) * 512].rearrange("p (c t) -> p c t", c=4)),
                       r=[pplb[half]], w=[plT_b])
            pyp, pypb = palloc(2)
            for half in range(2):
                def ypmm(e, half=half, pyp=pyp):
                    last = None
                    for g2 in range(2):
                        gi = half * 2 + g2
                        for cc2 in range(2):
                            cc = gi * 2 + cc2
                            last = e.matmul(pyp[:, gi * 256:(gi + 1) * 256], plT[:, cc, :], poolw_sb[:, cc, :],
                                            start=(cc2 == 0), stop=(cc2 == 1))
                    return last
                sch.op("pe", ypmm, r=[plT_b, B_["poolw"]], w=[pypb[half]])
                sch.op("dve", lambda e, half=half, pyp=pyp: e.tensor_tensor(out=yb[:, half * 512:(half + 1) * 512],
                                                                            in0=pyp[:, half * 512:(half + 1) * 512],
                                                                            in1=sz[:, half * 512:(half + 1) * 512], op=ALU.mult),
                       r=[pypb[half], szp_b], w=[ybp_b])
        if STAGE < 10:
            sch.disabled = True
        sch.cur_prio = 0
        pty, ptyb = palloc(2)
        ptyh = [(lambda i=i: pty.bank(i).bitcast(BF16).rearrange("p (e t) -> p e t", e=8)) for i in range(2)]
        for half in range(2):
            def trY(e, half=half):
                last = None
                for e8 in range(8):
                    ec = half * 8 + e8
                    last = e.transpose(ptyh[half]()[:, e8, :], yb[:, ec * 128:(ec + 1) * 128], ident[:, :])
                return last
            sch.op("pe", trY, r=[ybp_b if half == 0 else ybr_b, B_["ident"]], w=[ptyb[half]])
        sch.op("act", lambda e: e.copy(out=yT[:, 0:8, :], in_=ptyh[0]()), r=[ptyb[0]], w=[yTp_b])
        sch.op("dve", lambda e: e.tensor_copy(out=yT[:, 8:16, :], in_=ptyh[1]()), r=[ptyb[1]], w=[yTr_b])
        pout, poutb = palloc(2)
        sch.cur_prio = FINAL_PRIO
        for half in range(2):
            def outmm(e, half=half, pout=pout):
                last = None
                for ec in range(16):
                    last = e.matmul(pout[:, half * 512:(half + 1) * 512], yT[:, ec, :], wout_sb[:, ec, half * 512:(half + 1) * 512],
                                    start=(ec == 0), stop=(ec == 15))
                return last
            sch.op("pe", outmm, r=[yTp_b, yTr_b, B_["wout"]], w=[poutb[half]])
            sch.op("act", lambda e, half=half, pout=pout: e.activation(out=junk[:, half * 512:(half + 1) * 512], in_=pout[:, half * 512:(half + 1) * 512],
                                                                       func=AF.Square, accum_out=st[:, 56 + half:57 + half]),
                   r=[poutb[half]], w=[st_b, junk_b])
        sch.op("dve", lambda e: e.tensor_tensor(out=st[:, 58:59], in0=st[:, 56:57], in1=st[:, 57:58], op=ALU.add), r=[st_b], w=[st_b])
        sch.op("dve", lambda e: e.tensor_scalar(out=st[:, 59:60], in0=st[:, 58:59], scalar1=1.0 / D, scalar2=EPS,
                                                op0=ALU.mult, op1=ALU.add), r=[st_b], w=[st_b])
        sch.op("pool", lambda e: e.tensor_tensor(out=st[:, 60:61], in0=st[:, 59:60], in1=mhalf[:, 0:1], op=ALU.pow), r=[st_b], w=[st_b])
        for half in range(2):
            sch.op("dve", lambda e, half=half, pout=pout: e.scalar_tensor_tensor(out=scr4[:, half * 512:(half + 1) * 512],
                                                                                 in0=pout[:, half * 512:(half + 1) * 512], scalar=st[:, 60:61],
                                                                                 in1=ggtab[:, half * 512:(half + 1) * 512],
                                                                                 op0=ALU.mult, op1=ALU.mult),
                   r=[poutb[half], st_b, ggtab_b], w=[scr4_b])
        xr, xrb = xres_r[gc]
        sch.op(FIN_ENG, lambda e: e.tensor_tensor(out=xr[:, :], in0=xr[:, :], in1=scr4[:, :], op=ALU.add), r=[xrb, scr4_b], w=[xrb])
        sch.dma("sp", ys[seq].ap()[c * 128:(c + 1) * 128, :], xr[:, :], r=[xrb], w=[ysc[seq]], key=xrb)
        sch.cur_prio = 0

    itB = 0
    gbase = [0]
    for seq in range(nseq):
        gbase[0] = itB
        n = nchs[seq]
        sch.dma("sp", ggtab[:, :], ggscr.ap()[seq * 128:(seq + 1) * 128, :], r=[B_["ggscr"]], w=[ggtab_b])
        frontB(seq, 0, itB % 3, itB % 2)
        loadsB(seq, 0)
        for c in range(n):
            hs_c = (itB + c) % 3
            if c + 1 < n:
                frontB(seq, c + 1, (itB + c + 1) % 3, (itB + c + 1) % 2)
            backB(seq, c, hs_c)
            if c + 1 < n:
                loadsB(seq, c + 1)
        itB += n

    sch.disabled = False
    if WARM:
        sch.warm_fn = lambda e: e.matmul(psum[:, 7 * 512:8 * 512], ident[:, :], bands[:, 0:4, :], start=True, stop=True)
    sch.finish()
    sch.sbuf_free = sb_hi - sbuf_peak[0]

    with nc.Block() as block:
        @block.tensor
        def _(e):
            sch.emit("pe", e)

        @block.scalar
        def _(e):
            sch.emit("act", e)

        @block.vector
        def _(e):
            sch.emit("dve", e)

        @block.gpsimd
        def _(e):
            sch.emit("pool", e)

        @block.sync
        def _(e):
            sch.emit("sp", e)
    return nc


_PROG_CACHE = {}


def _get_prog(S_list):
    key = tuple(S_list)
    if key not in _PROG_CACHE:
        _PROG_CACHE[key] = build_program(list(S_list))
    return _PROG_CACHE[key]


def kernel(x_prompt, x_sample, c_prompt, c_sample, ada_w, ada_b, norm_pre, norm_post,
           w_in, pool_w, pool_scale, ret_decay_fwd, ret_decay_bwd, w_out):
    f = lambda a: np.ascontiguousarray(np.asarray(a, dtype=np.float32))
    x_prompt, x_sample = f(x_prompt), f(x_sample)
    nb = x_prompt.shape[0]
    assert nb == N_CORES and x_sample.shape[0] == N_CORES
    S0, S1 = x_prompt.shape[1], x_sample.shape[1]
    nc = _get_prog((S0, S1))
    c_prompt, c_sample = f(c_prompt), f(c_sample)
    shared = {
        "ada_w": f(ada_w)[0], "ada_b": f(ada_b), "norm_pre": f(norm_pre), "norm_post": f(norm_post),
        "w_in": f(w_in)[0], "pool_w": f(pool_w)[0], "pool_scale": f(pool_scale),
        "dec_f": f(ret_decay_fwd), "dec_b": f(ret_decay_bwd), "w_out": f(w_out)[0],
    }
    in_maps = []
    for i in range(N_CORES):
        m = dict(shared)
        m["x0"] = x_prompt[i]
        m["x1"] = x_sample[i]
        m["cvec"] = np.ascontiguousarray(np.stack([c_prompt[i], c_sample[i]], axis=0))
        in_maps.append(m)
    res = run_bass_kernel_spmd(nc, in_maps, core_ids=list(range(N_CORES)))
    y0 = np.stack([np.asarray(r["y0"], dtype=np.float32) for r in res.results], axis=0)
    y1 = np.stack([np.asarray(r["y1"], dtype=np.float32) for r in res.results], axis=0)
    return (y0, y1)
```
